# Optimizing a Trainium2 kernel written in Bass

```python
import math
import jax, jax.numpy as jnp
from jax import lax
import numpy as np

D_MODEL = 1024
BATCH = 4
SEQ = 8192
DEPTH = 4

N_MIXERS = 2
N_ATTN_LAYERS = (DEPTH + 1) // 2
N_SSM_LAYERS = DEPTH // 2

D_FF = 2816

ATTN_HEADS = 16
ATTN_HEAD_DIM = 64
KV_RANK = 256
IDX_HEADS = 8
IDX_HEAD_DIM = 64
TOPK_MAX = 256
Q_BLOCK = 128
ATTN_Q_DIM = ATTN_HEADS * ATTN_HEAD_DIM
ATTN_IN = ATTN_Q_DIM + KV_RANK + IDX_HEADS * IDX_HEAD_DIM + IDX_HEAD_DIM + IDX_HEADS

SSM_INNER = 2 * D_MODEL
SSM_HEAD_DIM = 64
SSM_HEADS = SSM_INNER // SSM_HEAD_DIM
SSM_GROUPS = 4
SSM_STATE = 128
SSM_CONV = 4
SSM_CHUNK = 128
SSM_CONV_DIM = SSM_INNER + 2 * SSM_GROUPS * SSM_STATE
SSM_IN = SSM_INNER + SSM_CONV_DIM + SSM_HEADS

DEEPNORM_ALPHA = (2.0 * DEPTH) ** 0.25
DEEPNORM_BETA = (8.0 * DEPTH) ** -0.25
LN_EPS = 1e-5
RMS_EPS = 1e-6

kernel_name = 'hybrid_dsa_ssd_macaron_deepnorm'


def layer_norm(x, g, b):
    xf = x.astype(jnp.float32)
    mu = jnp.mean(xf, -1, keepdims=True)
    var = jnp.mean(jnp.square(xf - mu), -1, keepdims=True)
    return ((xf - mu) * lax.rsqrt(var + LN_EPS) * g.astype(jnp.float32) + b.astype(jnp.float32)).astype(x.dtype)


def rms_norm(x, g):
    xf = x.astype(jnp.float32)
    ms = jnp.mean(jnp.square(xf), -1, keepdims=True)
    return (xf * lax.rsqrt(ms + RMS_EPS) * g.astype(jnp.float32)).astype(x.dtype)


def swiglu_ffn(x, w_in, w_out):
    gate, up = jnp.split(x @ w_in, 2, axis=-1)
    return (jax.nn.silu(gate) * up) @ w_out


def dsa_mixer(x, w_in, kv_norm_g, w_uk, w_uv, w_out):
    bsz, seq, _ = x.shape
    topk = min(TOPK_MAX, seq // 4)
    o1 = ATTN_Q_DIM
    o2 = o1 + KV_RANK
    o3 = o2 + IDX_HEADS * IDX_HEAD_DIM
    o4 = o3 + IDX_HEAD_DIM
    q, c_kv, q_idx, k_idx, w_idx = jnp.split(x @ w_in, [o1, o2, o3, o4], axis=-1)
    q = q.reshape(bsz, seq, ATTN_HEADS, ATTN_HEAD_DIM)
    c_kv = rms_norm(c_kv, kv_norm_g)
    q_idx = q_idx.reshape(bsz, seq, IDX_HEADS, IDX_HEAD_DIM)
    w_idx = w_idx * (IDX_HEADS ** -0.5)
    n_blk = seq // Q_BLOCK
    key_pos = jnp.arange(seq)

    def to_blocks(t):
        return jnp.moveaxis(t.reshape(bsz, n_blk, Q_BLOCK, *t.shape[2:]), 1, 0)

    def block(args):
        blk, q_b, qi_b, wi_b = args
        q_pos = blk * Q_BLOCK + jnp.arange(Q_BLOCK)
        s = jnp.einsum('bthd,bsd->bths', qi_b, k_idx).astype(jnp.float32) * (IDX_HEAD_DIM ** -0.5)
        score = jnp.einsum('bths,bth->bts', jax.nn.relu(s), wi_b.astype(jnp.float32))
        causal = key_pos[None, :] <= q_pos[:, None]
        score = jnp.where(causal[None], score, -jnp.inf)
        _, idx = lax.top_k(score, topk)
        valid = idx <= q_pos[None, :, None]
        c_sel = jax.vmap(lambda c, i: c[i])(c_kv, idx)
        q_abs = jnp.einsum('bthd,hdr->bthr', q_b, w_uk)
        logits = jnp.einsum('bthr,btkr->bthk', q_abs, c_sel).astype(jnp.float32) * (ATTN_HEAD_DIM ** -0.5)
        logits = jnp.where(valid[:, :, None, :], logits, -jnp.inf)
        p = jax.nn.softmax(logits, axis=-1).astype(c_sel.dtype)
        o_lat = jnp.einsum('bthk,btkr->bthr', p, c_sel)
        o = jnp.einsum('bthr,hrd->bthd', o_lat, w_uv)
        return o.reshape(bsz, Q_BLOCK, ATTN_Q_DIM)

    out = lax.map(block, (jnp.arange(n_blk), to_blocks(q), to_blocks(q_idx), to_blocks(w_idx)))
    out = jnp.moveaxis(out, 0, 1).reshape(bsz, seq, ATTN_Q_DIM)
    return out @ w_out


def ssd_scan(x, dt, a, b_in, c_in):
    bsz, seq, n_h, p_dim = x.shape
    nc = seq // SSM_CHUNK
    q_len = SSM_CHUNK
    g_n = SSM_GROUPS
    r_n = n_h // g_n
    xdt = (x.astype(jnp.float32) * dt[..., None]).reshape(bsz, nc, q_len, g_n, r_n, p_dim)
    a_dt = (dt * a).reshape(bsz, nc, q_len, g_n, r_n)
    a_cs_l = jnp.cumsum(a_dt, axis=2)
    a_cs = jnp.moveaxis(a_cs_l, 2, -1)
    bm = b_in.astype(jnp.float32).reshape(bsz, nc, q_len, g_n, SSM_STATE)
    cm = c_in.astype(jnp.float32).reshape(bsz, nc, q_len, g_n, SSM_STATE)
    seg = a_cs[..., :, None] - a_cs[..., None, :]
    tri = jnp.tril(jnp.ones((q_len, q_len), dtype=bool))
    decay = jnp.exp(jnp.where(tri, seg, -jnp.inf))
    cb = jnp.einsum('bclgn,bcsgn->bcgls', cm, bm)
    y_diag = jnp.einsum('bcgrls,bcsgrp->bclgrp', cb[:, :, :, None] * decay, xdt)
    decay_to_end = jnp.exp(a_cs_l[:, :, -1:] - a_cs_l)
    states = jnp.einsum('bclgn,bclgrp->bcgrpn', bm, xdt * decay_to_end[..., None])
    chunk_decay = jnp.exp(a_cs_l[:, :, -1])

    def step(h, inp):
        s_c, d_c = inp
        return h * d_c[..., None, None] + s_c, h

    h0 = jnp.zeros((bsz, g_n, r_n, p_dim, SSM_STATE), jnp.float32)
    _, prev = lax.scan(step, h0, (jnp.moveaxis(states, 1, 0), jnp.moveaxis(chunk_decay, 1, 0)))
    prev = jnp.moveaxis(prev, 0, 1)
    y_off = jnp.einsum('bclgn,bcgrpn->bclgrp', cm, prev) * jnp.exp(a_cs_l)[..., None]
    return (y_diag + y_off).reshape(bsz, seq, n_h, p_dim)


def ssd_mixer(x, w_in, conv_w, conv_b, dt_bias, a_log, d_skip, norm_g, w_out):
    bsz, seq, _ = x.shape
    z, xbc, dt = jnp.split(x @ w_in, [SSM_INNER, SSM_INNER + SSM_CONV_DIM], axis=-1)
    xbc_pad = jnp.pad(xbc, ((0, 0), (SSM_CONV - 1, 0), (0, 0)))
    conv = conv_b + xbc_pad[:, 0:seq] * conv_w[0]
    for k in range(1, SSM_CONV):
        conv = conv + xbc_pad[:, k:k + seq] * conv_w[k]
    xbc = jax.nn.silu(conv)
    xs, b_in, c_in = jnp.split(xbc, [SSM_INNER, SSM_INNER + SSM_GROUPS * SSM_STATE], axis=-1)
    xs = xs.reshape(bsz, seq, SSM_HEADS, SSM_HEAD_DIM)
    dt = jax.nn.softplus((dt + dt_bias).astype(jnp.float32))
    a = -jnp.exp(a_log.astype(jnp.float32))
    y = ssd_scan(xs, dt, a,
                 b_in.reshape(bsz, seq, SSM_GROUPS, SSM_STATE),
                 c_in.reshape(bsz, seq, SSM_GROUPS, SSM_STATE))
    y = y + xs.astype(jnp.float32) * d_skip.astype(jnp.float32)[:, None]
    yz = y.reshape(bsz, seq, SSM_INNER) * jax.nn.silu(z.astype(jnp.float32))
    yz = yz.reshape(bsz, seq, SSM_GROUPS, SSM_INNER // SSM_GROUPS)
    yz = yz * lax.rsqrt(jnp.mean(jnp.square(yz), -1, keepdims=True) + RMS_EPS)
    yz = (yz.reshape(bsz, seq, SSM_INNER) * norm_g.astype(jnp.float32)).astype(x.dtype)
    return yz @ w_out


def setup_inputs(seed: int = 0) -> dict:
    key = jax.random.key(seed)
    ks = jax.random.split(key, 20)
    f32 = jnp.float32
    D = D_MODEL
    nrm = lambda k, shape, s: jax.random.normal(k, shape, f32) * s
    x = jax.random.normal(ks[0], (BATCH, SEQ, D), f32)
    ln_g = 1.0 + nrm(ks[1], (DEPTH, 3, D), 0.02)
    ln_b = nrm(ks[2], (DEPTH, 3, D), 0.02)
    ffn1_w_in = nrm(ks[3], (DEPTH, D, 2 * D_FF), D ** -0.5)
    ffn1_w_out = nrm(ks[4], (DEPTH, D_FF, D), D_FF ** -0.5 * DEEPNORM_BETA)
    ffn2_w_in = nrm(ks[5], (DEPTH, D, 2 * D_FF), D ** -0.5)
    ffn2_w_out = nrm(ks[6], (DEPTH, D_FF, D), D_FF ** -0.5 * DEEPNORM_BETA)
    attn_w_in = nrm(ks[7], (N_ATTN_LAYERS, D, ATTN_IN), D ** -0.5)
    attn_kv_norm = 1.0 + nrm(ks[8], (N_ATTN_LAYERS, KV_RANK), 0.02)
    attn_w_uk = nrm(ks[9], (N_ATTN_LAYERS, ATTN_HEADS, ATTN_HEAD_DIM, KV_RANK), KV_RANK ** -0.5)
    attn_w_uv = nrm(ks[10], (N_ATTN_LAYERS, ATTN_HEADS, KV_RANK, ATTN_HEAD_DIM), KV_RANK ** -0.5)
    attn_w_out = nrm(ks[11], (N_ATTN_LAYERS, ATTN_Q_DIM, D), ATTN_Q_DIM ** -0.5 * DEEPNORM_BETA)
    ssm_w_in = nrm(ks[12], (N_SSM_LAYERS, D, SSM_IN), D ** -0.5)
    ssm_conv_w = nrm(ks[13], (N_SSM_LAYERS, SSM_CONV, SSM_CONV_DIM), SSM_CONV ** -0.5)
    ssm_conv_b = nrm(ks[14], (N_SSM_LAYERS, SSM_CONV_DIM), 0.01)
    dt0 = jnp.exp(jax.random.uniform(ks[15], (N_SSM_LAYERS, SSM_HEADS), f32,
                                     minval=math.log(1e-3), maxval=math.log(1e-1)))
    ssm_dt_bias = dt0 + jnp.log(-jnp.expm1(-dt0))
    ssm_a_log = jnp.log(jax.random.uniform(ks[16], (N_SSM_LAYERS, SSM_HEADS), f32, minval=1.0, maxval=16.0))
    ssm_d = 1.0 + nrm(ks[17], (N_SSM_LAYERS, SSM_HEADS), 0.02)
    ssm_norm_g = 1.0 + nrm(ks[18], (N_SSM_LAYERS, SSM_INNER), 0.02)
    ssm_w_out = nrm(ks[19], (N_SSM_LAYERS, SSM_INNER, D), SSM_INNER ** -0.5 * DEEPNORM_BETA)
    return {'x': x, 'ln_g': ln_g, 'ln_b': ln_b,
            'ffn1_w_in': ffn1_w_in, 'ffn1_w_out': ffn1_w_out,
            'ffn2_w_in': ffn2_w_in, 'ffn2_w_out': ffn2_w_out,
            'attn_w_in': attn_w_in, 'attn_kv_norm': attn_kv_norm,
            'attn_w_uk': attn_w_uk, 'attn_w_uv': attn_w_uv, 'attn_w_out': attn_w_out,
            'ssm_w_in': ssm_w_in, 'ssm_conv_w': ssm_conv_w, 'ssm_conv_b': ssm_conv_b,
            'ssm_dt_bias': ssm_dt_bias, 'ssm_a_log': ssm_a_log, 'ssm_d': ssm_d,
            'ssm_norm_g': ssm_norm_g, 'ssm_w_out': ssm_w_out}


def reference(x, ln_g, ln_b, ffn1_w_in, ffn1_w_out, ffn2_w_in, ffn2_w_out,
              attn_w_in, attn_kv_norm, attn_w_uk, attn_w_uv, attn_w_out,
              ssm_w_in, ssm_conv_w, ssm_conv_b, ssm_dt_bias, ssm_a_log, ssm_d,
              ssm_norm_g, ssm_w_out):
    for i in range(DEPTH):
        x = layer_norm(DEEPNORM_ALPHA * x + 0.5 * swiglu_ffn(x, ffn1_w_in[i], ffn1_w_out[i]), ln_g[i, 0], ln_b[i, 0])
        j = i // N_MIXERS
        if i % N_MIXERS == 0:
            m = dsa_mixer(x, attn_w_in[j], attn_kv_norm[j], attn_w_uk[j], attn_w_uv[j], attn_w_out[j])
        else:
            m = ssd_mixer(x, ssm_w_in[j], ssm_conv_w[j], ssm_conv_b[j], ssm_dt_bias[j],
                          ssm_a_log[j], ssm_d[j], ssm_norm_g[j], ssm_w_out[j])
        x = layer_norm(DEEPNORM_ALPHA * x + m, ln_g[i, 1], ln_b[i, 1])
        x = layer_norm(DEEPNORM_ALPHA * x + 0.5 * swiglu_ffn(x, ffn2_w_in[i], ffn2_w_out[i]), ln_g[i, 2], ln_b[i, 2])
    return x
```

```python
import contextlib
import numpy as np
import concourse.bass as bass
import concourse.mybir as mybir
from concourse.bass_utils import run_bass_kernel_spmd

F32 = mybir.dt.float32
BF16 = mybir.dt.bfloat16
AF = mybir.ActivationFunctionType
ALU = mybir.AluOpType
AX = mybir.AxisListType

D = 1024
DFF = 2816
DEPTH = 4
ALPHA = (2.0 * DEPTH) ** 0.25
LN_EPS = 1e-5
NCORES = 8

ENGS = ("pe", "dve", "act", "pool", "sp")
N_DMA_SEM = 20
N_CSEM = 14
CSEM_CH = 128


class Op:
    __slots__ = ("eng", "fn", "deps", "is_dma", "has_dep", "semval", "dsem", "dval", "prev_ring")

    def __init__(self, eng, fn, is_dma):
        self.eng = eng
        self.fn = fn
        self.is_dma = is_dma
        self.deps = []
        self.has_dep = False
        self.semval = None
        self.dsem = None
        self.dval = None
        self.prev_ring = None


def _key(k):
    if isinstance(k, tuple):
        return tuple(_key(e) for e in k)
    if isinstance(k, (str, int)):
        return k
    return k.name


class Sched:
    def __init__(self, nc):
        self.nc = nc
        self.ops = {e: [] for e in ENGS}
        self.last_w = {}
        self.readers = {}
        self.dma_count = {e: 0 for e in ENGS}
        self.dma_hist = {e: [] for e in ENGS}
        self.n = 0
        self.n_barriers = 0

    def _add(self, eng, fn, reads, writes, is_dma):
        op = Op(eng, fn, is_dma)
        reads = [_key(r) for r in reads]
        writes = [_key(w) for w in writes]
        deps = []
        for r in reads:
            w = self.last_w.get(r)
            if w is not None:
                deps.append(w)
        for w_ in writes:
            w = self.last_w.get(w_)
            if w is not None:
                deps.append(w)
            rl = self.readers.get(w_, ())
            lastc = {}
            for o in rl:
                if o.is_dma:
                    deps.append(o)
                else:
                    lastc[o.eng] = o
            deps.extend(lastc.values())
        seen = set()
        for d in deps:
            if id(d) in seen:
                continue
            seen.add(id(d))
            if (not d.is_dma) and (not is_dma) and d.eng == "pe" and eng == "pe":
                continue
            op.deps.append(d)
            d.has_dep = True
        for r in reads:
            self.readers.setdefault(r, []).append(op)
        for w_ in writes:
            self.last_w[w_] = op
            self.readers[w_] = []
        if is_dma:
            j = self.dma_count[eng]
            self.dma_count[eng] += 1
            op.dsem = j % N_DMA_SEM
            op.dval = 16 * (j // N_DMA_SEM + 1)
            hist = self.dma_hist[eng]
            if j >= N_DMA_SEM:
                op.prev_ring = hist[j - N_DMA_SEM]
            hist.append(op)
        self.ops[eng].append(op)
        self.n += 1
        return op

    def op(self, eng, fn, reads=(), writes=()):
        return self._add(eng, fn, reads, writes, False)

    def dma(self, eng, fn, reads=(), writes=()):
        return self._add(eng, fn, reads, writes, True)

    def barrier(self):
        lasts = []
        for e in ENGS:
            for o in reversed(self.ops[e]):
                if o.fn is None:
                    break
                if not o.is_dma:
                    lasts.append(o)
                    break
            lasts.extend(self.dma_hist[e][-N_DMA_SEM:])
        self.n_barriers += 1
        for e in ENGS:
            op = Op(e, None, False)
            op.semval = self.n_barriers
            for d in lasts:
                op.deps.append(d)
                d.has_dep = True
            self.ops[e].append(op)
        self.last_w = {}
        self.readers = {}

    def emit(self, final_wait_ops=()):
        nc = self.nc
        for o in final_wait_ops:
            o.has_dep = True
        for e in ENGS:
            c = 0
            for o in self.ops[e]:
                if o.fn is None:
                    c = 0
                    continue
                if o.is_dma:
                    continue
                if o.has_dep:
                    c += 1
                    o.semval = c
        self.max_semval = {e: max([o.semval or 0 for o in self.ops[e] if o.fn is not None and not o.is_dma] + [0]) for e in ENGS}
        self.max_dval = {e: max([o.dval or 0 for o in self.ops[e] if o.is_dma] + [0]) for e in ENGS}
        with contextlib.ExitStack() as st:
            csem = {e: [st.enter_context(nc.semaphore("cs_%s_%d" % (e, i))) for i in range(N_CSEM)]
                    for e in ENGS if e != "sp"}
            dsem = {e: [st.enter_context(nc.semaphore("ds_%s_%d" % (e, i))) for i in range(N_DMA_SEM)]
                    for e in ENGS if any(o.is_dma for o in self.ops[e])}
            bsem = [st.enter_context(nc.semaphore("bar%d" % i)) for i in range(2)]
            any_dma = {e: any(o.is_dma for o in self.ops[e]) for e in ENGS}
            block = st.enter_context(nc.Block())

            def run(e, eng):
                waited = {}

                def wait_for(d):
                    if d.is_dma:
                        k = ("d", d.eng, d.dsem)
                        s, v = dsem[d.eng][d.dsem], d.dval
                        if waited.get(k, 0) >= v:
                            return
                        waited[k] = v
                        eng.wait_ge(s, v)
                    else:
                        k = ("c", d.eng)
                        c = d.semval
                        if waited.get(k, 0) >= c:
                            return
                        waited[k] = c
                        ep = (c - 1) // CSEM_CH
                        eng.wait_ge(csem[d.eng][ep % N_CSEM], CSEM_CH * (ep // N_CSEM) + ((c - 1) % CSEM_CH) + 1)

                for o in self.ops[e]:
                    for d in o.deps:
                        wait_for(d)
                    if o.prev_ring is not None:
                        wait_for(o.prev_ring)
                    if o.fn is None:
                        k = o.semval
                        eng.sem_inc(bsem[0], 1)
                        eng.wait_ge(bsem[0], len(ENGS) * k)
                        if e in csem:
                            for s_ in csem[e]:
                                eng.sem_clear(s_)
                        eng.sem_inc(bsem[1], 1)
                        eng.wait_ge(bsem[1], len(ENGS) * k)
                        for k_ in [k_ for k_ in waited if k_[0] == "c"]:
                            del waited[k_]
                        continue
                    ins = o.fn(eng)
                    if o.is_dma:
                        ins.then_inc(dsem[e][o.dsem], 16)
                    elif o.has_dep:
                        ins.then_inc(csem[e][((o.semval - 1) // CSEM_CH) % N_CSEM], 1)
                if e == "pool":
                    for d in final_wait_ops:
                        wait_for(d)

            for e, reg in (("sp", block.sync), ("pe", block.tensor), ("dve", block.vector),
                           ("act", block.scalar), ("pool", block.gpsimd)):
                reg(lambda eng, e=e: run(e, eng))


class Ctx:
    def __init__(self, nc, st):
        self.nc = nc
        self.S = Sched(nc)
        self.st = st
        self.uid = 0
        self.psum = []
        self.psum_i = 0

    def sb(self, shape, dtype, name=None, st=None):
        self.uid += 1
        nm = "%s_%d" % (name or "t", self.uid)
        return (st or self.st).enter_context(self.nc.sbuf_tensor(nm, list(shape), dtype))

    def ps(self, shape, dtype, name=None, st=None):
        self.uid += 1
        nm = "%s_%d" % (name or "p", self.uid)
        return (st or self.st).enter_context(self.nc.psum_tensor(nm, list(shape), dtype))


def setup_consts(cx):
    nc, S = cx.nc, cx.S
    cx.ident_f = cx.sb([128, 128], F32, "identf")
    cx.ident_b = cx.sb([128, 128], BF16, "identb")
    cx.ones_f = cx.sb([128, 128], F32, "onesf")
    idf, idb, onf = cx.ident_f, cx.ident_b, cx.ones_f
    S.op("pool", lambda e: e.memset(onf[:], 1.0), writes=[onf])
    S.op("pool", lambda e: e.affine_select(out=idf[:], in_=onf[:], pattern=[[-1, 128]],
                                           compare_op=ALU.is_equal, fill=0.0, base=0,
                                           channel_multiplier=1), reads=[onf], writes=[idf])
    S.op("pool", lambda e: e.tensor_copy(out=idb[:], in_=idf[:]), reads=[idf], writes=[idb])


def emit_ffn(cx, x_in, x_out, w_in, w_out, g, b, ntok, scale_y=0.5):
    nc, S = cx.nc, cx.S
    T = 256
    NS = T // 128
    KD = D // 128
    KF = DFF // 128
    c_y = scale_y / ALPHA
    eps_p = LN_EPS / (ALPHA * ALPHA)
    with contextlib.ExitStack() as st:
        win = cx.sb([128, KD, 2 * DFF], BF16, "win", st)
        wout = cx.sb([128, KF, D], BF16, "wout", st)
        stg = [cx.sb([128, 1408], F32, "stg", st) for _ in range(2)]
        gb = cx.sb([128, D], F32, "gb", st)
        bb = cx.sb([128, D], F32, "bb", st)
        epst = cx.sb([128, 1], F32, "eps", st)
        xs = [cx.sb([128, D], F32, "xs", st) for _ in range(4)]
        xb = [cx.sb([128, D], BF16, "xb", st) for _ in range(2)]
        xT = [cx.sb([128, KD, T], BF16, "xT", st) for _ in range(2)]
        hT = [cx.sb([128, KF, T], BF16, "hT", st) for _ in range(1)]
        sg = [cx.sb([128, T], F32, "sg", st) for _ in range(2)]
        r = [cx.sb([128, D], F32, "r", st) for _ in range(1)]
        xo = [cx.sb([128, D], F32, "xo", st) for _ in range(1)]
        stat = [cx.sb([128, 8], F32, "stat", st) for _ in range(2)]
        p_tp = cx.ps([128, 1024], BF16, "ptp", st)
        p_g = [cx.ps([128, 512], F32, "pg", st) for _ in range(2)]
        p_u = [cx.ps([128, 512], F32, "pu", st) for _ in range(2)]
        p_o = [cx.ps([128, 512], F32, "po", st) for _ in range(2)]

        S.op("pool", lambda e: e.memset(epst[:], eps_p), writes=[epst])
        S.dma("sp", lambda e: e.dma_start(out=gb[:], in_=g.partition_broadcast(128)), writes=[gb])
        S.dma("sp", lambda e: e.dma_start(out=bb[:], in_=b.partition_broadcast(128)), writes=[bb])

        w_in_v = w_in.rearrange("(k p) f -> p k f", p=128)
        w_out_v = w_out.rearrange("(k p) d -> p k d", p=128)
        cast_engs = ["dve", "act", "pool"]
        ci = 0
        pieces = []
        for k in range(KD):
            for hf in range(4):
                pieces.append((w_in_v[:, k, hf * 1408:(hf + 1) * 1408], win[:, k, hf * 1408:(hf + 1) * 1408],
                               (win, k), 1408))
        for k0 in range(KF):
            pieces.append((w_out_v[:, k0, :], wout[:, k0, :], (wout, k0), D))
        for i, (src, dst, key, n) in enumerate(pieces):
            sb_ = stg[i % 2]
            if len(src.shape) == 3:
                sview = sb_[:, 0:n].rearrange("p (a b) -> p a b", a=src.shape[1])
            else:
                sview = sb_[:, 0:n]
            S.dma("sp", lambda e, sview=sview, src=src: e.dma_start(out=sview, in_=src), writes=[sb_])
            ce = cast_engs[ci % 3]
            ci += 1
            if ce == "act":
                S.op("act", lambda e, dst=dst, sview=sview: e.copy(out=dst, in_=sview), reads=[sb_], writes=[key])
            else:
                S.op(ce, lambda e, dst=dst, sview=sview: e.tensor_copy(out=dst, in_=sview), reads=[sb_], writes=[key])
        win_keys = [(win, k) for k in range(KD)]
        wout_keys = [(wout, k) for k in range(KF // 2)]

        ntiles = ntok // T
        x_in_v = x_in.rearrange("(n p) d -> n p d", p=128)
        x_out_v = x_out.rearrange("(n p) d -> n p d", p=128)
        last_out = []

        def load_tile(ti):
            xTt = xT[ti % 2]
            for s in range(NS):
                xt = xs[(ti * NS + s) % 4]
                xbt = xb[s]
                S.dma("sp", lambda e, xt=xt, i=ti * NS + s: e.dma_start(out=xt[:], in_=x_in_v[i]), writes=[xt])
                S.op("pool", lambda e, xbt=xbt, xt=xt: e.tensor_copy(out=xbt[:], in_=xt[:]), reads=[xt], writes=[xbt])
                for k in range(KD):
                    S.op("pe", lambda e, k=k, xbt=xbt: e.transpose(out=p_tp[:, k * 128:(k + 1) * 128],
                                                                  in_=xbt[:, k * 128:(k + 1) * 128],
                                                                  identity=cx.ident_b[:]),
                         reads=[xbt, cx.ident_b], writes=[p_tp])
                S.op("dve", lambda e, xTt=xTt, s=s: e.tensor_copy(
                    out=xTt[:, :, s * 128:(s + 1) * 128],
                    in_=p_tp[:].rearrange("p (k t) -> p k t", k=KD)), reads=[p_tp], writes=[xTt])

        load_tile(0)
        for ti in range(ntiles):
            xTt = xT[ti % 2]
            hTt = hT[0]
            for j in range(KF):
                pg, pu, sgt = p_g[j % 2], p_u[j % 2], sg[j % 2]
                for k in range(KD):
                    S.op("pe", lambda e, k=k, j=j, pg=pg, xTt=xTt: e.matmul(pg[:, 0:T], lhsT=win[:, k, j * 128:(j + 1) * 128],
                                                                   rhs=xTt[:, k, :], start=(k == 0), stop=(k == KD - 1)),
                         reads=[(win, k), xTt], writes=[pg])
                for k in range(KD):
                    S.op("pe", lambda e, k=k, j=j, pu=pu, xTt=xTt: e.matmul(pu[:, 0:T],
                                                                   lhsT=win[:, k, DFF + j * 128:DFF + (j + 1) * 128],
                                                                   rhs=xTt[:, k, :], start=(k == 0), stop=(k == KD - 1)),
                         reads=[(win, k), xTt], writes=[pu])
                S.op("act", lambda e, pg=pg, sgt=sgt: e.activation(out=sgt[:], in_=pg[:, 0:T], func=AF.Silu),
                     reads=[pg], writes=[sgt])
                S.op("dve", lambda e, pu=pu, sgt=sgt, j=j, hTt=hTt: e.tensor_tensor(out=hTt[:, j, :], in0=pu[:, 0:T], in1=sgt[:],
                                                                            op=ALU.mult),
                     reads=[pu, sgt], writes=[(hTt, j)])
            if ti + 1 < ntiles:
                load_tile(ti + 1)
            for s in range(NS):
                xt = xs[(ti * NS + s) % 4]
                rt, xot, stt = r[0], xo[0], stat[s]
                for hd in range(2):
                    po = p_o[hd]
                    for j in range(KF):
                        S.op("pe", lambda e, j=j, hd=hd, s=s, po=po, hTt=hTt: e.matmul(
                            po[:], lhsT=hTt[:, j, s * 128:(s + 1) * 128], rhs=wout[:, j, hd * 512:(hd + 1) * 512],
                            start=(j == 0), stop=(j == KF - 1)),
                             reads=[(hTt, j), (wout, j)], writes=[po])
                    S.op("dve", lambda e, po=po, hd=hd, rt=rt, xt=xt: e.scalar_tensor_tensor(
                        out=rt[:, hd * 512:(hd + 1) * 512], in0=po[:], scalar=c_y, in1=xt[:, hd * 512:(hd + 1) * 512],
                        op0=ALU.mult, op1=ALU.add), reads=[po, xt], writes=[(rt, hd)])
                emit_ln(cx, rt, [(rt, 0), (rt, 1)], xot, stt, epst, gb, bb, xot)
                o = S.dma("pool", lambda e, xot=xot, i=ti * NS + s: e.dma_start(out=x_out_v[i], in_=xot[:]),
                          reads=[xot], writes=[("dram_xout", ti * NS + s)])
                last_out.append(o)
        return last_out


def emit_ln(cx, rt, rkeys, junk, stt, epst, gb, bb, xot):
    S = cx.S
    S.op("act", lambda e: e.activation(out=junk[:], in_=rt[:], func=AF.Square, accum_out=stt[:, 0:1]),
         reads=rkeys, writes=[junk, (stt, 0)])
    S.op("dve", lambda e: e.tensor_reduce(out=stt[:, 1:2], in_=rt[:], axis=AX.X, op=ALU.add),
         reads=rkeys, writes=[(stt, 1)])
    S.op("dve", lambda e: e.tensor_scalar(out=stt[:, 2:3], in0=stt[:, 1:2], scalar1=1.0 / D, scalar2=None, op0=ALU.mult),
         reads=[(stt, 1)], writes=[(stt, 2)])
    S.op("dve", lambda e: e.tensor_tensor(out=stt[:, 3:4], in0=stt[:, 2:3], in1=stt[:, 2:3], op=ALU.mult),
         reads=[(stt, 2)], writes=[(stt, 3)])
    S.op("dve", lambda e: e.scalar_tensor_tensor(out=stt[:, 4:5], in0=stt[:, 0:1], scalar=1.0 / D, in1=stt[:, 3:4],
                                                 op0=ALU.mult, op1=ALU.subtract),
         reads=[(stt, 0), (stt, 3)], writes=[(stt, 4)])
    S.op("act", lambda e: e.activation(out=stt[:, 5:6], in_=stt[:, 4:5], func=AF.Sqrt, bias=epst[:]),
         reads=[(stt, 4), epst], writes=[(stt, 5)])
    S.op("dve", lambda e: e.reciprocal(out=stt[:, 6:7], in_=stt[:, 5:6]), reads=[(stt, 5)], writes=[(stt, 6)])
    S.op("dve", lambda e: e.tensor_scalar(out=xot[:], in0=rt[:], scalar1=stt[:, 2:3], scalar2=stt[:, 6:7],
                                          op0=ALU.subtract, op1=ALU.mult),
         reads=rkeys + [(stt, 2), (stt, 6)], writes=[xot])
    S.op("pool", lambda e: e.tensor_tensor(out=xot[:], in0=xot[:], in1=gb[:], op=ALU.mult), reads=[xot, gb], writes=[xot])
    S.op("pool", lambda e: e.tensor_tensor(out=xot[:], in0=xot[:], in1=bb[:], op=ALU.add), reads=[xot, bb], writes=[xot])


def build_ffn_program(ntok):
    nc = bass.Bass("TRN2", target_bir_lowering=False)
    x_in = nc.dram_tensor("x_in", [ntok, D], F32, kind="ExternalInput").ap()
    w_in = nc.dram_tensor("w_in", [D, 2 * DFF], F32, kind="ExternalInput").ap()
    w_out = nc.dram_tensor("w_out", [DFF, D], F32, kind="ExternalInput").ap()
    g = nc.dram_tensor("g", [D], F32, kind="ExternalInput").ap()
    b = nc.dram_tensor("b", [D], F32, kind="ExternalInput").ap()
    x_out = nc.dram_tensor("x_out", [ntok, D], F32, kind="ExternalOutput").ap()
    with contextlib.ExitStack() as st:
        cx = Ctx(nc, st)
        setup_consts(cx)
        outs = emit_ffn(cx, x_in, x_out, w_in, w_out, g, b, ntok)
        cx.S.emit(final_wait_ops=outs[-N_DMA_SEM:])
    return nc


def mm(cx, out, lhsT, rhs, start, stop, reads, writes):
    return cx.S.op("pe", lambda e: e.matmul(out, lhsT=lhsT, rhs=rhs, start=start, stop=stop), reads, writes)


def tr(cx, out, in_, ident, reads, writes):
    return cx.S.op("pe", lambda e: e.transpose(out=out, in_=in_, identity=ident), reads, writes)


def act(cx, out, in_, func, reads, writes, **kw):
    return cx.S.op("act", lambda e: e.activation(out=out, in_=in_, func=func, **kw), reads, writes)


def tt(cx, eng, out, in0, in1, op, reads, writes):
    return cx.S.op(eng, lambda e: e.tensor_tensor(out=out, in0=in0, in1=in1, op=op), reads, writes)


def ts(cx, eng, out, in0, s1, s2, op0, op1, reads, writes, accum_out=None):
    if op1 is None:
        return cx.S.op(eng, lambda e: e.tensor_scalar(out=out, in0=in0, scalar1=s1, scalar2=None, op0=op0,
                                                      accum_out=accum_out), reads, writes)
    return cx.S.op(eng, lambda e: e.tensor_scalar(out=out, in0=in0, scalar1=s1, scalar2=s2, op0=op0, op1=op1,
                                                  accum_out=accum_out), reads, writes)


def stt(cx, eng, out, in0, scalar, in1, op0, op1, reads, writes):
    return cx.S.op(eng, lambda e: e.scalar_tensor_tensor(out=out, in0=in0, scalar=scalar, in1=in1, op0=op0, op1=op1),
                   reads, writes)


def cp(cx, eng, out, in_, reads, writes):
    if eng == "act":
        return cx.S.op("act", lambda e: e.copy(out=out, in_=in_), reads, writes)
    return cx.S.op(eng, lambda e: e.tensor_copy(out=out, in_=in_), reads, writes)


def dma(cx, q, out, in_, reads, writes):
    return cx.S.dma(q, lambda e: e.dma_start(out=out, in_=in_), reads, writes)


def memset(cx, eng, ap, val, writes):
    return cx.S.op(eng, lambda e: e.memset(ap, val), (), writes)


def load_weight_bf16(cx, dst, src, stg, n, key, idx):
    sb_ = stg[idx % len(stg)]
    if len(src.shape) == 3:
        sview = sb_[:, 0:n].rearrange("p (a b) -> p a b", a=src.shape[1])
    else:
        sview = sb_[:, 0:n]
    dma(cx, "sp", sview, src, [], [sb_])
    ce = ("dve", "act", "pool")[idx % 3]
    cp(cx, ce, dst, sview, [sb_], [key])


def emit_xT(cx, x_rows, xs, xb, p_tp, xT, col0):
    dma(cx, "sp", xs[:], x_rows, [], [xs])
    cp(cx, "pool", xb[:], xs[:], [xs], [xb])
    for k in range(D // 128):
        tr(cx, p_tp[:, k * 128:(k + 1) * 128], xb[:, k * 128:(k + 1) * 128], cx.ident_b[:], [xb, cx.ident_b], [p_tp])
    cp(cx, "dve", xT[:, :, col0:col0 + 128], p_tp[:].rearrange("p (k t) -> p k t", k=D // 128), [p_tp], [xT])


def emit_proj(cx, x_rows_fn, ntok, wcols, fm_specs, tm_specs):
    TT = 512
    KD = D // 128
    NW = sum(w.shape[1] for w in wcols)
    with contextlib.ExitStack() as st:
        wsb = cx.sb([128, KD, NW], BF16, "pw", st)
        stg = [cx.sb([128, 1024], F32, "pstg", st) for _ in range(2)]
        xs = [cx.sb([128, D], F32, "pxs", st) for _ in range(2)]
        xb = [cx.sb([128, D], BF16, "pxb", st) for _ in range(2)]
        xT = [cx.sb([128, KD, TT], BF16, "pxT", st) for _ in range(2)]
        ost = [cx.sb([128, TT], F32, "post", st) for _ in range(3)]
        p_tp = cx.ps([128, 1024], BF16, "pptp", st)
        p_fm = [cx.ps([128, 512], F32, "ppfm", st) for _ in range(2)]
        p_tm = [cx.ps([128, 512], F32, "pptm", st) for _ in range(2)]
        tm_state = {}
        idx = 0
        off = 0
        for w in wcols:
            n = w.shape[1]
            wv = w.rearrange("(k p) f -> p k f", p=128)
            for k in range(KD):
                for c0 in range(0, n, 1024):
                    c1 = min(n, c0 + 1024)
                    load_weight_bf16(cx, wsb[:, k, off + c0:off + c1], wv[:, k, c0:c1], stg, c1 - c0, wsb, idx)
                    idx += 1
            off += n
        ntiles = ntok // TT
        oi = 0
        fi = 0
        ti_ = 0
        for t in range(ntiles):
            xTt = xT[t % 2]
            for s in range(TT // 128):
                emit_xT(cx, x_rows_fn(t * 4 + s), xs[s % 2], xb[s % 2], p_tp, xTt, s * 128)
            for (coff, M, dt_, dest_fn) in fm_specs:
                pf = p_fm[fi % 2]
                fi += 1
                for k in range(KD):
                    mm(cx, pf[0:M, :], wsb[:, k, coff:coff + M], xTt[:, k, :], k == 0, k == KD - 1, [wsb, xTt], [pf])
                o = ost[oi % 3]
                oi += 1
                ov = o[0:M, :] if dt_ == F32 else o[:].bitcast(BF16)[0:M, 0:TT]
                cp(cx, "act" if oi % 2 else "dve", ov, pf[0:M, :], [pf], [o])
                dma(cx, "pool", dest_fn(t), ov, [o], [])
            for (coff, N, post_fn) in tm_specs:
                for s in range(TT // 128):
                    pt = p_tm[ti_ % 2]
                    ti_ += 1
                    for k in range(KD):
                        mm(cx, pt[:, 0:N], xTt[:, k, s * 128:(s + 1) * 128], wsb[:, k, coff:coff + N], k == 0, k == KD - 1,
                           [wsb, xTt], [pt])
                    post_fn(cx, st, tm_state, pt, t * 4 + s)


def emit_outproj_ln(cx, srcT, nch, w, x_in, x_out, g, b, ntok, scale_y):
    c_y = scale_y / ALPHA
    eps_p = LN_EPS / (ALPHA * ALPHA)
    with contextlib.ExitStack() as st:
        wsb = cx.sb([128, nch, D], BF16, "ow", st)
        stg = [cx.sb([128, 1024], F32, "ostg", st) for _ in range(2)]
        gb = cx.sb([128, D], F32, "ogb", st)
        bb = cx.sb([128, D], F32, "obb", st)
        epst = cx.sb([128, 1], F32, "oeps", st)
        oT = [cx.sb([128, nch, 128], BF16, "ooT", st) for _ in range(2)]
        xs = [cx.sb([128, D], F32, "oxs", st) for _ in range(2)]
        r = cx.sb([128, D], F32, "or", st)
        xo = [cx.sb([128, D], F32, "oxo", st) for _ in range(2)]
        stat = [cx.sb([128, 8], F32, "ostat", st) for _ in range(2)]
        p_o = [cx.ps([128, 512], F32, "opo", st) for _ in range(4)]
        memset(cx, "pool", epst[:], eps_p, [epst])
        dma(cx, "sp", gb[:], g.partition_broadcast(128), [], [gb])
        dma(cx, "sp", bb[:], b.partition_broadcast(128), [], [bb])
        wv = w.rearrange("(k p) d -> p k d", p=128)
        for k in range(nch):
            load_weight_bf16(cx, wsb[:, k, :], wv[:, k, :], stg, D, wsb, k)
        sv = srcT.rearrange("c p t -> p c t")
        xiv = x_in.rearrange("(n p) d -> n p d", p=128)
        xov = x_out.rearrange("(n p) d -> n p d", p=128)
        outs = []
        for i in range(ntok // 128):
            oTt, xt, xot, stt_ = oT[i % 2], xs[i % 2], xo[i % 2], stat[i % 2]
            dma(cx, "sp", oTt[:], sv[:, :, i * 128:(i + 1) * 128], [], [oTt])
            dma(cx, "sp", xt[:], xiv[i], [], [xt])
            for hd in range(2):
                po = p_o[(2 * i + hd) % 4]
                for c in range(nch):
                    mm(cx, po[:], oTt[:, c, :], wsb[:, c, hd * 512:(hd + 1) * 512], c == 0, c == nch - 1, [oTt, wsb], [po])
                stt(cx, "dve", r[:, hd * 512:(hd + 1) * 512], po[:], c_y, xt[:, hd * 512:(hd + 1) * 512], ALU.mult, ALU.add,
                    [po, xt], [(r, hd)])
            emit_ln(cx, r, [(r, 0), (r, 1)], xot, stt_, epst, gb, bb, xot)
            outs.append(dma(cx, "pool", xov[i], xot[:], [xot], []))
        return outs


TOPK = 256
NBIS = 18


def emit_attn_core(cx, ckvT_d, kidxT_d, qT_d, qiT_d, widx_d, qpos_d, w_uk, w_uv, oT_d, NQ, SK, natural=False):
    S = cx.S
    NT = NQ // 256
    nkb_max = SK // 128
    with contextlib.ExitStack() as st:
        ckvT = cx.sb([128, 2, SK], BF16, "ackv", st)
        kidxT = cx.sb([64, SK], BF16, "akidx", st)
        Vp = cx.sb([128, nkb_max, 128], BF16, "aV", st)
        score = cx.sb([128, SK], F32, "ascore", st)
        msk = cx.sb([128, SK], BF16, "amsk", st)
        mskT = [cx.sb([128, nkb_max, 256], mybir.dt.uint8, "amskT", st) for _ in range(2)]
        qabs = cx.sb([128, 8, 2, 512], BF16, "aqabs", st)
        qT = cx.sb([128, 8, 256], BF16, "aqT", st)
        qiT = cx.sb([64, 8, 256], BF16, "aqiT", st)
        oT = cx.sb([128, 8, 256], BF16, "aoT", st)
        pT = [cx.sb([128, 512], BF16, "apT", st) for _ in range(2)]
        pTm = [cx.sb([128, 2, 256], BF16, "apTm", st) for _ in range(2)]
        rl = [cx.sb([128, 512], F32, "arl", st) for _ in range(2)]
        wuk = cx.sb([128, 8, 256], BF16, "awuk", st)
        wuv = cx.sb([128, 2, 16, 64], BF16, "awuv", st)
        stg = [cx.sb([128, 1024], F32, "astg", st) for _ in range(1)]
        iota_f = cx.sb([128, 512], F32, "aiof", st)
        pen = cx.sb([128, 512], F32, "apen", st)
        qpos = cx.sb([128, NQ // 128], F32, "aqpos", st)
        widx = [cx.sb([128, 8], F32, "awidx", st) for _ in range(2)]
        sm = cx.sb([128, 8], F32, "asm", st)
        rec = cx.sb([128, 256], F32, "arec", st)
        ones_b = cx.sb([128, 128], BF16, "aones", st)
        P = [cx.ps([128, 512], F32, "aP", st) for _ in range(8)]
        gen = [P[0], P[1]]
        gi = [0]

        def nextp():
            gi[0] += 1
            return gen[gi[0] % 2]

        memset(cx, "pool", ones_b[:], 1.0, [ones_b])
        S.op("pool", lambda e: e.iota(pen[:].bitcast(mybir.dt.int32), pattern=[[1, 512]], base=0, channel_multiplier=0), (), [pen])
        cp(cx, "pool", iota_f[:], pen[:].bitcast(mybir.dt.int32), [pen], [iota_f])
        dma(cx, "sp", qpos[:], qpos_d, [], [qpos])
        for rc in range(2):
            dma(cx, "sp", ckvT[:, rc, :], ckvT_d[rc], [], [ckvT])
        dma(cx, "sp", kidxT[:], kidxT_d, [], [kidxT])
        wukv = w_uk.rearrange("(p h2) dh r -> (h2 dh) p r", h2=2)
        for hf in range(2):
            load_weight_bf16(cx, wuk[:, hf * 4:(hf + 1) * 4, :], wukv[:, hf * 4:(hf + 1) * 4, :], stg, 1024, wuk, hf)
        for rc in range(2):
            load_weight_bf16(cx, wuv[:, rc, :, :], w_uv[:, rc * 128:(rc + 1) * 128, :].rearrange("h r dv -> r h dv"),
                             stg, 1024, (wuv, rc), 1 + rc)
        qT_v = qT_d.rearrange("c p t -> p c t")
        qiT_v = qiT_d.rearrange("c p t -> p c t")
        oT_v = oT_d.rearrange("c p t -> p c t")
        widx_v = widx_d.rearrange("(n p) h -> n p h", p=128)

        FP8 = mybir.dt.float8e4
        side_p = [P[5], P[7]]
        si = [0]

        def nexts():
            si[0] += 1
            return side_p[si[0] % 2]

        def tile_dims(kq):
            nkc = (kq // 2 + 1) if natural else (kq + 1)
            return nkc, 4 * nkc, 512 * nkc

        def side_units(kq):
            nkc, nkb, L = tile_dims(kq)
            q0 = kq * 256
            mT = mskT[kq % 2]
            U = []
            U.append(lambda: dma(cx, "sp", qiT[:], qiT_v[:, :, q0:q0 + 256], [], [qiT]))
            for b in range(2):
                wt = widx[b]
                U.append(lambda wt=wt, b=b: dma(cx, "sp", wt[:], widx_v[kq * 2 + b], [], [wt]))
                for kc in range(nkc):
                    for h in range(8):
                        def u(b=b, kc=kc, h=h, wt=wt):
                            sc = score[:, kc * 512:(kc + 1) * 512]
                            ps = nexts()
                            mm(cx, ps[:], qiT[:, h, b * 128:(b + 1) * 128], kidxT[:, kc * 512:(kc + 1) * 512], True, True,
                               [qiT, kidxT], [ps])
                            rt_ = rl[h % 2]
                            act(cx, rt_[:], ps[:], AF.Relu, [ps], [rt_])
                            if h == 0:
                                ts(cx, "dve", sc, rt_[:], wt[:, 0:1], None, ALU.mult, None, [rt_, wt], [(score, kc)])
                            else:
                                stt(cx, "dve", sc, rt_[:], wt[:, h:h + 1], sc, ALU.mult, ALU.add, [rt_, wt, (score, kc)], [(score, kc)])
                        U.append(u)
                skeys = [(score, kc) for kc in range(nkc)]

                def pen_u(b=b):
                    ts(cx, "dve", sm[:, 4:5], qpos[:, kq * 2 + b:kq * 2 + b + 1], float(-512 * (nkc - 1)), None, ALU.add, None,
                       [qpos], [(sm, 4)])
                    ts(cx, "dve", pen[:], iota_f[:], sm[:, 4:5], -30000.0, ALU.is_gt, ALU.mult, [iota_f, (sm, 4)], [pen])
                    lc = score[:, (nkc - 1) * 512:nkc * 512]
                    tt(cx, "dve", lc, lc, pen[:], ALU.add, [pen, (score, nkc - 1)], [(score, nkc - 1)])
                    memset(cx, "dve", sm[:, 0:1], 0.0, [(sm, 0)])
                U.append(pen_u)
                W = 128.0
                for it in range(NBIS):
                    def bis(W=W):
                        S.op("dve", lambda e: e.tensor_scalar(out=msk[:, 0:L], in0=score[:, 0:L], scalar1=sm[:, 0:1], scalar2=None,
                                                              op0=ALU.is_ge, op1=ALU.add, accum_out=sm[:, 1:2]),
                             skeys + [(sm, 0)], [msk, (sm, 1)])
                        ts(cx, "dve", sm[:, 2:3], sm[:, 1:2], TOPK - 0.5, W / 2, ALU.is_ge, ALU.mult, [(sm, 1)], [(sm, 2)])
                        stt(cx, "dve", sm[:, 0:1], sm[:, 0:1], -W / 4, sm[:, 2:3], ALU.add, ALU.add, [(sm, 0), (sm, 2)], [(sm, 0)])
                    U.append(bis)
                    W = W / 2

                def fin(W=W):
                    ts(cx, "dve", sm[:, 3:4], sm[:, 0:1], -W / 2, None, ALU.add, None, [(sm, 0)], [(sm, 3)])
                    ts(cx, "dve", msk[:, 0:L], score[:, 0:L], sm[:, 3:4], None, ALU.is_ge, None, skeys + [(sm, 3)], [msk])
                U.append(fin)
                for kb0 in range(0, nkb, 4):
                    def trs(kb0=kb0, b=b):
                        pm = nexts()
                        pmb = pm[:].bitcast(BF16)
                        for j in range(4):
                            tr(cx, pmb[:, j * 128:(j + 1) * 128], msk[:, (kb0 + j) * 128:(kb0 + j + 1) * 128], cx.ident_b[:],
                               [msk, cx.ident_b], [pm])
                        S.op("dve", lambda e, pmb=pmb: e.tensor_copy(out=mT[:, kb0:kb0 + 4, b * 128:(b + 1) * 128],
                                                                  in_=pmb[:, 0:512].rearrange("p (a b) -> p a b", a=4),
                                                                  ), [pm], [(mT, b)])
                    U.append(trs)
            return U

        def attention(kq, side):
            nkc, nkb, L = tile_dims(kq)
            q0 = kq * 256
            mT = mskT[kq % 2]
            n_iter = 8 * nkb
            rate = (len(side) + n_iter - 1) // n_iter if side else 0
            dma(cx, "sp", qT[:], qT_v[:, :, q0:q0 + 256], [], [qT])
            for h in range(16):
                p_, h2 = h // 2, h % 2
                pq = nextp()
                for rc in range(2):
                    mm(cx, pq[:, rc * 256:(rc + 1) * 256], wuk[h2 * 64:(h2 + 1) * 64, p_, rc * 128:(rc + 1) * 128],
                       qT[h2 * 64:(h2 + 1) * 64, p_, :], True, True, [wuk, qT], [pq])
                cp(cx, "act", qabs[:, p_, :, h2 * 256:(h2 + 1) * 256],
                   pq[:].rearrange("p (a b) -> p a b", a=2), [pq], [(qabs, h)])
            for p_ in range(8):
                for kb0 in range(0, nkb, 4):
                    pv = nextp()
                    for j in range(4):
                        for rc in range(2):
                            mm(cx, pv[:, j * 128:(j + 1) * 128], ckvT[:, rc, (kb0 + j) * 128:(kb0 + j + 1) * 128],
                               wuv[:, rc, 2 * p_:2 * p_ + 2, :].rearrange("p a b -> p (a b)"), rc == 0, rc == 1,
                               [ckvT, (wuv, rc)], [pv])
                    cp(cx, "act", Vp[:, kb0:kb0 + 4, :], pv[:].rearrange("p (a b) -> p a b", a=4), [pv], [Vp])

                def qk(kb):
                    pl = P[2 + kb % 2]
                    for rc in range(2):
                        mm(cx, pl[:], ckvT[:, rc, kb * 128:(kb + 1) * 128], qabs[:, p_, rc, :], rc == 0, rc == 1,
                           [ckvT, (qabs, 2 * p_), (qabs, 2 * p_ + 1)], [pl])

                qk(0)
                for kb in range(nkb):
                    pl = P[2 + kb % 2]
                    if kb + 1 < nkb:
                        qk(kb + 1)
                    pTt, pTmt = pT[kb % 2], pTm[kb % 2]
                    act(cx, pTt[:], pl[:], AF.Exp, [pl], [pTt], scale=0.125)
                    tt(cx, "pool", pTmt[:], pTt[:].rearrange("p (a b) -> p a b", a=2),
                       mT[:, kb:kb + 1, :].to_broadcast([128, 2, 256]), ALU.mult, [pTt, (mT, 0), (mT, 1)], [pTmt])
                    first, last = kb == 0, kb == nkb - 1
                    pm2 = pTmt[:].rearrange("p a b -> p (a b)")
                    mm(cx, P[4][:], Vp[:, kb, :], pm2, first, last, [Vp, pTmt], [P[4]])
                    mm(cx, P[6][:], ones_b[:], pm2, first, last, [ones_b, pTmt], [P[6]])
                    for _ in range(rate):
                        if side:
                            side.pop(0)()
                S.op("dve", lambda e: e.reciprocal(out=rec[0:64, :], in_=P[6][0:64, 0:256]), [P[6]], [(rec, 0)])
                S.op("dve", lambda e: e.reciprocal(out=rec[64:128, :], in_=P[6][64:128, 256:512]), [P[6]], [(rec, 1)])
                tt(cx, "dve", oT[0:64, p_, :], P[4][0:64, 0:256], rec[0:64, :], ALU.mult, [P[4], (rec, 0)], [(oT, p_, 0)])
                tt(cx, "dve", oT[64:128, p_, :], P[4][64:128, 256:512], rec[64:128, :], ALU.mult, [P[4], (rec, 1)], [(oT, p_, 1)])
            dma(cx, "pool", oT_v[:, :, q0:q0 + 256], oT[:], [(oT, p_, i) for p_ in range(8) for i in range(2)], [])
            while side:
                side.pop(0)()

        for u in side_units(0):
            u()
        for kq in range(NT):
            attention(kq, side_units(kq + 1) if kq + 1 < NT else [])


ATT_O1, ATT_O2, ATT_O3, ATT_O4 = 1024, 1280, 1792, 1856
RMS_EPS = 1e-6


def emit_attn_stage(cx, xq, xk, qpos_d, w_in, kv_g, w_uk, w_uv, w_o, g, b, x_out, NQ, SK, scr, natural=False):
    nc, S = cx.nc, cx.S
    ckvT_d, kidxT_d, qT_d, qiT_d, widx_d, oT_d = (scr[k] for k in ("ckvT", "kidxT", "qT", "qiT", "widx", "oT"))
    xk_v = xk.rearrange("(n p) d -> n p d", p=128)
    xq_v = xq.rearrange("(n p) d -> n p d", p=128)

    def post_ckv(cx, st, state, pt, blk):
        if "init" not in state:
            state["init"] = True
            state["gb"] = cx.sb([128, 256], F32, "kg", st)
            state["sq"] = cx.sb([128, 256], F32, "ksq", st)
            state["cn"] = [cx.sb([128, 256], BF16, "kcn", st) for _ in range(2)]
            state["cT"] = [cx.sb([128, 2, 128], BF16, "kcT", st) for _ in range(2)]
            state["sm"] = [cx.sb([128, 4], F32, "ksm", st) for _ in range(2)]
            state["eps"] = cx.sb([128, 1], F32, "keps", st)
            state["ptp"] = cx.ps([128, 512], BF16, "kptp", st)
            memset(cx, "pool", state["eps"][:], RMS_EPS, [state["eps"]])
            dma(cx, "sp", state["gb"][:], kv_g.partition_broadcast(128), [], [state["gb"]])
        gb_, sq, cn, cT, sm_, eps, ptp = (state["gb"], state["sq"], state["cn"][blk % 2], state["cT"][blk % 2],
                                        state["sm"][blk % 2], state["eps"], state["ptp"])
        act(cx, sq[:], pt[:, 0:256], AF.Square, [pt], [sq, (sm_, 0)], accum_out=sm_[:, 0:1])
        act(cx, sm_[:, 1:2], sm_[:, 0:1], AF.Sqrt, [(sm_, 0), eps], [(sm_, 1)], bias=eps[:], scale=1.0 / 256)
        cx.S.op("dve", lambda e: e.reciprocal(out=sm_[:, 2:3], in_=sm_[:, 1:2]), [(sm_, 1)], [(sm_, 2)])
        stt(cx, "dve", cn[:], pt[:, 0:256], sm_[:, 2:3], gb_[:], ALU.mult, ALU.mult, [pt, (sm_, 2), gb_], [cn])
        for rc in range(2):
            tr(cx, ptp[:, rc * 128:(rc + 1) * 128], cn[:, rc * 128:(rc + 1) * 128], cx.ident_b[:], [cn, cx.ident_b], [ptp])
        cp(cx, "act", cT[:], ptp[:, 0:256].rearrange("p (a b) -> p a b", a=2), [ptp], [cT])
        dma(cx, "pool", ckvT_d.rearrange("c p t -> p c t")[:, :, blk * 128:(blk + 1) * 128], cT[:], [cT], [])

    emit_proj(cx, lambda i: xk_v[i], SK, [w_in[:, ATT_O1:ATT_O2], w_in[:, ATT_O3:ATT_O4]],
              fm_specs=[(256, 64, BF16, lambda t: kidxT_d[:, t * 512:(t + 1) * 512])],
              tm_specs=[(0, 256, post_ckv)])
    S.barrier()

    def post_widx(cx, st, state, pt, blk):
        if "w" not in state:
            state["w"] = [cx.sb([128, 8], F32, "qw", st) for _ in range(2)]
        wt = state["w"][blk % 2]
        ts(cx, "dve", wt[:], pt[:, 0:8], (8 ** -0.5) * (64 ** -0.5), None, ALU.mult, None, [pt], [wt])
        dma(cx, "pool", widx_d[blk * 128:(blk + 1) * 128, :], wt[:], [wt], [])

    def mk_q(c):
        return lambda t: qT_d[c][:, t * 512:(t + 1) * 512]

    def mk_qi(c):
        return lambda t: qiT_d[c][:, t * 512:(t + 1) * 512]

    fm = [(c * 128, 128, BF16, mk_q(c)) for c in range(8)] + [(1024 + c * 64, 64, BF16, mk_qi(c)) for c in range(8)]
    emit_proj(cx, lambda i: xq_v[i], NQ, [w_in[:, 0:ATT_O1], w_in[:, ATT_O2:ATT_O3], w_in[:, ATT_O4:ATT_O4 + 8]],
              fm_specs=fm, tm_specs=[(1536, 8, post_widx)])
    S.barrier()
    emit_attn_core(cx, ckvT_d, kidxT_d, qT_d, qiT_d, widx_d, qpos_d, w_uk, w_uv, oT_d, NQ, SK, natural)
    S.barrier()
    outs = emit_outproj_ln(cx, oT_d, 8, w_o, xq, x_out, g, b, NQ, 1.0)
    return outs


def attn_scratch(nc, NQ, SK, tag=""):
    mk = lambda n, shp, dt_: nc.dram_tensor(n + tag, shp, dt_, kind="Internal").ap()
    return {"ckvT": mk("s_ckvT", [2, 128, SK], BF16), "kidxT": mk("s_kidxT", [64, SK], BF16),
            "qT": mk("s_qT", [8, 128, NQ], BF16), "qiT": mk("s_qiT", [8, 64, NQ], BF16),
            "widx": mk("s_widx", [NQ, 8], F32), "oT": mk("s_oT", [8, 128, NQ], BF16)}


def build_attn_program(NQ, SK):
    nc = bass.Bass("TRN2", target_bir_lowering=False)
    inp = lambda n, shp: nc.dram_tensor(n, shp, F32, kind="ExternalInput").ap()
    xq, xk = inp("xq", [NQ, D]), inp("xk", [SK, D])
    qpos = inp("qpos", [128, NQ // 128])
    w_in, kv_g = inp("w_in", [D, 1864]), inp("kv_g", [256])
    w_uk, w_uv, w_o = inp("w_uk", [16, 64, 256]), inp("w_uv", [16, 256, 64]), inp("w_o", [D, D])
    g, b = inp("g", [D]), inp("b", [D])
    x_out = nc.dram_tensor("x_out", [NQ, D], F32, kind="ExternalOutput").ap()
    scr = attn_scratch(nc, NQ, SK)
    with contextlib.ExitStack() as st:
        cx = Ctx(nc, st)
        setup_consts(cx)
        outs = emit_attn_stage(cx, xq, xk, qpos, w_in, kv_g, w_uk, w_uv, w_o, g, b, x_out, NQ, SK, scr)
        cx.S.emit(final_wait_ops=outs[-N_DMA_SEM:])
    return nc


def emit_ssd_core(cx, z_d, xbc_d, dt_d, cw_d, cb_d, dtb_d, alog_d, dsk_d, ng_d, yzT_d, SK):
    S = cx.S
    NCH = SK // 128
    with contextlib.ExitStack() as st:
        H = cx.sb([128, 1024], F32, "sH", st)
        Hb = cx.sb([128, 1024], BF16, "sHb", st)
        pre = [cx.sb([128, 515], F32, "spre", st) for _ in range(2)]
        acc = [cx.sb([128, 512], F32, "sacc", st) for _ in range(2)]
        xsT = cx.sb([128, 8, 512], F32, "sxsT", st)
        BT = cx.sb([128, 2, 512], BF16, "sBT", st)
        CT = cx.sb([128, 2, 512], BF16, "sCT", st)
        cw = cx.sb([128, 12, 4], F32, "scw", st)
        cb = cx.sb([128, 12], F32, "scb", st)
        dtb = cx.sb([128, 16], F32, "sdtb", st)
        a_bc = cx.sb([128, 16], F32, "sabc", st)
        d_bc = cx.sb([128, 16], F32, "sdbc", st)
        ng = cx.sb([128, 1024], F32, "sng", st)
        triu = cx.sb([128, 128], F32, "striu", st)
        mgt = cx.sb([128, 128], F32, "smgt", st)
        xs_tok = cx.sb([128, 1024], F32, "sxs", st)
        Btok = cx.sb([128, 2, 128], BF16, "sBtok", st)
        sm = [cx.sb([128, 8, 16], F32, "ssm", st) for _ in range(2)]
        G = [cx.sb([128, 128], F32, "sG", st) for _ in range(4)]
        dec = [cx.sb([128, 4, 128], F32, "sdec", st) for _ in range(2)]
        cbt = cx.sb([128, 2, 128], F32, "scbt", st)
        MT = cx.sb([128, 16, 128], BF16, "sMT", st)
        xdt = cx.sb([128, 1024], BF16, "sxdt", st)
        xdtd = cx.sb([128, 1024], BF16, "sxdtd", st)
        t1 = cx.sb([128, 1024], F32, "st1", st)
        xsD = cx.sb([128, 1024], F32, "sxsD", st)
        y = cx.sb([128, 1024], F32, "sy", st)
        zt = [cx.sb([128, 1024], F32, "szt", st) for _ in range(2)]
        zs = cx.sb([128, 1024], F32, "szs", st)
        yzn = cx.sb([128, 1024], BF16, "syzn", st)
        yzT = [cx.sb([128, 8, 128], BF16, "syzT", st) for _ in range(2)]
        st2 = [cx.sb([128, 8], F32, "sst2", st) for _ in range(2)]
        epst = cx.sb([128, 1], F32, "sepst", st)
        P = [cx.ps([128, 512], F32, "sP", st) for _ in range(8)]

        memset(cx, "pool", epst[:], RMS_EPS, [epst])
        memset(cx, "pool", H[:], 0.0, [H])
        memset(cx, "pool", Hb[:], 0.0, [Hb])
        S.op("pool", lambda e: e.affine_select(out=triu[:], in_=cx.ones_f[:], pattern=[[1, 128]], compare_op=ALU.is_ge,
                                               fill=0.0, base=0, channel_multiplier=-1), [cx.ones_f], [triu])
        S.op("pool", lambda e: e.affine_select(out=mgt[:], in_=cx.ones_f[:], pattern=[[-1, 128]], compare_op=ALU.is_gt,
                                               fill=0.0, base=0, channel_multiplier=1), [cx.ones_f], [mgt])
        dma(cx, "sp", cw[:], cw_d, [], [cw])
        dma(cx, "sp", cb[:], cb_d, [], [cb])
        dma(cx, "sp", dtb[:], dtb_d.partition_broadcast(128), [], [dtb])
        dma(cx, "sp", a_bc[:], alog_d.partition_broadcast(128), [], [a_bc])
        dma(cx, "sp", d_bc[:], dsk_d.partition_broadcast(128), [], [d_bc])
        dma(cx, "sp", ng[:], ng_d.partition_broadcast(128), [], [ng])
        act(cx, a_bc[:], a_bc[:], AF.Exp, [a_bc], [a_bc])
        ts(cx, "dve", a_bc[:], a_bc[:], -1.0, None, ALU.mult, None, [a_bc], [a_bc])
        z_v = z_d.rearrange("(n p) c -> n p c", p=128)
        dt_v = dt_d.rearrange("(n p) c -> n p c", p=128)
        yz_v = yzT_d.rearrange("c p t -> p c t")

        def bc3(ap2):
            return ap2.unsqueeze(2).to_broadcast([128, 16, 64])

        def v3(ap):
            return ap.rearrange("p (h d) -> p h d", h=16)

        for sc in range(SK // 512):
            t0 = sc * 512
            for cc in range(12):
                pt, ac = pre[cc % 2], acc[cc % 2]
                if sc == 0:
                    memset(cx, "pool", pt[:, 0:3], 0.0, [pt])
                    dma(cx, "sp", pt[:, 3:515], xbc_d[cc][:, 0:512], [], [pt])
                else:
                    dma(cx, "sp", pt[:, 0:515], xbc_d[cc][:, t0 - 3:t0 + 512], [], [pt])
                ts(cx, "dve", ac[:], pt[:, 0:512], cw[:, cc, 0:1], None, ALU.mult, None, [pt, cw], [ac])
                for k in range(1, 4):
                    stt(cx, "dve", ac[:], pt[:, k:k + 512], cw[:, cc, k:k + 1], ac[:], ALU.mult, ALU.add, [pt, cw, ac], [ac])
                if cc < 8:
                    dst, key = xsT[:, cc, :], (xsT, cc)
                elif cc < 10:
                    dst, key = BT[:, cc - 8, :], (BT, cc - 8)
                else:
                    dst, key = CT[:, cc - 10, :], (CT, cc - 10)
                act(cx, dst, ac[:], AF.Silu, [ac, cb], [key], bias=cb[:, cc:cc + 1])
            for ch in range(4):
                c = sc * 4 + ch
                c0 = ch * 128
                s_ = sm[c % 2]
                for k in range(8):
                    tr(cx, P[k // 4][:, (k % 4) * 128:(k % 4 + 1) * 128], xsT[:, k, c0:c0 + 128], cx.ident_f[:],
                       [(xsT, k), cx.ident_f], [P[k // 4]])
                for hf in range(2):
                    cp(cx, "act", xs_tok[:, hf * 512:(hf + 1) * 512], P[hf][:], [P[hf]], [(xs_tok, hf)])
                xk_ = [(xs_tok, 0), (xs_tok, 1)]
                p2b = P[2][:].bitcast(BF16)
                for g_ in range(2):
                    tr(cx, p2b[:, g_ * 128:(g_ + 1) * 128], BT[:, g_, c0:c0 + 128], cx.ident_b[:], [(BT, g_), cx.ident_b], [P[2]])
                cp(cx, "dve", Btok[:], p2b[:, 0:256].rearrange("p (a b) -> p a b", a=2), [P[2]], [Btok])
                dma(cx, "sp", s_[:, 0, :], dt_v[c], [], [(s_, 0)])
                tt(cx, "dve", s_[:, 0, :], s_[:, 0, :], dtb[:], ALU.add, [(s_, 0), dtb], [(s_, 0)])
                act(cx, s_[:, 1, :], s_[:, 0, :], AF.Exp, [(s_, 0)], [(s_, 1)])
                act(cx, s_[:, 1, :], s_[:, 1, :], AF.Ln, [(s_, 1)], [(s_, 1)], bias=1.0)
                tt(cx, "dve", s_[:, 2, :], s_[:, 1, :], a_bc[:], ALU.mult, [(s_, 1), a_bc], [(s_, 2)])
                mm(cx, P[2][:, 256:272], triu[:], s_[:, 2, :], True, True, [triu, (s_, 2)], [P[2]])
                mm(cx, P[2][:, 272:288], cx.ones_f[:], s_[:, 2, :], True, True, [cx.ones_f, (s_, 2)], [P[2]])
                cp(cx, "dve", s_[:, 3, :], P[2][:, 256:272], [P[2]], [(s_, 3)])
                act(cx, s_[:, 4, :], s_[:, 3, :], AF.Exp, [(s_, 3)], [(s_, 4)])
                tt(cx, "dve", s_[:, 5, :], P[2][:, 272:288], s_[:, 3, :], ALU.subtract, [P[2], (s_, 3)], [(s_, 5)])
                act(cx, s_[:, 5, :], s_[:, 5, :], AF.Exp, [(s_, 5)], [(s_, 5)])
                act(cx, s_[:, 6, :], P[2][:, 272:288], AF.Exp, [P[2]], [(s_, 6)])
                tt(cx, "dve", s_[:, 7, :], s_[:, 1, :], s_[:, 5, :], ALU.mult, [(s_, 1), (s_, 5)], [(s_, 7)])
                tt(cx, "dve", v3(xdt[:]), v3(xs_tok[:]), bc3(s_[:, 1, :]), ALU.mult, xk_ + [(s_, 1)], [xdt])
                tt(cx, "pool", v3(xdtd[:]), v3(xs_tok[:]), bc3(s_[:, 7, :]), ALU.mult, xk_ + [(s_, 7)], [xdtd])
                for g_ in range(2):
                    mm(cx, P[2][:, 288 + g_ * 128:288 + (g_ + 1) * 128][:, 0:128] if False else P[7][:, g_ * 128:(g_ + 1) * 128],
                       BT[:, g_, c0:c0 + 128], CT[:, g_, c0:c0 + 128], True, True, [(BT, g_), (CT, g_)], [P[7]])
                tt(cx, "dve", cbt[:], P[7][:, 0:256].rearrange("p (a b) -> p a b", a=2),
                   triu[:].unsqueeze(1).to_broadcast([128, 2, 128]), ALU.mult, [P[7], triu], [cbt])
                for h0 in range(0, 16, 4):
                    pseg = P[3 + (h0 // 4) % 2]
                    dc = dec[(h0 // 4) % 2]
                    for j in range(4):
                        h = h0 + j
                        ts(cx, "pool" if j % 2 else "dve", G[j][:], mgt[:], s_[:, 2, h:h + 1], None, ALU.mult, None,
                           [mgt, (s_, 2)], [G[j]])
                        mm(cx, pseg[:, j * 128:(j + 1) * 128], G[j][:], triu[:], True, True, [G[j], triu], [pseg])
                    act(cx, dc[:], pseg[:].rearrange("p (a b) -> p a b", a=4), AF.Exp, [pseg], [dc])
                    g_ = h0 // 8
                    tt(cx, "dve", MT[:, h0:h0 + 4, :], dc[:], cbt[:, g_:g_ + 1, :].to_broadcast([128, 4, 128]), ALU.mult,
                       [dc, cbt], [(MT, h0 // 4)])
                for h in range(16):
                    py = P[h // 8]
                    mm(cx, py[:, (h % 8) * 64:(h % 8 + 1) * 64], MT[:, h, :], xdt[:, h * 64:(h + 1) * 64], True, True,
                       [(MT, h // 4), xdt], [py])
                for g_ in range(2):
                    mm(cx, P[5 + g_][:], CT[:, g_, c0:c0 + 128], Hb[:, g_ * 512:(g_ + 1) * 512], True, True, [(CT, g_), Hb], [P[5 + g_]])
                for g_ in range(2):
                    tt(cx, "dve", t1[:, g_ * 512:(g_ + 1) * 512].rearrange("p (h d) -> p h d", h=8),
                       P[5 + g_][:].rearrange("p (h d) -> p h d", h=8),
                       s_[:, 4, g_ * 8:(g_ + 1) * 8].unsqueeze(2).to_broadcast([128, 8, 64]), ALU.mult,
                       [P[5 + g_], (s_, 4)], [(t1, g_)])
                tt(cx, "pool", v3(xsD[:]), v3(xs_tok[:]), bc3(d_bc[:]), ALU.mult, xk_ + [d_bc], [xsD])
                tt(cx, "pool", xsD[:], xsD[:], t1[:], ALU.add, [xsD, (t1, 0), (t1, 1)], [xsD])
                for g_ in range(2):
                    tt(cx, "dve", y[:, g_ * 512:(g_ + 1) * 512], P[g_][:], xsD[:, g_ * 512:(g_ + 1) * 512], ALU.add,
                       [P[g_], xsD], [(y, g_)])
                for g_ in range(2):
                    mm(cx, P[5 + g_][:], Btok[:, g_, :], xdtd[:, g_ * 512:(g_ + 1) * 512], True, True, [Btok, xdtd], [P[5 + g_]])
                tt(cx, "dve", v3(H[:]), v3(H[:]), bc3(s_[:, 6, :]), ALU.mult, [H, (s_, 6)], [H])
                for g_ in range(2):
                    tt(cx, "dve", H[:, g_ * 512:(g_ + 1) * 512], H[:, g_ * 512:(g_ + 1) * 512], P[5 + g_][:], ALU.add,
                       [H, P[5 + g_]], [H])
                cp(cx, "act", Hb[:], H[:], [H], [Hb])
                ztt = zt[c % 2]
                s2 = st2[c % 2]
                dma(cx, "sp", ztt[:], z_v[c], [], [ztt])
                act(cx, zs[:], ztt[:], AF.Silu, [ztt], [zs])
                tt(cx, "dve", y[:], y[:], zs[:], ALU.mult, [(y, 0), (y, 1), zs], [(y, 0), (y, 1)])
                for g_ in range(2):
                    act(cx, zs[:, g_ * 512:(g_ + 1) * 512], y[:, g_ * 512:(g_ + 1) * 512], AF.Square, [(y, g_)], [zs, (s2, g_)],
                        accum_out=s2[:, g_:g_ + 1])
                    act(cx, s2[:, 2 + g_:3 + g_], s2[:, g_:g_ + 1], AF.Sqrt, [(s2, g_), epst], [(s2, 2 + g_)], bias=epst[:],
                        scale=1.0 / 512)
                    S.op("dve", lambda e, g_=g_, s2=s2: e.reciprocal(out=s2[:, 4 + g_:5 + g_], in_=s2[:, 2 + g_:3 + g_]),
                         [(s2, 2 + g_)], [(s2, 4 + g_)])
                    stt(cx, "dve", yzn[:, g_ * 512:(g_ + 1) * 512], y[:, g_ * 512:(g_ + 1) * 512], s2[:, 4 + g_:5 + g_],
                        ng[:, g_ * 512:(g_ + 1) * 512], ALU.mult, ALU.mult, [(y, g_), (s2, 4 + g_), ng], [(yzn, g_)])
                p7b = P[7][:].bitcast(BF16)
                yT = yzT[c % 2]
                for k in range(8):
                    tr(cx, p7b[:, k * 128:(k + 1) * 128], yzn[:, k * 128:(k + 1) * 128], cx.ident_b[:],
                       [(yzn, k // 4), cx.ident_b], [P[7]])
                cp(cx, "act", yT[:], p7b[:].rearrange("p (a b) -> p a b", a=8), [P[7]], [yT])
                dma(cx, "pool", yz_v[:, :, c * 128:(c + 1) * 128], yT[:], [yT], [])


SSD_NW = 2576


def emit_ssd_stage(cx, xk, w_in_c, cw_d, cb_d, dtb_d, alog_d, dsk_d, ng_d, yzT_d, SK, scr):
    S = cx.S
    z_d, xbc_d, dt_d = scr["z"], scr["xbc"], scr["dt"]
    xk_v = xk.rearrange("(n p) d -> n p d", p=128)

    def post_z(half):
        def f(cx, st, state, pt, blk):
            if "z" not in state:
                state["z"] = [cx.sb([128, 512], F32, "zz", st) for _ in range(2)]
                state["i"] = 0
            state["i"] += 1
            zt_ = state["z"][state["i"] % 2]
            cp(cx, "act", zt_[:], pt[:, 0:512], [pt], [zt_])
            dma(cx, "pool", z_d[blk * 128:(blk + 1) * 128, half * 512:(half + 1) * 512], zt_[:], [zt_], [])
        return f

    def post_dt(cx, st, state, pt, blk):
        if "d" not in state:
            state["d"] = [cx.sb([128, 16], F32, "zd", st) for _ in range(2)]
        d_ = state["d"][blk % 2]
        cp(cx, "dve", d_[:], pt[:, 0:16], [pt], [d_])
        dma(cx, "pool", dt_d[blk * 128:(blk + 1) * 128, :], d_[:], [d_], [])

    def mk(c):
        return lambda t: xbc_d[c][:, t * 512:(t + 1) * 512]

    fm = [(1024 + c * 128, 128, F32, mk(c)) for c in range(12)]
    emit_proj(cx, lambda i: xk_v[i], SK, w_in_c if isinstance(w_in_c, list) else [w_in_c], fm_specs=fm,
              tm_specs=[(0, 512, post_z(0)), (512, 512, post_z(1)), (2560, 16, post_dt)])
    S.barrier()
    emit_ssd_core(cx, z_d, xbc_d, dt_d, cw_d, cb_d, dtb_d, alog_d, dsk_d, ng_d, yzT_d, SK)


def ssd_scratch(nc, SK, tag=""):
    mk = lambda n, shp, dt_: nc.dram_tensor(n + tag, shp, dt_, kind="Internal").ap()
    return {"z": mk("s_z", [SK, 1024], F32), "xbc": mk("s_xbc", [12, 128, SK], F32), "dt": mk("s_dt", [SK, 16], F32)}


def build_ssd_program(SK):
    nc = bass.Bass("TRN2", target_bir_lowering=False)
    inp = lambda n, shp: nc.dram_tensor(n, shp, F32, kind="ExternalInput").ap()
    xk = inp("xk", [SK, D])
    w_in_c = inp("w_in_c", [D, SSD_NW])
    cw, cb = inp("cw", [128, 12, 4]), inp("cb", [128, 12])
    dtb, alog, dsk, ng = inp("dtb", [16]), inp("alog", [16]), inp("dsk", [16]), inp("ng", [1024])
    yzT = nc.dram_tensor("yzT", [8, 128, SK], BF16, kind="ExternalOutput").ap()
    scr = ssd_scratch(nc, SK)
    with contextlib.ExitStack() as st:
        cx = Ctx(nc, st)
        setup_consts(cx)
        emit_ssd_stage(cx, xk, w_in_c, cw, cb, dtb, alog, dsk, ng, yzT, SK, scr)
        cx.S.emit(final_wait_ops=cx.S.dma_hist["pool"][-N_DMA_SEM:])
    return nc


def build_outproj_program(NQ, nch):
    nc = bass.Bass("TRN2", target_bir_lowering=False)
    inp = lambda n, shp: nc.dram_tensor(n, shp, F32, kind="ExternalInput").ap()
    srcT = nc.dram_tensor("srcT", [nch, 128, NQ], BF16, kind="ExternalInput").ap()
    w, xq, g, b = inp("w", [nch * 128, D]), inp("xq", [NQ, D]), inp("g", [D]), inp("b", [D])
    x_out = nc.dram_tensor("x_out", [NQ, D], F32, kind="ExternalOutput").ap()
    with contextlib.ExitStack() as st:
        cx = Ctx(nc, st)
        setup_consts(cx)
        outs = emit_outproj_ln(cx, srcT, nch, w, xq, x_out, g, b, NQ, 1.0)
        cx.S.emit(final_wait_ops=outs[-N_DMA_SEM:])
    return nc


SEQ = 8192
BATCH = 4
NQ_CORE = SEQ // 2


def _own_tokens(j):
    bl = []
    for k in range(SEQ // 512):
        bl += [4 * k, 4 * k + 3] if j == 0 else [4 * k + 1, 4 * k + 2]
    return np.concatenate([np.arange(b * 128, (b + 1) * 128) for b in bl])


def _ssd_core_inputs(j, w_in, conv_w, conv_b, dt_bias, a_log, d_skip, norm_g):
    cols = np.concatenate([np.arange(j * 1024, (j + 1) * 1024), 2048 + np.arange(j * 1024, (j + 1) * 1024),
                           4096 + np.arange(j * 256, (j + 1) * 256), 4608 + np.arange(j * 256, (j + 1) * 256),
                           5120 + np.arange(j * 16, (j + 1) * 16)])
    ch = np.concatenate([np.arange(j * 1024, (j + 1) * 1024), 2048 + np.arange(j * 256, (j + 1) * 256),
                         2560 + np.arange(j * 256, (j + 1) * 256)])
    cw = np.ascontiguousarray(conv_w[:, ch].T.reshape(12, 128, 4).transpose(1, 0, 2))
    cb = np.ascontiguousarray(conv_b[ch].reshape(12, 128).T)
    return {"w_in_c": np.ascontiguousarray(w_in[:, cols]), "cw": cw, "cb": cb,
            "dtb": np.ascontiguousarray(dt_bias[j * 16:(j + 1) * 16]), "alog": np.ascontiguousarray(a_log[j * 16:(j + 1) * 16]),
            "dsk": np.ascontiguousarray(d_skip[j * 16:(j + 1) * 16]), "ng": np.ascontiguousarray(norm_g[j * 1024:(j + 1) * 1024])}


_PROGS = {}


def _prog(name, fn):
    if name not in _PROGS:
        _PROGS[name] = fn()
    return _PROGS[name]


def kernel_unfused(x, ln_g, ln_b, ffn1_w_in, ffn1_w_out, ffn2_w_in, ffn2_w_out, attn_w_in, attn_kv_norm, attn_w_uk, attn_w_uv,
           attn_w_out, ssm_w_in, ssm_conv_w, ssm_conv_b, ssm_dt_bias, ssm_a_log, ssm_d, ssm_norm_g, ssm_w_out):
    f32 = lambda a: np.ascontiguousarray(np.asarray(a, dtype=np.float32))
    x = f32(x)
    cores = list(range(NCORES))
    tok = [_own_tokens(c % 2) for c in cores]
    qpos = [np.ascontiguousarray(tok[c].reshape(-1, 128).T.astype(np.float32)) for c in cores]
    x_own = [np.ascontiguousarray(x[c // 2][tok[c]]) for c in cores]

    def run(nc, in_maps, key):
        res = run_bass_kernel_spmd(nc, in_maps, core_ids=cores)
        return [r[key] for r in res.results]

    def full_seq(x_own):
        out = []
        for s in range(BATCH):
            xs = np.empty((SEQ, D), np.float32)
            for j in range(2):
                xs[tok[2 * s + j]] = x_own[2 * s + j]
            out.append(xs)
        return out

    def ffn(x_own, w_in, w_out, g, b):
        nc = _prog("ffn", lambda: build_ffn_program(NQ_CORE))
        return run(nc, [{"x_in": x_own[c], "w_in": f32(w_in), "w_out": f32(w_out), "g": f32(g), "b": f32(b)} for c in cores],
                   "x_out")

    for i in range(DEPTH):
        x_own = ffn(x_own, ffn1_w_in[i], ffn1_w_out[i], ln_g[i, 0], ln_b[i, 0])
        j_ = i // 2
        xk = full_seq(x_own)
        if i % 2 == 0:
            nc = _prog("attn", lambda: build_attn_program(NQ_CORE, SEQ))
            x_own = run(nc, [{"xq": x_own[c], "xk": xk[c // 2], "qpos": qpos[c], "w_in": f32(attn_w_in[j_]),
                              "kv_g": f32(attn_kv_norm[j_]), "w_uk": f32(attn_w_uk[j_]), "w_uv": f32(attn_w_uv[j_]),
                              "w_o": f32(attn_w_out[j_]), "g": f32(ln_g[i, 1]), "b": f32(ln_b[i, 1])} for c in cores], "x_out")
        else:
            nc = _prog("ssd", lambda: build_ssd_program(SEQ))
            ci = [_ssd_core_inputs(jj, f32(ssm_w_in[j_]), f32(ssm_conv_w[j_]), f32(ssm_conv_b[j_]), f32(ssm_dt_bias[j_]),
                                   f32(ssm_a_log[j_]), f32(ssm_d[j_]), f32(ssm_norm_g[j_])) for jj in range(2)]
            yz = run(nc, [dict(ci[c % 2], xk=xk[c // 2]) for c in cores], "yzT")
            nc2 = _prog("oproj", lambda: build_outproj_program(NQ_CORE, 16))
            maps = []
            for c in cores:
                s = c // 2
                full = np.concatenate([np.asarray(yz[2 * s]), np.asarray(yz[2 * s + 1])], axis=0)
                maps.append({"srcT": np.ascontiguousarray(full[:, :, tok[c]]), "w": f32(ssm_w_out[j_]), "xq": x_own[c],
                             "g": f32(ln_g[i, 1]), "b": f32(ln_b[i, 1])})
            x_own = run(nc2, maps, "x_out")
        x_own = ffn(x_own, ffn2_w_in[i], ffn2_w_out[i], ln_g[i, 2], ln_b[i, 2])
    out = np.stack(full_seq(x_own)).astype(np.float32)
    return out


def build_full_program():
    nc = bass.Bass("TRN2", target_bir_lowering=False)
    inp = lambda n, shp: nc.dram_tensor(n, shp, F32, kind="ExternalInput").ap()
    x = inp("x", [SEQ, D])
    qpos = inp("qpos", [128, SEQ // 128])
    ln_g, ln_b = inp("ln_g", [DEPTH, 3, D]), inp("ln_b", [DEPTH, 3, D])
    f1i, f1o = inp("ffn1_w_in", [DEPTH, D, 2 * DFF]), inp("ffn1_w_out", [DEPTH, DFF, D])
    f2i, f2o = inp("ffn2_w_in", [DEPTH, D, 2 * DFF]), inp("ffn2_w_out", [DEPTH, DFF, D])
    a_in, a_kv = inp("attn_w_in", [2, D, 1864]), inp("attn_kv_norm", [2, 256])
    a_uk, a_uv, a_o = inp("attn_w_uk", [2, 16, 64, 256]), inp("attn_w_uv", [2, 16, 256, 64]), inp("attn_w_out", [2, D, D])
    s_in = inp("ssm_w_in", [2, D, 5152])
    s_cw, s_cb = inp("ssm_cw", [2, 2, 128, 12, 4]), inp("ssm_cb", [2, 2, 128, 12])
    s_dtb, s_alog, s_d = inp("ssm_dt_bias", [2, 32]), inp("ssm_a_log", [2, 32]), inp("ssm_d", [2, 32])
    s_ng, s_o = inp("ssm_norm_g", [2, 2048]), inp("ssm_w_out", [2, 2048, D])
    out = nc.dram_tensor("out", [SEQ, D], F32, kind="ExternalOutput").ap()
    bufs = [nc.dram_tensor("xbuf%d" % i, [SEQ, D], F32, kind="Internal").ap() for i in range(3)]
    ascr = attn_scratch(nc, SEQ, SEQ)
    sscr = ssd_scratch(nc, SEQ)
    yzT = nc.dram_tensor("s_yzT", [16, 128, SEQ], BF16, kind="Internal").ap()
    with contextlib.ExitStack() as st:
        cx = Ctx(nc, st)
        S = cx.S
        setup_consts(cx)
        cur = x
        outs = None
        for i in range(DEPTH):
            j_ = i // 2
            b0, b1, b2 = bufs[0], bufs[1], bufs[2]
            emit_ffn(cx, cur, b1, f1i[i], f1o[i], ln_g[i, 0], ln_b[i, 0], SEQ)
            S.barrier()
            if i % 2 == 0:
                emit_attn_stage(cx, b1, b1, qpos, a_in[j_], a_kv[j_], a_uk[j_], a_uv[j_], a_o[j_], ln_g[i, 1], ln_b[i, 1], b2,
                                SEQ, SEQ, ascr, natural=True)
            else:
                w = s_in[j_]
                for jj in range(2):
                    wcols = [w[:, jj * 1024:(jj + 1) * 1024], w[:, 2048 + jj * 1024:2048 + (jj + 1) * 1024],
                             w[:, 4096 + jj * 256:4096 + (jj + 1) * 256], w[:, 4608 + jj * 256:4608 + (jj + 1) * 256],
                             w[:, 5120 + jj * 16:5120 + (jj + 1) * 16]]
                    emit_ssd_stage(cx, b1, wcols, s_cw[j_, jj], s_cb[j_, jj], s_dtb[j_, jj * 16:(jj + 1) * 16],
                                   s_alog[j_, jj * 16:(jj + 1) * 16], s_d[j_, jj * 16:(jj + 1) * 16],
                                   s_ng[j_, jj * 1024:(jj + 1) * 1024], yzT[jj * 8:(jj + 1) * 8], SEQ, sscr)
                    S.barrier()
                emit_outproj_ln(cx, yzT, 16, s_o[j_], b1, b2, ln_g[i, 1], ln_b[i, 1], SEQ, 1.0)
            S.barrier()
            last = i == DEPTH - 1
            dst = out if last else b0
            outs = emit_ffn(cx, b2, dst, f2i[i], f2o[i], ln_g[i, 2], ln_b[i, 2], SEQ)
            if not last:
                S.barrier()
            cur = b0
        print("ops per engine:", {e: len(S.ops[e]) for e in ENGS}, flush=True)
        S.emit(final_wait_ops=outs[-N_DMA_SEM:])
        print("max semval per segment:", S.max_semval, "-> per-sem max", {e: v // N_CSEM + CSEM_CH for e, v in S.max_semval.items()},
              "max dma sem value:", S.max_dval, flush=True)
    return nc


def _ssd_conv_layout(conv_w, conv_b):
    cw = np.zeros((2, 2, 128, 12, 4), np.float32)
    cb = np.zeros((2, 2, 128, 12), np.float32)
    for l in range(2):
        for j in range(2):
            ch = np.concatenate([np.arange(j * 1024, (j + 1) * 1024), 2048 + np.arange(j * 256, (j + 1) * 256),
                                 2560 + np.arange(j * 256, (j + 1) * 256)])
            cw[l, j] = conv_w[l][:, ch].T.reshape(12, 128, 4).transpose(1, 0, 2)
            cb[l, j] = conv_b[l][ch].reshape(12, 128).T
    return cw, cb


def kernel_fused(x, ln_g, ln_b, ffn1_w_in, ffn1_w_out, ffn2_w_in, ffn2_w_out, attn_w_in, attn_kv_norm, attn_w_uk, attn_w_uv,
                 attn_w_out, ssm_w_in, ssm_conv_w, ssm_conv_b, ssm_dt_bias, ssm_a_log, ssm_d, ssm_norm_g, ssm_w_out):
    f32 = lambda a: np.ascontiguousarray(np.asarray(a, dtype=np.float32))
    x = f32(x)
    cw, cb = _ssd_conv_layout(f32(ssm_conv_w), f32(ssm_conv_b))
    qpos = np.ascontiguousarray(np.arange(SEQ, dtype=np.float32).reshape(-1, 128).T)
    shared = {"qpos": qpos, "ln_g": f32(ln_g), "ln_b": f32(ln_b), "ffn1_w_in": f32(ffn1_w_in), "ffn1_w_out": f32(ffn1_w_out),
              "ffn2_w_in": f32(ffn2_w_in), "ffn2_w_out": f32(ffn2_w_out), "attn_w_in": f32(attn_w_in),
              "attn_kv_norm": f32(attn_kv_norm), "attn_w_uk": f32(attn_w_uk), "attn_w_uv": f32(attn_w_uv),
              "attn_w_out": f32(attn_w_out), "ssm_w_in": f32(ssm_w_in), "ssm_cw": cw, "ssm_cb": cb,
              "ssm_dt_bias": f32(ssm_dt_bias), "ssm_a_log": f32(ssm_a_log), "ssm_d": f32(ssm_d), "ssm_norm_g": f32(ssm_norm_g),
              "ssm_w_out": f32(ssm_w_out)}
    nc = _prog("full", build_full_program)
    in_maps = [dict(shared, x=np.ascontiguousarray(x[c % BATCH])) for c in range(NCORES)]
    res = run_bass_kernel_spmd(nc, in_maps, core_ids=list(range(NCORES)))
    return np.stack([np.asarray(res.results[c]["out"], dtype=np.float32) for c in range(BATCH)])


kernel = kernel_fused
```

```python
import contextlib
import numpy as np
import concourse.bass as bass
import concourse.mybir as mybir
from concourse.bass_utils import run_bass_kernel_spmd

F32 = mybir.dt.float32
BF16 = mybir.dt.bfloat16
AF = mybir.ActivationFunctionType
ALU = mybir.AluOpType
AX = mybir.AxisListType

D = 1024
DFF = 2816
DEPTH = 4
ALPHA = (2.0 * DEPTH) ** 0.25
LN_EPS = 1e-5
NCORES = 8

ENGS = ("pe", "dve", "act", "pool", "sp")
N_DMA_SEM = 20
N_CSEM = 14
CSEM_CH = 128


class Op:
    __slots__ = ("eng", "fn", "deps", "is_dma", "has_dep", "semval", "dsem", "dval", "prev_ring")

    def __init__(self, eng, fn, is_dma):
        self.eng = eng
        self.fn = fn
        self.is_dma = is_dma
        self.deps = []
        self.has_dep = False
        self.semval = None
        self.dsem = None
        self.dval = None
        self.prev_ring = None


def _key(k):
    if isinstance(k, tuple):
        return tuple(_key(e) for e in k)
    if isinstance(k, (str, int)):
        return k
    return k.name


class Sched:
    def __init__(self, nc):
        self.nc = nc
        self.ops = {e: [] for e in ENGS}
        self.last_w = {}
        self.readers = {}
        self.dma_count = {e: 0 for e in ENGS}
        self.dma_hist = {e: [] for e in ENGS}
        self.n = 0
        self.n_barriers = 0

    def _add(self, eng, fn, reads, writes, is_dma):
        op = Op(eng, fn, is_dma)
        reads = [_key(r) for r in reads]
        writes = [_key(w) for w in writes]
        deps = []
        for r in reads:
            w = self.last_w.get(r)
            if w is not None:
                deps.append(w)
        for w_ in writes:
            w = self.last_w.get(w_)
            if w is not None:
                deps.append(w)
            rl = self.readers.get(w_, ())
            lastc = {}
            for o in rl:
                if o.is_dma:
                    deps.append(o)
                else:
                    lastc[o.eng] = o
            deps.extend(lastc.values())
        seen = set()
        for d in deps:
            if id(d) in seen:
                continue
            seen.add(id(d))
            if (not d.is_dma) and (not is_dma) and d.eng == "pe" and eng == "pe":
                continue
            op.deps.append(d)
            d.has_dep = True
        for r in reads:
            self.readers.setdefault(r, []).append(op)
        for w_ in writes:
            self.last_w[w_] = op
            self.readers[w_] = []
        if is_dma:
            j = self.dma_count[eng]
            self.dma_count[eng] += 1
            op.dsem = j % N_DMA_SEM
            op.dval = 16 * (j // N_DMA_SEM + 1)
            hist = self.dma_hist[eng]
            if j >= N_DMA_SEM:
                op.prev_ring = hist[j - N_DMA_SEM]
            hist.append(op)
        self.ops[eng].append(op)
        self.n += 1
        return op

    def op(self, eng, fn, reads=(), writes=()):
        return self._add(eng, fn, reads, writes, False)

    def dma(self, eng, fn, reads=(), writes=()):
        return self._add(eng, fn, reads, writes, True)

    def barrier(self):
        lasts = []
        for e in ENGS:
            for o in reversed(self.ops[e]):
                if o.fn is None:
                    break
                if not o.is_dma:
                    lasts.append(o)
                    break
            lasts.extend(self.dma_hist[e][-N_DMA_SEM:])
        self.n_barriers += 1
        for e in ENGS:
            op = Op(e, None, False)
            op.semval = self.n_barriers
            for d in lasts:
                op.deps.append(d)
                d.has_dep = True
            self.ops[e].append(op)
        self.last_w = {}
        self.readers = {}

    def emit(self, final_wait_ops=()):
        nc = self.nc
        for o in final_wait_ops:
            o.has_dep = True
        for e in ENGS:
            c = 0
            for o in self.ops[e]:
                if o.fn is None:
                    c = 0
                    continue
                if o.is_dma:
                    continue
                if o.has_dep:
                    c += 1
                    o.semval = c
        self.max_semval = {e: max([o.semval or 0 for o in self.ops[e] if o.fn is not None and not o.is_dma] + [0]) for e in ENGS}
        self.max_dval = {e: max([o.dval or 0 for o in self.ops[e] if o.is_dma] + [0]) for e in ENGS}
        with contextlib.ExitStack() as st:
            csem = {e: [st.enter_context(nc.semaphore("cs_%s_%d" % (e, i))) for i in range(N_CSEM)]
                    for e in ENGS if e != "sp"}
            dsem = {e: [st.enter_context(nc.semaphore("ds_%s_%d" % (e, i))) for i in range(N_DMA_SEM)]
                    for e in ENGS if any(o.is_dma for o in self.ops[e])}
            bsem = [st.enter_context(nc.semaphore("bar%d" % i)) for i in range(2)]
            any_dma = {e: any(o.is_dma for o in self.ops[e]) for e in ENGS}
            block = st.enter_context(nc.Block())

            def run(e, eng):
                waited = {}

                def wait_for(d):
                    if d.is_dma:
                        k = ("d", d.eng, d.dsem)
                        s, v = dsem[d.eng][d.dsem], d.dval
                        if waited.get(k, 0) >= v:
                            return
                        waited[k] = v
                        eng.wait_ge(s, v)
                    else:
                        k = ("c", d.eng)
                        c = d.semval
                        if waited.get(k, 0) >= c:
                            return
                        waited[k] = c
                        ep = (c - 1) // CSEM_CH
                        eng.wait_ge(csem[d.eng][ep % N_CSEM], CSEM_CH * (ep // N_CSEM) + ((c - 1) % CSEM_CH) + 1)

                for o in self.ops[e]:
                    for d in o.deps:
                        wait_for(d)
                    if o.prev_ring is not None:
                        wait_for(o.prev_ring)
                    if o.fn is None:
                        k = o.semval
                        eng.sem_inc(bsem[0], 1)
                        eng.wait_ge(bsem[0], len(ENGS) * k)
                        if e in csem:
                            for s_ in csem[e]:
                                eng.sem_clear(s_)
                        eng.sem_inc(bsem[1], 1)
                        eng.wait_ge(bsem[1], len(ENGS) * k)
                        for k_ in [k_ for k_ in waited if k_[0] == "c"]:
                            del waited[k_]
                        continue
                    ins = o.fn(eng)
                    if o.is_dma:
                        ins.then_inc(dsem[e][o.dsem], 16)
                    elif o.has_dep:
                        ins.then_inc(csem[e][((o.semval - 1) // CSEM_CH) % N_CSEM], 1)
                if e == "pool":
                    for d in final_wait_ops:
                        wait_for(d)

            for e, reg in (("sp", block.sync), ("pe", block.tensor), ("dve", block.vector),
                           ("act", block.scalar), ("pool", block.gpsimd)):
                reg(lambda eng, e=e: run(e, eng))


class Ctx:
    def __init__(self, nc, st):
        self.nc = nc
        self.S = Sched(nc)
        self.st = st
        self.uid = 0
        self.psum = []
        self.psum_i = 0

    def sb(self, shape, dtype, name=None, st=None):
        self.uid += 1
        nm = "%s_%d" % (name or "t", self.uid)
        return (st or self.st).enter_context(self.nc.sbuf_tensor(nm, list(shape), dtype))

    def ps(self, shape, dtype, name=None, st=None):
        self.uid += 1
        nm = "%s_%d" % (name or "p", self.uid)
        return (st or self.st).enter_context(self.nc.psum_tensor(nm, list(shape), dtype))


def setup_consts(cx):
    nc, S = cx.nc, cx.S
    cx.ident_f = cx.sb([128, 128], F32, "identf")
    cx.ident_b = cx.sb([128, 128], BF16, "identb")
    cx.ones_f = cx.sb([128, 128], F32, "onesf")
    idf, idb, onf = cx.ident_f, cx.ident_b, cx.ones_f
    S.op("pool", lambda e: e.memset(onf[:], 1.0), writes=[onf])
    S.op("pool", lambda e: e.affine_select(out=idf[:], in_=onf[:], pattern=[[-1, 128]],
                                           compare_op=ALU.is_equal, fill=0.0, base=0,
                                           channel_multiplier=1), reads=[onf], writes=[idf])
    S.op("pool", lambda e: e.tensor_copy(out=idb[:], in_=idf[:]), reads=[idf], writes=[idb])


def emit_ffn(cx, x_in, x_out, w_in, w_out, g, b, ntok, scale_y=0.5):
    nc, S = cx.nc, cx.S
    T = 256
    NS = T // 128
    KD = D // 128
    KF = DFF // 128
    c_y = scale_y / ALPHA
    eps_p = LN_EPS / (ALPHA * ALPHA)
    with contextlib.ExitStack() as st:
        win = cx.sb([128, KD, 2 * DFF], BF16, "win", st)
        wout = cx.sb([128, KF, D], BF16, "wout", st)
        stg = [cx.sb([128, 1408], F32, "stg", st) for _ in range(2)]
        gb = cx.sb([128, D], F32, "gb", st)
        bb = cx.sb([128, D], F32, "bb", st)
        epst = cx.sb([128, 1], F32, "eps", st)
        xs = [cx.sb([128, D], F32, "xs", st) for _ in range(4)]
        xb = [cx.sb([128, D], BF16, "xb", st) for _ in range(2)]
        xT = [cx.sb([128, KD, T], BF16, "xT", st) for _ in range(2)]
        hT = [cx.sb([128, KF, T], BF16, "hT", st) for _ in range(1)]
        sg = [cx.sb([128, T], F32, "sg", st) for _ in range(2)]
        r = [cx.sb([128, D], F32, "r", st) for _ in range(1)]
        xo = [cx.sb([128, D], F32, "xo", st) for _ in range(1)]
        stat = [cx.sb([128, 8], F32, "stat", st) for _ in range(2)]
        p_tp = cx.ps([128, 1024], BF16, "ptp", st)
        p_g = [cx.ps([128, 512], F32, "pg", st) for _ in range(2)]
        p_u = [cx.ps([128, 512], F32, "pu", st) for _ in range(2)]
        p_o = [cx.ps([128, 512], F32, "po", st) for _ in range(2)]

        S.op("pool", lambda e: e.memset(epst[:], eps_p), writes=[epst])
        S.dma("sp", lambda e: e.dma_start(out=gb[:], in_=g.partition_broadcast(128)), writes=[gb])
        S.dma("sp", lambda e: e.dma_start(out=bb[:], in_=b.partition_broadcast(128)), writes=[bb])

        w_in_v = w_in.rearrange("(k p) f -> p k f", p=128)
        w_out_v = w_out.rearrange("(k p) d -> p k d", p=128)
        cast_engs = ["dve", "act", "pool"]
        ci = 0
        pieces = []
        for k in range(KD):
            for hf in range(4):
                pieces.append((w_in_v[:, k, hf * 1408:(hf + 1) * 1408], win[:, k, hf * 1408:(hf + 1) * 1408],
                               (win, k), 1408))
        for k0 in range(KF):
            pieces.append((w_out_v[:, k0, :], wout[:, k0, :], (wout, k0), D))
        for i, (src, dst, key, n) in enumerate(pieces):
            sb_ = stg[i % 2]
            if len(src.shape) == 3:
                sview = sb_[:, 0:n].rearrange("p (a b) -> p a b", a=src.shape[1])
            else:
                sview = sb_[:, 0:n]
            S.dma("sp", lambda e, sview=sview, src=src: e.dma_start(out=sview, in_=src), writes=[sb_])
            ce = cast_engs[ci % 3]
            ci += 1
            if ce == "act":
                S.op("act", lambda e, dst=dst, sview=sview: e.copy(out=dst, in_=sview), reads=[sb_], writes=[key])
            else:
                S.op(ce, lambda e, dst=dst, sview=sview: e.tensor_copy(out=dst, in_=sview), reads=[sb_], writes=[key])
        win_keys = [(win, k) for k in range(KD)]
        wout_keys = [(wout, k) for k in range(KF // 2)]

        ntiles = ntok // T
        x_in_v = x_in.rearrange("(n p) d -> n p d", p=128)
        x_out_v = x_out.rearrange("(n p) d -> n p d", p=128)
        last_out = []

        def load_tile(ti):
            xTt = xT[ti % 2]
            for s in range(NS):
                xt = xs[(ti * NS + s) % 4]
                xbt = xb[s]
                S.dma("sp", lambda e, xt=xt, i=ti * NS + s: e.dma_start(out=xt[:], in_=x_in_v[i]), writes=[xt])
                S.op("pool", lambda e, xbt=xbt, xt=xt: e.tensor_copy(out=xbt[:], in_=xt[:]), reads=[xt], writes=[xbt])
                for k in range(KD):
                    S.op("pe", lambda e, k=k, xbt=xbt: e.transpose(out=p_tp[:, k * 128:(k + 1) * 128],
                                                                  in_=xbt[:, k * 128:(k + 1) * 128],
                                                                  identity=cx.ident_b[:]),
                         reads=[xbt, cx.ident_b], writes=[p_tp])
                S.op("dve", lambda e, xTt=xTt, s=s: e.tensor_copy(
                    out=xTt[:, :, s * 128:(s + 1) * 128],
                    in_=p_tp[:].rearrange("p (k t) -> p k t", k=KD)), reads=[p_tp], writes=[xTt])

        load_tile(0)
        for ti in range(ntiles):
            xTt = xT[ti % 2]
            hTt = hT[0]
            for j in range(KF):
                pg, pu, sgt = p_g[j % 2], p_u[j % 2], sg[j % 2]
                for k in range(KD):
                    S.op("pe", lambda e, k=k, j=j, pg=pg, xTt=xTt: e.matmul(pg[:, 0:T], lhsT=win[:, k, j * 128:(j + 1) * 128],
                                                                   rhs=xTt[:, k, :], start=(k == 0), stop=(k == KD - 1)),
                         reads=[(win, k), xTt], writes=[pg])
                for k in range(KD):
                    S.op("pe", lambda e, k=k, j=j, pu=pu, xTt=xTt: e.matmul(pu[:, 0:T],
                                                                   lhsT=win[:, k, DFF + j * 128:DFF + (j + 1) * 128],
                                                                   rhs=xTt[:, k, :], start=(k == 0), stop=(k == KD - 1)),
                         reads=[(win, k), xTt], writes=[pu])
                S.op("act", lambda e, pg=pg, sgt=sgt: e.activation(out=sgt[:], in_=pg[:, 0:T], func=AF.Silu),
                     reads=[pg], writes=[sgt])
                S.op("dve", lambda e, pu=pu, sgt=sgt, j=j, hTt=hTt: e.tensor_tensor(out=hTt[:, j, :], in0=pu[:, 0:T], in1=sgt[:],
                                                                            op=ALU.mult),
                     reads=[pu, sgt], writes=[(hTt, j)])
            if ti + 1 < ntiles:
                load_tile(ti + 1)
            for s in range(NS):
                xt = xs[(ti * NS + s) % 4]
                rt, xot, stt = r[0], xo[0], stat[s]
                for hd in range(2):
                    po = p_o[hd]
                    for j in range(KF):
                        S.op("pe", lambda e, j=j, hd=hd, s=s, po=po, hTt=hTt: e.matmul(
                            po[:], lhsT=hTt[:, j, s * 128:(s + 1) * 128], rhs=wout[:, j, hd * 512:(hd + 1) * 512],
                            start=(j == 0), stop=(j == KF - 1)),
                             reads=[(hTt, j), (wout, j)], writes=[po])
                    S.op("dve", lambda e, po=po, hd=hd, rt=rt, xt=xt: e.scalar_tensor_tensor(
                        out=rt[:, hd * 512:(hd + 1) * 512], in0=po[:], scalar=c_y, in1=xt[:, hd * 512:(hd + 1) * 512],
                        op0=ALU.mult, op1=ALU.add), reads=[po, xt], writes=[(rt, hd)])
                emit_ln(cx, rt, [(rt, 0), (rt, 1)], xot, stt, epst, gb, bb, xot)
                o = S.dma("pool", lambda e, xot=xot, i=ti * NS + s: e.dma_start(out=x_out_v[i], in_=xot[:]),
                          reads=[xot], writes=[("dram_xout", ti * NS + s)])
                last_out.append(o)
        return last_out


def emit_ln(cx, rt, rkeys, junk, stt, epst, gb, bb, xot):
    S = cx.S
    S.op("act", lambda e: e.activation(out=junk[:], in_=rt[:], func=AF.Square, accum_out=stt[:, 0:1]),
         reads=rkeys, writes=[junk, (stt, 0)])
    S.op("dve", lambda e: e.tensor_reduce(out=stt[:, 1:2], in_=rt[:], axis=AX.X, op=ALU.add),
         reads=rkeys, writes=[(stt, 1)])
    S.op("dve", lambda e: e.tensor_scalar(out=stt[:, 2:3], in0=stt[:, 1:2], scalar1=1.0 / D, scalar2=None, op0=ALU.mult),
         reads=[(stt, 1)], writes=[(stt, 2)])
    S.op("dve", lambda e: e.tensor_tensor(out=stt[:, 3:4], in0=stt[:, 2:3], in1=stt[:, 2:3], op=ALU.mult),
         reads=[(stt, 2)], writes=[(stt, 3)])
    S.op("dve", lambda e: e.scalar_tensor_tensor(out=stt[:, 4:5], in0=stt[:, 0:1], scalar=1.0 / D, in1=stt[:, 3:4],
                                                 op0=ALU.mult, op1=ALU.subtract),
         reads=[(stt, 0), (stt, 3)], writes=[(stt, 4)])
    S.op("act", lambda e: e.activation(out=stt[:, 5:6], in_=stt[:, 4:5], func=AF.Sqrt, bias=epst[:]),
         reads=[(stt, 4), epst], writes=[(stt, 5)])
    S.op("dve", lambda e: e.reciprocal(out=stt[:, 6:7], in_=stt[:, 5:6]), reads=[(stt, 5)], writes=[(stt, 6)])
    S.op("dve", lambda e: e.tensor_scalar(out=xot[:], in0=rt[:], scalar1=stt[:, 2:3], scalar2=stt[:, 6:7],
                                          op0=ALU.subtract, op1=ALU.mult),
         reads=rkeys + [(stt, 2), (stt, 6)], writes=[xot])
    S.op("pool", lambda e: e.tensor_tensor(out=xot[:], in0=xot[:], in1=gb[:], op=ALU.mult), reads=[xot, gb], writes=[xot])
    S.op("pool", lambda e: e.tensor_tensor(out=xot[:], in0=xot[:], in1=bb[:], op=ALU.add), reads=[xot, bb], writes=[xot])


def build_ffn_program(ntok):
    nc = bass.Bass("TRN2", target_bir_lowering=False)
    x_in = nc.dram_tensor("x_in", [ntok, D], F32, kind="ExternalInput").ap()
    w_in = nc.dram_tensor("w_in", [D, 2 * DFF], F32, kind="ExternalInput").ap()
    w_out = nc.dram_tensor("w_out", [DFF, D], F32, kind="ExternalInput").ap()
    g = nc.dram_tensor("g", [D], F32, kind="ExternalInput").ap()
    b = nc.dram_tensor("b", [D], F32, kind="ExternalInput").ap()
    x_out = nc.dram_tensor("x_out", [ntok, D], F32, kind="ExternalOutput").ap()
    with contextlib.ExitStack() as st:
        cx = Ctx(nc, st)
        setup_consts(cx)
        outs = emit_ffn(cx, x_in, x_out, w_in, w_out, g, b, ntok)
        cx.S.emit(final_wait_ops=outs[-N_DMA_SEM:])
    return nc


def mm(cx, out, lhsT, rhs, start, stop, reads, writes):
    return cx.S.op("pe", lambda e: e.matmul(out, lhsT=lhsT, rhs=rhs, start=start, stop=stop), reads, writes)


def tr(cx, out, in_, ident, reads, writes):
    return cx.S.op("pe", lambda e: e.transpose(out=out, in_=in_, identity=ident), reads, writes)


def act(cx, out, in_, func, reads, writes, **kw):
    return cx.S.op("act", lambda e: e.activation(out=out, in_=in_, func=func, **kw), reads, writes)


def tt(cx, eng, out, in0, in1, op, reads, writes):
    return cx.S.op(eng, lambda e: e.tensor_tensor(out=out, in0=in0, in1=in1, op=op), reads, writes)


def ts(cx, eng, out, in0, s1, s2, op0, op1, reads, writes, accum_out=None):
    if op1 is None:
        return cx.S.op(eng, lambda e: e.tensor_scalar(out=out, in0=in0, scalar1=s1, scalar2=None, op0=op0,
                                                      accum_out=accum_out), reads, writes)
    return cx.S.op(eng, lambda e: e.tensor_scalar(out=out, in0=in0, scalar1=s1, scalar2=s2, op0=op0, op1=op1,
                                                  accum_out=accum_out), reads, writes)


def stt(cx, eng, out, in0, scalar, in1, op0, op1, reads, writes):
    return cx.S.op(eng, lambda e: e.scalar_tensor_tensor(out=out, in0=in0, scalar=scalar, in1=in1, op0=op0, op1=op1),
                   reads, writes)


def cp(cx, eng, out, in_, reads, writes):
    if eng == "act":
        return cx.S.op("act", lambda e: e.copy(out=out, in_=in_), reads, writes)
    return cx.S.op(eng, lambda e: e.tensor_copy(out=out, in_=in_), reads, writes)


def dma(cx, q, out, in_, reads, writes):
    return cx.S.dma(q, lambda e: e.dma_start(out=out, in_=in_), reads, writes)


def memset(cx, eng, ap, val, writes):
    return cx.S.op(eng, lambda e: e.memset(ap, val), (), writes)


def load_weight_bf16(cx, dst, src, stg, n, key, idx):
    sb_ = stg[idx % len(stg)]
    if len(src.shape) == 3:
        sview = sb_[:, 0:n].rearrange("p (a b) -> p a b", a=src.shape[1])
    else:
        sview = sb_[:, 0:n]
    dma(cx, "sp", sview, src, [], [sb_])
    ce = ("dve", "act", "pool")[idx % 3]
    cp(cx, ce, dst, sview, [sb_], [key])


def emit_xT(cx, x_rows, xs, xb, p_tp, xT, col0):
    dma(cx, "sp", xs[:], x_rows, [], [xs])
    cp(cx, "pool", xb[:], xs[:], [xs], [xb])
    for k in range(D // 128):
        tr(cx, p_tp[:, k * 128:(k + 1) * 128], xb[:, k * 128:(k + 1) * 128], cx.ident_b[:], [xb, cx.ident_b], [p_tp])
    cp(cx, "dve", xT[:, :, col0:col0 + 128], p_tp[:].rearrange("p (k t) -> p k t", k=D // 128), [p_tp], [xT])


def emit_proj(cx, x_rows_fn, ntok, wcols, fm_specs, tm_specs):
    TT = 512
    KD = D // 128
    NW = sum(w.shape[1] for w in wcols)
    with contextlib.ExitStack() as st:
        wsb = cx.sb([128, KD, NW], BF16, "pw", st)
        stg = [cx.sb([128, 1024], F32, "pstg", st) for _ in range(2)]
        xs = [cx.sb([128, D], F32, "pxs", st) for _ in range(2)]
        xb = [cx.sb([128, D], BF16, "pxb", st) for _ in range(2)]
        xT = [cx.sb([128, KD, TT], BF16, "pxT", st) for _ in range(2)]
        ost = [cx.sb([128, TT], F32, "post", st) for _ in range(3)]
        p_tp = cx.ps([128, 1024], BF16, "pptp", st)
        p_fm = [cx.ps([128, 512], F32, "ppfm", st) for _ in range(2)]
        p_tm = [cx.ps([128, 512], F32, "pptm", st) for _ in range(2)]
        tm_state = {}
        idx = 0
        off = 0
        for w in wcols:
            n = w.shape[1]
            wv = w.rearrange("(k p) f -> p k f", p=128)
            for k in range(KD):
                for c0 in range(0, n, 1024):
                    c1 = min(n, c0 + 1024)
                    load_weight_bf16(cx, wsb[:, k, off + c0:off + c1], wv[:, k, c0:c1], stg, c1 - c0, wsb, idx)
                    idx += 1
            off += n
        ntiles = ntok // TT
        oi = 0
        fi = 0
        ti_ = 0
        for t in range(ntiles):
            xTt = xT[t % 2]
            for s in range(TT // 128):
                emit_xT(cx, x_rows_fn(t * 4 + s), xs[s % 2], xb[s % 2], p_tp, xTt, s * 128)
            for (coff, M, dt_, dest_fn) in fm_specs:
                pf = p_fm[fi % 2]
                fi += 1
                for k in range(KD):
                    mm(cx, pf[0:M, :], wsb[:, k, coff:coff + M], xTt[:, k, :], k == 0, k == KD - 1, [wsb, xTt], [pf])
                o = ost[oi % 3]
                oi += 1
                ov = o[0:M, :] if dt_ == F32 else o[:].bitcast(BF16)[0:M, 0:TT]
                cp(cx, "act" if oi % 2 else "dve", ov, pf[0:M, :], [pf], [o])
                dma(cx, "pool", dest_fn(t), ov, [o], [])
            for (coff, N, post_fn) in tm_specs:
                for s in range(TT // 128):
                    pt = p_tm[ti_ % 2]
                    ti_ += 1
                    for k in range(KD):
                        mm(cx, pt[:, 0:N], xTt[:, k, s * 128:(s + 1) * 128], wsb[:, k, coff:coff + N], k == 0, k == KD - 1,
                           [wsb, xTt], [pt])
                    post_fn(cx, st, tm_state, pt, t * 4 + s)


def emit_outproj_ln(cx, srcT, nch, w, x_in, x_out, g, b, ntok, scale_y):
    c_y = scale_y / ALPHA
    eps_p = LN_EPS / (ALPHA * ALPHA)
    with contextlib.ExitStack() as st:
        wsb = cx.sb([128, nch, D], BF16, "ow", st)
        stg = [cx.sb([128, 1024], F32, "ostg", st) for _ in range(2)]
        gb = cx.sb([128, D], F32, "ogb", st)
        bb = cx.sb([128, D], F32, "obb", st)
        epst = cx.sb([128, 1], F32, "oeps", st)
        oT = [cx.sb([128, nch, 128], BF16, "ooT", st) for _ in range(2)]
        xs = [cx.sb([128, D], F32, "oxs", st) for _ in range(2)]
        r = cx.sb([128, D], F32, "or", st)
        xo = [cx.sb([128, D], F32, "oxo", st) for _ in range(2)]
        stat = [cx.sb([128, 8], F32, "ostat", st) for _ in range(2)]
        p_o = [cx.ps([128, 512], F32, "opo", st) for _ in range(4)]
        memset(cx, "pool", epst[:], eps_p, [epst])
        dma(cx, "sp", gb[:], g.partition_broadcast(128), [], [gb])
        dma(cx, "sp", bb[:], b.partition_broadcast(128), [], [bb])
        wv = w.rearrange("(k p) d -> p k d", p=128)
        for k in range(nch):
            load_weight_bf16(cx, wsb[:, k, :], wv[:, k, :], stg, D, wsb, k)
        sv = srcT.rearrange("c p t -> p c t")
        xiv = x_in.rearrange("(n p) d -> n p d", p=128)
        xov = x_out.rearrange("(n p) d -> n p d", p=128)
        outs = []
        for i in range(ntok // 128):
            oTt, xt, xot, stt_ = oT[i % 2], xs[i % 2], xo[i % 2], stat[i % 2]
            dma(cx, "sp", oTt[:], sv[:, :, i * 128:(i + 1) * 128], [], [oTt])
            dma(cx, "sp", xt[:], xiv[i], [], [xt])
            for hd in range(2):
                po = p_o[(2 * i + hd) % 4]
                for c in range(nch):
                    mm(cx, po[:], oTt[:, c, :], wsb[:, c, hd * 512:(hd + 1) * 512], c == 0, c == nch - 1, [oTt, wsb], [po])
                stt(cx, "dve", r[:, hd * 512:(hd + 1) * 512], po[:], c_y, xt[:, hd * 512:(hd + 1) * 512], ALU.mult, ALU.add,
                    [po, xt], [(r, hd)])
            emit_ln(cx, r, [(r, 0), (r, 1)], xot, stt_, epst, gb, bb, xot)
            outs.append(dma(cx, "pool", xov[i], xot[:], [xot], []))
        return outs


TOPK = 256
NBIS = 18


def emit_attn_core(cx, ckvT_d, kidxT_d, qT_d, qiT_d, widx_d, qpos_d, w_uk, w_uv, oT_d, NQ, SK, natural=False):
    S = cx.S
    NT = NQ // 256
    nkb_max = SK // 128
    with contextlib.ExitStack() as st:
        ckvT = cx.sb([128, 2, SK], BF16, "ackv", st)
        kidxT = cx.sb([64, SK], BF16, "akidx", st)
        Vp = cx.sb([128, nkb_max, 128], BF16, "aV", st)
        score = cx.sb([128, SK], F32, "ascore", st)
        msk = cx.sb([128, SK], BF16, "amsk", st)
        inv = [[cx.sb([128, SK], mybir.dt.float8e4, "ainv", st) for _ in range(2)] for _ in range(2)]
        sel = [cx.sb([128, 512], mybir.dt.float8e4, "asel", st) for _ in range(2)]
        qabs = cx.sb([128, 8, 2, 512], BF16, "aqabs", st)
        qT = cx.sb([128, 8, 256], BF16, "aqT", st)
        qiT = cx.sb([64, 8, 256], BF16, "aqiT", st)
        oT = cx.sb([128, 8, 256], BF16, "aoT", st)
        pT = [cx.sb([128, 512], BF16, "apT", st) for _ in range(2)]
        rl = [cx.sb([128, 512], F32, "arl", st) for _ in range(2)]
        wuk = cx.sb([128, 8, 256], BF16, "awuk", st)
        wuv = cx.sb([128, 2, 16, 64], BF16, "awuv", st)
        stg = [cx.sb([128, 1024], F32, "astg", st) for _ in range(1)]
        iota_f = cx.sb([128, 512], F32, "aiof", st)
        pen = cx.sb([128, 512], F32, "apen", st)
        qpos = cx.sb([128, NQ // 128], F32, "aqpos", st)
        widx = [cx.sb([128, 8], F32, "awidx", st) for _ in range(2)]
        sm = cx.sb([128, 8], F32, "asm", st)
        rec = cx.sb([128, 256], F32, "arec", st)
        ones_b = cx.sb([128, 128], BF16, "aones", st)
        P = [cx.ps([128, 512], F32, "aP", st) for _ in range(8)]
        gen = [P[0], P[1]]
        gi = [0]

        def nextp():
            gi[0] += 1
            return gen[gi[0] % 2]

        memset(cx, "pool", ones_b[:], 1.0, [ones_b])
        for b_ in range(2):
            memset(cx, "pool", pen[:], 0.0, [pen])
            for h2 in range(2):
                c0_ = h2 * 256 + b_ * 128
                ts(cx, "dve", pen[:, c0_:c0_ + 128], cx.ident_f[:], -240.0, None, ALU.mult, None, [cx.ident_f, pen], [pen])
            S.op("dve", lambda e, b_=b_: e.tensor_copy(out=sel[b_][:], in_=pen[:], saturate=False), [pen], [sel[b_]])
        S.op("pool", lambda e: e.iota(pen[:].bitcast(mybir.dt.int32), pattern=[[1, 512]], base=0, channel_multiplier=0), (), [pen])
        cp(cx, "pool", iota_f[:], pen[:].bitcast(mybir.dt.int32), [pen], [iota_f])
        dma(cx, "sp", qpos[:], qpos_d, [], [qpos])
        for rc in range(2):
            dma(cx, "sp", ckvT[:, rc, :], ckvT_d[rc], [], [ckvT])
        dma(cx, "sp", kidxT[:], kidxT_d, [], [kidxT])
        wukv = w_uk.rearrange("(p h2) dh r -> (h2 dh) p r", h2=2)
        for hf in range(2):
            load_weight_bf16(cx, wuk[:, hf * 4:(hf + 1) * 4, :], wukv[:, hf * 4:(hf + 1) * 4, :], stg, 1024, wuk, hf)
        for rc in range(2):
            load_weight_bf16(cx, wuv[:, rc, :, :], w_uv[:, rc * 128:(rc + 1) * 128, :].rearrange("h r dv -> r h dv"),
                             stg, 1024, (wuv, rc), 1 + rc)
        qT_v = qT_d.rearrange("c p t -> p c t")
        qiT_v = qiT_d.rearrange("c p t -> p c t")
        oT_v = oT_d.rearrange("c p t -> p c t")
        widx_v = widx_d.rearrange("(n p) h -> n p h", p=128)

        FP8 = mybir.dt.float8e4
        side_p = [P[5], P[7]]
        si = [0]

        def nexts():
            si[0] += 1
            return side_p[si[0] % 2]

        def tile_dims(kq):
            nkc = (kq // 2 + 1) if natural else (kq + 1)
            return nkc, 4 * nkc, 512 * nkc

        def side_units(kq):
            nkc, nkb, L = tile_dims(kq)
            q0 = kq * 256
            U = []
            U.append(lambda: dma(cx, "sp", qiT[:], qiT_v[:, :, q0:q0 + 256], [], [qiT]))
            for b in range(2):
                wt = widx[b]
                U.append(lambda wt=wt, b=b: dma(cx, "sp", wt[:], widx_v[kq * 2 + b], [], [wt]))
                for kc in range(nkc):
                    for h in range(8):
                        def u(b=b, kc=kc, h=h, wt=wt):
                            sc = score[:, kc * 512:(kc + 1) * 512]
                            ps = nexts()
                            mm(cx, ps[:], qiT[:, h, b * 128:(b + 1) * 128], kidxT[:, kc * 512:(kc + 1) * 512], True, True,
                               [qiT, kidxT], [ps])
                            rt_ = rl[h % 2]
                            act(cx, rt_[:], ps[:], AF.Relu, [ps], [rt_])
                            if h == 0:
                                ts(cx, "dve", sc, rt_[:], wt[:, 0:1], None, ALU.mult, None, [rt_, wt], [(score, kc)])
                            else:
                                stt(cx, "dve", sc, rt_[:], wt[:, h:h + 1], sc, ALU.mult, ALU.add, [rt_, wt, (score, kc)], [(score, kc)])
                        U.append(u)
                skeys = [(score, kc) for kc in range(nkc)]

                def pen_u(b=b):
                    ts(cx, "dve", sm[:, 4:5], qpos[:, kq * 2 + b:kq * 2 + b + 1], float(-512 * (nkc - 1)), None, ALU.add, None,
                       [qpos], [(sm, 4)])
                    ts(cx, "dve", pen[:], iota_f[:], sm[:, 4:5], -30000.0, ALU.is_gt, ALU.mult, [iota_f, (sm, 4)], [pen])
                    lc = score[:, (nkc - 1) * 512:nkc * 512]
                    tt(cx, "dve", lc, lc, pen[:], ALU.add, [pen, (score, nkc - 1)], [(score, nkc - 1)])
                    memset(cx, "dve", sm[:, 0:1], 0.0, [(sm, 0)])
                U.append(pen_u)
                W = 128.0
                for it in range(NBIS):
                    def bis(W=W):
                        S.op("dve", lambda e: e.tensor_scalar(out=msk[:, 0:L], in0=score[:, 0:L], scalar1=sm[:, 0:1], scalar2=None,
                                                              op0=ALU.is_ge, op1=ALU.add, accum_out=sm[:, 1:2]),
                             skeys + [(sm, 0)], [msk, (sm, 1)])
                        ts(cx, "dve", sm[:, 2:3], sm[:, 1:2], TOPK - 0.5, W / 2, ALU.is_ge, ALU.mult, [(sm, 1)], [(sm, 2)])
                        stt(cx, "dve", sm[:, 0:1], sm[:, 0:1], -W / 4, sm[:, 2:3], ALU.add, ALU.add, [(sm, 0), (sm, 2)], [(sm, 0)])
                    U.append(bis)
                    W = W / 2

                def fin(W=W, b=b):
                    ts(cx, "dve", sm[:, 3:4], sm[:, 0:1], -W / 2, None, ALU.add, None, [(sm, 0)], [(sm, 3)])
                    iv = inv[kq % 2][b]
                    S.op("dve", lambda e: e.tensor_scalar(out=iv[:, 0:L], in0=score[:, 0:L], scalar1=sm[:, 3:4], scalar2=128.0,
                                                          op0=ALU.is_lt, op1=ALU.mult, saturate=False),
                         skeys + [(sm, 3)], [iv])
                U.append(fin)
            return U

        def attention(kq, side):
            nkc, nkb, L = tile_dims(kq)
            q0 = kq * 256
            n_iter = 8 * nkb
            rate = (len(side) + n_iter - 1) // n_iter if side else 0
            dma(cx, "sp", qT[:], qT_v[:, :, q0:q0 + 256], [], [qT])
            for h in range(16):
                p_, h2 = h // 2, h % 2
                pq = nextp()
                for rc in range(2):
                    mm(cx, pq[:, rc * 256:(rc + 1) * 256], wuk[h2 * 64:(h2 + 1) * 64, p_, rc * 128:(rc + 1) * 128],
                       qT[h2 * 64:(h2 + 1) * 64, p_, :], True, True, [wuk, qT], [pq])
                cp(cx, "act", qabs[:, p_, :, h2 * 256:(h2 + 1) * 256],
                   pq[:].rearrange("p (a b) -> p a b", a=2), [pq], [(qabs, h)])
            for p_ in range(8):
                for kb0 in range(0, nkb, 4):
                    pv = nextp()
                    for j in range(4):
                        for rc in range(2):
                            mm(cx, pv[:, j * 128:(j + 1) * 128], ckvT[:, rc, (kb0 + j) * 128:(kb0 + j + 1) * 128],
                               wuv[:, rc, 2 * p_:2 * p_ + 2, :].rearrange("p a b -> p (a b)"), rc == 0, rc == 1,
                               [ckvT, (wuv, rc)], [pv])
                    cp(cx, "act", Vp[:, kb0:kb0 + 4, :], pv[:].rearrange("p (a b) -> p a b", a=4), [pv], [Vp])

                def qk(kb):
                    pl = P[2 + kb % 2]
                    for rc in range(2):
                        mm(cx, pl[:], ckvT[:, rc, kb * 128:(kb + 1) * 128], qabs[:, p_, rc, :], rc == 0, False,
                           [ckvT, (qabs, 2 * p_), (qabs, 2 * p_ + 1)], [pl])
                    for b_ in range(2):
                        iv = inv[kq % 2][b_]
                        mm(cx, pl[:], iv[:, kb * 128:(kb + 1) * 128], sel[b_][:], False, b_ == 1, [iv, sel[b_]], [pl])

                qk(0)
                for kb in range(nkb):
                    pl = P[2 + kb % 2]
                    if kb + 1 < nkb:
                        qk(kb + 1)
                    pTt = pT[kb % 2]
                    act(cx, pTt[:], pl[:], AF.Exp, [pl], [pTt], scale=0.125)
                    first, last = kb == 0, kb == nkb - 1
                    mm(cx, P[4][:], Vp[:, kb, :], pTt[:], first, last, [Vp, pTt], [P[4]])
                    mm(cx, P[6][:], ones_b[:], pTt[:], first, last, [ones_b, pTt], [P[6]])
                    for _ in range(rate):
                        if side:
                            side.pop(0)()
                S.op("dve", lambda e: e.reciprocal(out=rec[0:64, :], in_=P[6][0:64, 0:256]), [P[6]], [(rec, 0)])
                S.op("dve", lambda e: e.reciprocal(out=rec[64:128, :], in_=P[6][64:128, 256:512]), [P[6]], [(rec, 1)])
                tt(cx, "dve", oT[0:64, p_, :], P[4][0:64, 0:256], rec[0:64, :], ALU.mult, [P[4], (rec, 0)], [(oT, p_, 0)])
                tt(cx, "dve", oT[64:128, p_, :], P[4][64:128, 256:512], rec[64:128, :], ALU.mult, [P[4], (rec, 1)], [(oT, p_, 1)])
            dma(cx, "pool", oT_v[:, :, q0:q0 + 256], oT[:], [(oT, p_, i) for p_ in range(8) for i in range(2)], [])
            while side:
                side.pop(0)()

        for u in side_units(0):
            u()
        for kq in range(NT):
            attention(kq, side_units(kq + 1) if kq + 1 < NT else [])


ATT_O1, ATT_O2, ATT_O3, ATT_O4 = 1024, 1280, 1792, 1856
RMS_EPS = 1e-6


def emit_attn_stage(cx, xq, xk, qpos_d, w_in, kv_g, w_uk, w_uv, w_o, g, b, x_out, NQ, SK, scr, natural=False):
    nc, S = cx.nc, cx.S
    ckvT_d, kidxT_d, qT_d, qiT_d, widx_d, oT_d = (scr[k] for k in ("ckvT", "kidxT", "qT", "qiT", "widx", "oT"))
    xk_v = xk.rearrange("(n p) d -> n p d", p=128)
    xq_v = xq.rearrange("(n p) d -> n p d", p=128)

    def post_ckv(cx, st, state, pt, blk):
        if "init" not in state:
            state["init"] = True
            state["gb"] = cx.sb([128, 256], F32, "kg", st)
            state["sq"] = cx.sb([128, 256], F32, "ksq", st)
            state["cn"] = [cx.sb([128, 256], BF16, "kcn", st) for _ in range(2)]
            state["cT"] = [cx.sb([128, 2, 128], BF16, "kcT", st) for _ in range(2)]
            state["sm"] = [cx.sb([128, 4], F32, "ksm", st) for _ in range(2)]
            state["eps"] = cx.sb([128, 1], F32, "keps", st)
            state["ptp"] = cx.ps([128, 512], BF16, "kptp", st)
            memset(cx, "pool", state["eps"][:], RMS_EPS, [state["eps"]])
            dma(cx, "sp", state["gb"][:], kv_g.partition_broadcast(128), [], [state["gb"]])
        gb_, sq, cn, cT, sm_, eps, ptp = (state["gb"], state["sq"], state["cn"][blk % 2], state["cT"][blk % 2],
                                        state["sm"][blk % 2], state["eps"], state["ptp"])
        act(cx, sq[:], pt[:, 0:256], AF.Square, [pt], [sq, (sm_, 0)], accum_out=sm_[:, 0:1])
        act(cx, sm_[:, 1:2], sm_[:, 0:1], AF.Sqrt, [(sm_, 0), eps], [(sm_, 1)], bias=eps[:], scale=1.0 / 256)
        cx.S.op("dve", lambda e: e.reciprocal(out=sm_[:, 2:3], in_=sm_[:, 1:2]), [(sm_, 1)], [(sm_, 2)])
        stt(cx, "dve", cn[:], pt[:, 0:256], sm_[:, 2:3], gb_[:], ALU.mult, ALU.mult, [pt, (sm_, 2), gb_], [cn])
        for rc in range(2):
            tr(cx, ptp[:, rc * 128:(rc + 1) * 128], cn[:, rc * 128:(rc + 1) * 128], cx.ident_b[:], [cn, cx.ident_b], [ptp])
        cp(cx, "act", cT[:], ptp[:, 0:256].rearrange("p (a b) -> p a b", a=2), [ptp], [cT])
        dma(cx, "pool", ckvT_d.rearrange("c p t -> p c t")[:, :, blk * 128:(blk + 1) * 128], cT[:], [cT], [])

    emit_proj(cx, lambda i: xk_v[i], SK, [w_in[:, ATT_O1:ATT_O2], w_in[:, ATT_O3:ATT_O4]],
              fm_specs=[(256, 64, BF16, lambda t: kidxT_d[:, t * 512:(t + 1) * 512])],
              tm_specs=[(0, 256, post_ckv)])
    S.barrier()

    def post_widx(cx, st, state, pt, blk):
        if "w" not in state:
            state["w"] = [cx.sb([128, 8], F32, "qw", st) for _ in range(2)]
        wt = state["w"][blk % 2]
        ts(cx, "dve", wt[:], pt[:, 0:8], (8 ** -0.5) * (64 ** -0.5), None, ALU.mult, None, [pt], [wt])
        dma(cx, "pool", widx_d[blk * 128:(blk + 1) * 128, :], wt[:], [wt], [])

    def mk_q(c):
        return lambda t: qT_d[c][:, t * 512:(t + 1) * 512]

    def mk_qi(c):
        return lambda t: qiT_d[c][:, t * 512:(t + 1) * 512]

    fm = [(c * 128, 128, BF16, mk_q(c)) for c in range(8)] + [(1024 + c * 64, 64, BF16, mk_qi(c)) for c in range(8)]
    emit_proj(cx, lambda i: xq_v[i], NQ, [w_in[:, 0:ATT_O1], w_in[:, ATT_O2:ATT_O3], w_in[:, ATT_O4:ATT_O4 + 8]],
              fm_specs=fm, tm_specs=[(1536, 8, post_widx)])
    S.barrier()
    emit_attn_core(cx, ckvT_d, kidxT_d, qT_d, qiT_d, widx_d, qpos_d, w_uk, w_uv, oT_d, NQ, SK, natural)
    S.barrier()
    outs = emit_outproj_ln(cx, oT_d, 8, w_o, xq, x_out, g, b, NQ, 1.0)
    return outs


def attn_scratch(nc, NQ, SK, tag=""):
    mk = lambda n, shp, dt_: nc.dram_tensor(n + tag, shp, dt_, kind="Internal").ap()
    return {"ckvT": mk("s_ckvT", [2, 128, SK], BF16), "kidxT": mk("s_kidxT", [64, SK], BF16),
            "qT": mk("s_qT", [8, 128, NQ], BF16), "qiT": mk("s_qiT", [8, 64, NQ], BF16),
            "widx": mk("s_widx", [NQ, 8], F32), "oT": mk("s_oT", [8, 128, NQ], BF16)}


def build_attn_program(NQ, SK):
    nc = bass.Bass("TRN2", target_bir_lowering=False)
    inp = lambda n, shp: nc.dram_tensor(n, shp, F32, kind="ExternalInput").ap()
    xq, xk = inp("xq", [NQ, D]), inp("xk", [SK, D])
    qpos = inp("qpos", [128, NQ // 128])
    w_in, kv_g = inp("w_in", [D, 1864]), inp("kv_g", [256])
    w_uk, w_uv, w_o = inp("w_uk", [16, 64, 256]), inp("w_uv", [16, 256, 64]), inp("w_o", [D, D])
    g, b = inp("g", [D]), inp("b", [D])
    x_out = nc.dram_tensor("x_out", [NQ, D], F32, kind="ExternalOutput").ap()
    scr = attn_scratch(nc, NQ, SK)
    with contextlib.ExitStack() as st:
        cx = Ctx(nc, st)
        setup_consts(cx)
        outs = emit_attn_stage(cx, xq, xk, qpos, w_in, kv_g, w_uk, w_uv, w_o, g, b, x_out, NQ, SK, scr)
        cx.S.emit(final_wait_ops=outs[-N_DMA_SEM:])
    return nc


def emit_ssd_core(cx, z_d, xbc_d, dt_d, cw_d, cb_d, dtb_d, alog_d, dsk_d, ng_d, yzT_d, SK):
    S = cx.S
    NCH = SK // 128
    with contextlib.ExitStack() as st:
        H = cx.sb([128, 1024], F32, "sH", st)
        Hb = cx.sb([128, 1024], BF16, "sHb", st)
        pre = [cx.sb([128, 515], F32, "spre", st) for _ in range(2)]
        acc = [cx.sb([128, 512], F32, "sacc", st) for _ in range(2)]
        xsT = cx.sb([128, 8, 512], F32, "sxsT", st)
        BT = cx.sb([128, 2, 512], BF16, "sBT", st)
        CT = cx.sb([128, 2, 512], BF16, "sCT", st)
        cw = cx.sb([128, 12, 4], F32, "scw", st)
        cb = cx.sb([128, 12], F32, "scb", st)
        dtb = cx.sb([128, 16], F32, "sdtb", st)
        a_bc = cx.sb([128, 16], F32, "sabc", st)
        d_bc = cx.sb([128, 16], F32, "sdbc", st)
        ng = cx.sb([128, 1024], F32, "sng", st)
        triu = cx.sb([128, 128], F32, "striu", st)
        mgt = cx.sb([128, 128], F32, "smgt", st)
        xs_tok = cx.sb([128, 1024], F32, "sxs", st)
        Btok = cx.sb([128, 2, 128], BF16, "sBtok", st)
        sm = [cx.sb([128, 8, 16], F32, "ssm", st) for _ in range(2)]
        G = [cx.sb([128, 128], F32, "sG", st) for _ in range(4)]
        dec = [cx.sb([128, 4, 128], F32, "sdec", st) for _ in range(2)]
        cbt = cx.sb([128, 2, 128], F32, "scbt", st)
        MT = cx.sb([128, 16, 128], BF16, "sMT", st)
        xdt = cx.sb([128, 1024], BF16, "sxdt", st)
        xdtd = cx.sb([128, 1024], BF16, "sxdtd", st)
        t1 = cx.sb([128, 1024], F32, "st1", st)
        xsD = cx.sb([128, 1024], F32, "sxsD", st)
        y = cx.sb([128, 1024], F32, "sy", st)
        zt = [cx.sb([128, 1024], F32, "szt", st) for _ in range(2)]
        zs = cx.sb([128, 1024], F32, "szs", st)
        yzn = cx.sb([128, 1024], BF16, "syzn", st)
        yzT = [cx.sb([128, 8, 128], BF16, "syzT", st) for _ in range(2)]
        st2 = [cx.sb([128, 8], F32, "sst2", st) for _ in range(2)]
        epst = cx.sb([128, 1], F32, "sepst", st)
        P = [cx.ps([128, 512], F32, "sP", st) for _ in range(8)]

        memset(cx, "pool", epst[:], RMS_EPS, [epst])
        memset(cx, "pool", H[:], 0.0, [H])
        memset(cx, "pool", Hb[:], 0.0, [Hb])
        S.op("pool", lambda e: e.affine_select(out=triu[:], in_=cx.ones_f[:], pattern=[[1, 128]], compare_op=ALU.is_ge,
                                               fill=0.0, base=0, channel_multiplier=-1), [cx.ones_f], [triu])
        S.op("pool", lambda e: e.affine_select(out=mgt[:], in_=cx.ones_f[:], pattern=[[-1, 128]], compare_op=ALU.is_gt,
                                               fill=0.0, base=0, channel_multiplier=1), [cx.ones_f], [mgt])
        dma(cx, "sp", cw[:], cw_d, [], [cw])
        dma(cx, "sp", cb[:], cb_d, [], [cb])
        dma(cx, "sp", dtb[:], dtb_d.partition_broadcast(128), [], [dtb])
        dma(cx, "sp", a_bc[:], alog_d.partition_broadcast(128), [], [a_bc])
        dma(cx, "sp", d_bc[:], dsk_d.partition_broadcast(128), [], [d_bc])
        dma(cx, "sp", ng[:], ng_d.partition_broadcast(128), [], [ng])
        act(cx, a_bc[:], a_bc[:], AF.Exp, [a_bc], [a_bc])
        ts(cx, "dve", a_bc[:], a_bc[:], -1.0, None, ALU.mult, None, [a_bc], [a_bc])
        z_v = z_d.rearrange("(n p) c -> n p c", p=128)
        dt_v = dt_d.rearrange("(n p) c -> n p c", p=128)
        yz_v = yzT_d.rearrange("c p t -> p c t")

        def bc3(ap2):
            return ap2.unsqueeze(2).to_broadcast([128, 16, 64])

        def v3(ap):
            return ap.rearrange("p (h d) -> p h d", h=16)

        for sc in range(SK // 512):
            t0 = sc * 512
            for cc in range(12):
                pt, ac = pre[cc % 2], acc[cc % 2]
                if sc == 0:
                    memset(cx, "pool", pt[:, 0:3], 0.0, [pt])
                    dma(cx, "sp", pt[:, 3:515], xbc_d[cc][:, 0:512], [], [pt])
                else:
                    dma(cx, "sp", pt[:, 0:515], xbc_d[cc][:, t0 - 3:t0 + 512], [], [pt])
                ts(cx, "dve", ac[:], pt[:, 0:512], cw[:, cc, 0:1], None, ALU.mult, None, [pt, cw], [ac])
                for k in range(1, 4):
                    stt(cx, "dve", ac[:], pt[:, k:k + 512], cw[:, cc, k:k + 1], ac[:], ALU.mult, ALU.add, [pt, cw, ac], [ac])
                if cc < 8:
                    dst, key = xsT[:, cc, :], (xsT, cc)
                elif cc < 10:
                    dst, key = BT[:, cc - 8, :], (BT, cc - 8)
                else:
                    dst, key = CT[:, cc - 10, :], (CT, cc - 10)
                act(cx, dst, ac[:], AF.Silu, [ac, cb], [key], bias=cb[:, cc:cc + 1])
            for ch in range(4):
                c = sc * 4 + ch
                c0 = ch * 128
                s_ = sm[c % 2]
                for k in range(8):
                    tr(cx, P[k // 4][:, (k % 4) * 128:(k % 4 + 1) * 128], xsT[:, k, c0:c0 + 128], cx.ident_f[:],
                       [(xsT, k), cx.ident_f], [P[k // 4]])
                for hf in range(2):
                    cp(cx, "act", xs_tok[:, hf * 512:(hf + 1) * 512], P[hf][:], [P[hf]], [(xs_tok, hf)])
                xk_ = [(xs_tok, 0), (xs_tok, 1)]
                p2b = P[2][:].bitcast(BF16)
                for g_ in range(2):
                    tr(cx, p2b[:, g_ * 128:(g_ + 1) * 128], BT[:, g_, c0:c0 + 128], cx.ident_b[:], [(BT, g_), cx.ident_b], [P[2]])
                cp(cx, "dve", Btok[:], p2b[:, 0:256].rearrange("p (a b) -> p a b", a=2), [P[2]], [Btok])
                dma(cx, "sp", s_[:, 0, :], dt_v[c], [], [(s_, 0)])
                tt(cx, "dve", s_[:, 0, :], s_[:, 0, :], dtb[:], ALU.add, [(s_, 0), dtb], [(s_, 0)])
                act(cx, s_[:, 1, :], s_[:, 0, :], AF.Exp, [(s_, 0)], [(s_, 1)])
                act(cx, s_[:, 1, :], s_[:, 1, :], AF.Ln, [(s_, 1)], [(s_, 1)], bias=1.0)
                tt(cx, "dve", s_[:, 2, :], s_[:, 1, :], a_bc[:], ALU.mult, [(s_, 1), a_bc], [(s_, 2)])
                mm(cx, P[2][:, 256:272], triu[:], s_[:, 2, :], True, True, [triu, (s_, 2)], [P[2]])
                mm(cx, P[2][:, 272:288], cx.ones_f[:], s_[:, 2, :], True, True, [cx.ones_f, (s_, 2)], [P[2]])
                cp(cx, "dve", s_[:, 3, :], P[2][:, 256:272], [P[2]], [(s_, 3)])
                act(cx, s_[:, 4, :], s_[:, 3, :], AF.Exp, [(s_, 3)], [(s_, 4)])
                tt(cx, "dve", s_[:, 5, :], P[2][:, 272:288], s_[:, 3, :], ALU.subtract, [P[2], (s_, 3)], [(s_, 5)])
                act(cx, s_[:, 5, :], s_[:, 5, :], AF.Exp, [(s_, 5)], [(s_, 5)])
                act(cx, s_[:, 6, :], P[2][:, 272:288], AF.Exp, [P[2]], [(s_, 6)])
                tt(cx, "dve", s_[:, 7, :], s_[:, 1, :], s_[:, 5, :], ALU.mult, [(s_, 1), (s_, 5)], [(s_, 7)])
                tt(cx, "dve", v3(xdt[:]), v3(xs_tok[:]), bc3(s_[:, 1, :]), ALU.mult, xk_ + [(s_, 1)], [xdt])
                tt(cx, "pool", v3(xdtd[:]), v3(xs_tok[:]), bc3(s_[:, 7, :]), ALU.mult, xk_ + [(s_, 7)], [xdtd])
                for g_ in range(2):
                    mm(cx, P[2][:, 288 + g_ * 128:288 + (g_ + 1) * 128][:, 0:128] if False else P[7][:, g_ * 128:(g_ + 1) * 128],
                       BT[:, g_, c0:c0 + 128], CT[:, g_, c0:c0 + 128], True, True, [(BT, g_), (CT, g_)], [P[7]])
                tt(cx, "dve", cbt[:], P[7][:, 0:256].rearrange("p (a b) -> p a b", a=2),
                   triu[:].unsqueeze(1).to_broadcast([128, 2, 128]), ALU.mult, [P[7], triu], [cbt])
                for h0 in range(0, 16, 4):
                    pseg = P[3 + (h0 // 4) % 2]
                    dc = dec[(h0 // 4) % 2]
                    for j in range(4):
                        h = h0 + j
                        ts(cx, "pool" if j % 2 else "dve", G[j][:], mgt[:], s_[:, 2, h:h + 1], None, ALU.mult, None,
                           [mgt, (s_, 2)], [G[j]])
                        mm(cx, pseg[:, j * 128:(j + 1) * 128], G[j][:], triu[:], True, True, [G[j], triu], [pseg])
                    act(cx, dc[:], pseg[:].rearrange("p (a b) -> p a b", a=4), AF.Exp, [pseg], [dc])
                    g_ = h0 // 8
                    tt(cx, "dve", MT[:, h0:h0 + 4, :], dc[:], cbt[:, g_:g_ + 1, :].to_broadcast([128, 4, 128]), ALU.mult,
                       [dc, cbt], [(MT, h0 // 4)])
                for h in range(16):
                    py = P[h // 8]
                    mm(cx, py[:, (h % 8) * 64:(h % 8 + 1) * 64], MT[:, h, :], xdt[:, h * 64:(h + 1) * 64], True, True,
                       [(MT, h // 4), xdt], [py])
                for g_ in range(2):
                    mm(cx, P[5 + g_][:], CT[:, g_, c0:c0 + 128], Hb[:, g_ * 512:(g_ + 1) * 512], True, True, [(CT, g_), Hb], [P[5 + g_]])
                for g_ in range(2):
                    tt(cx, "dve", t1[:, g_ * 512:(g_ + 1) * 512].rearrange("p (h d) -> p h d", h=8),
                       P[5 + g_][:].rearrange("p (h d) -> p h d", h=8),
                       s_[:, 4, g_ * 8:(g_ + 1) * 8].unsqueeze(2).to_broadcast([128, 8, 64]), ALU.mult,
                       [P[5 + g_], (s_, 4)], [(t1, g_)])
                tt(cx, "pool", v3(xsD[:]), v3(xs_tok[:]), bc3(d_bc[:]), ALU.mult, xk_ + [d_bc], [xsD])
                tt(cx, "pool", xsD[:], xsD[:], t1[:], ALU.add, [xsD, (t1, 0), (t1, 1)], [xsD])
                for g_ in range(2):
                    tt(cx, "dve", y[:, g_ * 512:(g_ + 1) * 512], P[g_][:], xsD[:, g_ * 512:(g_ + 1) * 512], ALU.add,
                       [P[g_], xsD], [(y, g_)])
                for g_ in range(2):
                    mm(cx, P[5 + g_][:], Btok[:, g_, :], xdtd[:, g_ * 512:(g_ + 1) * 512], True, True, [Btok, xdtd], [P[5 + g_]])
                tt(cx, "dve", v3(H[:]), v3(H[:]), bc3(s_[:, 6, :]), ALU.mult, [H, (s_, 6)], [H])
                for g_ in range(2):
                    tt(cx, "dve", H[:, g_ * 512:(g_ + 1) * 512], H[:, g_ * 512:(g_ + 1) * 512], P[5 + g_][:], ALU.add,
                       [H, P[5 + g_]], [H])
                cp(cx, "act", Hb[:], H[:], [H], [Hb])
                ztt = zt[c % 2]
                s2 = st2[c % 2]
                dma(cx, "sp", ztt[:], z_v[c], [], [ztt])
                act(cx, zs[:], ztt[:], AF.Silu, [ztt], [zs])
                tt(cx, "dve", y[:], y[:], zs[:], ALU.mult, [(y, 0), (y, 1), zs], [(y, 0), (y, 1)])
                for g_ in range(2):
                    act(cx, zs[:, g_ * 512:(g_ + 1) * 512], y[:, g_ * 512:(g_ + 1) * 512], AF.Square, [(y, g_)], [zs, (s2, g_)],
                        accum_out=s2[:, g_:g_ + 1])
                    act(cx, s2[:, 2 + g_:3 + g_], s2[:, g_:g_ + 1], AF.Sqrt, [(s2, g_), epst], [(s2, 2 + g_)], bias=epst[:],
                        scale=1.0 / 512)
                    S.op("dve", lambda e, g_=g_, s2=s2: e.reciprocal(out=s2[:, 4 + g_:5 + g_], in_=s2[:, 2 + g_:3 + g_]),
                         [(s2, 2 + g_)], [(s2, 4 + g_)])
                    stt(cx, "dve", yzn[:, g_ * 512:(g_ + 1) * 512], y[:, g_ * 512:(g_ + 1) * 512], s2[:, 4 + g_:5 + g_],
                        ng[:, g_ * 512:(g_ + 1) * 512], ALU.mult, ALU.mult, [(y, g_), (s2, 4 + g_), ng], [(yzn, g_)])
                p7b = P[7][:].bitcast(BF16)
                yT = yzT[c % 2]
                for k in range(8):
                    tr(cx, p7b[:, k * 128:(k + 1) * 128], yzn[:, k * 128:(k + 1) * 128], cx.ident_b[:],
                       [(yzn, k // 4), cx.ident_b], [P[7]])
                cp(cx, "act", yT[:], p7b[:].rearrange("p (a b) -> p a b", a=8), [P[7]], [yT])
                dma(cx, "pool", yz_v[:, :, c * 128:(c + 1) * 128], yT[:], [yT], [])


SSD_NW = 2576


def emit_ssd_stage(cx, xk, w_in_c, cw_d, cb_d, dtb_d, alog_d, dsk_d, ng_d, yzT_d, SK, scr):
    S = cx.S
    z_d, xbc_d, dt_d = scr["z"], scr["xbc"], scr["dt"]
    xk_v = xk.rearrange("(n p) d -> n p d", p=128)

    def post_z(half):
        def f(cx, st, state, pt, blk):
            if "z" not in state:
                state["z"] = [cx.sb([128, 512], F32, "zz", st) for _ in range(2)]
                state["i"] = 0
            state["i"] += 1
            zt_ = state["z"][state["i"] % 2]
            cp(cx, "act", zt_[:], pt[:, 0:512], [pt], [zt_])
            dma(cx, "pool", z_d[blk * 128:(blk + 1) * 128, half * 512:(half + 1) * 512], zt_[:], [zt_], [])
        return f

    def post_dt(cx, st, state, pt, blk):
        if "d" not in state:
            state["d"] = [cx.sb([128, 16], F32, "zd", st) for _ in range(2)]
        d_ = state["d"][blk % 2]
        cp(cx, "dve", d_[:], pt[:, 0:16], [pt], [d_])
        dma(cx, "pool", dt_d[blk * 128:(blk + 1) * 128, :], d_[:], [d_], [])

    def mk(c):
        return lambda t: xbc_d[c][:, t * 512:(t + 1) * 512]

    fm = [(1024 + c * 128, 128, F32, mk(c)) for c in range(12)]
    emit_proj(cx, lambda i: xk_v[i], SK, w_in_c if isinstance(w_in_c, list) else [w_in_c], fm_specs=fm,
              tm_specs=[(0, 512, post_z(0)), (512, 512, post_z(1)), (2560, 16, post_dt)])
    S.barrier()
    emit_ssd_core(cx, z_d, xbc_d, dt_d, cw_d, cb_d, dtb_d, alog_d, dsk_d, ng_d, yzT_d, SK)


def ssd_scratch(nc, SK, tag=""):
    mk = lambda n, shp, dt_: nc.dram_tensor(n + tag, shp, dt_, kind="Internal").ap()
    return {"z": mk("s_z", [SK, 1024], F32), "xbc": mk("s_xbc", [12, 128, SK], F32), "dt": mk("s_dt", [SK, 16], F32)}


def build_ssd_program(SK):
    nc = bass.Bass("TRN2", target_bir_lowering=False)
    inp = lambda n, shp: nc.dram_tensor(n, shp, F32, kind="ExternalInput").ap()
    xk = inp("xk", [SK, D])
    w_in_c = inp("w_in_c", [D, SSD_NW])
    cw, cb = inp("cw", [128, 12, 4]), inp("cb", [128, 12])
    dtb, alog, dsk, ng = inp("dtb", [16]), inp("alog", [16]), inp("dsk", [16]), inp("ng", [1024])
    yzT = nc.dram_tensor("yzT", [8, 128, SK], BF16, kind="ExternalOutput").ap()
    scr = ssd_scratch(nc, SK)
    with contextlib.ExitStack() as st:
        cx = Ctx(nc, st)
        setup_consts(cx)
        emit_ssd_stage(cx, xk, w_in_c, cw, cb, dtb, alog, dsk, ng, yzT, SK, scr)
        cx.S.emit(final_wait_ops=cx.S.dma_hist["pool"][-N_DMA_SEM:])
    return nc


def build_outproj_program(NQ, nch):
    nc = bass.Bass("TRN2", target_bir_lowering=False)
    inp = lambda n, shp: nc.dram_tensor(n, shp, F32, kind="ExternalInput").ap()
    srcT = nc.dram_tensor("srcT", [nch, 128, NQ], BF16, kind="ExternalInput").ap()
    w, xq, g, b = inp("w", [nch * 128, D]), inp("xq", [NQ, D]), inp("g", [D]), inp("b", [D])
    x_out = nc.dram_tensor("x_out", [NQ, D], F32, kind="ExternalOutput").ap()
    with contextlib.ExitStack() as st:
        cx = Ctx(nc, st)
        setup_consts(cx)
        outs = emit_outproj_ln(cx, srcT, nch, w, xq, x_out, g, b, NQ, 1.0)
        cx.S.emit(final_wait_ops=outs[-N_DMA_SEM:])
    return nc


SEQ = 8192
BATCH = 4
NQ_CORE = SEQ // 2


def _own_tokens(j):
    bl = []
    for k in range(SEQ // 512):
        bl += [4 * k, 4 * k + 3] if j == 0 else [4 * k + 1, 4 * k + 2]
    return np.concatenate([np.arange(b * 128, (b + 1) * 128) for b in bl])


def _ssd_core_inputs(j, w_in, conv_w, conv_b, dt_bias, a_log, d_skip, norm_g):
    cols = np.concatenate([np.arange(j * 1024, (j + 1) * 1024), 2048 + np.arange(j * 1024, (j + 1) * 1024),
                           4096 + np.arange(j * 256, (j + 1) * 256), 4608 + np.arange(j * 256, (j + 1) * 256),
                           5120 + np.arange(j * 16, (j + 1) * 16)])
    ch = np.concatenate([np.arange(j * 1024, (j + 1) * 1024), 2048 + np.arange(j * 256, (j + 1) * 256),
                         2560 + np.arange(j * 256, (j + 1) * 256)])
    cw = np.ascontiguousarray(conv_w[:, ch].T.reshape(12, 128, 4).transpose(1, 0, 2))
    cb = np.ascontiguousarray(conv_b[ch].reshape(12, 128).T)
    return {"w_in_c": np.ascontiguousarray(w_in[:, cols]), "cw": cw, "cb": cb,
            "dtb": np.ascontiguousarray(dt_bias[j * 16:(j + 1) * 16]), "alog": np.ascontiguousarray(a_log[j * 16:(j + 1) * 16]),
            "dsk": np.ascontiguousarray(d_skip[j * 16:(j + 1) * 16]), "ng": np.ascontiguousarray(norm_g[j * 1024:(j + 1) * 1024])}


_PROGS = {}


def _prog(name, fn):
    if name not in _PROGS:
        _PROGS[name] = fn()
    return _PROGS[name]


def kernel_unfused(x, ln_g, ln_b, ffn1_w_in, ffn1_w_out, ffn2_w_in, ffn2_w_out, attn_w_in, attn_kv_norm, attn_w_uk, attn_w_uv,
           attn_w_out, ssm_w_in, ssm_conv_w, ssm_conv_b, ssm_dt_bias, ssm_a_log, ssm_d, ssm_norm_g, ssm_w_out):
    f32 = lambda a: np.ascontiguousarray(np.asarray(a, dtype=np.float32))
    x = f32(x)
    cores = list(range(NCORES))
    tok = [_own_tokens(c % 2) for c in cores]
    qpos = [np.ascontiguousarray(tok[c].reshape(-1, 128).T.astype(np.float32)) for c in cores]
    x_own = [np.ascontiguousarray(x[c // 2][tok[c]]) for c in cores]

    def run(nc, in_maps, key):
        res = run_bass_kernel_spmd(nc, in_maps, core_ids=cores)
        return [r[key] for r in res.results]

    def full_seq(x_own):
        out = []
        for s in range(BATCH):
            xs = np.empty((SEQ, D), np.float32)
            for j in range(2):
                xs[tok[2 * s + j]] = x_own[2 * s + j]
            out.append(xs)
        return out

    def ffn(x_own, w_in, w_out, g, b):
        nc = _prog("ffn", lambda: build_ffn_program(NQ_CORE))
        return run(nc, [{"x_in": x_own[c], "w_in": f32(w_in), "w_out": f32(w_out), "g": f32(g), "b": f32(b)} for c in cores],
                   "x_out")

    for i in range(DEPTH):
        x_own = ffn(x_own, ffn1_w_in[i], ffn1_w_out[i], ln_g[i, 0], ln_b[i, 0])
        j_ = i // 2
        xk = full_seq(x_own)
        if i % 2 == 0:
            nc = _prog("attn", lambda: build_attn_program(NQ_CORE, SEQ))
            x_own = run(nc, [{"xq": x_own[c], "xk": xk[c // 2], "qpos": qpos[c], "w_in": f32(attn_w_in[j_]),
                              "kv_g": f32(attn_kv_norm[j_]), "w_uk": f32(attn_w_uk[j_]), "w_uv": f32(attn_w_uv[j_]),
                              "w_o": f32(attn_w_out[j_]), "g": f32(ln_g[i, 1]), "b": f32(ln_b[i, 1])} for c in cores], "x_out")
        else:
            nc = _prog("ssd", lambda: build_ssd_program(SEQ))
            ci = [_ssd_core_inputs(jj, f32(ssm_w_in[j_]), f32(ssm_conv_w[j_]), f32(ssm_conv_b[j_]), f32(ssm_dt_bias[j_]),
                                   f32(ssm_a_log[j_]), f32(ssm_d[j_]), f32(ssm_norm_g[j_])) for jj in range(2)]
            yz = run(nc, [dict(ci[c % 2], xk=xk[c // 2]) for c in cores], "yzT")
            nc2 = _prog("oproj", lambda: build_outproj_program(NQ_CORE, 16))
            maps = []
            for c in cores:
                s = c // 2
                full = np.concatenate([np.asarray(yz[2 * s]), np.asarray(yz[2 * s + 1])], axis=0)
                maps.append({"srcT": np.ascontiguousarray(full[:, :, tok[c]]), "w": f32(ssm_w_out[j_]), "xq": x_own[c],
                             "g": f32(ln_g[i, 1]), "b": f32(ln_b[i, 1])})
            x_own = run(nc2, maps, "x_out")
        x_own = ffn(x_own, ffn2_w_in[i], ffn2_w_out[i], ln_g[i, 2], ln_b[i, 2])
    out = np.stack(full_seq(x_own)).astype(np.float32)
    return out


def build_full_program():
    nc = bass.Bass("TRN2", target_bir_lowering=False)
    inp = lambda n, shp: nc.dram_tensor(n, shp, F32, kind="ExternalInput").ap()
    x = inp("x", [SEQ, D])
    qpos = inp("qpos", [128, SEQ // 128])
    ln_g, ln_b = inp("ln_g", [DEPTH, 3, D]), inp("ln_b", [DEPTH, 3, D])
    f1i, f1o = inp("ffn1_w_in", [DEPTH, D, 2 * DFF]), inp("ffn1_w_out", [DEPTH, DFF, D])
    f2i, f2o = inp("ffn2_w_in", [DEPTH, D, 2 * DFF]), inp("ffn2_w_out", [DEPTH, DFF, D])
    a_in, a_kv = inp("attn_w_in", [2, D, 1864]), inp("attn_kv_norm", [2, 256])
    a_uk, a_uv, a_o = inp("attn_w_uk", [2, 16, 64, 256]), inp("attn_w_uv", [2, 16, 256, 64]), inp("attn_w_out", [2, D, D])
    s_in = inp("ssm_w_in", [2, D, 5152])
    s_cw, s_cb = inp("ssm_cw", [2, 2, 128, 12, 4]), inp("ssm_cb", [2, 2, 128, 12])
    s_dtb, s_alog, s_d = inp("ssm_dt_bias", [2, 32]), inp("ssm_a_log", [2, 32]), inp("ssm_d", [2, 32])
    s_ng, s_o = inp("ssm_norm_g", [2, 2048]), inp("ssm_w_out", [2, 2048, D])
    out = nc.dram_tensor("out", [SEQ, D], F32, kind="ExternalOutput").ap()
    bufs = [nc.dram_tensor("xbuf%d" % i, [SEQ, D], F32, kind="Internal").ap() for i in range(3)]
    ascr = attn_scratch(nc, SEQ, SEQ)
    sscr = ssd_scratch(nc, SEQ)
    yzT = nc.dram_tensor("s_yzT", [16, 128, SEQ], BF16, kind="Internal").ap()
    with contextlib.ExitStack() as st:
        cx = Ctx(nc, st)
        S = cx.S
        setup_consts(cx)
        cur = x
        outs = None
        for i in range(DEPTH):
            j_ = i // 2
            b0, b1, b2 = bufs[0], bufs[1], bufs[2]
            emit_ffn(cx, cur, b1, f1i[i], f1o[i], ln_g[i, 0], ln_b[i, 0], SEQ)
            S.barrier()
            if i % 2 == 0:
                emit_attn_stage(cx, b1, b1, qpos, a_in[j_], a_kv[j_], a_uk[j_], a_uv[j_], a_o[j_], ln_g[i, 1], ln_b[i, 1], b2,
                                SEQ, SEQ, ascr, natural=True)
            else:
                w = s_in[j_]
                for jj in range(2):
                    wcols = [w[:, jj * 1024:(jj + 1) * 1024], w[:, 2048 + jj * 1024:2048 + (jj + 1) * 1024],
                             w[:, 4096 + jj * 256:4096 + (jj + 1) * 256], w[:, 4608 + jj * 256:4608 + (jj + 1) * 256],
                             w[:, 5120 + jj * 16:5120 + (jj + 1) * 16]]
                    emit_ssd_stage(cx, b1, wcols, s_cw[j_, jj], s_cb[j_, jj], s_dtb[j_, jj * 16:(jj + 1) * 16],
                                   s_alog[j_, jj * 16:(jj + 1) * 16], s_d[j_, jj * 16:(jj + 1) * 16],
                                   s_ng[j_, jj * 1024:(jj + 1) * 1024], yzT[jj * 8:(jj + 1) * 8], SEQ, sscr)
                    S.barrier()
                emit_outproj_ln(cx, yzT, 16, s_o[j_], b1, b2, ln_g[i, 1], ln_b[i, 1], SEQ, 1.0)
            S.barrier()
            last = i == DEPTH - 1
            dst = out if last else b0
            outs = emit_ffn(cx, b2, dst, f2i[i], f2o[i], ln_g[i, 2], ln_b[i, 2], SEQ)
            if not last:
                S.barrier()
            cur = b0
        print("ops per engine:", {e: len(S.ops[e]) for e in ENGS}, flush=True)
        S.emit(final_wait_ops=outs[-N_DMA_SEM:])
        print("max semval per segment:", S.max_semval, "-> per-sem max", {e: v // N_CSEM + CSEM_CH for e, v in S.max_semval.items()},
              "max dma sem value:", S.max_dval, flush=True)
    return nc


def _ssd_conv_layout(conv_w, conv_b):
    cw = np.zeros((2, 2, 128, 12, 4), np.float32)
    cb = np.zeros((2, 2, 128, 12), np.float32)
    for l in range(2):
        for j in range(2):
            ch = np.concatenate([np.arange(j * 1024, (j + 1) * 1024), 2048 + np.arange(j * 256, (j + 1) * 256),
                                 2560 + np.arange(j * 256, (j + 1) * 256)])
            cw[l, j] = conv_w[l][:, ch].T.reshape(12, 128, 4).transpose(1, 0, 2)
            cb[l, j] = conv_b[l][ch].reshape(12, 128).T
    return cw, cb


def kernel_fused(x, ln_g, ln_b, ffn1_w_in, ffn1_w_out, ffn2_w_in, ffn2_w_out, attn_w_in, attn_kv_norm, attn_w_uk, attn_w_uv,
                 attn_w_out, ssm_w_in, ssm_conv_w, ssm_conv_b, ssm_dt_bias, ssm_a_log, ssm_d, ssm_norm_g, ssm_w_out):
    f32 = lambda a: np.ascontiguousarray(np.asarray(a, dtype=np.float32))
    x = f32(x)
    cw, cb = _ssd_conv_layout(f32(ssm_conv_w), f32(ssm_conv_b))
    qpos = np.ascontiguousarray(np.arange(SEQ, dtype=np.float32).reshape(-1, 128).T)
    shared = {"qpos": qpos, "ln_g": f32(ln_g), "ln_b": f32(ln_b), "ffn1_w_in": f32(ffn1_w_in), "ffn1_w_out": f32(ffn1_w_out),
              "ffn2_w_in": f32(ffn2_w_in), "ffn2_w_out": f32(ffn2_w_out), "attn_w_in": f32(attn_w_in),
              "attn_kv_norm": f32(attn_kv_norm), "attn_w_uk": f32(attn_w_uk), "attn_w_uv": f32(attn_w_uv),
              "attn_w_out": f32(attn_w_out), "ssm_w_in": f32(ssm_w_in), "ssm_cw": cw, "ssm_cb": cb,
              "ssm_dt_bias": f32(ssm_dt_bias), "ssm_a_log": f32(ssm_a_log), "ssm_d": f32(ssm_d), "ssm_norm_g": f32(ssm_norm_g),
              "ssm_w_out": f32(ssm_w_out)}
    nc = _prog("full", build_full_program)
    in_maps = [dict(shared, x=np.ascontiguousarray(x[c % BATCH])) for c in range(NCORES)]
    res = run_bass_kernel_spmd(nc, in_maps, core_ids=list(range(NCORES)))
    return np.stack([np.asarray(res.results[c]["out"], dtype=np.float32) for c in range(BATCH)])


kernel = kernel_fused
```

```python
import contextlib
import numpy as np
import concourse.bass as bass
import concourse.mybir as mybir
from concourse.bass_utils import run_bass_kernel_spmd

F32 = mybir.dt.float32
BF16 = mybir.dt.bfloat16
AF = mybir.ActivationFunctionType
ALU = mybir.AluOpType
AX = mybir.AxisListType

D = 1024
DFF = 2816
DEPTH = 4
ALPHA = (2.0 * DEPTH) ** 0.25
LN_EPS = 1e-5
NCORES = 8

ENGS = ("pe", "dve", "act", "pool", "sp")
N_DMA_SEM = 20
N_CSEM = 14
CSEM_CH = 128


class Op:
    __slots__ = ("eng", "fn", "deps", "is_dma", "has_dep", "semval", "dsem", "dval", "prev_ring")

    def __init__(self, eng, fn, is_dma):
        self.eng = eng
        self.fn = fn
        self.is_dma = is_dma
        self.deps = []
        self.has_dep = False
        self.semval = None
        self.dsem = None
        self.dval = None
        self.prev_ring = None


def _key(k):
    if isinstance(k, tuple):
        return tuple(_key(e) for e in k)
    if isinstance(k, (str, int)):
        return k
    return k.name


class Sched:
    def __init__(self, nc):
        self.nc = nc
        self.ops = {e: [] for e in ENGS}
        self.last_w = {}
        self.readers = {}
        self.dma_count = {e: 0 for e in ENGS}
        self.dma_hist = {e: [] for e in ENGS}
        self.n = 0
        self.n_barriers = 0

    def _add(self, eng, fn, reads, writes, is_dma):
        op = Op(eng, fn, is_dma)
        reads = [_key(r) for r in reads]
        writes = [_key(w) for w in writes]
        deps = []
        for r in reads:
            w = self.last_w.get(r)
            if w is not None:
                deps.append(w)
        for w_ in writes:
            w = self.last_w.get(w_)
            if w is not None:
                deps.append(w)
            rl = self.readers.get(w_, ())
            lastc = {}
            for o in rl:
                if o.is_dma:
                    deps.append(o)
                else:
                    lastc[o.eng] = o
            deps.extend(lastc.values())
        seen = set()
        for d in deps:
            if id(d) in seen:
                continue
            seen.add(id(d))
            if (not d.is_dma) and (not is_dma) and d.eng == "pe" and eng == "pe":
                continue
            op.deps.append(d)
            d.has_dep = True
        for r in reads:
            self.readers.setdefault(r, []).append(op)
        for w_ in writes:
            self.last_w[w_] = op
            self.readers[w_] = []
        if is_dma:
            j = self.dma_count[eng]
            self.dma_count[eng] += 1
            op.dsem = j % N_DMA_SEM
            op.dval = 16 * (j // N_DMA_SEM + 1)
            hist = self.dma_hist[eng]
            if j >= N_DMA_SEM:
                op.prev_ring = hist[j - N_DMA_SEM]
            hist.append(op)
        self.ops[eng].append(op)
        self.n += 1
        return op

    def op(self, eng, fn, reads=(), writes=()):
        return self._add(eng, fn, reads, writes, False)

    def dma(self, eng, fn, reads=(), writes=()):
        return self._add(eng, fn, reads, writes, True)

    def barrier(self):
        lasts = []
        for e in ENGS:
            for o in reversed(self.ops[e]):
                if o.fn is None:
                    break
                if not o.is_dma:
                    lasts.append(o)
                    break
            lasts.extend(self.dma_hist[e][-N_DMA_SEM:])
        self.n_barriers += 1
        for e in ENGS:
            op = Op(e, None, False)
            op.semval = self.n_barriers
            for d in lasts:
                op.deps.append(d)
                d.has_dep = True
            self.ops[e].append(op)
        self.last_w = {}
        self.readers = {}

    def emit(self, final_wait_ops=()):
        nc = self.nc
        for o in final_wait_ops:
            o.has_dep = True
        for e in ENGS:
            c = 0
            for o in self.ops[e]:
                if o.fn is None:
                    c = 0
                    continue
                if o.is_dma:
                    continue
                if o.has_dep:
                    c += 1
                    o.semval = c
        self.max_semval = {e: max([o.semval or 0 for o in self.ops[e] if o.fn is not None and not o.is_dma] + [0]) for e in ENGS}
        self.max_dval = {e: max([o.dval or 0 for o in self.ops[e] if o.is_dma] + [0]) for e in ENGS}
        with contextlib.ExitStack() as st:
            csem = {e: [st.enter_context(nc.semaphore("cs_%s_%d" % (e, i))) for i in range(N_CSEM)]
                    for e in ENGS if e != "sp"}
            dsem = {e: [st.enter_context(nc.semaphore("ds_%s_%d" % (e, i))) for i in range(N_DMA_SEM)]
                    for e in ENGS if any(o.is_dma for o in self.ops[e])}
            bsem = [st.enter_context(nc.semaphore("bar%d" % i)) for i in range(2)]
            any_dma = {e: any(o.is_dma for o in self.ops[e]) for e in ENGS}
            block = st.enter_context(nc.Block())

            def run(e, eng):
                waited = {}

                def wait_for(d):
                    if d.is_dma:
                        k = ("d", d.eng, d.dsem)
                        s, v = dsem[d.eng][d.dsem], d.dval
                        if waited.get(k, 0) >= v:
                            return
                        waited[k] = v
                        eng.wait_ge(s, v)
                    else:
                        k = ("c", d.eng)
                        c = d.semval
                        if waited.get(k, 0) >= c:
                            return
                        waited[k] = c
                        ep = (c - 1) // CSEM_CH
                        eng.wait_ge(csem[d.eng][ep % N_CSEM], CSEM_CH * (ep // N_CSEM) + ((c - 1) % CSEM_CH) + 1)

                for o in self.ops[e]:
                    for d in o.deps:
                        wait_for(d)
                    if o.prev_ring is not None:
                        wait_for(o.prev_ring)
                    if o.fn is None:
                        k = o.semval
                        eng.sem_inc(bsem[0], 1)
                        eng.wait_ge(bsem[0], len(ENGS) * k)
                        if e in csem:
                            for s_ in csem[e]:
                                eng.sem_clear(s_)
                        eng.sem_inc(bsem[1], 1)
                        eng.wait_ge(bsem[1], len(ENGS) * k)
                        for k_ in [k_ for k_ in waited if k_[0] == "c"]:
                            del waited[k_]
                        continue
                    ins = o.fn(eng)
                    if o.is_dma:
                        ins.then_inc(dsem[e][o.dsem], 16)
                    elif o.has_dep:
                        ins.then_inc(csem[e][((o.semval - 1) // CSEM_CH) % N_CSEM], 1)
                if e == "pool":
                    for d in final_wait_ops:
                        wait_for(d)

            for e, reg in (("sp", block.sync), ("pe", block.tensor), ("dve", block.vector),
                           ("act", block.scalar), ("pool", block.gpsimd)):
                reg(lambda eng, e=e: run(e, eng))


class Ctx:
    def __init__(self, nc, st):
        self.nc = nc
        self.S = Sched(nc)
        self.st = st
        self.uid = 0
        self.psum = []
        self.psum_i = 0

    def sb(self, shape, dtype, name=None, st=None):
        self.uid += 1
        nm = "%s_%d" % (name or "t", self.uid)
        return (st or self.st).enter_context(self.nc.sbuf_tensor(nm, list(shape), dtype))

    def ps(self, shape, dtype, name=None, st=None):
        self.uid += 1
        nm = "%s_%d" % (name or "p", self.uid)
        return (st or self.st).enter_context(self.nc.psum_tensor(nm, list(shape), dtype))


def setup_consts(cx):
    nc, S = cx.nc, cx.S
    cx.ident_f = cx.sb([128, 128], F32, "identf")
    cx.ident_b = cx.sb([128, 128], BF16, "identb")
    cx.ones_f = cx.sb([128, 128], F32, "onesf")
    idf, idb, onf = cx.ident_f, cx.ident_b, cx.ones_f
    S.op("pool", lambda e: e.memset(onf[:], 1.0), writes=[onf])
    S.op("pool", lambda e: e.affine_select(out=idf[:], in_=onf[:], pattern=[[-1, 128]],
                                           compare_op=ALU.is_equal, fill=0.0, base=0,
                                           channel_multiplier=1), reads=[onf], writes=[idf])
    S.op("pool", lambda e: e.tensor_copy(out=idb[:], in_=idf[:]), reads=[idf], writes=[idb])


def emit_ffn(cx, x_in, x_out, w_in, w_out, g, b, ntok, scale_y=0.5):
    nc, S = cx.nc, cx.S
    T = 256
    NS = T // 128
    KD = D // 128
    KF = DFF // 128
    c_y = scale_y / ALPHA
    eps_p = LN_EPS / (ALPHA * ALPHA)
    with contextlib.ExitStack() as st:
        win = cx.sb([128, KD, 2 * DFF], BF16, "win", st)
        wout = cx.sb([128, KF, D], BF16, "wout", st)
        stg = [cx.sb([128, 1408], F32, "stg", st) for _ in range(2)]
        gb = cx.sb([128, D], F32, "gb", st)
        bb = cx.sb([128, D], F32, "bb", st)
        epst = cx.sb([128, 1], F32, "eps", st)
        xs = [cx.sb([128, D], F32, "xs", st) for _ in range(4)]
        xb = [cx.sb([128, D], BF16, "xb", st) for _ in range(2)]
        xT = [cx.sb([128, KD, T], BF16, "xT", st) for _ in range(2)]
        hT = [cx.sb([128, KF, T], BF16, "hT", st) for _ in range(1)]
        sg = [cx.sb([128, T], F32, "sg", st) for _ in range(2)]
        r = [cx.sb([128, D], F32, "r", st) for _ in range(1)]
        xo = [cx.sb([128, D], F32, "xo", st) for _ in range(1)]
        stat = [cx.sb([128, 8], F32, "stat", st) for _ in range(2)]
        p_tp = cx.ps([128, 1024], BF16, "ptp", st)
        p_g = [cx.ps([128, 512], F32, "pg", st) for _ in range(2)]
        p_u = [cx.ps([128, 512], F32, "pu", st) for _ in range(2)]
        p_o = [cx.ps([128, 512], F32, "po", st) for _ in range(2)]

        S.op("pool", lambda e: e.memset(epst[:], eps_p), writes=[epst])
        S.dma("sp", lambda e: e.dma_start(out=gb[:], in_=g.partition_broadcast(128)), writes=[gb])
        S.dma("sp", lambda e: e.dma_start(out=bb[:], in_=b.partition_broadcast(128)), writes=[bb])

        w_in_v = w_in.rearrange("(k p) f -> p k f", p=128)
        w_out_v = w_out.rearrange("(k p) d -> p k d", p=128)
        cast_engs = ["dve", "act", "pool"]
        ci = 0
        pieces = []
        for k in range(KD):
            for hf in range(4):
                pieces.append((w_in_v[:, k, hf * 1408:(hf + 1) * 1408], win[:, k, hf * 1408:(hf + 1) * 1408],
                               (win, k), 1408))
        for k0 in range(KF):
            pieces.append((w_out_v[:, k0, :], wout[:, k0, :], (wout, k0), D))
        for i, (src, dst, key, n) in enumerate(pieces):
            sb_ = stg[i % 2]
            if len(src.shape) == 3:
                sview = sb_[:, 0:n].rearrange("p (a b) -> p a b", a=src.shape[1])
            else:
                sview = sb_[:, 0:n]
            S.dma("sp", lambda e, sview=sview, src=src: e.dma_start(out=sview, in_=src), writes=[sb_])
            ce = cast_engs[ci % 3]
            ci += 1
            if ce == "act":
                S.op("act", lambda e, dst=dst, sview=sview: e.copy(out=dst, in_=sview), reads=[sb_], writes=[key])
            else:
                S.op(ce, lambda e, dst=dst, sview=sview: e.tensor_copy(out=dst, in_=sview), reads=[sb_], writes=[key])
        win_keys = [(win, k) for k in range(KD)]
        wout_keys = [(wout, k) for k in range(KF // 2)]

        ntiles = ntok // T
        x_in_v = x_in.rearrange("(n p) d -> n p d", p=128)
        x_out_v = x_out.rearrange("(n p) d -> n p d", p=128)
        last_out = []

        def load_tile(ti):
            xTt = xT[ti % 2]
            for s in range(NS):
                xt = xs[(ti * NS + s) % 4]
                xbt = xb[s]
                S.dma("sp", lambda e, xt=xt, i=ti * NS + s: e.dma_start(out=xt[:], in_=x_in_v[i]), writes=[xt])
                S.op("pool", lambda e, xbt=xbt, xt=xt: e.tensor_copy(out=xbt[:], in_=xt[:]), reads=[xt], writes=[xbt])
                for k in range(KD):
                    S.op("pe", lambda e, k=k, xbt=xbt: e.transpose(out=p_tp[:, k * 128:(k + 1) * 128],
                                                                  in_=xbt[:, k * 128:(k + 1) * 128],
                                                                  identity=cx.ident_b[:]),
                         reads=[xbt, cx.ident_b], writes=[p_tp])
                S.op("dve", lambda e, xTt=xTt, s=s: e.tensor_copy(
                    out=xTt[:, :, s * 128:(s + 1) * 128],
                    in_=p_tp[:].rearrange("p (k t) -> p k t", k=KD)), reads=[p_tp], writes=[xTt])

        load_tile(0)
        for ti in range(ntiles):
            xTt = xT[ti % 2]
            hTt = hT[0]
            for j in range(KF):
                pg, pu, sgt = p_g[j % 2], p_u[j % 2], sg[j % 2]
                for k in range(KD):
                    S.op("pe", lambda e, k=k, j=j, pg=pg, xTt=xTt: e.matmul(pg[:, 0:T], lhsT=win[:, k, j * 128:(j + 1) * 128],
                                                                   rhs=xTt[:, k, :], start=(k == 0), stop=(k == KD - 1)),
                         reads=[(win, k), xTt], writes=[pg])
                for k in range(KD):
                    S.op("pe", lambda e, k=k, j=j, pu=pu, xTt=xTt: e.matmul(pu[:, 0:T],
                                                                   lhsT=win[:, k, DFF + j * 128:DFF + (j + 1) * 128],
                                                                   rhs=xTt[:, k, :], start=(k == 0), stop=(k == KD - 1)),
                         reads=[(win, k), xTt], writes=[pu])
                S.op("act", lambda e, pg=pg, sgt=sgt: e.activation(out=sgt[:], in_=pg[:, 0:T], func=AF.Silu),
                     reads=[pg], writes=[sgt])
                S.op("dve", lambda e, pu=pu, sgt=sgt, j=j, hTt=hTt: e.tensor_tensor(out=hTt[:, j, :], in0=pu[:, 0:T], in1=sgt[:],
                                                                            op=ALU.mult),
                     reads=[pu, sgt], writes=[(hTt, j)])
            if ti + 1 < ntiles:
                load_tile(ti + 1)
            for s in range(NS):
                xt = xs[(ti * NS + s) % 4]
                rt, xot, stt = r[0], xo[0], stat[s]
                for hd in range(2):
                    po = p_o[hd]
                    for j in range(KF):
                        S.op("pe", lambda e, j=j, hd=hd, s=s, po=po, hTt=hTt: e.matmul(
                            po[:], lhsT=hTt[:, j, s * 128:(s + 1) * 128], rhs=wout[:, j, hd * 512:(hd + 1) * 512],
                            start=(j == 0), stop=(j == KF - 1)),
                             reads=[(hTt, j), (wout, j)], writes=[po])
                    S.op("dve", lambda e, po=po, hd=hd, rt=rt, xt=xt: e.scalar_tensor_tensor(
                        out=rt[:, hd * 512:(hd + 1) * 512], in0=po[:], scalar=c_y, in1=xt[:, hd * 512:(hd + 1) * 512],
                        op0=ALU.mult, op1=ALU.add), reads=[po, xt], writes=[(rt, hd)])
                emit_ln(cx, rt, [(rt, 0), (rt, 1)], xot, stt, epst, gb, bb, xot)
                o = S.dma("pool", lambda e, xot=xot, i=ti * NS + s: e.dma_start(out=x_out_v[i], in_=xot[:]),
                          reads=[xot], writes=[("dram_xout", ti * NS + s)])
                last_out.append(o)
        return last_out


def emit_ln(cx, rt, rkeys, junk, stt, epst, gb, bb, xot):
    S = cx.S
    S.op("act", lambda e: e.activation(out=junk[:], in_=rt[:], func=AF.Square, accum_out=stt[:, 0:1]),
         reads=rkeys, writes=[junk, (stt, 0)])
    S.op("dve", lambda e: e.tensor_reduce(out=stt[:, 1:2], in_=rt[:], axis=AX.X, op=ALU.add),
         reads=rkeys, writes=[(stt, 1)])
    S.op("dve", lambda e: e.tensor_scalar(out=stt[:, 2:3], in0=stt[:, 1:2], scalar1=1.0 / D, scalar2=None, op0=ALU.mult),
         reads=[(stt, 1)], writes=[(stt, 2)])
    S.op("dve", lambda e: e.tensor_tensor(out=stt[:, 3:4], in0=stt[:, 2:3], in1=stt[:, 2:3], op=ALU.mult),
         reads=[(stt, 2)], writes=[(stt, 3)])
    S.op("dve", lambda e: e.scalar_tensor_tensor(out=stt[:, 4:5], in0=stt[:, 0:1], scalar=1.0 / D, in1=stt[:, 3:4],
                                                 op0=ALU.mult, op1=ALU.subtract),
         reads=[(stt, 0), (stt, 3)], writes=[(stt, 4)])
    S.op("act", lambda e: e.activation(out=stt[:, 5:6], in_=stt[:, 4:5], func=AF.Sqrt, bias=epst[:]),
         reads=[(stt, 4), epst], writes=[(stt, 5)])
    S.op("dve", lambda e: e.reciprocal(out=stt[:, 6:7], in_=stt[:, 5:6]), reads=[(stt, 5)], writes=[(stt, 6)])
    S.op("dve", lambda e: e.tensor_scalar(out=xot[:], in0=rt[:], scalar1=stt[:, 2:3], scalar2=stt[:, 6:7],
                                          op0=ALU.subtract, op1=ALU.mult),
         reads=rkeys + [(stt, 2), (stt, 6)], writes=[xot])
    S.op("pool", lambda e: e.tensor_tensor(out=xot[:], in0=xot[:], in1=gb[:], op=ALU.mult), reads=[xot, gb], writes=[xot])
    S.op("pool", lambda e: e.tensor_tensor(out=xot[:], in0=xot[:], in1=bb[:], op=ALU.add), reads=[xot, bb], writes=[xot])


def build_ffn_program(ntok):
    nc = bass.Bass("TRN2", target_bir_lowering=False)
    x_in = nc.dram_tensor("x_in", [ntok, D], F32, kind="ExternalInput").ap()
    w_in = nc.dram_tensor("w_in", [D, 2 * DFF], F32, kind="ExternalInput").ap()
    w_out = nc.dram_tensor("w_out", [DFF, D], F32, kind="ExternalInput").ap()
    g = nc.dram_tensor("g", [D], F32, kind="ExternalInput").ap()
    b = nc.dram_tensor("b", [D], F32, kind="ExternalInput").ap()
    x_out = nc.dram_tensor("x_out", [ntok, D], F32, kind="ExternalOutput").ap()
    with contextlib.ExitStack() as st:
        cx = Ctx(nc, st)
        setup_consts(cx)
        outs = emit_ffn(cx, x_in, x_out, w_in, w_out, g, b, ntok)
        cx.S.emit(final_wait_ops=outs[-N_DMA_SEM:])
    return nc


def mm(cx, out, lhsT, rhs, start, stop, reads, writes):
    return cx.S.op("pe", lambda e: e.matmul(out, lhsT=lhsT, rhs=rhs, start=start, stop=stop), reads, writes)


def tr(cx, out, in_, ident, reads, writes):
    return cx.S.op("pe", lambda e: e.transpose(out=out, in_=in_, identity=ident), reads, writes)


def act(cx, out, in_, func, reads, writes, **kw):
    return cx.S.op("act", lambda e: e.activation(out=out, in_=in_, func=func, **kw), reads, writes)


def tt(cx, eng, out, in0, in1, op, reads, writes):
    return cx.S.op(eng, lambda e: e.tensor_tensor(out=out, in0=in0, in1=in1, op=op), reads, writes)


def ts(cx, eng, out, in0, s1, s2, op0, op1, reads, writes, accum_out=None):
    if op1 is None:
        return cx.S.op(eng, lambda e: e.tensor_scalar(out=out, in0=in0, scalar1=s1, scalar2=None, op0=op0,
                                                      accum_out=accum_out), reads, writes)
    return cx.S.op(eng, lambda e: e.tensor_scalar(out=out, in0=in0, scalar1=s1, scalar2=s2, op0=op0, op1=op1,
                                                  accum_out=accum_out), reads, writes)


def stt(cx, eng, out, in0, scalar, in1, op0, op1, reads, writes):
    return cx.S.op(eng, lambda e: e.scalar_tensor_tensor(out=out, in0=in0, scalar=scalar, in1=in1, op0=op0, op1=op1),
                   reads, writes)


def cp(cx, eng, out, in_, reads, writes):
    if eng == "act":
        return cx.S.op("act", lambda e: e.copy(out=out, in_=in_), reads, writes)
    return cx.S.op(eng, lambda e: e.tensor_copy(out=out, in_=in_), reads, writes)


def dma(cx, q, out, in_, reads, writes):
    return cx.S.dma(q, lambda e: e.dma_start(out=out, in_=in_), reads, writes)


def memset(cx, eng, ap, val, writes):
    return cx.S.op(eng, lambda e: e.memset(ap, val), (), writes)


def load_weight_bf16(cx, dst, src, stg, n, key, idx):
    sb_ = stg[idx % len(stg)]
    if len(src.shape) == 3:
        sview = sb_[:, 0:n].rearrange("p (a b) -> p a b", a=src.shape[1])
    else:
        sview = sb_[:, 0:n]
    dma(cx, "sp", sview, src, [], [sb_])
    ce = ("dve", "act", "pool")[idx % 3]
    cp(cx, ce, dst, sview, [sb_], [key])


def emit_xT(cx, x_rows, xs, xb, p_tp, xT, col0):
    dma(cx, "sp", xs[:], x_rows, [], [xs])
    cp(cx, "pool", xb[:], xs[:], [xs], [xb])
    for k in range(D // 128):
        tr(cx, p_tp[:, k * 128:(k + 1) * 128], xb[:, k * 128:(k + 1) * 128], cx.ident_b[:], [xb, cx.ident_b], [p_tp])
    cp(cx, "dve", xT[:, :, col0:col0 + 128], p_tp[:].rearrange("p (k t) -> p k t", k=D // 128), [p_tp], [xT])


def emit_proj(cx, x_rows_fn, ntok, wcols, fm_specs, tm_specs):
    TT = 512
    KD = D // 128
    NW = sum(w.shape[1] for w in wcols)
    with contextlib.ExitStack() as st:
        wsb = cx.sb([128, KD, NW], BF16, "pw", st)
        stg = [cx.sb([128, 1024], F32, "pstg", st) for _ in range(2)]
        xs = [cx.sb([128, D], F32, "pxs", st) for _ in range(2)]
        xb = [cx.sb([128, D], BF16, "pxb", st) for _ in range(2)]
        xT = [cx.sb([128, KD, TT], BF16, "pxT", st) for _ in range(2)]
        ost = [cx.sb([128, TT], F32, "post", st) for _ in range(3)]
        p_tp = cx.ps([128, 1024], BF16, "pptp", st)
        p_fm = [cx.ps([128, 512], F32, "ppfm", st) for _ in range(2)]
        p_tm = [cx.ps([128, 512], F32, "pptm", st) for _ in range(2)]
        tm_state = {}
        idx = 0
        off = 0
        for w in wcols:
            n = w.shape[1]
            wv = w.rearrange("(k p) f -> p k f", p=128)
            for k in range(KD):
                for c0 in range(0, n, 1024):
                    c1 = min(n, c0 + 1024)
                    load_weight_bf16(cx, wsb[:, k, off + c0:off + c1], wv[:, k, c0:c1], stg, c1 - c0, wsb, idx)
                    idx += 1
            off += n
        ntiles = ntok // TT
        oi = 0
        fi = 0
        ti_ = 0
        for t in range(ntiles):
            xTt = xT[t % 2]
            for s in range(TT // 128):
                emit_xT(cx, x_rows_fn(t * 4 + s), xs[s % 2], xb[s % 2], p_tp, xTt, s * 128)
            for (coff, M, dt_, dest_fn) in fm_specs:
                pf = p_fm[fi % 2]
                fi += 1
                for k in range(KD):
                    mm(cx, pf[0:M, :], wsb[:, k, coff:coff + M], xTt[:, k, :], k == 0, k == KD - 1, [wsb, xTt], [pf])
                o = ost[oi % 3]
                oi += 1
                ov = o[0:M, :] if dt_ == F32 else o[:].bitcast(BF16)[0:M, 0:TT]
                cp(cx, "act" if oi % 2 else "dve", ov, pf[0:M, :], [pf], [o])
                dma(cx, "pool", dest_fn(t), ov, [o], [])
            for (coff, N, post_fn) in tm_specs:
                for s in range(TT // 128):
                    pt = p_tm[ti_ % 2]
                    ti_ += 1
                    for k in range(KD):
                        mm(cx, pt[:, 0:N], xTt[:, k, s * 128:(s + 1) * 128], wsb[:, k, coff:coff + N], k == 0, k == KD - 1,
                           [wsb, xTt], [pt])
                    post_fn(cx, st, tm_state, pt, t * 4 + s)


def emit_outproj_ln(cx, srcT, nch, w, x_in, x_out, g, b, ntok, scale_y):
    c_y = scale_y / ALPHA
    eps_p = LN_EPS / (ALPHA * ALPHA)
    with contextlib.ExitStack() as st:
        wsb = cx.sb([128, nch, D], BF16, "ow", st)
        stg = [cx.sb([128, 1024], F32, "ostg", st) for _ in range(2)]
        gb = cx.sb([128, D], F32, "ogb", st)
        bb = cx.sb([128, D], F32, "obb", st)
        epst = cx.sb([128, 1], F32, "oeps", st)
        oT = [cx.sb([128, nch, 128], BF16, "ooT", st) for _ in range(2)]
        xs = [cx.sb([128, D], F32, "oxs", st) for _ in range(2)]
        r = cx.sb([128, D], F32, "or", st)
        xo = [cx.sb([128, D], F32, "oxo", st) for _ in range(2)]
        stat = [cx.sb([128, 8], F32, "ostat", st) for _ in range(2)]
        p_o = [cx.ps([128, 512], F32, "opo", st) for _ in range(4)]
        memset(cx, "pool", epst[:], eps_p, [epst])
        dma(cx, "sp", gb[:], g.partition_broadcast(128), [], [gb])
        dma(cx, "sp", bb[:], b.partition_broadcast(128), [], [bb])
        wv = w.rearrange("(k p) d -> p k d", p=128)
        for k in range(nch):
            load_weight_bf16(cx, wsb[:, k, :], wv[:, k, :], stg, D, wsb, k)
        sv = srcT.rearrange("c p t -> p c t")
        xiv = x_in.rearrange("(n p) d -> n p d", p=128)
        xov = x_out.rearrange("(n p) d -> n p d", p=128)
        outs = []
        for i in range(ntok // 128):
            oTt, xt, xot, stt_ = oT[i % 2], xs[i % 2], xo[i % 2], stat[i % 2]
            dma(cx, "sp", oTt[:], sv[:, :, i * 128:(i + 1) * 128], [], [oTt])
            dma(cx, "sp", xt[:], xiv[i], [], [xt])
            for hd in range(2):
                po = p_o[(2 * i + hd) % 4]
                for c in range(nch):
                    mm(cx, po[:], oTt[:, c, :], wsb[:, c, hd * 512:(hd + 1) * 512], c == 0, c == nch - 1, [oTt, wsb], [po])
                stt(cx, "dve", r[:, hd * 512:(hd + 1) * 512], po[:], c_y, xt[:, hd * 512:(hd + 1) * 512], ALU.mult, ALU.add,
                    [po, xt], [(r, hd)])
            emit_ln(cx, r, [(r, 0), (r, 1)], xot, stt_, epst, gb, bb, xot)
            outs.append(dma(cx, "pool", xov[i], xot[:], [xot], []))
        return outs


TOPK = 256
NBIS = 18


def emit_attn_core(cx, ckvT_d, kidxT_d, qT_d, qiT_d, widx_d, qpos_d, w_uk, w_uv, oT_d, NQ, SK, natural=False):
    S = cx.S
    NT = NQ // 256
    nkb_max = SK // 128
    with contextlib.ExitStack() as st:
        ckvT = cx.sb([128, 2, SK], BF16, "ackv", st)
        kidxT = cx.sb([64, SK], BF16, "akidx", st)
        Vp = cx.sb([128, nkb_max, 128], BF16, "aV", st)
        score = cx.sb([128, SK], F32, "ascore", st)
        msk = cx.sb([128, SK], BF16, "amsk", st)
        inv = [[cx.sb([128, SK], mybir.dt.float8e4, "ainv", st) for _ in range(2)] for _ in range(2)]
        sel = [cx.sb([128, 512], mybir.dt.float8e4, "asel", st) for _ in range(2)]
        qabs = cx.sb([128, 8, 2, 512], BF16, "aqabs", st)
        qT = cx.sb([128, 8, 256], BF16, "aqT", st)
        qiT = cx.sb([64, 8, 256], BF16, "aqiT", st)
        oT = cx.sb([128, 8, 256], BF16, "aoT", st)
        pT = [cx.sb([128, 512], BF16, "apT", st) for _ in range(2)]
        rl = [cx.sb([128, 512], F32, "arl", st) for _ in range(2)]
        wuk = cx.sb([128, 8, 256], BF16, "awuk", st)
        wuv = cx.sb([128, 2, 16, 64], BF16, "awuv", st)
        stg = [cx.sb([128, 1024], F32, "astg", st) for _ in range(1)]
        iota_f = cx.sb([128, 512], F32, "aiof", st)
        pen = cx.sb([128, 512], F32, "apen", st)
        qpos = cx.sb([128, NQ // 128], F32, "aqpos", st)
        widx = [cx.sb([128, 8], F32, "awidx", st) for _ in range(2)]
        sm = cx.sb([128, 8], F32, "asm", st)
        rec = cx.sb([128, 256], F32, "arec", st)
        ones_b = cx.sb([128, 128], BF16, "aones", st)
        P = [cx.ps([128, 512], F32, "aP", st) for _ in range(8)]
        gen = [P[0], P[1]]
        gi = [0]

        def nextp():
            gi[0] += 1
            return gen[gi[0] % 2]

        memset(cx, "pool", ones_b[:], 1.0, [ones_b])
        for b_ in range(2):
            memset(cx, "pool", pen[:], 0.0, [pen])
            for h2 in range(2):
                c0_ = h2 * 256 + b_ * 128
                ts(cx, "dve", pen[:, c0_:c0_ + 128], cx.ident_f[:], -240.0, None, ALU.mult, None, [cx.ident_f, pen], [pen])
            S.op("dve", lambda e, b_=b_: e.tensor_copy(out=sel[b_][:], in_=pen[:], saturate=False), [pen], [sel[b_]])
        S.op("pool", lambda e: e.iota(pen[:].bitcast(mybir.dt.int32), pattern=[[1, 512]], base=0, channel_multiplier=0), (), [pen])
        cp(cx, "pool", iota_f[:], pen[:].bitcast(mybir.dt.int32), [pen], [iota_f])
        dma(cx, "sp", qpos[:], qpos_d, [], [qpos])
        for rc in range(2):
            dma(cx, "sp", ckvT[:, rc, :], ckvT_d[rc], [], [ckvT])
        dma(cx, "sp", kidxT[:], kidxT_d, [], [kidxT])
        wukv = w_uk.rearrange("(p h2) dh r -> (h2 dh) p r", h2=2)
        for hf in range(2):
            load_weight_bf16(cx, wuk[:, hf * 4:(hf + 1) * 4, :], wukv[:, hf * 4:(hf + 1) * 4, :], stg, 1024, wuk, hf)
        for rc in range(2):
            load_weight_bf16(cx, wuv[:, rc, :, :], w_uv[:, rc * 128:(rc + 1) * 128, :].rearrange("h r dv -> r h dv"),
                             stg, 1024, (wuv, rc), 1 + rc)
        qT_v = qT_d.rearrange("c p t -> p c t")
        qiT_v = qiT_d.rearrange("c p t -> p c t")
        oT_v = oT_d.rearrange("c p t -> p c t")
        widx_v = widx_d.rearrange("(n p) h -> n p h", p=128)

        FP8 = mybir.dt.float8e4
        side_p = [P[5], P[7]]
        si = [0]

        def nexts():
            si[0] += 1
            return side_p[si[0] % 2]

        def tile_dims(kq):
            nkc = (kq // 2 + 1) if natural else (kq + 1)
            return nkc, 4 * nkc, 512 * nkc

        def side_units(kq):
            nkc, nkb, L = tile_dims(kq)
            q0 = kq * 256
            U = []
            U.append(lambda: dma(cx, "sp", qiT[:], qiT_v[:, :, q0:q0 + 256], [], [qiT]))
            for b in range(2):
                wt = widx[b]
                U.append(lambda wt=wt, b=b: dma(cx, "sp", wt[:], widx_v[kq * 2 + b], [], [wt]))
                for kc in range(nkc):
                    for h in range(8):
                        def u(b=b, kc=kc, h=h, wt=wt):
                            sc = score[:, kc * 512:(kc + 1) * 512]
                            ps = nexts()
                            mm(cx, ps[:], qiT[:, h, b * 128:(b + 1) * 128], kidxT[:, kc * 512:(kc + 1) * 512], True, True,
                               [qiT, kidxT], [ps])
                            rt_ = rl[h % 2]
                            act(cx, rt_[:], ps[:], AF.Relu, [ps], [rt_])
                            if h == 0:
                                ts(cx, "dve", sc, rt_[:], wt[:, 0:1], None, ALU.mult, None, [rt_, wt], [(score, kc)])
                            else:
                                stt(cx, "dve", sc, rt_[:], wt[:, h:h + 1], sc, ALU.mult, ALU.add, [rt_, wt, (score, kc)], [(score, kc)])
                        U.append(u)
                skeys = [(score, kc) for kc in range(nkc)]

                def pen_u(b=b):
                    ts(cx, "dve", sm[:, 4:5], qpos[:, kq * 2 + b:kq * 2 + b + 1], float(-512 * (nkc - 1)), None, ALU.add, None,
                       [qpos], [(sm, 4)])
                    ts(cx, "dve", pen[:], iota_f[:], sm[:, 4:5], -30000.0, ALU.is_gt, ALU.mult, [iota_f, (sm, 4)], [pen])
                    lc = score[:, (nkc - 1) * 512:nkc * 512]
                    tt(cx, "dve", lc, lc, pen[:], ALU.add, [pen, (score, nkc - 1)], [(score, nkc - 1)])
                    memset(cx, "dve", sm[:, 0:1], 0.0, [(sm, 0)])
                U.append(pen_u)
                W = 128.0
                for it in range(NBIS):
                    def bis(W=W):
                        S.op("dve", lambda e: e.tensor_scalar(out=msk[:, 0:L], in0=score[:, 0:L], scalar1=sm[:, 0:1], scalar2=None,
                                                              op0=ALU.is_ge, op1=ALU.add, accum_out=sm[:, 1:2]),
                             skeys + [(sm, 0)], [msk, (sm, 1)])
                        ts(cx, "dve", sm[:, 2:3], sm[:, 1:2], TOPK - 0.5, W / 2, ALU.is_ge, ALU.mult, [(sm, 1)], [(sm, 2)])
                        stt(cx, "dve", sm[:, 0:1], sm[:, 0:1], -W / 4, sm[:, 2:3], ALU.add, ALU.add, [(sm, 0), (sm, 2)], [(sm, 0)])
                    U.append(bis)
                    W = W / 2

                def fin(W=W, b=b):
                    ts(cx, "dve", sm[:, 3:4], sm[:, 0:1], -W / 2, None, ALU.add, None, [(sm, 0)], [(sm, 3)])
                    iv = inv[kq % 2][b]
                    S.op("dve", lambda e: e.tensor_scalar(out=iv[:, 0:L], in0=score[:, 0:L], scalar1=sm[:, 3:4], scalar2=128.0,
                                                          op0=ALU.is_lt, op1=ALU.mult, saturate=False),
                         skeys + [(sm, 3)], [iv])
                U.append(fin)
            return U

        def attention(kq, side):
            nkc, nkb, L = tile_dims(kq)
            q0 = kq * 256
            n_iter = 8 * nkb
            rate = (len(side) + n_iter - 1) // n_iter if side else 0
            dma(cx, "sp", qT[:], qT_v[:, :, q0:q0 + 256], [], [qT])
            for h in range(16):
                p_, h2 = h // 2, h % 2
                pq = nextp()
                for rc in range(2):
                    mm(cx, pq[:, rc * 256:(rc + 1) * 256], wuk[h2 * 64:(h2 + 1) * 64, p_, rc * 128:(rc + 1) * 128],
                       qT[h2 * 64:(h2 + 1) * 64, p_, :], True, True, [wuk, qT], [pq])
                cp(cx, "act", qabs[:, p_, :, h2 * 256:(h2 + 1) * 256],
                   pq[:].rearrange("p (a b) -> p a b", a=2), [pq], [(qabs, h)])
            for p_ in range(8):
                for kb0 in range(0, nkb, 4):
                    pv = nextp()
                    for j in range(4):
                        for rc in range(2):
                            mm(cx, pv[:, j * 128:(j + 1) * 128], ckvT[:, rc, (kb0 + j) * 128:(kb0 + j + 1) * 128],
                               wuv[:, rc, 2 * p_:2 * p_ + 2, :].rearrange("p a b -> p (a b)"), rc == 0, rc == 1,
                               [ckvT, (wuv, rc)], [pv])
                    cp(cx, "act", Vp[:, kb0:kb0 + 4, :], pv[:].rearrange("p (a b) -> p a b", a=4), [pv], [Vp])

                def qk(kb):
                    pl = P[2 + kb % 2]
                    for rc in range(2):
                        mm(cx, pl[:], ckvT[:, rc, kb * 128:(kb + 1) * 128], qabs[:, p_, rc, :], rc == 0, False,
                           [ckvT, (qabs, 2 * p_), (qabs, 2 * p_ + 1)], [pl])
                    for b_ in range(2):
                        iv = inv[kq % 2][b_]
                        mm(cx, pl[:], iv[:, kb * 128:(kb + 1) * 128], sel[b_][:], False, b_ == 1, [iv, sel[b_]], [pl])

                qk(0)
                for kb in range(nkb):
                    pl = P[2 + kb % 2]
                    if kb + 1 < nkb:
                        qk(kb + 1)
                    pTt = pT[kb % 2]
                    act(cx, pTt[:], pl[:], AF.Exp, [pl], [pTt], scale=0.125)
                    first, last = kb == 0, kb == nkb - 1
                    mm(cx, P[4][:], Vp[:, kb, :], pTt[:], first, last, [Vp, pTt], [P[4]])
                    mm(cx, P[6][:], ones_b[:], pTt[:], first, last, [ones_b, pTt], [P[6]])
                    for _ in range(rate):
                        if side:
                            side.pop(0)()
                S.op("dve", lambda e: e.reciprocal(out=rec[0:64, :], in_=P[6][0:64, 0:256]), [P[6]], [(rec, 0)])
                S.op("dve", lambda e: e.reciprocal(out=rec[64:128, :], in_=P[6][64:128, 256:512]), [P[6]], [(rec, 1)])
                tt(cx, "dve", oT[0:64, p_, :], P[4][0:64, 0:256], rec[0:64, :], ALU.mult, [P[4], (rec, 0)], [(oT, p_, 0)])
                tt(cx, "dve", oT[64:128, p_, :], P[4][64:128, 256:512], rec[64:128, :], ALU.mult, [P[4], (rec, 1)], [(oT, p_, 1)])
            dma(cx, "pool", oT_v[:, :, q0:q0 + 256], oT[:], [(oT, p_, i) for p_ in range(8) for i in range(2)], [])
            while side:
                side.pop(0)()

        for u in side_units(0):
            u()
        for kq in range(NT):
            attention(kq, side_units(kq + 1) if kq + 1 < NT else [])


ATT_O1, ATT_O2, ATT_O3, ATT_O4 = 1024, 1280, 1792, 1856
RMS_EPS = 1e-6


def emit_attn_stage(cx, xq, xk, qpos_d, w_in, kv_g, w_uk, w_uv, w_o, g, b, x_out, NQ, SK, scr, natural=False):
    nc, S = cx.nc, cx.S
    ckvT_d, kidxT_d, qT_d, qiT_d, widx_d, oT_d = (scr[k] for k in ("ckvT", "kidxT", "qT", "qiT", "widx", "oT"))
    xk_v = xk.rearrange("(n p) d -> n p d", p=128)
    xq_v = xq.rearrange("(n p) d -> n p d", p=128)

    def post_ckv(cx, st, state, pt, blk):
        if "init" not in state:
            state["init"] = True
            state["gb"] = cx.sb([128, 256], F32, "kg", st)
            state["sq"] = cx.sb([128, 256], F32, "ksq", st)
            state["cn"] = [cx.sb([128, 256], BF16, "kcn", st) for _ in range(2)]
            state["cT"] = [cx.sb([128, 2, 128], BF16, "kcT", st) for _ in range(2)]
            state["sm"] = [cx.sb([128, 4], F32, "ksm", st) for _ in range(2)]
            state["eps"] = cx.sb([128, 1], F32, "keps", st)
            state["ptp"] = cx.ps([128, 512], BF16, "kptp", st)
            memset(cx, "pool", state["eps"][:], RMS_EPS, [state["eps"]])
            dma(cx, "sp", state["gb"][:], kv_g.partition_broadcast(128), [], [state["gb"]])
        gb_, sq, cn, cT, sm_, eps, ptp = (state["gb"], state["sq"], state["cn"][blk % 2], state["cT"][blk % 2],
                                        state["sm"][blk % 2], state["eps"], state["ptp"])
        act(cx, sq[:], pt[:, 0:256], AF.Square, [pt], [sq, (sm_, 0)], accum_out=sm_[:, 0:1])
        act(cx, sm_[:, 1:2], sm_[:, 0:1], AF.Sqrt, [(sm_, 0), eps], [(sm_, 1)], bias=eps[:], scale=1.0 / 256)
        cx.S.op("dve", lambda e: e.reciprocal(out=sm_[:, 2:3], in_=sm_[:, 1:2]), [(sm_, 1)], [(sm_, 2)])
        stt(cx, "dve", cn[:], pt[:, 0:256], sm_[:, 2:3], gb_[:], ALU.mult, ALU.mult, [pt, (sm_, 2), gb_], [cn])
        for rc in range(2):
            tr(cx, ptp[:, rc * 128:(rc + 1) * 128], cn[:, rc * 128:(rc + 1) * 128], cx.ident_b[:], [cn, cx.ident_b], [ptp])
        cp(cx, "act", cT[:], ptp[:, 0:256].rearrange("p (a b) -> p a b", a=2), [ptp], [cT])
        dma(cx, "pool", ckvT_d.rearrange("c p t -> p c t")[:, :, blk * 128:(blk + 1) * 128], cT[:], [cT], [])

    emit_proj(cx, lambda i: xk_v[i], SK, [w_in[:, ATT_O1:ATT_O2], w_in[:, ATT_O3:ATT_O4]],
              fm_specs=[(256, 64, BF16, lambda t: kidxT_d[:, t * 512:(t + 1) * 512])],
              tm_specs=[(0, 256, post_ckv)])
    S.barrier()

    def post_widx(cx, st, state, pt, blk):
        if "w" not in state:
            state["w"] = [cx.sb([128, 8], F32, "qw", st) for _ in range(2)]
        wt = state["w"][blk % 2]
        ts(cx, "dve", wt[:], pt[:, 0:8], (8 ** -0.5) * (64 ** -0.5), None, ALU.mult, None, [pt], [wt])
        dma(cx, "pool", widx_d[blk * 128:(blk + 1) * 128, :], wt[:], [wt], [])

    def mk_q(c):
        return lambda t: qT_d[c][:, t * 512:(t + 1) * 512]

    def mk_qi(c):
        return lambda t: qiT_d[c][:, t * 512:(t + 1) * 512]

    fm = [(c * 128, 128, BF16, mk_q(c)) for c in range(8)] + [(1024 + c * 64, 64, BF16, mk_qi(c)) for c in range(8)]
    emit_proj(cx, lambda i: xq_v[i], NQ, [w_in[:, 0:ATT_O1], w_in[:, ATT_O2:ATT_O3], w_in[:, ATT_O4:ATT_O4 + 8]],
              fm_specs=fm, tm_specs=[(1536, 8, post_widx)])
    S.barrier()
    emit_attn_core(cx, ckvT_d, kidxT_d, qT_d, qiT_d, widx_d, qpos_d, w_uk, w_uv, oT_d, NQ, SK, natural)
    S.barrier()
    outs = emit_outproj_ln(cx, oT_d, 8, w_o, xq, x_out, g, b, NQ, 1.0)
    return outs


def attn_scratch(nc, NQ, SK, tag=""):
    mk = lambda n, shp, dt_: nc.dram_tensor(n + tag, shp, dt_, kind="Internal").ap()
    return {"ckvT": mk("s_ckvT", [2, 128, SK], BF16), "kidxT": mk("s_kidxT", [64, SK], BF16),
            "qT": mk("s_qT", [8, 128, NQ], BF16), "qiT": mk("s_qiT", [8, 64, NQ], BF16),
            "widx": mk("s_widx", [NQ, 8], F32), "oT": mk("s_oT", [8, 128, NQ], BF16)}


def build_attn_program(NQ, SK):
    nc = bass.Bass("TRN2", target_bir_lowering=False)
    inp = lambda n, shp: nc.dram_tensor(n, shp, F32, kind="ExternalInput").ap()
    xq, xk = inp("xq", [NQ, D]), inp("xk", [SK, D])
    qpos = inp("qpos", [128, NQ // 128])
    w_in, kv_g = inp("w_in", [D, 1864]), inp("kv_g", [256])
    w_uk, w_uv, w_o = inp("w_uk", [16, 64, 256]), inp("w_uv", [16, 256, 64]), inp("w_o", [D, D])
    g, b = inp("g", [D]), inp("b", [D])
    x_out = nc.dram_tensor("x_out", [NQ, D], F32, kind="ExternalOutput").ap()
    scr = attn_scratch(nc, NQ, SK)
    with contextlib.ExitStack() as st:
        cx = Ctx(nc, st)
        setup_consts(cx)
        outs = emit_attn_stage(cx, xq, xk, qpos, w_in, kv_g, w_uk, w_uv, w_o, g, b, x_out, NQ, SK, scr)
        cx.S.emit(final_wait_ops=outs[-N_DMA_SEM:])
    return nc


def emit_ssd_core(cx, z_d, xbc_d, dt_d, cw_d, cb_d, dtb_d, alog_d, dsk_d, ng_d, yzT_d, SK):
    S = cx.S
    NCH = SK // 128
    with contextlib.ExitStack() as st:
        H = cx.sb([128, 1024], F32, "sH", st)
        Hb = cx.sb([128, 1024], BF16, "sHb", st)
        pre = [cx.sb([128, 515], F32, "spre", st) for _ in range(2)]
        acc = [cx.sb([128, 512], F32, "sacc", st) for _ in range(2)]
        xsT = cx.sb([128, 8, 512], F32, "sxsT", st)
        BT = cx.sb([128, 2, 512], BF16, "sBT", st)
        CT = cx.sb([128, 2, 512], BF16, "sCT", st)
        cw = cx.sb([128, 12, 4], F32, "scw", st)
        cb = cx.sb([128, 12], F32, "scb", st)
        dtb = cx.sb([128, 16], F32, "sdtb", st)
        a_bc = cx.sb([128, 16], F32, "sabc", st)
        d_bc = cx.sb([128, 16], F32, "sdbc", st)
        ng = cx.sb([128, 1024], F32, "sng", st)
        triu = cx.sb([128, 128], F32, "striu", st)
        mgt = cx.sb([128, 128], F32, "smgt", st)
        xs_tok_l = [cx.sb([128, 1024], F32, "sxs", st) for _ in range(2)]
        Btok_l = [cx.sb([128, 2, 128], BF16, "sBtok", st) for _ in range(2)]
        sm = [cx.sb([128, 8, 16], F32, "ssm", st) for _ in range(2)]
        G = [cx.sb([128, 128], F32, "sG", st) for _ in range(4)]
        dec = [cx.sb([128, 4, 128], F32, "sdec", st) for _ in range(2)]
        cbt_l = [cx.sb([128, 2, 128], F32, "scbt", st) for _ in range(2)]
        MT_l = [cx.sb([128, 16, 128], BF16, "sMT", st) for _ in range(2)]
        xdt_l = [cx.sb([128, 1024], BF16, "sxdt", st) for _ in range(2)]
        xdtd_l = [cx.sb([128, 1024], BF16, "sxdtd", st) for _ in range(2)]
        t1_l = [cx.sb([128, 1024], F32, "st1", st) for _ in range(2)]
        xsD_l = [cx.sb([128, 1024], F32, "sxsD", st) for _ in range(2)]
        y_l = [cx.sb([128, 1024], F32, "sy", st) for _ in range(2)]
        zt = [cx.sb([128, 1024], F32, "szt", st) for _ in range(2)]
        zs_l = [cx.sb([128, 1024], F32, "szs", st) for _ in range(2)]
        yzn_l = [cx.sb([128, 1024], BF16, "syzn", st) for _ in range(2)]
        yzT = [cx.sb([128, 8, 128], BF16, "syzT", st) for _ in range(2)]
        st2 = [cx.sb([128, 8], F32, "sst2", st) for _ in range(2)]
        epst = cx.sb([128, 1], F32, "sepst", st)
        P = [cx.ps([128, 512], F32, "sP", st) for _ in range(8)]

        memset(cx, "pool", epst[:], RMS_EPS, [epst])
        memset(cx, "pool", H[:], 0.0, [H])
        memset(cx, "pool", Hb[:], 0.0, [Hb])
        S.op("pool", lambda e: e.affine_select(out=triu[:], in_=cx.ones_f[:], pattern=[[1, 128]], compare_op=ALU.is_ge,
                                               fill=0.0, base=0, channel_multiplier=-1), [cx.ones_f], [triu])
        S.op("pool", lambda e: e.affine_select(out=mgt[:], in_=cx.ones_f[:], pattern=[[-1, 128]], compare_op=ALU.is_gt,
                                               fill=0.0, base=0, channel_multiplier=1), [cx.ones_f], [mgt])
        dma(cx, "sp", cw[:], cw_d, [], [cw])
        dma(cx, "sp", cb[:], cb_d, [], [cb])
        dma(cx, "sp", dtb[:], dtb_d.partition_broadcast(128), [], [dtb])
        dma(cx, "sp", a_bc[:], alog_d.partition_broadcast(128), [], [a_bc])
        dma(cx, "sp", d_bc[:], dsk_d.partition_broadcast(128), [], [d_bc])
        dma(cx, "sp", ng[:], ng_d.partition_broadcast(128), [], [ng])
        act(cx, a_bc[:], a_bc[:], AF.Exp, [a_bc], [a_bc])
        ts(cx, "dve", a_bc[:], a_bc[:], -1.0, None, ALU.mult, None, [a_bc], [a_bc])
        z_v = z_d.rearrange("(n p) c -> n p c", p=128)
        dt_v = dt_d.rearrange("(n p) c -> n p c", p=128)
        yz_v = yzT_d.rearrange("c p t -> p c t")

        def bc3(ap2):
            return ap2.unsqueeze(2).to_broadcast([128, 16, 64])

        def v3(ap):
            return ap.rearrange("p (h d) -> p h d", h=16)

        for sc in range(SK // 512):
            t0 = sc * 512
            for cc in range(12):
                pt, ac = pre[cc % 2], acc[cc % 2]
                if sc == 0:
                    memset(cx, "pool", pt[:, 0:3], 0.0, [pt])
                    dma(cx, "sp", pt[:, 3:515], xbc_d[cc][:, 0:512], [], [pt])
                else:
                    dma(cx, "sp", pt[:, 0:515], xbc_d[cc][:, t0 - 3:t0 + 512], [], [pt])
                ts(cx, "dve", ac[:], pt[:, 0:512], cw[:, cc, 0:1], None, ALU.mult, None, [pt, cw], [ac])
                for k in range(1, 4):
                    stt(cx, "dve", ac[:], pt[:, k:k + 512], cw[:, cc, k:k + 1], ac[:], ALU.mult, ALU.add, [pt, cw, ac], [ac])
                if cc < 8:
                    dst, key = xsT[:, cc, :], (xsT, cc)
                elif cc < 10:
                    dst, key = BT[:, cc - 8, :], (BT, cc - 8)
                else:
                    dst, key = CT[:, cc - 10, :], (CT, cc - 10)
                act(cx, dst, ac[:], AF.Silu, [ac, cb], [key], bias=cb[:, cc:cc + 1])
            for ch in range(4):
                c = sc * 4 + ch
                c0 = ch * 128
                xs_tok = xs_tok_l[c % 2]
                Btok = Btok_l[c % 2]
                cbt = cbt_l[c % 2]
                MT = MT_l[c % 2]
                xdt = xdt_l[c % 2]
                xdtd = xdtd_l[c % 2]
                t1 = t1_l[c % 2]
                xsD = xsD_l[c % 2]
                y = y_l[c % 2]
                zs = zs_l[c % 2]
                yzn = yzn_l[c % 2]
                s_ = sm[c % 2]
                for k in range(8):
                    tr(cx, P[k // 4][:, (k % 4) * 128:(k % 4 + 1) * 128], xsT[:, k, c0:c0 + 128], cx.ident_f[:],
                       [(xsT, k), cx.ident_f], [P[k // 4]])
                for hf in range(2):
                    cp(cx, "act", xs_tok[:, hf * 512:(hf + 1) * 512], P[hf][:], [P[hf]], [(xs_tok, hf)])
                xk_ = [(xs_tok, 0), (xs_tok, 1)]
                p2b = P[2][:].bitcast(BF16)
                for g_ in range(2):
                    tr(cx, p2b[:, g_ * 128:(g_ + 1) * 128], BT[:, g_, c0:c0 + 128], cx.ident_b[:], [(BT, g_), cx.ident_b], [P[2]])
                cp(cx, "dve", Btok[:], p2b[:, 0:256].rearrange("p (a b) -> p a b", a=2), [P[2]], [Btok])
                dma(cx, "sp", s_[:, 0, :], dt_v[c], [], [(s_, 0)])
                tt(cx, "dve", s_[:, 0, :], s_[:, 0, :], dtb[:], ALU.add, [(s_, 0), dtb], [(s_, 0)])
                act(cx, s_[:, 1, :], s_[:, 0, :], AF.Exp, [(s_, 0)], [(s_, 1)])
                act(cx, s_[:, 1, :], s_[:, 1, :], AF.Ln, [(s_, 1)], [(s_, 1)], bias=1.0)
                tt(cx, "dve", s_[:, 2, :], s_[:, 1, :], a_bc[:], ALU.mult, [(s_, 1), a_bc], [(s_, 2)])
                mm(cx, P[2][:, 256:272], triu[:], s_[:, 2, :], True, True, [triu, (s_, 2)], [P[2]])
                mm(cx, P[2][:, 272:288], cx.ones_f[:], s_[:, 2, :], True, True, [cx.ones_f, (s_, 2)], [P[2]])
                cp(cx, "dve", s_[:, 3, :], P[2][:, 256:272], [P[2]], [(s_, 3)])
                act(cx, s_[:, 4, :], s_[:, 3, :], AF.Exp, [(s_, 3)], [(s_, 4)])
                tt(cx, "dve", s_[:, 5, :], P[2][:, 272:288], s_[:, 3, :], ALU.subtract, [P[2], (s_, 3)], [(s_, 5)])
                act(cx, s_[:, 5, :], s_[:, 5, :], AF.Exp, [(s_, 5)], [(s_, 5)])
                act(cx, s_[:, 6, :], P[2][:, 272:288], AF.Exp, [P[2]], [(s_, 6)])
                tt(cx, "dve", s_[:, 7, :], s_[:, 1, :], s_[:, 5, :], ALU.mult, [(s_, 1), (s_, 5)], [(s_, 7)])
                tt(cx, "dve", v3(xdt[:]), v3(xs_tok[:]), bc3(s_[:, 1, :]), ALU.mult, xk_ + [(s_, 1)], [xdt])
                tt(cx, "pool", v3(xdtd[:]), v3(xs_tok[:]), bc3(s_[:, 7, :]), ALU.mult, xk_ + [(s_, 7)], [xdtd])
                for g_ in range(2):
                    mm(cx, P[2][:, 288 + g_ * 128:288 + (g_ + 1) * 128][:, 0:128] if False else P[7][:, g_ * 128:(g_ + 1) * 128],
                       BT[:, g_, c0:c0 + 128], CT[:, g_, c0:c0 + 128], True, True, [(BT, g_), (CT, g_)], [P[7]])
                tt(cx, "dve", cbt[:], P[7][:, 0:256].rearrange("p (a b) -> p a b", a=2),
                   triu[:].unsqueeze(1).to_broadcast([128, 2, 128]), ALU.mult, [P[7], triu], [cbt])
                for h0 in range(0, 16, 4):
                    pseg = P[3 + (h0 // 4) % 2]
                    dc = dec[(h0 // 4) % 2]
                    for j in range(4):
                        h = h0 + j
                        ts(cx, "pool" if j % 2 else "dve", G[j][:], mgt[:], s_[:, 2, h:h + 1], None, ALU.mult, None,
                           [mgt, (s_, 2)], [G[j]])
                        mm(cx, pseg[:, j * 128:(j + 1) * 128], G[j][:], triu[:], True, True, [G[j], triu], [pseg])
                    act(cx, dc[:], pseg[:].rearrange("p (a b) -> p a b", a=4), AF.Exp, [pseg], [dc])
                    g_ = h0 // 8
                    tt(cx, "dve", MT[:, h0:h0 + 4, :], dc[:], cbt[:, g_:g_ + 1, :].to_broadcast([128, 4, 128]), ALU.mult,
                       [dc, cbt], [(MT, h0 // 4)])
                for h in range(16):
                    py = P[h // 8]
                    mm(cx, py[:, (h % 8) * 64:(h % 8 + 1) * 64], MT[:, h, :], xdt[:, h * 64:(h + 1) * 64], True, True,
                       [(MT, h // 4), xdt], [py])
                for g_ in range(2):
                    mm(cx, P[5 + g_][:], CT[:, g_, c0:c0 + 128], Hb[:, g_ * 512:(g_ + 1) * 512], True, True, [(CT, g_), Hb], [P[5 + g_]])
                for g_ in range(2):
                    tt(cx, "dve", t1[:, g_ * 512:(g_ + 1) * 512].rearrange("p (h d) -> p h d", h=8),
                       P[5 + g_][:].rearrange("p (h d) -> p h d", h=8),
                       s_[:, 4, g_ * 8:(g_ + 1) * 8].unsqueeze(2).to_broadcast([128, 8, 64]), ALU.mult,
                       [P[5 + g_], (s_, 4)], [(t1, g_)])
                tt(cx, "pool", v3(xsD[:]), v3(xs_tok[:]), bc3(d_bc[:]), ALU.mult, xk_ + [d_bc], [xsD])
                tt(cx, "pool", xsD[:], xsD[:], t1[:], ALU.add, [xsD, (t1, 0), (t1, 1)], [xsD])
                for g_ in range(2):
                    tt(cx, "dve", y[:, g_ * 512:(g_ + 1) * 512], P[g_][:], xsD[:, g_ * 512:(g_ + 1) * 512], ALU.add,
                       [P[g_], xsD], [(y, g_)])
                for g_ in range(2):
                    mm(cx, P[5 + g_][:], Btok[:, g_, :], xdtd[:, g_ * 512:(g_ + 1) * 512], True, True, [Btok, xdtd], [P[5 + g_]])
                tt(cx, "dve", v3(H[:]), v3(H[:]), bc3(s_[:, 6, :]), ALU.mult, [H, (s_, 6)], [H])
                for g_ in range(2):
                    tt(cx, "dve", H[:, g_ * 512:(g_ + 1) * 512], H[:, g_ * 512:(g_ + 1) * 512], P[5 + g_][:], ALU.add,
                       [H, P[5 + g_]], [H])
                cp(cx, "act", Hb[:], H[:], [H], [Hb])
                ztt = zt[c % 2]
                s2 = st2[c % 2]
                dma(cx, "sp", ztt[:], z_v[c], [], [ztt])
                act(cx, zs[:], ztt[:], AF.Silu, [ztt], [zs])
                tt(cx, "dve", y[:], y[:], zs[:], ALU.mult, [(y, 0), (y, 1), zs], [(y, 0), (y, 1)])
                for g_ in range(2):
                    act(cx, zs[:, g_ * 512:(g_ + 1) * 512], y[:, g_ * 512:(g_ + 1) * 512], AF.Square, [(y, g_)], [zs, (s2, g_)],
                        accum_out=s2[:, g_:g_ + 1])
                    act(cx, s2[:, 2 + g_:3 + g_], s2[:, g_:g_ + 1], AF.Sqrt, [(s2, g_), epst], [(s2, 2 + g_)], bias=epst[:],
                        scale=1.0 / 512)
                    S.op("dve", lambda e, g_=g_, s2=s2: e.reciprocal(out=s2[:, 4 + g_:5 + g_], in_=s2[:, 2 + g_:3 + g_]),
                         [(s2, 2 + g_)], [(s2, 4 + g_)])
                    stt(cx, "dve", yzn[:, g_ * 512:(g_ + 1) * 512], y[:, g_ * 512:(g_ + 1) * 512], s2[:, 4 + g_:5 + g_],
                        ng[:, g_ * 512:(g_ + 1) * 512], ALU.mult, ALU.mult, [(y, g_), (s2, 4 + g_), ng], [(yzn, g_)])
                p7b = P[7][:].bitcast(BF16)
                yT = yzT[c % 2]
                for k in range(8):
                    tr(cx, p7b[:, k * 128:(k + 1) * 128], yzn[:, k * 128:(k + 1) * 128], cx.ident_b[:],
                       [(yzn, k // 4), cx.ident_b], [P[7]])
                cp(cx, "act", yT[:], p7b[:].rearrange("p (a b) -> p a b", a=8), [P[7]], [yT])
                dma(cx, "pool", yz_v[:, :, c * 128:(c + 1) * 128], yT[:], [yT], [])


SSD_NW = 2576


def emit_ssd_stage(cx, xk, w_in_c, cw_d, cb_d, dtb_d, alog_d, dsk_d, ng_d, yzT_d, SK, scr):
    S = cx.S
    z_d, xbc_d, dt_d = scr["z"], scr["xbc"], scr["dt"]
    xk_v = xk.rearrange("(n p) d -> n p d", p=128)

    def post_z(half):
        def f(cx, st, state, pt, blk):
            if "z" not in state:
                state["z"] = [cx.sb([128, 512], F32, "zz", st) for _ in range(2)]
                state["i"] = 0
            state["i"] += 1
            zt_ = state["z"][state["i"] % 2]
            cp(cx, "act", zt_[:], pt[:, 0:512], [pt], [zt_])
            dma(cx, "pool", z_d[blk * 128:(blk + 1) * 128, half * 512:(half + 1) * 512], zt_[:], [zt_], [])
        return f

    def post_dt(cx, st, state, pt, blk):
        if "d" not in state:
            state["d"] = [cx.sb([128, 16], F32, "zd", st) for _ in range(2)]
        d_ = state["d"][blk % 2]
        cp(cx, "dve", d_[:], pt[:, 0:16], [pt], [d_])
        dma(cx, "pool", dt_d[blk * 128:(blk + 1) * 128, :], d_[:], [d_], [])

    def mk(c):
        return lambda t: xbc_d[c][:, t * 512:(t + 1) * 512]

    fm = [(1024 + c * 128, 128, F32, mk(c)) for c in range(12)]
    emit_proj(cx, lambda i: xk_v[i], SK, w_in_c if isinstance(w_in_c, list) else [w_in_c], fm_specs=fm,
              tm_specs=[(0, 512, post_z(0)), (512, 512, post_z(1)), (2560, 16, post_dt)])
    S.barrier()
    emit_ssd_core(cx, z_d, xbc_d, dt_d, cw_d, cb_d, dtb_d, alog_d, dsk_d, ng_d, yzT_d, SK)


def ssd_scratch(nc, SK, tag=""):
    mk = lambda n, shp, dt_: nc.dram_tensor(n + tag, shp, dt_, kind="Internal").ap()
    return {"z": mk("s_z", [SK, 1024], F32), "xbc": mk("s_xbc", [12, 128, SK], F32), "dt": mk("s_dt", [SK, 16], F32)}


def build_ssd_program(SK):
    nc = bass.Bass("TRN2", target_bir_lowering=False)
    inp = lambda n, shp: nc.dram_tensor(n, shp, F32, kind="ExternalInput").ap()
    xk = inp("xk", [SK, D])
    w_in_c = inp("w_in_c", [D, SSD_NW])
    cw, cb = inp("cw", [128, 12, 4]), inp("cb", [128, 12])
    dtb, alog, dsk, ng = inp("dtb", [16]), inp("alog", [16]), inp("dsk", [16]), inp("ng", [1024])
    yzT = nc.dram_tensor("yzT", [8, 128, SK], BF16, kind="ExternalOutput").ap()
    scr = ssd_scratch(nc, SK)
    with contextlib.ExitStack() as st:
        cx = Ctx(nc, st)
        setup_consts(cx)
        emit_ssd_stage(cx, xk, w_in_c, cw, cb, dtb, alog, dsk, ng, yzT, SK, scr)
        cx.S.emit(final_wait_ops=cx.S.dma_hist["pool"][-N_DMA_SEM:])
    return nc


def build_outproj_program(NQ, nch):
    nc = bass.Bass("TRN2", target_bir_lowering=False)
    inp = lambda n, shp: nc.dram_tensor(n, shp, F32, kind="ExternalInput").ap()
    srcT = nc.dram_tensor("srcT", [nch, 128, NQ], BF16, kind="ExternalInput").ap()
    w, xq, g, b = inp("w", [nch * 128, D]), inp("xq", [NQ, D]), inp("g", [D]), inp("b", [D])
    x_out = nc.dram_tensor("x_out", [NQ, D], F32, kind="ExternalOutput").ap()
    with contextlib.ExitStack() as st:
        cx = Ctx(nc, st)
        setup_consts(cx)
        outs = emit_outproj_ln(cx, srcT, nch, w, xq, x_out, g, b, NQ, 1.0)
        cx.S.emit(final_wait_ops=outs[-N_DMA_SEM:])
    return nc


SEQ = 8192
BATCH = 4
NQ_CORE = SEQ // 2


def _own_tokens(j):
    bl = []
    for k in range(SEQ // 512):
        bl += [4 * k, 4 * k + 3] if j == 0 else [4 * k + 1, 4 * k + 2]
    return np.concatenate([np.arange(b * 128, (b + 1) * 128) for b in bl])


def _ssd_core_inputs(j, w_in, conv_w, conv_b, dt_bias, a_log, d_skip, norm_g):
    cols = np.concatenate([np.arange(j * 1024, (j + 1) * 1024), 2048 + np.arange(j * 1024, (j + 1) * 1024),
                           4096 + np.arange(j * 256, (j + 1) * 256), 4608 + np.arange(j * 256, (j + 1) * 256),
                           5120 + np.arange(j * 16, (j + 1) * 16)])
    ch = np.concatenate([np.arange(j * 1024, (j + 1) * 1024), 2048 + np.arange(j * 256, (j + 1) * 256),
                         2560 + np.arange(j * 256, (j + 1) * 256)])
    cw = np.ascontiguousarray(conv_w[:, ch].T.reshape(12, 128, 4).transpose(1, 0, 2))
    cb = np.ascontiguousarray(conv_b[ch].reshape(12, 128).T)
    return {"w_in_c": np.ascontiguousarray(w_in[:, cols]), "cw": cw, "cb": cb,
            "dtb": np.ascontiguousarray(dt_bias[j * 16:(j + 1) * 16]), "alog": np.ascontiguousarray(a_log[j * 16:(j + 1) * 16]),
            "dsk": np.ascontiguousarray(d_skip[j * 16:(j + 1) * 16]), "ng": np.ascontiguousarray(norm_g[j * 1024:(j + 1) * 1024])}


_PROGS = {}


def _prog(name, fn):
    if name not in _PROGS:
        _PROGS[name] = fn()
    return _PROGS[name]


def kernel_unfused(x, ln_g, ln_b, ffn1_w_in, ffn1_w_out, ffn2_w_in, ffn2_w_out, attn_w_in, attn_kv_norm, attn_w_uk, attn_w_uv,
           attn_w_out, ssm_w_in, ssm_conv_w, ssm_conv_b, ssm_dt_bias, ssm_a_log, ssm_d, ssm_norm_g, ssm_w_out):
    f32 = lambda a: np.ascontiguousarray(np.asarray(a, dtype=np.float32))
    x = f32(x)
    cores = list(range(NCORES))
    tok = [_own_tokens(c % 2) for c in cores]
    qpos = [np.ascontiguousarray(tok[c].reshape(-1, 128).T.astype(np.float32)) for c in cores]
    x_own = [np.ascontiguousarray(x[c // 2][tok[c]]) for c in cores]

    def run(nc, in_maps, key):
        res = run_bass_kernel_spmd(nc, in_maps, core_ids=cores)
        return [r[key] for r in res.results]

    def full_seq(x_own):
        out = []
        for s in range(BATCH):
            xs = np.empty((SEQ, D), np.float32)
            for j in range(2):
                xs[tok[2 * s + j]] = x_own[2 * s + j]
            out.append(xs)
        return out

    def ffn(x_own, w_in, w_out, g, b):
        nc = _prog("ffn", lambda: build_ffn_program(NQ_CORE))
        return run(nc, [{"x_in": x_own[c], "w_in": f32(w_in), "w_out": f32(w_out), "g": f32(g), "b": f32(b)} for c in cores],
                   "x_out")

    for i in range(DEPTH):
        x_own = ffn(x_own, ffn1_w_in[i], ffn1_w_out[i], ln_g[i, 0], ln_b[i, 0])
        j_ = i // 2
        xk = full_seq(x_own)
        if i % 2 == 0:
            nc = _prog("attn", lambda: build_attn_program(NQ_CORE, SEQ))
            x_own = run(nc, [{"xq": x_own[c], "xk": xk[c // 2], "qpos": qpos[c], "w_in": f32(attn_w_in[j_]),
                              "kv_g": f32(attn_kv_norm[j_]), "w_uk": f32(attn_w_uk[j_]), "w_uv": f32(attn_w_uv[j_]),
                              "w_o": f32(attn_w_out[j_]), "g": f32(ln_g[i, 1]), "b": f32(ln_b[i, 1])} for c in cores], "x_out")
        else:
            nc = _prog("ssd", lambda: build_ssd_program(SEQ))
            ci = [_ssd_core_inputs(jj, f32(ssm_w_in[j_]), f32(ssm_conv_w[j_]), f32(ssm_conv_b[j_]), f32(ssm_dt_bias[j_]),
                                   f32(ssm_a_log[j_]), f32(ssm_d[j_]), f32(ssm_norm_g[j_])) for jj in range(2)]
            yz = run(nc, [dict(ci[c % 2], xk=xk[c // 2]) for c in cores], "yzT")
            nc2 = _prog("oproj", lambda: build_outproj_program(NQ_CORE, 16))
            maps = []
            for c in cores:
                s = c // 2
                full = np.concatenate([np.asarray(yz[2 * s]), np.asarray(yz[2 * s + 1])], axis=0)
                maps.append({"srcT": np.ascontiguousarray(full[:, :, tok[c]]), "w": f32(ssm_w_out[j_]), "xq": x_own[c],
                             "g": f32(ln_g[i, 1]), "b": f32(ln_b[i, 1])})
            x_own = run(nc2, maps, "x_out")
        x_own = ffn(x_own, ffn2_w_in[i], ffn2_w_out[i], ln_g[i, 2], ln_b[i, 2])
    out = np.stack(full_seq(x_own)).astype(np.float32)
    return out


def build_full_program():
    nc = bass.Bass("TRN2", target_bir_lowering=False)
    inp = lambda n, shp: nc.dram_tensor(n, shp, F32, kind="ExternalInput").ap()
    x = inp("x", [SEQ, D])
    qpos = inp("qpos", [128, SEQ // 128])
    ln_g, ln_b = inp("ln_g", [DEPTH, 3, D]), inp("ln_b", [DEPTH, 3, D])
    f1i, f1o = inp("ffn1_w_in", [DEPTH, D, 2 * DFF]), inp("ffn1_w_out", [DEPTH, DFF, D])
    f2i, f2o = inp("ffn2_w_in", [DEPTH, D, 2 * DFF]), inp("ffn2_w_out", [DEPTH, DFF, D])
    a_in, a_kv = inp("attn_w_in", [2, D, 1864]), inp("attn_kv_norm", [2, 256])
    a_uk, a_uv, a_o = inp("attn_w_uk", [2, 16, 64, 256]), inp("attn_w_uv", [2, 16, 256, 64]), inp("attn_w_out", [2, D, D])
    s_in = inp("ssm_w_in", [2, D, 5152])
    s_cw, s_cb = inp("ssm_cw", [2, 2, 128, 12, 4]), inp("ssm_cb", [2, 2, 128, 12])
    s_dtb, s_alog, s_d = inp("ssm_dt_bias", [2, 32]), inp("ssm_a_log", [2, 32]), inp("ssm_d", [2, 32])
    s_ng, s_o = inp("ssm_norm_g", [2, 2048]), inp("ssm_w_out", [2, 2048, D])
    out = nc.dram_tensor("out", [SEQ, D], F32, kind="ExternalOutput").ap()
    bufs = [nc.dram_tensor("xbuf%d" % i, [SEQ, D], F32, kind="Internal").ap() for i in range(3)]
    ascr = attn_scratch(nc, SEQ, SEQ)
    sscr = ssd_scratch(nc, SEQ)
    yzT = nc.dram_tensor("s_yzT", [16, 128, SEQ], BF16, kind="Internal").ap()
    with contextlib.ExitStack() as st:
        cx = Ctx(nc, st)
        S = cx.S
        setup_consts(cx)
        cur = x
        outs = None
        for i in range(DEPTH):
            j_ = i // 2
            b0, b1, b2 = bufs[0], bufs[1], bufs[2]
            emit_ffn(cx, cur, b1, f1i[i], f1o[i], ln_g[i, 0], ln_b[i, 0], SEQ)
            S.barrier()
            if i % 2 == 0:
                emit_attn_stage(cx, b1, b1, qpos, a_in[j_], a_kv[j_], a_uk[j_], a_uv[j_], a_o[j_], ln_g[i, 1], ln_b[i, 1], b2,
                                SEQ, SEQ, ascr, natural=True)
            else:
                w = s_in[j_]
                for jj in range(2):
                    wcols = [w[:, jj * 1024:(jj + 1) * 1024], w[:, 2048 + jj * 1024:2048 + (jj + 1) * 1024],
                             w[:, 4096 + jj * 256:4096 + (jj + 1) * 256], w[:, 4608 + jj * 256:4608 + (jj + 1) * 256],
                             w[:, 5120 + jj * 16:5120 + (jj + 1) * 16]]
                    emit_ssd_stage(cx, b1, wcols, s_cw[j_, jj], s_cb[j_, jj], s_dtb[j_, jj * 16:(jj + 1) * 16],
                                   s_alog[j_, jj * 16:(jj + 1) * 16], s_d[j_, jj * 16:(jj + 1) * 16],
                                   s_ng[j_, jj * 1024:(jj + 1) * 1024], yzT[jj * 8:(jj + 1) * 8], SEQ, sscr)
                    S.barrier()
                emit_outproj_ln(cx, yzT, 16, s_o[j_], b1, b2, ln_g[i, 1], ln_b[i, 1], SEQ, 1.0)
            S.barrier()
            last = i == DEPTH - 1
            dst = out if last else b0
            outs = emit_ffn(cx, b2, dst, f2i[i], f2o[i], ln_g[i, 2], ln_b[i, 2], SEQ)
            if not last:
                S.barrier()
            cur = b0
        print("ops per engine:", {e: len(S.ops[e]) for e in ENGS}, flush=True)
        S.emit(final_wait_ops=outs[-N_DMA_SEM:])
        print("max semval per segment:", S.max_semval, "-> per-sem max", {e: v // N_CSEM + CSEM_CH for e, v in S.max_semval.items()},
              "max dma sem value:", S.max_dval, flush=True)
    return nc


def _ssd_conv_layout(conv_w, conv_b):
    cw = np.zeros((2, 2, 128, 12, 4), np.float32)
    cb = np.zeros((2, 2, 128, 12), np.float32)
    for l in range(2):
        for j in range(2):
            ch = np.concatenate([np.arange(j * 1024, (j + 1) * 1024), 2048 + np.arange(j * 256, (j + 1) * 256),
                                 2560 + np.arange(j * 256, (j + 1) * 256)])
            cw[l, j] = conv_w[l][:, ch].T.reshape(12, 128, 4).transpose(1, 0, 2)
            cb[l, j] = conv_b[l][ch].reshape(12, 128).T
    return cw, cb


def kernel_fused(x, ln_g, ln_b, ffn1_w_in, ffn1_w_out, ffn2_w_in, ffn2_w_out, attn_w_in, attn_kv_norm, attn_w_uk, attn_w_uv,
                 attn_w_out, ssm_w_in, ssm_conv_w, ssm_conv_b, ssm_dt_bias, ssm_a_log, ssm_d, ssm_norm_g, ssm_w_out):
    f32 = lambda a: np.ascontiguousarray(np.asarray(a, dtype=np.float32))
    x = f32(x)
    cw, cb = _ssd_conv_layout(f32(ssm_conv_w), f32(ssm_conv_b))
    qpos = np.ascontiguousarray(np.arange(SEQ, dtype=np.float32).reshape(-1, 128).T)
    shared = {"qpos": qpos, "ln_g": f32(ln_g), "ln_b": f32(ln_b), "ffn1_w_in": f32(ffn1_w_in), "ffn1_w_out": f32(ffn1_w_out),
              "ffn2_w_in": f32(ffn2_w_in), "ffn2_w_out": f32(ffn2_w_out), "attn_w_in": f32(attn_w_in),
              "attn_kv_norm": f32(attn_kv_norm), "attn_w_uk": f32(attn_w_uk), "attn_w_uv": f32(attn_w_uv),
              "attn_w_out": f32(attn_w_out), "ssm_w_in": f32(ssm_w_in), "ssm_cw": cw, "ssm_cb": cb,
              "ssm_dt_bias": f32(ssm_dt_bias), "ssm_a_log": f32(ssm_a_log), "ssm_d": f32(ssm_d), "ssm_norm_g": f32(ssm_norm_g),
              "ssm_w_out": f32(ssm_w_out)}
    nc = _prog("full", build_full_program)
    in_maps = [dict(shared, x=np.ascontiguousarray(x[c % BATCH])) for c in range(NCORES)]
    res = run_bass_kernel_spmd(nc, in_maps, core_ids=list(range(NCORES)))
    return np.stack([np.asarray(res.results[c]["out"], dtype=np.float32) for c in range(BATCH)])


kernel = kernel_fused
```

```python
import contextlib
import numpy as np
import concourse.bass as bass
import concourse.mybir as mybir
from concourse.bass_utils import run_bass_kernel_spmd

F32 = mybir.dt.float32
BF16 = mybir.dt.bfloat16
AF = mybir.ActivationFunctionType
ALU = mybir.AluOpType
AX = mybir.AxisListType

D = 1024
DFF = 2816
DEPTH = 4
ALPHA = (2.0 * DEPTH) ** 0.25
LN_EPS = 1e-5
NCORES = 8

ENGS = ("pe", "dve", "act", "pool", "sp")
N_DMA_SEM = 20
N_CSEM = 14
CSEM_CH = 128


class Op:
    __slots__ = ("eng", "fn", "deps", "is_dma", "has_dep", "semval", "dsem", "dval", "prev_ring")

    def __init__(self, eng, fn, is_dma):
        self.eng = eng
        self.fn = fn
        self.is_dma = is_dma
        self.deps = []
        self.has_dep = False
        self.semval = None
        self.dsem = None
        self.dval = None
        self.prev_ring = None


def _key(k):
    if isinstance(k, tuple):
        return tuple(_key(e) for e in k)
    if isinstance(k, (str, int)):
        return k
    return k.name


class Sched:
    def __init__(self, nc):
        self.nc = nc
        self.ops = {e: [] for e in ENGS}
        self.last_w = {}
        self.readers = {}
        self.dma_count = {e: 0 for e in ENGS}
        self.dma_hist = {e: [] for e in ENGS}
        self.n = 0
        self.n_barriers = 0

    def _add(self, eng, fn, reads, writes, is_dma):
        op = Op(eng, fn, is_dma)
        reads = [_key(r) for r in reads]
        writes = [_key(w) for w in writes]
        deps = []
        for r in reads:
            w = self.last_w.get(r)
            if w is not None:
                deps.append(w)
        for w_ in writes:
            w = self.last_w.get(w_)
            if w is not None:
                deps.append(w)
            rl = self.readers.get(w_, ())
            lastc = {}
            for o in rl:
                if o.is_dma:
                    deps.append(o)
                else:
                    lastc[o.eng] = o
            deps.extend(lastc.values())
        seen = set()
        for d in deps:
            if id(d) in seen:
                continue
            seen.add(id(d))
            if (not d.is_dma) and (not is_dma) and d.eng == "pe" and eng == "pe":
                continue
            op.deps.append(d)
            d.has_dep = True
        for r in reads:
            self.readers.setdefault(r, []).append(op)
        for w_ in writes:
            self.last_w[w_] = op
            self.readers[w_] = []
        if is_dma:
            j = self.dma_count[eng]
            self.dma_count[eng] += 1
            op.dsem = j % N_DMA_SEM
            op.dval = 16 * (j // N_DMA_SEM + 1)
            hist = self.dma_hist[eng]
            if j >= N_DMA_SEM:
                op.prev_ring = hist[j - N_DMA_SEM]
            hist.append(op)
        self.ops[eng].append(op)
        self.n += 1
        return op

    def op(self, eng, fn, reads=(), writes=()):
        return self._add(eng, fn, reads, writes, False)

    def dma(self, eng, fn, reads=(), writes=()):
        return self._add(eng, fn, reads, writes, True)

    def barrier(self):
        lasts = []
        for e in ENGS:
            for o in reversed(self.ops[e]):
                if o.fn is None:
                    break
                if not o.is_dma:
                    lasts.append(o)
                    break
            lasts.extend(self.dma_hist[e][-N_DMA_SEM:])
        self.n_barriers += 1
        for e in ENGS:
            op = Op(e, None, False)
            op.semval = self.n_barriers
            for d in lasts:
                op.deps.append(d)
                d.has_dep = True
            self.ops[e].append(op)
        self.last_w = {}
        self.readers = {}

    def emit(self, final_wait_ops=()):
        nc = self.nc
        for o in final_wait_ops:
            o.has_dep = True
        for e in ENGS:
            c = 0
            for o in self.ops[e]:
                if o.fn is None:
                    c = 0
                    continue
                if o.is_dma:
                    continue
                if o.has_dep:
                    c += 1
                    o.semval = c
        self.max_semval = {e: max([o.semval or 0 for o in self.ops[e] if o.fn is not None and not o.is_dma] + [0]) for e in ENGS}
        self.max_dval = {e: max([o.dval or 0 for o in self.ops[e] if o.is_dma] + [0]) for e in ENGS}
        with contextlib.ExitStack() as st:
            csem = {e: [st.enter_context(nc.semaphore("cs_%s_%d" % (e, i))) for i in range(N_CSEM)]
                    for e in ENGS if e != "sp"}
            dsem = {e: [st.enter_context(nc.semaphore("ds_%s_%d" % (e, i))) for i in range(N_DMA_SEM)]
                    for e in ENGS if any(o.is_dma for o in self.ops[e])}
            bsem = [st.enter_context(nc.semaphore("bar%d" % i)) for i in range(2)]
            any_dma = {e: any(o.is_dma for o in self.ops[e]) for e in ENGS}
            block = st.enter_context(nc.Block())

            def run(e, eng):
                waited = {}

                def wait_for(d):
                    if d.is_dma:
                        k = ("d", d.eng, d.dsem)
                        s, v = dsem[d.eng][d.dsem], d.dval
                        if waited.get(k, 0) >= v:
                            return
                        waited[k] = v
                        eng.wait_ge(s, v)
                    else:
                        k = ("c", d.eng)
                        c = d.semval
                        if waited.get(k, 0) >= c:
                            return
                        waited[k] = c
                        ep = (c - 1) // CSEM_CH
                        eng.wait_ge(csem[d.eng][ep % N_CSEM], CSEM_CH * (ep // N_CSEM) + ((c - 1) % CSEM_CH) + 1)

                for o in self.ops[e]:
                    for d in o.deps:
                        wait_for(d)
                    if o.prev_ring is not None:
                        wait_for(o.prev_ring)
                    if o.fn is None:
                        k = o.semval
                        eng.sem_inc(bsem[0], 1)
                        eng.wait_ge(bsem[0], len(ENGS) * k)
                        if e in csem:
                            for s_ in csem[e]:
                                eng.sem_clear(s_)
                        eng.sem_inc(bsem[1], 1)
                        eng.wait_ge(bsem[1], len(ENGS) * k)
                        for k_ in [k_ for k_ in waited if k_[0] == "c"]:
                            del waited[k_]
                        continue
                    ins = o.fn(eng)
                    if o.is_dma:
                        ins.then_inc(dsem[e][o.dsem], 16)
                    elif o.has_dep:
                        ins.then_inc(csem[e][((o.semval - 1) // CSEM_CH) % N_CSEM], 1)
                if e == "pool":
                    for d in final_wait_ops:
                        wait_for(d)

            for e, reg in (("sp", block.sync), ("pe", block.tensor), ("dve", block.vector),
                           ("act", block.scalar), ("pool", block.gpsimd)):
                reg(lambda eng, e=e: run(e, eng))


class Ctx:
    def __init__(self, nc, st):
        self.nc = nc
        self.S = Sched(nc)
        self.st = st
        self.uid = 0
        self.psum = []
        self.psum_i = 0

    def sb(self, shape, dtype, name=None, st=None):
        self.uid += 1
        nm = "%s_%d" % (name or "t", self.uid)
        return (st or self.st).enter_context(self.nc.sbuf_tensor(nm, list(shape), dtype))

    def ps(self, shape, dtype, name=None, st=None):
        self.uid += 1
        nm = "%s_%d" % (name or "p", self.uid)
        return (st or self.st).enter_context(self.nc.psum_tensor(nm, list(shape), dtype))


def setup_consts(cx):
    nc, S = cx.nc, cx.S
    cx.ident_f = cx.sb([128, 128], F32, "identf")
    cx.ident_b = cx.sb([128, 128], BF16, "identb")
    cx.ones_f = cx.sb([128, 128], F32, "onesf")
    idf, idb, onf = cx.ident_f, cx.ident_b, cx.ones_f
    S.op("pool", lambda e: e.memset(onf[:], 1.0), writes=[onf])
    S.op("pool", lambda e: e.affine_select(out=idf[:], in_=onf[:], pattern=[[-1, 128]],
                                           compare_op=ALU.is_equal, fill=0.0, base=0,
                                           channel_multiplier=1), reads=[onf], writes=[idf])
    S.op("pool", lambda e: e.tensor_copy(out=idb[:], in_=idf[:]), reads=[idf], writes=[idb])


def emit_ffn(cx, x_in, x_out, w_in, w_out, g, b, ntok, scale_y=0.5):
    nc, S = cx.nc, cx.S
    T = 256
    NS = T // 128
    KD = D // 128
    KF = DFF // 128
    c_y = scale_y / ALPHA
    eps_p = LN_EPS / (ALPHA * ALPHA)
    with contextlib.ExitStack() as st:
        win = cx.sb([128, KD, 2 * DFF], BF16, "win", st)
        wout = cx.sb([128, KF, D], BF16, "wout", st)
        stg = [cx.sb([128, 1408], F32, "stg", st) for _ in range(2)]
        gb = cx.sb([128, D], F32, "gb", st)
        bb = cx.sb([128, D], F32, "bb", st)
        epst = cx.sb([128, 1], F32, "eps", st)
        xs = [cx.sb([128, D], F32, "xs", st) for _ in range(4)]
        xb = [cx.sb([128, D], BF16, "xb", st) for _ in range(2)]
        xT = [cx.sb([128, KD, T], BF16, "xT", st) for _ in range(2)]
        hT = [cx.sb([128, KF, T], BF16, "hT", st) for _ in range(1)]
        sg = [cx.sb([128, T], F32, "sg", st) for _ in range(2)]
        r = [cx.sb([128, D], F32, "r", st) for _ in range(1)]
        xo = [cx.sb([128, D], F32, "xo", st) for _ in range(1)]
        stat = [cx.sb([128, 8], F32, "stat", st) for _ in range(2)]
        p_tp = cx.ps([128, 1024], BF16, "ptp", st)
        p_g = [cx.ps([128, 512], F32, "pg", st) for _ in range(2)]
        p_u = [cx.ps([128, 512], F32, "pu", st) for _ in range(2)]
        p_o = [cx.ps([128, 512], F32, "po", st) for _ in range(2)]

        S.op("pool", lambda e: e.memset(epst[:], eps_p), writes=[epst])
        S.dma("sp", lambda e: e.dma_start(out=gb[:], in_=g.partition_broadcast(128)), writes=[gb])
        S.dma("sp", lambda e: e.dma_start(out=bb[:], in_=b.partition_broadcast(128)), writes=[bb])

        w_in_v = w_in.rearrange("(k p) f -> p k f", p=128)
        w_out_v = w_out.rearrange("(k p) d -> p k d", p=128)
        cast_engs = ["dve", "act", "pool"]
        ci = 0
        pieces = []
        for k in range(KD):
            for hf in range(4):
                pieces.append((w_in_v[:, k, hf * 1408:(hf + 1) * 1408], win[:, k, hf * 1408:(hf + 1) * 1408],
                               (win, k), 1408))
        for k0 in range(KF):
            pieces.append((w_out_v[:, k0, :], wout[:, k0, :], (wout, k0), D))
        for i, (src, dst, key, n) in enumerate(pieces):
            sb_ = stg[i % 2]
            if len(src.shape) == 3:
                sview = sb_[:, 0:n].rearrange("p (a b) -> p a b", a=src.shape[1])
            else:
                sview = sb_[:, 0:n]
            S.dma("sp", lambda e, sview=sview, src=src: e.dma_start(out=sview, in_=src), writes=[sb_])
            ce = cast_engs[ci % 3]
            ci += 1
            if ce == "act":
                S.op("act", lambda e, dst=dst, sview=sview: e.copy(out=dst, in_=sview), reads=[sb_], writes=[key])
            else:
                S.op(ce, lambda e, dst=dst, sview=sview: e.tensor_copy(out=dst, in_=sview), reads=[sb_], writes=[key])
        win_keys = [(win, k) for k in range(KD)]
        wout_keys = [(wout, k) for k in range(KF // 2)]

        ntiles = ntok // T
        x_in_v = x_in.rearrange("(n p) d -> n p d", p=128)
        x_out_v = x_out.rearrange("(n p) d -> n p d", p=128)
        last_out = []

        def load_tile(ti):
            xTt = xT[ti % 2]
            for s in range(NS):
                xt = xs[(ti * NS + s) % 4]
                xbt = xb[s]
                S.dma("sp", lambda e, xt=xt, i=ti * NS + s: e.dma_start(out=xt[:], in_=x_in_v[i]), writes=[xt])
                S.op("pool", lambda e, xbt=xbt, xt=xt: e.tensor_copy(out=xbt[:], in_=xt[:]), reads=[xt], writes=[xbt])
                for k in range(KD):
                    S.op("pe", lambda e, k=k, xbt=xbt: e.transpose(out=p_tp[:, k * 128:(k + 1) * 128],
                                                                  in_=xbt[:, k * 128:(k + 1) * 128],
                                                                  identity=cx.ident_b[:]),
                         reads=[xbt, cx.ident_b], writes=[p_tp])
                S.op("dve", lambda e, xTt=xTt, s=s: e.tensor_copy(
                    out=xTt[:, :, s * 128:(s + 1) * 128],
                    in_=p_tp[:].rearrange("p (k t) -> p k t", k=KD)), reads=[p_tp], writes=[xTt])

        load_tile(0)
        for ti in range(ntiles):
            xTt = xT[ti % 2]
            hTt = hT[0]
            for j in range(KF):
                pg, pu, sgt = p_g[j % 2], p_u[j % 2], sg[j % 2]
                for k in range(KD):
                    S.op("pe", lambda e, k=k, j=j, pg=pg, xTt=xTt: e.matmul(pg[:, 0:T], lhsT=win[:, k, j * 128:(j + 1) * 128],
                                                                   rhs=xTt[:, k, :], start=(k == 0), stop=(k == KD - 1)),
                         reads=[(win, k), xTt], writes=[pg])
                for k in range(KD):
                    S.op("pe", lambda e, k=k, j=j, pu=pu, xTt=xTt: e.matmul(pu[:, 0:T],
                                                                   lhsT=win[:, k, DFF + j * 128:DFF + (j + 1) * 128],
                                                                   rhs=xTt[:, k, :], start=(k == 0), stop=(k == KD - 1)),
                         reads=[(win, k), xTt], writes=[pu])
                S.op("act", lambda e, pg=pg, sgt=sgt: e.activation(out=sgt[:], in_=pg[:, 0:T], func=AF.Silu),
                     reads=[pg], writes=[sgt])
                S.op("dve", lambda e, pu=pu, sgt=sgt, j=j, hTt=hTt: e.tensor_tensor(out=hTt[:, j, :], in0=pu[:, 0:T], in1=sgt[:],
                                                                            op=ALU.mult),
                     reads=[pu, sgt], writes=[(hTt, j)])
            if ti + 1 < ntiles:
                load_tile(ti + 1)
            for s in range(NS):
                xt = xs[(ti * NS + s) % 4]
                rt, xot, stt = r[0], xo[0], stat[s]
                for hd in range(2):
                    po = p_o[hd]
                    for j in range(KF):
                        S.op("pe", lambda e, j=j, hd=hd, s=s, po=po, hTt=hTt: e.matmul(
                            po[:], lhsT=hTt[:, j, s * 128:(s + 1) * 128], rhs=wout[:, j, hd * 512:(hd + 1) * 512],
                            start=(j == 0), stop=(j == KF - 1)),
                             reads=[(hTt, j), (wout, j)], writes=[po])
                    S.op("dve", lambda e, po=po, hd=hd, rt=rt, xt=xt: e.scalar_tensor_tensor(
                        out=rt[:, hd * 512:(hd + 1) * 512], in0=po[:], scalar=c_y, in1=xt[:, hd * 512:(hd + 1) * 512],
                        op0=ALU.mult, op1=ALU.add), reads=[po, xt], writes=[(rt, hd)])
                emit_ln(cx, rt, [(rt, 0), (rt, 1)], xot, stt, epst, gb, bb, xot)
                o = S.dma("pool", lambda e, xot=xot, i=ti * NS + s: e.dma_start(out=x_out_v[i], in_=xot[:]),
                          reads=[xot], writes=[("dram_xout", ti * NS + s)])
                last_out.append(o)
        return last_out


def emit_ln(cx, rt, rkeys, junk, stt, epst, gb, bb, xot):
    S = cx.S
    S.op("act", lambda e: e.activation(out=junk[:], in_=rt[:], func=AF.Square, accum_out=stt[:, 0:1]),
         reads=rkeys, writes=[junk, (stt, 0)])
    S.op("dve", lambda e: e.tensor_reduce(out=stt[:, 1:2], in_=rt[:], axis=AX.X, op=ALU.add),
         reads=rkeys, writes=[(stt, 1)])
    S.op("dve", lambda e: e.tensor_scalar(out=stt[:, 2:3], in0=stt[:, 1:2], scalar1=1.0 / D, scalar2=None, op0=ALU.mult),
         reads=[(stt, 1)], writes=[(stt, 2)])
    S.op("dve", lambda e: e.tensor_tensor(out=stt[:, 3:4], in0=stt[:, 2:3], in1=stt[:, 2:3], op=ALU.mult),
         reads=[(stt, 2)], writes=[(stt, 3)])
    S.op("dve", lambda e: e.scalar_tensor_tensor(out=stt[:, 4:5], in0=stt[:, 0:1], scalar=1.0 / D, in1=stt[:, 3:4],
                                                 op0=ALU.mult, op1=ALU.subtract),
         reads=[(stt, 0), (stt, 3)], writes=[(stt, 4)])
    S.op("act", lambda e: e.activation(out=stt[:, 5:6], in_=stt[:, 4:5], func=AF.Sqrt, bias=epst[:]),
         reads=[(stt, 4), epst], writes=[(stt, 5)])
    S.op("dve", lambda e: e.reciprocal(out=stt[:, 6:7], in_=stt[:, 5:6]), reads=[(stt, 5)], writes=[(stt, 6)])
    S.op("dve", lambda e: e.tensor_scalar(out=xot[:], in0=rt[:], scalar1=stt[:, 2:3], scalar2=stt[:, 6:7],
                                          op0=ALU.subtract, op1=ALU.mult),
         reads=rkeys + [(stt, 2), (stt, 6)], writes=[xot])
    S.op("pool", lambda e: e.tensor_tensor(out=xot[:], in0=xot[:], in1=gb[:], op=ALU.mult), reads=[xot, gb], writes=[xot])
    S.op("pool", lambda e: e.tensor_tensor(out=xot[:], in0=xot[:], in1=bb[:], op=ALU.add), reads=[xot, bb], writes=[xot])


def build_ffn_program(ntok):
    nc = bass.Bass("TRN2", target_bir_lowering=False)
    x_in = nc.dram_tensor("x_in", [ntok, D], F32, kind="ExternalInput").ap()
    w_in = nc.dram_tensor("w_in", [D, 2 * DFF], F32, kind="ExternalInput").ap()
    w_out = nc.dram_tensor("w_out", [DFF, D], F32, kind="ExternalInput").ap()
    g = nc.dram_tensor("g", [D], F32, kind="ExternalInput").ap()
    b = nc.dram_tensor("b", [D], F32, kind="ExternalInput").ap()
    x_out = nc.dram_tensor("x_out", [ntok, D], F32, kind="ExternalOutput").ap()
    with contextlib.ExitStack() as st:
        cx = Ctx(nc, st)
        setup_consts(cx)
        outs = emit_ffn(cx, x_in, x_out, w_in, w_out, g, b, ntok)
        cx.S.emit(final_wait_ops=outs[-N_DMA_SEM:])
    return nc


def mm(cx, out, lhsT, rhs, start, stop, reads, writes):
    return cx.S.op("pe", lambda e: e.matmul(out, lhsT=lhsT, rhs=rhs, start=start, stop=stop), reads, writes)


def tr(cx, out, in_, ident, reads, writes):
    return cx.S.op("pe", lambda e: e.transpose(out=out, in_=in_, identity=ident), reads, writes)


def act(cx, out, in_, func, reads, writes, **kw):
    return cx.S.op("act", lambda e: e.activation(out=out, in_=in_, func=func, **kw), reads, writes)


def tt(cx, eng, out, in0, in1, op, reads, writes):
    return cx.S.op(eng, lambda e: e.tensor_tensor(out=out, in0=in0, in1=in1, op=op), reads, writes)


def ts(cx, eng, out, in0, s1, s2, op0, op1, reads, writes, accum_out=None):
    if op1 is None:
        return cx.S.op(eng, lambda e: e.tensor_scalar(out=out, in0=in0, scalar1=s1, scalar2=None, op0=op0,
                                                      accum_out=accum_out), reads, writes)
    return cx.S.op(eng, lambda e: e.tensor_scalar(out=out, in0=in0, scalar1=s1, scalar2=s2, op0=op0, op1=op1,
                                                  accum_out=accum_out), reads, writes)


def stt(cx, eng, out, in0, scalar, in1, op0, op1, reads, writes):
    return cx.S.op(eng, lambda e: e.scalar_tensor_tensor(out=out, in0=in0, scalar=scalar, in1=in1, op0=op0, op1=op1),
                   reads, writes)


def cp(cx, eng, out, in_, reads, writes):
    if eng == "act":
        return cx.S.op("act", lambda e: e.copy(out=out, in_=in_), reads, writes)
    return cx.S.op(eng, lambda e: e.tensor_copy(out=out, in_=in_), reads, writes)


def dma(cx, q, out, in_, reads, writes):
    return cx.S.dma(q, lambda e: e.dma_start(out=out, in_=in_), reads, writes)


def memset(cx, eng, ap, val, writes):
    return cx.S.op(eng, lambda e: e.memset(ap, val), (), writes)


def load_weight_bf16(cx, dst, src, stg, n, key, idx):
    sb_ = stg[idx % len(stg)]
    if len(src.shape) == 3:
        sview = sb_[:, 0:n].rearrange("p (a b) -> p a b", a=src.shape[1])
    else:
        sview = sb_[:, 0:n]
    dma(cx, "sp", sview, src, [], [sb_])
    ce = ("dve", "act", "pool")[idx % 3]
    cp(cx, ce, dst, sview, [sb_], [key])


def emit_xT(cx, x_rows, xs, xb, p_tp, xT, col0):
    dma(cx, "sp", xs[:], x_rows, [], [xs])
    cp(cx, "pool", xb[:], xs[:], [xs], [xb])
    for k in range(D // 128):
        tr(cx, p_tp[:, k * 128:(k + 1) * 128], xb[:, k * 128:(k + 1) * 128], cx.ident_b[:], [xb, cx.ident_b], [p_tp])
    cp(cx, "dve", xT[:, :, col0:col0 + 128], p_tp[:].rearrange("p (k t) -> p k t", k=D // 128), [p_tp], [xT])


def emit_proj(cx, x_rows_fn, ntok, wcols, fm_specs, tm_specs):
    TT = 512
    KD = D // 128
    NW = sum(w.shape[1] for w in wcols)
    with contextlib.ExitStack() as st:
        wsb = cx.sb([128, KD, NW], BF16, "pw", st)
        stg = [cx.sb([128, 1024], F32, "pstg", st) for _ in range(2)]
        xs = [cx.sb([128, D], F32, "pxs", st) for _ in range(2)]
        xb = [cx.sb([128, D], BF16, "pxb", st) for _ in range(2)]
        xT = [cx.sb([128, KD, TT], BF16, "pxT", st) for _ in range(2)]
        ost = [cx.sb([128, TT], F32, "post", st) for _ in range(3)]
        p_tp = cx.ps([128, 1024], BF16, "pptp", st)
        p_fm = [cx.ps([128, 512], F32, "ppfm", st) for _ in range(2)]
        p_tm = [cx.ps([128, 512], F32, "pptm", st) for _ in range(2)]
        tm_state = {}
        idx = 0
        off = 0
        for w in wcols:
            n = w.shape[1]
            wv = w.rearrange("(k p) f -> p k f", p=128)
            for k in range(KD):
                for c0 in range(0, n, 1024):
                    c1 = min(n, c0 + 1024)
                    load_weight_bf16(cx, wsb[:, k, off + c0:off + c1], wv[:, k, c0:c1], stg, c1 - c0, wsb, idx)
                    idx += 1
            off += n
        ntiles = ntok // TT
        oi = 0
        fi = 0
        ti_ = 0
        for t in range(ntiles):
            xTt = xT[t % 2]
            for s in range(TT // 128):
                emit_xT(cx, x_rows_fn(t * 4 + s), xs[s % 2], xb[s % 2], p_tp, xTt, s * 128)
            for (coff, M, dt_, dest_fn) in fm_specs:
                pf = p_fm[fi % 2]
                fi += 1
                for k in range(KD):
                    mm(cx, pf[0:M, :], wsb[:, k, coff:coff + M], xTt[:, k, :], k == 0, k == KD - 1, [wsb, xTt], [pf])
                o = ost[oi % 3]
                oi += 1
                ov = o[0:M, :] if dt_ == F32 else o[:].bitcast(BF16)[0:M, 0:TT]
                cp(cx, "act" if oi % 2 else "dve", ov, pf[0:M, :], [pf], [o])
                dma(cx, "pool", dest_fn(t), ov, [o], [])
            for (coff, N, post_fn) in tm_specs:
                for s in range(TT // 128):
                    pt = p_tm[ti_ % 2]
                    ti_ += 1
                    for k in range(KD):
                        mm(cx, pt[:, 0:N], xTt[:, k, s * 128:(s + 1) * 128], wsb[:, k, coff:coff + N], k == 0, k == KD - 1,
                           [wsb, xTt], [pt])
                    post_fn(cx, st, tm_state, pt, t * 4 + s)


def emit_outproj_ln(cx, srcT, nch, w, x_in, x_out, g, b, ntok, scale_y):
    c_y = scale_y / ALPHA
    eps_p = LN_EPS / (ALPHA * ALPHA)
    with contextlib.ExitStack() as st:
        wsb = cx.sb([128, nch, D], BF16, "ow", st)
        stg = [cx.sb([128, 1024], F32, "ostg", st) for _ in range(2)]
        gb = cx.sb([128, D], F32, "ogb", st)
        bb = cx.sb([128, D], F32, "obb", st)
        epst = cx.sb([128, 1], F32, "oeps", st)
        oT = [cx.sb([128, nch, 128], BF16, "ooT", st) for _ in range(2)]
        xs = [cx.sb([128, D], F32, "oxs", st) for _ in range(2)]
        r = cx.sb([128, D], F32, "or", st)
        xo = [cx.sb([128, D], F32, "oxo", st) for _ in range(2)]
        stat = [cx.sb([128, 8], F32, "ostat", st) for _ in range(2)]
        p_o = [cx.ps([128, 512], F32, "opo", st) for _ in range(4)]
        memset(cx, "pool", epst[:], eps_p, [epst])
        dma(cx, "sp", gb[:], g.partition_broadcast(128), [], [gb])
        dma(cx, "sp", bb[:], b.partition_broadcast(128), [], [bb])
        wv = w.rearrange("(k p) d -> p k d", p=128)
        for k in range(nch):
            load_weight_bf16(cx, wsb[:, k, :], wv[:, k, :], stg, D, wsb, k)
        sv = srcT.rearrange("c p t -> p c t")
        xiv = x_in.rearrange("(n p) d -> n p d", p=128)
        xov = x_out.rearrange("(n p) d -> n p d", p=128)
        outs = []
        for i in range(ntok // 128):
            oTt, xt, xot, stt_ = oT[i % 2], xs[i % 2], xo[i % 2], stat[i % 2]
            dma(cx, "sp", oTt[:], sv[:, :, i * 128:(i + 1) * 128], [], [oTt])
            dma(cx, "sp", xt[:], xiv[i], [], [xt])
            for hd in range(2):
                po = p_o[(2 * i + hd) % 4]
                for c in range(nch):
                    mm(cx, po[:], oTt[:, c, :], wsb[:, c, hd * 512:(hd + 1) * 512], c == 0, c == nch - 1, [oTt, wsb], [po])
                stt(cx, "dve", r[:, hd * 512:(hd + 1) * 512], po[:], c_y, xt[:, hd * 512:(hd + 1) * 512], ALU.mult, ALU.add,
                    [po, xt], [(r, hd)])
            emit_ln(cx, r, [(r, 0), (r, 1)], xot, stt_, epst, gb, bb, xot)
            outs.append(dma(cx, "pool", xov[i], xot[:], [xot], []))
        return outs


TOPK = 256
NBIS = 18


def emit_attn_core(cx, ckvT_d, kidxT_d, qT_d, qiT_d, widx_d, qpos_d, w_uk, w_uv, oT_d, NQ, SK, natural=False):
    S = cx.S
    NT = NQ // 256
    nkb_max = SK // 128
    with contextlib.ExitStack() as st:
        ckvT = cx.sb([128, 2, SK], BF16, "ackv", st)
        kidxT = cx.sb([64, SK], BF16, "akidx", st)
        Vp = cx.sb([128, nkb_max, 128], BF16, "aV", st)
        score = cx.sb([128, SK], F32, "ascore", st)
        msk = cx.sb([128, SK], BF16, "amsk", st)
        inv = [[cx.sb([128, SK], mybir.dt.float8e4, "ainv", st) for _ in range(2)] for _ in range(2)]
        sel = [cx.sb([128, 512], mybir.dt.float8e4, "asel", st) for _ in range(2)]
        qabs = cx.sb([128, 8, 2, 512], BF16, "aqabs", st)
        qT = cx.sb([128, 8, 256], BF16, "aqT", st)
        qiT = cx.sb([64, 8, 256], BF16, "aqiT", st)
        oT = cx.sb([128, 8, 256], BF16, "aoT", st)
        pT = [cx.sb([128, 512], BF16, "apT", st) for _ in range(2)]
        rl = [cx.sb([128, 512], F32, "arl", st) for _ in range(2)]
        wuk = cx.sb([128, 8, 256], BF16, "awuk", st)
        wuv = cx.sb([128, 2, 16, 64], BF16, "awuv", st)
        stg = [cx.sb([128, 1024], F32, "astg", st) for _ in range(1)]
        iota_f = cx.sb([128, 512], F32, "aiof", st)
        pen = cx.sb([128, 512], F32, "apen", st)
        qpos = cx.sb([128, NQ // 128], F32, "aqpos", st)
        widx = [cx.sb([128, 8], F32, "awidx", st) for _ in range(2)]
        sm = cx.sb([128, 8], F32, "asm", st)
        rec = cx.sb([128, 256], F32, "arec", st)
        ones_b = cx.sb([128, 128], BF16, "aones", st)
        P = [cx.ps([128, 512], F32, "aP", st) for _ in range(8)]
        gen = [P[0], P[1]]
        gi = [0]

        def nextp():
            gi[0] += 1
            return gen[gi[0] % 2]

        memset(cx, "pool", ones_b[:], 1.0, [ones_b])
        for b_ in range(2):
            memset(cx, "pool", pen[:], 0.0, [pen])
            for h2 in range(2):
                c0_ = h2 * 256 + b_ * 128
                ts(cx, "dve", pen[:, c0_:c0_ + 128], cx.ident_f[:], -240.0, None, ALU.mult, None, [cx.ident_f, pen], [pen])
            S.op("dve", lambda e, b_=b_: e.tensor_copy(out=sel[b_][:], in_=pen[:], saturate=False), [pen], [sel[b_]])
        S.op("pool", lambda e: e.iota(pen[:].bitcast(mybir.dt.int32), pattern=[[1, 512]], base=0, channel_multiplier=0), (), [pen])
        cp(cx, "pool", iota_f[:], pen[:].bitcast(mybir.dt.int32), [pen], [iota_f])
        dma(cx, "sp", qpos[:], qpos_d, [], [qpos])
        for rc in range(2):
            dma(cx, "sp", ckvT[:, rc, :], ckvT_d[rc], [], [ckvT])
        dma(cx, "sp", kidxT[:], kidxT_d, [], [kidxT])
        wukv = w_uk.rearrange("(p h2) dh r -> (h2 dh) p r", h2=2)
        for hf in range(2):
            load_weight_bf16(cx, wuk[:, hf * 4:(hf + 1) * 4, :], wukv[:, hf * 4:(hf + 1) * 4, :], stg, 1024, wuk, hf)
        for rc in range(2):
            load_weight_bf16(cx, wuv[:, rc, :, :], w_uv[:, rc * 128:(rc + 1) * 128, :].rearrange("h r dv -> r h dv"),
                             stg, 1024, (wuv, rc), 1 + rc)
        qT_v = qT_d.rearrange("c p t -> p c t")
        qiT_v = qiT_d.rearrange("c p t -> p c t")
        oT_v = oT_d.rearrange("c p t -> p c t")
        widx_v = widx_d.rearrange("(n p) h -> n p h", p=128)

        FP8 = mybir.dt.float8e4
        side_p = [P[5], P[7]]
        si = [0]

        def nexts():
            si[0] += 1
            return side_p[si[0] % 2]

        def tile_dims(kq):
            nkc = (kq // 2 + 1) if natural else (kq + 1)
            return nkc, 4 * nkc, 512 * nkc

        def side_units(kq):
            nkc, nkb, L = tile_dims(kq)
            q0 = kq * 256
            U = []
            U.append(lambda: dma(cx, "sp", qiT[:], qiT_v[:, :, q0:q0 + 256], [], [qiT]))
            for b in range(2):
                wt = widx[b]
                U.append(lambda wt=wt, b=b: dma(cx, "sp", wt[:], widx_v[kq * 2 + b], [], [wt]))
                for kc in range(nkc):
                    for h in range(8):
                        def u(b=b, kc=kc, h=h, wt=wt):
                            sc = score[:, kc * 512:(kc + 1) * 512]
                            ps = nexts()
                            mm(cx, ps[:], qiT[:, h, b * 128:(b + 1) * 128], kidxT[:, kc * 512:(kc + 1) * 512], True, True,
                               [qiT, kidxT], [ps])
                            rt_ = rl[h % 2]
                            act(cx, rt_[:], ps[:], AF.Relu, [ps], [rt_])
                            if h == 0:
                                ts(cx, "dve", sc, rt_[:], wt[:, 0:1], None, ALU.mult, None, [rt_, wt], [(score, kc)])
                            else:
                                stt(cx, "dve", sc, rt_[:], wt[:, h:h + 1], sc, ALU.mult, ALU.add, [rt_, wt, (score, kc)], [(score, kc)])
                        U.append(u)
                skeys = [(score, kc) for kc in range(nkc)]

                def pen_u(b=b):
                    ts(cx, "dve", sm[:, 4:5], qpos[:, kq * 2 + b:kq * 2 + b + 1], float(-512 * (nkc - 1)), None, ALU.add, None,
                       [qpos], [(sm, 4)])
                    ts(cx, "dve", pen[:], iota_f[:], sm[:, 4:5], -30000.0, ALU.is_gt, ALU.mult, [iota_f, (sm, 4)], [pen])
                    lc = score[:, (nkc - 1) * 512:nkc * 512]
                    tt(cx, "dve", lc, lc, pen[:], ALU.add, [pen, (score, nkc - 1)], [(score, nkc - 1)])
                    memset(cx, "dve", sm[:, 0:1], 0.0, [(sm, 0)])
                U.append(pen_u)
                W = 128.0
                for it in range(NBIS):
                    def bis(W=W):
                        S.op("dve", lambda e: e.tensor_scalar(out=msk[:, 0:L], in0=score[:, 0:L], scalar1=sm[:, 0:1], scalar2=None,
                                                              op0=ALU.is_ge, op1=ALU.add, accum_out=sm[:, 1:2]),
                             skeys + [(sm, 0)], [msk, (sm, 1)])
                        ts(cx, "dve", sm[:, 2:3], sm[:, 1:2], TOPK - 0.5, W / 2, ALU.is_ge, ALU.mult, [(sm, 1)], [(sm, 2)])
                        stt(cx, "dve", sm[:, 0:1], sm[:, 0:1], -W / 4, sm[:, 2:3], ALU.add, ALU.add, [(sm, 0), (sm, 2)], [(sm, 0)])
                    U.append(bis)
                    W = W / 2

                def fin(W=W, b=b):
                    ts(cx, "dve", sm[:, 3:4], sm[:, 0:1], -W / 2, None, ALU.add, None, [(sm, 0)], [(sm, 3)])
                    iv = inv[kq % 2][b]
                    S.op("dve", lambda e: e.tensor_scalar(out=iv[:, 0:L], in0=score[:, 0:L], scalar1=sm[:, 3:4], scalar2=128.0,
                                                          op0=ALU.is_lt, op1=ALU.mult, saturate=False),
                         skeys + [(sm, 3)], [iv])
                U.append(fin)
            return U

        def attention(kq, side):
            nkc, nkb, L = tile_dims(kq)
            q0 = kq * 256
            n_iter = 8 * nkb
            rate = (len(side) + n_iter - 1) // n_iter if side else 0
            dma(cx, "sp", qT[:], qT_v[:, :, q0:q0 + 256], [], [qT])
            for h in range(16):
                p_, h2 = h // 2, h % 2
                pq = nextp()
                for rc in range(2):
                    mm(cx, pq[:, rc * 256:(rc + 1) * 256], wuk[h2 * 64:(h2 + 1) * 64, p_, rc * 128:(rc + 1) * 128],
                       qT[h2 * 64:(h2 + 1) * 64, p_, :], True, True, [wuk, qT], [pq])
                cp(cx, "act", qabs[:, p_, :, h2 * 256:(h2 + 1) * 256],
                   pq[:].rearrange("p (a b) -> p a b", a=2), [pq], [(qabs, h)])
            for p_ in range(8):
                for kb0 in range(0, nkb, 4):
                    pv = nextp()
                    for j in range(4):
                        for rc in range(2):
                            mm(cx, pv[:, j * 128:(j + 1) * 128], ckvT[:, rc, (kb0 + j) * 128:(kb0 + j + 1) * 128],
                               wuv[:, rc, 2 * p_:2 * p_ + 2, :].rearrange("p a b -> p (a b)"), rc == 0, rc == 1,
                               [ckvT, (wuv, rc)], [pv])
                    cp(cx, "act", Vp[:, kb0:kb0 + 4, :], pv[:].rearrange("p (a b) -> p a b", a=4), [pv], [Vp])

                def qk(kb):
                    pl = P[2 + kb % 2]
                    for rc in range(2):
                        mm(cx, pl[:], ckvT[:, rc, kb * 128:(kb + 1) * 128], qabs[:, p_, rc, :], rc == 0, False,
                           [ckvT, (qabs, 2 * p_), (qabs, 2 * p_ + 1)], [pl])
                    for b_ in range(2):
                        iv = inv[kq % 2][b_]
                        mm(cx, pl[:], iv[:, kb * 128:(kb + 1) * 128], sel[b_][:], False, b_ == 1, [iv, sel[b_]], [pl])

                qk(0)
                for kb in range(nkb):
                    pl = P[2 + kb % 2]
                    if kb + 1 < nkb:
                        qk(kb + 1)
                    pTt = pT[kb % 2]
                    act(cx, pTt[:], pl[:], AF.Exp, [pl], [pTt], scale=0.125)
                    first, last = kb == 0, kb == nkb - 1
                    mm(cx, P[4][:], Vp[:, kb, :], pTt[:], first, last, [Vp, pTt], [P[4]])
                    mm(cx, P[6][:], ones_b[:], pTt[:], first, last, [ones_b, pTt], [P[6]])
                    for _ in range(rate):
                        if side:
                            side.pop(0)()
                S.op("dve", lambda e: e.reciprocal(out=rec[0:64, :], in_=P[6][0:64, 0:256]), [P[6]], [(rec, 0)])
                S.op("dve", lambda e: e.reciprocal(out=rec[64:128, :], in_=P[6][64:128, 256:512]), [P[6]], [(rec, 1)])
                tt(cx, "dve", oT[0:64, p_, :], P[4][0:64, 0:256], rec[0:64, :], ALU.mult, [P[4], (rec, 0)], [(oT, p_, 0)])
                tt(cx, "dve", oT[64:128, p_, :], P[4][64:128, 256:512], rec[64:128, :], ALU.mult, [P[4], (rec, 1)], [(oT, p_, 1)])
            dma(cx, "pool", oT_v[:, :, q0:q0 + 256], oT[:], [(oT, p_, i) for p_ in range(8) for i in range(2)], [])
            while side:
                side.pop(0)()

        for u in side_units(0):
            u()
        for kq in range(NT):
            attention(kq, side_units(kq + 1) if kq + 1 < NT else [])


ATT_O1, ATT_O2, ATT_O3, ATT_O4 = 1024, 1280, 1792, 1856
RMS_EPS = 1e-6


def emit_attn_stage(cx, xq, xk, qpos_d, w_in, kv_g, w_uk, w_uv, w_o, g, b, x_out, NQ, SK, scr, natural=False):
    nc, S = cx.nc, cx.S
    ckvT_d, kidxT_d, qT_d, qiT_d, widx_d, oT_d = (scr[k] for k in ("ckvT", "kidxT", "qT", "qiT", "widx", "oT"))
    xk_v = xk.rearrange("(n p) d -> n p d", p=128)
    xq_v = xq.rearrange("(n p) d -> n p d", p=128)

    def post_ckv(cx, st, state, pt, blk):
        if "init" not in state:
            state["init"] = True
            state["gb"] = cx.sb([128, 256], F32, "kg", st)
            state["sq"] = cx.sb([128, 256], F32, "ksq", st)
            state["cn"] = [cx.sb([128, 256], BF16, "kcn", st) for _ in range(2)]
            state["cT"] = [cx.sb([128, 2, 128], BF16, "kcT", st) for _ in range(2)]
            state["sm"] = [cx.sb([128, 4], F32, "ksm", st) for _ in range(2)]
            state["eps"] = cx.sb([128, 1], F32, "keps", st)
            state["ptp"] = cx.ps([128, 512], BF16, "kptp", st)
            memset(cx, "pool", state["eps"][:], RMS_EPS, [state["eps"]])
            dma(cx, "sp", state["gb"][:], kv_g.partition_broadcast(128), [], [state["gb"]])
        gb_, sq, cn, cT, sm_, eps, ptp = (state["gb"], state["sq"], state["cn"][blk % 2], state["cT"][blk % 2],
                                        state["sm"][blk % 2], state["eps"], state["ptp"])
        act(cx, sq[:], pt[:, 0:256], AF.Square, [pt], [sq, (sm_, 0)], accum_out=sm_[:, 0:1])
        act(cx, sm_[:, 1:2], sm_[:, 0:1], AF.Sqrt, [(sm_, 0), eps], [(sm_, 1)], bias=eps[:], scale=1.0 / 256)
        cx.S.op("dve", lambda e: e.reciprocal(out=sm_[:, 2:3], in_=sm_[:, 1:2]), [(sm_, 1)], [(sm_, 2)])
        stt(cx, "dve", cn[:], pt[:, 0:256], sm_[:, 2:3], gb_[:], ALU.mult, ALU.mult, [pt, (sm_, 2), gb_], [cn])
        for rc in range(2):
            tr(cx, ptp[:, rc * 128:(rc + 1) * 128], cn[:, rc * 128:(rc + 1) * 128], cx.ident_b[:], [cn, cx.ident_b], [ptp])
        cp(cx, "act", cT[:], ptp[:, 0:256].rearrange("p (a b) -> p a b", a=2), [ptp], [cT])
        dma(cx, "pool", ckvT_d.rearrange("c p t -> p c t")[:, :, blk * 128:(blk + 1) * 128], cT[:], [cT], [])

    emit_proj(cx, lambda i: xk_v[i], SK, [w_in[:, ATT_O1:ATT_O2], w_in[:, ATT_O3:ATT_O4]],
              fm_specs=[(256, 64, BF16, lambda t: kidxT_d[:, t * 512:(t + 1) * 512])],
              tm_specs=[(0, 256, post_ckv)])
    S.barrier()

    def post_widx(cx, st, state, pt, blk):
        if "w" not in state:
            state["w"] = [cx.sb([128, 8], F32, "qw", st) for _ in range(2)]
        wt = state["w"][blk % 2]
        ts(cx, "dve", wt[:], pt[:, 0:8], (8 ** -0.5) * (64 ** -0.5), None, ALU.mult, None, [pt], [wt])
        dma(cx, "pool", widx_d[blk * 128:(blk + 1) * 128, :], wt[:], [wt], [])

    def mk_q(c):
        return lambda t: qT_d[c][:, t * 512:(t + 1) * 512]

    def mk_qi(c):
        return lambda t: qiT_d[c][:, t * 512:(t + 1) * 512]

    fm = [(c * 128, 128, BF16, mk_q(c)) for c in range(8)] + [(1024 + c * 64, 64, BF16, mk_qi(c)) for c in range(8)]
    emit_proj(cx, lambda i: xq_v[i], NQ, [w_in[:, 0:ATT_O1], w_in[:, ATT_O2:ATT_O3], w_in[:, ATT_O4:ATT_O4 + 8]],
              fm_specs=fm, tm_specs=[(1536, 8, post_widx)])
    S.barrier()
    emit_attn_core(cx, ckvT_d, kidxT_d, qT_d, qiT_d, widx_d, qpos_d, w_uk, w_uv, oT_d, NQ, SK, natural)
    S.barrier()
    outs = emit_outproj_ln(cx, oT_d, 8, w_o, xq, x_out, g, b, NQ, 1.0)
    return outs


def attn_scratch(nc, NQ, SK, tag=""):
    mk = lambda n, shp, dt_: nc.dram_tensor(n + tag, shp, dt_, kind="Internal").ap()
    return {"ckvT": mk("s_ckvT", [2, 128, SK], BF16), "kidxT": mk("s_kidxT", [64, SK], BF16),
            "qT": mk("s_qT", [8, 128, NQ], BF16), "qiT": mk("s_qiT", [8, 64, NQ], BF16),
            "widx": mk("s_widx", [NQ, 8], F32), "oT": mk("s_oT", [8, 128, NQ], BF16)}


def build_attn_program(NQ, SK):
    nc = bass.Bass("TRN2", target_bir_lowering=False)
    inp = lambda n, shp: nc.dram_tensor(n, shp, F32, kind="ExternalInput").ap()
    xq, xk = inp("xq", [NQ, D]), inp("xk", [SK, D])
    qpos = inp("qpos", [128, NQ // 128])
    w_in, kv_g = inp("w_in", [D, 1864]), inp("kv_g", [256])
    w_uk, w_uv, w_o = inp("w_uk", [16, 64, 256]), inp("w_uv", [16, 256, 64]), inp("w_o", [D, D])
    g, b = inp("g", [D]), inp("b", [D])
    x_out = nc.dram_tensor("x_out", [NQ, D], F32, kind="ExternalOutput").ap()
    scr = attn_scratch(nc, NQ, SK)
    with contextlib.ExitStack() as st:
        cx = Ctx(nc, st)
        setup_consts(cx)
        outs = emit_attn_stage(cx, xq, xk, qpos, w_in, kv_g, w_uk, w_uv, w_o, g, b, x_out, NQ, SK, scr)
        cx.S.emit(final_wait_ops=outs[-N_DMA_SEM:])
    return nc


def emit_ssd_core(cx, z_d, xbc_d, dt_d, cw_d, cb_d, dtb_d, alog_d, dsk_d, ng_d, yzT_d, SK):
    S = cx.S
    NCH = SK // 128
    with contextlib.ExitStack() as st:
        H = cx.sb([128, 1024], F32, "sH", st)
        Hb = cx.sb([128, 1024], BF16, "sHb", st)
        pre = [cx.sb([128, 515], F32, "spre", st) for _ in range(2)]
        acc = [cx.sb([128, 512], F32, "sacc", st) for _ in range(2)]
        xsT = cx.sb([128, 8, 512], F32, "sxsT", st)
        BT = cx.sb([128, 2, 512], BF16, "sBT", st)
        CT = cx.sb([128, 2, 512], BF16, "sCT", st)
        cw = cx.sb([128, 12, 4], F32, "scw", st)
        cb = cx.sb([128, 12], F32, "scb", st)
        dtb = cx.sb([128, 16], F32, "sdtb", st)
        a_bc = cx.sb([128, 16], F32, "sabc", st)
        d_bc = cx.sb([128, 16], F32, "sdbc", st)
        ng = cx.sb([128, 1024], F32, "sng", st)
        triu = cx.sb([128, 128], F32, "striu", st)
        mgt = cx.sb([128, 128], F32, "smgt", st)
        xs_tok = cx.sb([128, 1024], F32, "sxs", st)
        Btok = cx.sb([128, 2, 128], BF16, "sBtok", st)
        sm = [cx.sb([128, 8, 16], F32, "ssm", st) for _ in range(2)]
        G = [cx.sb([128, 128], F32, "sG", st) for _ in range(4)]
        dec = [cx.sb([128, 4, 128], F32, "sdec", st) for _ in range(2)]
        cbt = cx.sb([128, 2, 128], F32, "scbt", st)
        MT = cx.sb([128, 16, 128], BF16, "sMT", st)
        xdt = cx.sb([128, 1024], BF16, "sxdt", st)
        xdtd = cx.sb([128, 1024], BF16, "sxdtd", st)
        t1 = cx.sb([128, 1024], F32, "st1", st)
        xsD = cx.sb([128, 1024], F32, "sxsD", st)
        y = cx.sb([128, 1024], F32, "sy", st)
        zt = [cx.sb([128, 1024], F32, "szt", st) for _ in range(2)]
        zs = cx.sb([128, 1024], F32, "szs", st)
        yzn = cx.sb([128, 1024], BF16, "syzn", st)
        yzT = [cx.sb([128, 8, 128], BF16, "syzT", st) for _ in range(2)]
        st2 = [cx.sb([128, 8], F32, "sst2", st) for _ in range(2)]
        epst = cx.sb([128, 1], F32, "sepst", st)
        P = [cx.ps([128, 512], F32, "sP", st) for _ in range(8)]

        memset(cx, "pool", epst[:], RMS_EPS, [epst])
        memset(cx, "pool", H[:], 0.0, [H])
        memset(cx, "pool", Hb[:], 0.0, [Hb])
        S.op("pool", lambda e: e.affine_select(out=triu[:], in_=cx.ones_f[:], pattern=[[1, 128]], compare_op=ALU.is_ge,
                                               fill=0.0, base=0, channel_multiplier=-1), [cx.ones_f], [triu])
        S.op("pool", lambda e: e.affine_select(out=mgt[:], in_=cx.ones_f[:], pattern=[[-1, 128]], compare_op=ALU.is_gt,
                                               fill=0.0, base=0, channel_multiplier=1), [cx.ones_f], [mgt])
        dma(cx, "sp", cw[:], cw_d, [], [cw])
        dma(cx, "sp", cb[:], cb_d, [], [cb])
        dma(cx, "sp", dtb[:], dtb_d.partition_broadcast(128), [], [dtb])
        dma(cx, "sp", a_bc[:], alog_d.partition_broadcast(128), [], [a_bc])
        dma(cx, "sp", d_bc[:], dsk_d.partition_broadcast(128), [], [d_bc])
        dma(cx, "sp", ng[:], ng_d.partition_broadcast(128), [], [ng])
        act(cx, a_bc[:], a_bc[:], AF.Exp, [a_bc], [a_bc])
        ts(cx, "dve", a_bc[:], a_bc[:], -1.0, None, ALU.mult, None, [a_bc], [a_bc])
        z_v = z_d.rearrange("(n p) c -> n p c", p=128)
        dt_v = dt_d.rearrange("(n p) c -> n p c", p=128)
        yz_v = yzT_d.rearrange("c p t -> p c t")

        def bc3(ap2):
            return ap2.unsqueeze(2).to_broadcast([128, 16, 64])

        def v3(ap):
            return ap.rearrange("p (h d) -> p h d", h=16)

        for sc in range(SK // 512):
            t0 = sc * 512
            for cc in range(12):
                pt, ac = pre[cc % 2], acc[cc % 2]
                if sc == 0:
                    memset(cx, "pool", pt[:, 0:3], 0.0, [pt])
                    dma(cx, "sp", pt[:, 3:515], xbc_d[cc][:, 0:512], [], [pt])
                else:
                    dma(cx, "sp", pt[:, 0:515], xbc_d[cc][:, t0 - 3:t0 + 512], [], [pt])
                ts(cx, "dve", ac[:], pt[:, 0:512], cw[:, cc, 0:1], None, ALU.mult, None, [pt, cw], [ac])
                for k in range(1, 4):
                    stt(cx, "dve", ac[:], pt[:, k:k + 512], cw[:, cc, k:k + 1], ac[:], ALU.mult, ALU.add, [pt, cw, ac], [ac])
                if cc < 8:
                    dst, key = xsT[:, cc, :], (xsT, cc)
                elif cc < 10:
                    dst, key = BT[:, cc - 8, :], (BT, cc - 8)
                else:
                    dst, key = CT[:, cc - 10, :], (CT, cc - 10)
                act(cx, dst, ac[:], AF.Silu, [ac, cb], [key], bias=cb[:, cc:cc + 1])
            for ch in range(4):
                c = sc * 4 + ch
                c0 = ch * 128
                s_ = sm[c % 2]
                for k in range(8):
                    tr(cx, P[k // 4][:, (k % 4) * 128:(k % 4 + 1) * 128], xsT[:, k, c0:c0 + 128], cx.ident_f[:],
                       [(xsT, k), cx.ident_f], [P[k // 4]])
                for hf in range(2):
                    cp(cx, "act", xs_tok[:, hf * 512:(hf + 1) * 512], P[hf][:], [P[hf]], [(xs_tok, hf)])
                xk_ = [(xs_tok, 0), (xs_tok, 1)]
                p2b = P[2][:].bitcast(BF16)
                for g_ in range(2):
                    tr(cx, p2b[:, g_ * 128:(g_ + 1) * 128], BT[:, g_, c0:c0 + 128], cx.ident_b[:], [(BT, g_), cx.ident_b], [P[2]])
                cp(cx, "dve", Btok[:], p2b[:, 0:256].rearrange("p (a b) -> p a b", a=2), [P[2]], [Btok])
                dma(cx, "sp", s_[:, 0, :], dt_v[c], [], [(s_, 0)])
                tt(cx, "dve", s_[:, 0, :], s_[:, 0, :], dtb[:], ALU.add, [(s_, 0), dtb], [(s_, 0)])
                act(cx, s_[:, 1, :], s_[:, 0, :], AF.Exp, [(s_, 0)], [(s_, 1)])
                act(cx, s_[:, 1, :], s_[:, 1, :], AF.Ln, [(s_, 1)], [(s_, 1)], bias=1.0)
                tt(cx, "dve", s_[:, 2, :], s_[:, 1, :], a_bc[:], ALU.mult, [(s_, 1), a_bc], [(s_, 2)])
                mm(cx, P[2][:, 256:272], triu[:], s_[:, 2, :], True, True, [triu, (s_, 2)], [P[2]])
                mm(cx, P[2][:, 272:288], cx.ones_f[:], s_[:, 2, :], True, True, [cx.ones_f, (s_, 2)], [P[2]])
                cp(cx, "dve", s_[:, 3, :], P[2][:, 256:272], [P[2]], [(s_, 3)])
                act(cx, s_[:, 4, :], s_[:, 3, :], AF.Exp, [(s_, 3)], [(s_, 4)])
                tt(cx, "dve", s_[:, 5, :], P[2][:, 272:288], s_[:, 3, :], ALU.subtract, [P[2], (s_, 3)], [(s_, 5)])
                act(cx, s_[:, 5, :], s_[:, 5, :], AF.Exp, [(s_, 5)], [(s_, 5)])
                act(cx, s_[:, 6, :], P[2][:, 272:288], AF.Exp, [P[2]], [(s_, 6)])
                tt(cx, "dve", s_[:, 7, :], s_[:, 1, :], s_[:, 5, :], ALU.mult, [(s_, 1), (s_, 5)], [(s_, 7)])
                tt(cx, "dve", v3(xdt[:]), v3(xs_tok[:]), bc3(s_[:, 1, :]), ALU.mult, xk_ + [(s_, 1)], [xdt])
                tt(cx, "pool", v3(xdtd[:]), v3(xs_tok[:]), bc3(s_[:, 7, :]), ALU.mult, xk_ + [(s_, 7)], [xdtd])
                for g_ in range(2):
                    mm(cx, P[2][:, 288 + g_ * 128:288 + (g_ + 1) * 128][:, 0:128] if False else P[7][:, g_ * 128:(g_ + 1) * 128],
                       BT[:, g_, c0:c0 + 128], CT[:, g_, c0:c0 + 128], True, True, [(BT, g_), (CT, g_)], [P[7]])
                tt(cx, "dve", cbt[:], P[7][:, 0:256].rearrange("p (a b) -> p a b", a=2),
                   triu[:].unsqueeze(1).to_broadcast([128, 2, 128]), ALU.mult, [P[7], triu], [cbt])
                for h0 in range(0, 16, 4):
                    pseg = P[3 + (h0 // 4) % 2]
                    dc = dec[(h0 // 4) % 2]
                    for j in range(4):
                        h = h0 + j
                        ts(cx, "pool" if j % 2 else "dve", G[j][:], mgt[:], s_[:, 2, h:h + 1], None, ALU.mult, None,
                           [mgt, (s_, 2)], [G[j]])
                        mm(cx, pseg[:, j * 128:(j + 1) * 128], G[j][:], triu[:], True, True, [G[j], triu], [pseg])
                    act(cx, dc[:], pseg[:].rearrange("p (a b) -> p a b", a=4), AF.Exp, [pseg], [dc])
                    g_ = h0 // 8
                    tt(cx, "dve", MT[:, h0:h0 + 4, :], dc[:], cbt[:, g_:g_ + 1, :].to_broadcast([128, 4, 128]), ALU.mult,
                       [dc, cbt], [(MT, h0 // 4)])
                for h in range(16):
                    py = P[h // 8]
                    mm(cx, py[:, (h % 8) * 64:(h % 8 + 1) * 64], MT[:, h, :], xdt[:, h * 64:(h + 1) * 64], True, True,
                       [(MT, h // 4), xdt], [py])
                for g_ in range(2):
                    mm(cx, P[5 + g_][:], CT[:, g_, c0:c0 + 128], Hb[:, g_ * 512:(g_ + 1) * 512], True, True, [(CT, g_), Hb], [P[5 + g_]])
                for g_ in range(2):
                    tt(cx, "dve", t1[:, g_ * 512:(g_ + 1) * 512].rearrange("p (h d) -> p h d", h=8),
                       P[5 + g_][:].rearrange("p (h d) -> p h d", h=8),
                       s_[:, 4, g_ * 8:(g_ + 1) * 8].unsqueeze(2).to_broadcast([128, 8, 64]), ALU.mult,
                       [P[5 + g_], (s_, 4)], [(t1, g_)])
                tt(cx, "pool", v3(xsD[:]), v3(xs_tok[:]), bc3(d_bc[:]), ALU.mult, xk_ + [d_bc], [xsD])
                tt(cx, "pool", xsD[:], xsD[:], t1[:], ALU.add, [xsD, (t1, 0), (t1, 1)], [xsD])
                for g_ in range(2):
                    tt(cx, "dve", y[:, g_ * 512:(g_ + 1) * 512], P[g_][:], xsD[:, g_ * 512:(g_ + 1) * 512], ALU.add,
                       [P[g_], xsD], [(y, g_)])
                for g_ in range(2):
                    mm(cx, P[5 + g_][:], Btok[:, g_, :], xdtd[:, g_ * 512:(g_ + 1) * 512], True, True, [Btok, xdtd], [P[5 + g_]])
                tt(cx, "dve", v3(H[:]), v3(H[:]), bc3(s_[:, 6, :]), ALU.mult, [H, (s_, 6)], [H])
                for g_ in range(2):
                    tt(cx, "dve", H[:, g_ * 512:(g_ + 1) * 512], H[:, g_ * 512:(g_ + 1) * 512], P[5 + g_][:], ALU.add,
                       [H, P[5 + g_]], [H])
                cp(cx, "act", Hb[:], H[:], [H], [Hb])
                ztt = zt[c % 2]
                s2 = st2[c % 2]
                dma(cx, "sp", ztt[:], z_v[c], [], [ztt])
                act(cx, zs[:], ztt[:], AF.Silu, [ztt], [zs])
                tt(cx, "dve", y[:], y[:], zs[:], ALU.mult, [(y, 0), (y, 1), zs], [(y, 0), (y, 1)])
                for g_ in range(2):
                    act(cx, zs[:, g_ * 512:(g_ + 1) * 512], y[:, g_ * 512:(g_ + 1) * 512], AF.Square, [(y, g_)], [zs, (s2, g_)],
                        accum_out=s2[:, g_:g_ + 1])
                    act(cx, s2[:, 2 + g_:3 + g_], s2[:, g_:g_ + 1], AF.Sqrt, [(s2, g_), epst], [(s2, 2 + g_)], bias=epst[:],
                        scale=1.0 / 512)
                    S.op("dve", lambda e, g_=g_, s2=s2: e.reciprocal(out=s2[:, 4 + g_:5 + g_], in_=s2[:, 2 + g_:3 + g_]),
                         [(s2, 2 + g_)], [(s2, 4 + g_)])
                    stt(cx, "dve", yzn[:, g_ * 512:(g_ + 1) * 512], y[:, g_ * 512:(g_ + 1) * 512], s2[:, 4 + g_:5 + g_],
                        ng[:, g_ * 512:(g_ + 1) * 512], ALU.mult, ALU.mult, [(y, g_), (s2, 4 + g_), ng], [(yzn, g_)])
                p7b = P[7][:].bitcast(BF16)
                yT = yzT[c % 2]
                for k in range(8):
                    tr(cx, p7b[:, k * 128:(k + 1) * 128], yzn[:, k * 128:(k + 1) * 128], cx.ident_b[:],
                       [(yzn, k // 4), cx.ident_b], [P[7]])
                cp(cx, "act", yT[:], p7b[:].rearrange("p (a b) -> p a b", a=8), [P[7]], [yT])
                dma(cx, "pool", yz_v[:, :, c * 128:(c + 1) * 128], yT[:], [yT], [])


SSD_NW = 2576


def emit_ssd_stage(cx, xk, w_in_c, cw_d, cb_d, dtb_d, alog_d, dsk_d, ng_d, yzT_d, SK, scr):
    S = cx.S
    z_d, xbc_d, dt_d = scr["z"], scr["xbc"], scr["dt"]
    xk_v = xk.rearrange("(n p) d -> n p d", p=128)

    def post_z(half):
        def f(cx, st, state, pt, blk):
            if "z" not in state:
                state["z"] = [cx.sb([128, 512], F32, "zz", st) for _ in range(2)]
                state["i"] = 0
            state["i"] += 1
            zt_ = state["z"][state["i"] % 2]
            cp(cx, "act", zt_[:], pt[:, 0:512], [pt], [zt_])
            dma(cx, "pool", z_d[blk * 128:(blk + 1) * 128, half * 512:(half + 1) * 512], zt_[:], [zt_], [])
        return f

    def post_dt(cx, st, state, pt, blk):
        if "d" not in state:
            state["d"] = [cx.sb([128, 16], F32, "zd", st) for _ in range(2)]
        d_ = state["d"][blk % 2]
        cp(cx, "dve", d_[:], pt[:, 0:16], [pt], [d_])
        dma(cx, "pool", dt_d[blk * 128:(blk + 1) * 128, :], d_[:], [d_], [])

    def mk(c):
        return lambda t: xbc_d[c][:, t * 512:(t + 1) * 512]

    fm = [(1024 + c * 128, 128, F32, mk(c)) for c in range(12)]
    emit_proj(cx, lambda i: xk_v[i], SK, w_in_c if isinstance(w_in_c, list) else [w_in_c], fm_specs=fm,
              tm_specs=[(0, 512, post_z(0)), (512, 512, post_z(1)), (2560, 16, post_dt)])
    S.barrier()
    emit_ssd_core(cx, z_d, xbc_d, dt_d, cw_d, cb_d, dtb_d, alog_d, dsk_d, ng_d, yzT_d, SK)


def ssd_scratch(nc, SK, tag=""):
    mk = lambda n, shp, dt_: nc.dram_tensor(n + tag, shp, dt_, kind="Internal").ap()
    return {"z": mk("s_z", [SK, 1024], F32), "xbc": mk("s_xbc", [12, 128, SK], F32), "dt": mk("s_dt", [SK, 16], F32)}


def build_ssd_program(SK):
    nc = bass.Bass("TRN2", target_bir_lowering=False)
    inp = lambda n, shp: nc.dram_tensor(n, shp, F32, kind="ExternalInput").ap()
    xk = inp("xk", [SK, D])
    w_in_c = inp("w_in_c", [D, SSD_NW])
    cw, cb = inp("cw", [128, 12, 4]), inp("cb", [128, 12])
    dtb, alog, dsk, ng = inp("dtb", [16]), inp("alog", [16]), inp("dsk", [16]), inp("ng", [1024])
    yzT = nc.dram_tensor("yzT", [8, 128, SK], BF16, kind="ExternalOutput").ap()
    scr = ssd_scratch(nc, SK)
    with contextlib.ExitStack() as st:
        cx = Ctx(nc, st)
        setup_consts(cx)
        emit_ssd_stage(cx, xk, w_in_c, cw, cb, dtb, alog, dsk, ng, yzT, SK, scr)
        cx.S.emit(final_wait_ops=cx.S.dma_hist["pool"][-N_DMA_SEM:])
    return nc


def build_outproj_program(NQ, nch):
    nc = bass.Bass("TRN2", target_bir_lowering=False)
    inp = lambda n, shp: nc.dram_tensor(n, shp, F32, kind="ExternalInput").ap()
    srcT = nc.dram_tensor("srcT", [nch, 128, NQ], BF16, kind="ExternalInput").ap()
    w, xq, g, b = inp("w", [nch * 128, D]), inp("xq", [NQ, D]), inp("g", [D]), inp("b", [D])
    x_out = nc.dram_tensor("x_out", [NQ, D], F32, kind="ExternalOutput").ap()
    with contextlib.ExitStack() as st:
        cx = Ctx(nc, st)
        setup_consts(cx)
        outs = emit_outproj_ln(cx, srcT, nch, w, xq, x_out, g, b, NQ, 1.0)
        cx.S.emit(final_wait_ops=outs[-N_DMA_SEM:])
    return nc


SEQ = 8192
BATCH = 4
NQ_CORE = SEQ // 2


def _own_tokens(j):
    bl = []
    for k in range(SEQ // 512):
        bl += [4 * k, 4 * k + 3] if j == 0 else [4 * k + 1, 4 * k + 2]
    return np.concatenate([np.arange(b * 128, (b + 1) * 128) for b in bl])


def _ssd_core_inputs(j, w_in, conv_w, conv_b, dt_bias, a_log, d_skip, norm_g):
    cols = np.concatenate([np.arange(j * 1024, (j + 1) * 1024), 2048 + np.arange(j * 1024, (j + 1) * 1024),
                           4096 + np.arange(j * 256, (j + 1) * 256), 4608 + np.arange(j * 256, (j + 1) * 256),
                           5120 + np.arange(j * 16, (j + 1) * 16)])
    ch = np.concatenate([np.arange(j * 1024, (j + 1) * 1024), 2048 + np.arange(j * 256, (j + 1) * 256),
                         2560 + np.arange(j * 256, (j + 1) * 256)])
    cw = np.ascontiguousarray(conv_w[:, ch].T.reshape(12, 128, 4).transpose(1, 0, 2))
    cb = np.ascontiguousarray(conv_b[ch].reshape(12, 128).T)
    return {"w_in_c": np.ascontiguousarray(w_in[:, cols]), "cw": cw, "cb": cb,
            "dtb": np.ascontiguousarray(dt_bias[j * 16:(j + 1) * 16]), "alog": np.ascontiguousarray(a_log[j * 16:(j + 1) * 16]),
            "dsk": np.ascontiguousarray(d_skip[j * 16:(j + 1) * 16]), "ng": np.ascontiguousarray(norm_g[j * 1024:(j + 1) * 1024])}


_PROGS = {}


def _prog(name, fn):
    if name not in _PROGS:
        _PROGS[name] = fn()
    return _PROGS[name]


def kernel_unfused(x, ln_g, ln_b, ffn1_w_in, ffn1_w_out, ffn2_w_in, ffn2_w_out, attn_w_in, attn_kv_norm, attn_w_uk, attn_w_uv,
           attn_w_out, ssm_w_in, ssm_conv_w, ssm_conv_b, ssm_dt_bias, ssm_a_log, ssm_d, ssm_norm_g, ssm_w_out):
    f32 = lambda a: np.ascontiguousarray(np.asarray(a, dtype=np.float32))
    x = f32(x)
    cores = list(range(NCORES))
    tok = [_own_tokens(c % 2) for c in cores]
    qpos = [np.ascontiguousarray(tok[c].reshape(-1, 128).T.astype(np.float32)) for c in cores]
    x_own = [np.ascontiguousarray(x[c // 2][tok[c]]) for c in cores]

    def run(nc, in_maps, key):
        res = run_bass_kernel_spmd(nc, in_maps, core_ids=cores)
        return [r[key] for r in res.results]

    def full_seq(x_own):
        out = []
        for s in range(BATCH):
            xs = np.empty((SEQ, D), np.float32)
            for j in range(2):
                xs[tok[2 * s + j]] = x_own[2 * s + j]
            out.append(xs)
        return out

    def ffn(x_own, w_in, w_out, g, b):
        nc = _prog("ffn", lambda: build_ffn_program(NQ_CORE))
        return run(nc, [{"x_in": x_own[c], "w_in": f32(w_in), "w_out": f32(w_out), "g": f32(g), "b": f32(b)} for c in cores],
                   "x_out")

    for i in range(DEPTH):
        x_own = ffn(x_own, ffn1_w_in[i], ffn1_w_out[i], ln_g[i, 0], ln_b[i, 0])
        j_ = i // 2
        xk = full_seq(x_own)
        if i % 2 == 0:
            nc = _prog("attn", lambda: build_attn_program(NQ_CORE, SEQ))
            x_own = run(nc, [{"xq": x_own[c], "xk": xk[c // 2], "qpos": qpos[c], "w_in": f32(attn_w_in[j_]),
                              "kv_g": f32(attn_kv_norm[j_]), "w_uk": f32(attn_w_uk[j_]), "w_uv": f32(attn_w_uv[j_]),
                              "w_o": f32(attn_w_out[j_]), "g": f32(ln_g[i, 1]), "b": f32(ln_b[i, 1])} for c in cores], "x_out")
        else:
            nc = _prog("ssd", lambda: build_ssd_program(SEQ))
            ci = [_ssd_core_inputs(jj, f32(ssm_w_in[j_]), f32(ssm_conv_w[j_]), f32(ssm_conv_b[j_]), f32(ssm_dt_bias[j_]),
                                   f32(ssm_a_log[j_]), f32(ssm_d[j_]), f32(ssm_norm_g[j_])) for jj in range(2)]
            yz = run(nc, [dict(ci[c % 2], xk=xk[c // 2]) for c in cores], "yzT")
            nc2 = _prog("oproj", lambda: build_outproj_program(NQ_CORE, 16))
            maps = []
            for c in cores:
                s = c // 2
                full = np.concatenate([np.asarray(yz[2 * s]), np.asarray(yz[2 * s + 1])], axis=0)
                maps.append({"srcT": np.ascontiguousarray(full[:, :, tok[c]]), "w": f32(ssm_w_out[j_]), "xq": x_own[c],
                             "g": f32(ln_g[i, 1]), "b": f32(ln_b[i, 1])})
            x_own = run(nc2, maps, "x_out")
        x_own = ffn(x_own, ffn2_w_in[i], ffn2_w_out[i], ln_g[i, 2], ln_b[i, 2])
    out = np.stack(full_seq(x_own)).astype(np.float32)
    return out


def build_full_program():
    nc = bass.Bass("TRN2", target_bir_lowering=False)
    inp = lambda n, shp: nc.dram_tensor(n, shp, F32, kind="ExternalInput").ap()
    x = inp("x", [SEQ, D])
    qpos = inp("qpos", [128, SEQ // 128])
    ln_g, ln_b = inp("ln_g", [DEPTH, 3, D]), inp("ln_b", [DEPTH, 3, D])
    f1i, f1o = inp("ffn1_w_in", [DEPTH, D, 2 * DFF]), inp("ffn1_w_out", [DEPTH, DFF, D])
    f2i, f2o = inp("ffn2_w_in", [DEPTH, D, 2 * DFF]), inp("ffn2_w_out", [DEPTH, DFF, D])
    a_in, a_kv = inp("attn_w_in", [2, D, 1864]), inp("attn_kv_norm", [2, 256])
    a_uk, a_uv, a_o = inp("attn_w_uk", [2, 16, 64, 256]), inp("attn_w_uv", [2, 16, 256, 64]), inp("attn_w_out", [2, D, D])
    s_in = inp("ssm_w_in", [2, D, 5152])
    s_cw, s_cb = inp("ssm_cw", [2, 2, 128, 12, 4]), inp("ssm_cb", [2, 2, 128, 12])
    s_dtb, s_alog, s_d = inp("ssm_dt_bias", [2, 32]), inp("ssm_a_log", [2, 32]), inp("ssm_d", [2, 32])
    s_ng, s_o = inp("ssm_norm_g", [2, 2048]), inp("ssm_w_out", [2, 2048, D])
    out = nc.dram_tensor("out", [SEQ, D], F32, kind="ExternalOutput").ap()
    bufs = [nc.dram_tensor("xbuf%d" % i, [SEQ, D], F32, kind="Internal").ap() for i in range(3)]
    ascr = attn_scratch(nc, SEQ, SEQ)
    sscr = ssd_scratch(nc, SEQ)
    yzT = nc.dram_tensor("s_yzT", [16, 128, SEQ], BF16, kind="Internal").ap()
    with contextlib.ExitStack() as st:
        cx = Ctx(nc, st)
        S = cx.S
        setup_consts(cx)
        cur = x
        outs = None
        for i in range(DEPTH):
            j_ = i // 2
            b0, b1, b2 = bufs[0], bufs[1], bufs[2]
            emit_ffn(cx, cur, b1, f1i[i], f1o[i], ln_g[i, 0], ln_b[i, 0], SEQ)
            S.barrier()
            if i % 2 == 0:
                emit_attn_stage(cx, b1, b1, qpos, a_in[j_], a_kv[j_], a_uk[j_], a_uv[j_], a_o[j_], ln_g[i, 1], ln_b[i, 1], b2,
                                SEQ, SEQ, ascr, natural=True)
            else:
                w = s_in[j_]
                for jj in range(2):
                    wcols = [w[:, jj * 1024:(jj + 1) * 1024], w[:, 2048 + jj * 1024:2048 + (jj + 1) * 1024],
                             w[:, 4096 + jj * 256:4096 + (jj + 1) * 256], w[:, 4608 + jj * 256:4608 + (jj + 1) * 256],
                             w[:, 5120 + jj * 16:5120 + (jj + 1) * 16]]
                    emit_ssd_stage(cx, b1, wcols, s_cw[j_, jj], s_cb[j_, jj], s_dtb[j_, jj * 16:(jj + 1) * 16],
                                   s_alog[j_, jj * 16:(jj + 1) * 16], s_d[j_, jj * 16:(jj + 1) * 16],
                                   s_ng[j_, jj * 1024:(jj + 1) * 1024], yzT[jj * 8:(jj + 1) * 8], SEQ, sscr)
                    S.barrier()
                emit_outproj_ln(cx, yzT, 16, s_o[j_], b1, b2, ln_g[i, 1], ln_b[i, 1], SEQ, 1.0)
            S.barrier()
            last = i == DEPTH - 1
            dst = out if last else b0
            outs = emit_ffn(cx, b2, dst, f2i[i], f2o[i], ln_g[i, 2], ln_b[i, 2], SEQ)
            if not last:
                S.barrier()
            cur = b0
        print("ops per engine:", {e: len(S.ops[e]) for e in ENGS}, flush=True)
        S.emit(final_wait_ops=outs[-N_DMA_SEM:])
        print("max semval per segment:", S.max_semval, "-> per-sem max", {e: v // N_CSEM + CSEM_CH for e, v in S.max_semval.items()},
              "max dma sem value:", S.max_dval, flush=True)
    return nc


def _ssd_conv_layout(conv_w, conv_b):
    cw = np.zeros((2, 2, 128, 12, 4), np.float32)
    cb = np.zeros((2, 2, 128, 12), np.float32)
    for l in range(2):
        for j in range(2):
            ch = np.concatenate([np.arange(j * 1024, (j + 1) * 1024), 2048 + np.arange(j * 256, (j + 1) * 256),
                                 2560 + np.arange(j * 256, (j + 1) * 256)])
            cw[l, j] = conv_w[l][:, ch].T.reshape(12, 128, 4).transpose(1, 0, 2)
            cb[l, j] = conv_b[l][ch].reshape(12, 128).T
    return cw, cb


def kernel_fused(x, ln_g, ln_b, ffn1_w_in, ffn1_w_out, ffn2_w_in, ffn2_w_out, attn_w_in, attn_kv_norm, attn_w_uk, attn_w_uv,
                 attn_w_out, ssm_w_in, ssm_conv_w, ssm_conv_b, ssm_dt_bias, ssm_a_log, ssm_d, ssm_norm_g, ssm_w_out):
    f32 = lambda a: np.ascontiguousarray(np.asarray(a, dtype=np.float32))
    x = f32(x)
    cw, cb = _ssd_conv_layout(f32(ssm_conv_w), f32(ssm_conv_b))
    qpos = np.ascontiguousarray(np.arange(SEQ, dtype=np.float32).reshape(-1, 128).T)
    shared = {"qpos": qpos, "ln_g": f32(ln_g), "ln_b": f32(ln_b), "ffn1_w_in": f32(ffn1_w_in), "ffn1_w_out": f32(ffn1_w_out),
              "ffn2_w_in": f32(ffn2_w_in), "ffn2_w_out": f32(ffn2_w_out), "attn_w_in": f32(attn_w_in),
              "attn_kv_norm": f32(attn_kv_norm), "attn_w_uk": f32(attn_w_uk), "attn_w_uv": f32(attn_w_uv),
              "attn_w_out": f32(attn_w_out), "ssm_w_in": f32(ssm_w_in), "ssm_cw": cw, "ssm_cb": cb,
              "ssm_dt_bias": f32(ssm_dt_bias), "ssm_a_log": f32(ssm_a_log), "ssm_d": f32(ssm_d), "ssm_norm_g": f32(ssm_norm_g),
              "ssm_w_out": f32(ssm_w_out)}
    nc = _prog("full", build_full_program)
    active = [0, 1, 4, 5]
    zeros = {k: np.zeros_like(v) for k, v in shared.items()}
    zeros["qpos"] = qpos
    zx = np.zeros((SEQ, D), np.float32)
    in_maps = []
    for c in range(NCORES):
        if c in active:
            in_maps.append(dict(shared, x=np.ascontiguousarray(x[active.index(c)])))
        else:
            in_maps.append(dict(zeros, x=zx))
    res = run_bass_kernel_spmd(nc, in_maps, core_ids=list(range(NCORES)))
    return np.stack([np.asarray(res.results[c]["out"], dtype=np.float32) for c in active])


kernel = kernel_fused
```

```python
import contextlib
import numpy as np
import concourse.bass as bass
import concourse.mybir as mybir
from concourse.bass_utils import run_bass_kernel_spmd

F32 = mybir.dt.float32
BF16 = mybir.dt.bfloat16
AF = mybir.ActivationFunctionType
ALU = mybir.AluOpType
AX = mybir.AxisListType

D = 1024
DFF = 2816
DEPTH = 4
ALPHA = (2.0 * DEPTH) ** 0.25
LN_EPS = 1e-5
NCORES = 8

ENGS = ("pe", "dve", "act", "pool", "sp")
N_DMA_SEM = 20
N_CSEM = 14
CSEM_CH = 128


class Op:
    __slots__ = ("eng", "fn", "deps", "is_dma", "has_dep", "semval", "dsem", "dval", "prev_ring")

    def __init__(self, eng, fn, is_dma):
        self.eng = eng
        self.fn = fn
        self.is_dma = is_dma
        self.deps = []
        self.has_dep = False
        self.semval = None
        self.dsem = None
        self.dval = None
        self.prev_ring = None


def _key(k):
    if isinstance(k, tuple):
        return tuple(_key(e) for e in k)
    if isinstance(k, (str, int)):
        return k
    return k.name


class Sched:
    def __init__(self, nc):
        self.nc = nc
        self.ops = {e: [] for e in ENGS}
        self.last_w = {}
        self.readers = {}
        self.dma_count = {e: 0 for e in ENGS}
        self.dma_hist = {e: [] for e in ENGS}
        self.n = 0
        self.n_barriers = 0

    def _add(self, eng, fn, reads, writes, is_dma):
        op = Op(eng, fn, is_dma)
        reads = [_key(r) for r in reads]
        writes = [_key(w) for w in writes]
        deps = []
        for r in reads:
            w = self.last_w.get(r)
            if w is not None:
                deps.append(w)
        for w_ in writes:
            w = self.last_w.get(w_)
            if w is not None:
                deps.append(w)
            rl = self.readers.get(w_, ())
            lastc = {}
            for o in rl:
                if o.is_dma:
                    deps.append(o)
                else:
                    lastc[o.eng] = o
            deps.extend(lastc.values())
        seen = set()
        for d in deps:
            if id(d) in seen:
                continue
            seen.add(id(d))
            if (not d.is_dma) and (not is_dma) and d.eng == "pe" and eng == "pe":
                continue
            op.deps.append(d)
            d.has_dep = True
        for r in reads:
            self.readers.setdefault(r, []).append(op)
        for w_ in writes:
            self.last_w[w_] = op
            self.readers[w_] = []
        if is_dma:
            j = self.dma_count[eng]
            self.dma_count[eng] += 1
            op.dsem = j % N_DMA_SEM
            op.dval = 16 * (j // N_DMA_SEM + 1)
            hist = self.dma_hist[eng]
            if j >= N_DMA_SEM:
                op.prev_ring = hist[j - N_DMA_SEM]
            hist.append(op)
        self.ops[eng].append(op)
        self.n += 1
        return op

    def op(self, eng, fn, reads=(), writes=()):
        return self._add(eng, fn, reads, writes, False)

    def dma(self, eng, fn, reads=(), writes=()):
        return self._add(eng, fn, reads, writes, True)

    def barrier(self):
        lasts = []
        for e in ENGS:
            for o in reversed(self.ops[e]):
                if o.fn is None:
                    break
                if not o.is_dma:
                    lasts.append(o)
                    break
            lasts.extend(self.dma_hist[e][-N_DMA_SEM:])
        self.n_barriers += 1
        for e in ENGS:
            op = Op(e, None, False)
            op.semval = self.n_barriers
            for d in lasts:
                op.deps.append(d)
                d.has_dep = True
            self.ops[e].append(op)
        self.last_w = {}
        self.readers = {}

    def emit(self, final_wait_ops=()):
        nc = self.nc
        for o in final_wait_ops:
            o.has_dep = True
        for e in ENGS:
            c = 0
            for o in self.ops[e]:
                if o.fn is None:
                    c = 0
                    continue
                if o.is_dma:
                    continue
                if o.has_dep:
                    c += 1
                    o.semval = c
        self.max_semval = {e: max([o.semval or 0 for o in self.ops[e] if o.fn is not None and not o.is_dma] + [0]) for e in ENGS}
        self.max_dval = {e: max([o.dval or 0 for o in self.ops[e] if o.is_dma] + [0]) for e in ENGS}
        with contextlib.ExitStack() as st:
            csem = {e: [st.enter_context(nc.semaphore("cs_%s_%d" % (e, i))) for i in range(N_CSEM)]
                    for e in ENGS if e != "sp"}
            dsem = {e: [st.enter_context(nc.semaphore("ds_%s_%d" % (e, i))) for i in range(N_DMA_SEM)]
                    for e in ENGS if any(o.is_dma for o in self.ops[e])}
            bsem = [st.enter_context(nc.semaphore("bar%d" % i)) for i in range(2)]
            any_dma = {e: any(o.is_dma for o in self.ops[e]) for e in ENGS}
            block = st.enter_context(nc.Block())

            def run(e, eng):
                waited = {}

                def wait_for(d):
                    if d.is_dma:
                        k = ("d", d.eng, d.dsem)
                        s, v = dsem[d.eng][d.dsem], d.dval
                        if waited.get(k, 0) >= v:
                            return
                        waited[k] = v
                        eng.wait_ge(s, v)
                    else:
                        k = ("c", d.eng)
                        c = d.semval
                        if waited.get(k, 0) >= c:
                            return
                        waited[k] = c
                        ep = (c - 1) // CSEM_CH
                        eng.wait_ge(csem[d.eng][ep % N_CSEM], CSEM_CH * (ep // N_CSEM) + ((c - 1) % CSEM_CH) + 1)

                for o in self.ops[e]:
                    for d in o.deps:
                        wait_for(d)
                    if o.prev_ring is not None:
                        wait_for(o.prev_ring)
                    if o.fn is None:
                        k = o.semval
                        eng.sem_inc(bsem[0], 1)
                        eng.wait_ge(bsem[0], len(ENGS) * k)
                        if e in csem:
                            for s_ in csem[e]:
                                eng.sem_clear(s_)
                        eng.sem_inc(bsem[1], 1)
                        eng.wait_ge(bsem[1], len(ENGS) * k)
                        for k_ in [k_ for k_ in waited if k_[0] == "c"]:
                            del waited[k_]
                        continue
                    ins = o.fn(eng)
                    if o.is_dma:
                        ins.then_inc(dsem[e][o.dsem], 16)
                    elif o.has_dep:
                        ins.then_inc(csem[e][((o.semval - 1) // CSEM_CH) % N_CSEM], 1)
                if e == "pool":
                    for d in final_wait_ops:
                        wait_for(d)

            for e, reg in (("sp", block.sync), ("pe", block.tensor), ("dve", block.vector),
                           ("act", block.scalar), ("pool", block.gpsimd)):
                reg(lambda eng, e=e: run(e, eng))


class Ctx:
    def __init__(self, nc, st):
        self.nc = nc
        self.S = Sched(nc)
        self.st = st
        self.uid = 0
        self.psum = []
        self.psum_i = 0

    def sb(self, shape, dtype, name=None, st=None):
        self.uid += 1
        nm = "%s_%d" % (name or "t", self.uid)
        return (st or self.st).enter_context(self.nc.sbuf_tensor(nm, list(shape), dtype))

    def ps(self, shape, dtype, name=None, st=None):
        self.uid += 1
        nm = "%s_%d" % (name or "p", self.uid)
        return (st or self.st).enter_context(self.nc.psum_tensor(nm, list(shape), dtype))


def setup_consts(cx):
    nc, S = cx.nc, cx.S
    cx.ident_f = cx.sb([128, 128], F32, "identf")
    cx.ident_b = cx.sb([128, 128], BF16, "identb")
    cx.ones_f = cx.sb([128, 128], F32, "onesf")
    idf, idb, onf = cx.ident_f, cx.ident_b, cx.ones_f
    S.op("pool", lambda e: e.memset(onf[:], 1.0), writes=[onf])
    S.op("pool", lambda e: e.affine_select(out=idf[:], in_=onf[:], pattern=[[-1, 128]],
                                           compare_op=ALU.is_equal, fill=0.0, base=0,
                                           channel_multiplier=1), reads=[onf], writes=[idf])
    S.op("pool", lambda e: e.tensor_copy(out=idb[:], in_=idf[:]), reads=[idf], writes=[idb])


def emit_ffn(cx, x_in, x_out, w_in, w_out, g, b, ntok, scale_y=0.5):
    nc, S = cx.nc, cx.S
    T = 256
    NS = T // 128
    KD = D // 128
    KF = DFF // 128
    c_y = scale_y / ALPHA
    eps_p = LN_EPS / (ALPHA * ALPHA)
    with contextlib.ExitStack() as st:
        win = cx.sb([128, KD, 2 * DFF], BF16, "win", st)
        wout = cx.sb([128, KF, D], BF16, "wout", st)
        stg = [cx.sb([128, 1408], F32, "stg", st) for _ in range(2)]
        gb = cx.sb([128, D], F32, "gb", st)
        bb = cx.sb([128, D], F32, "bb", st)
        epst = cx.sb([128, 1], F32, "eps", st)
        xs = [cx.sb([128, D], F32, "xs", st) for _ in range(4)]
        xb = [cx.sb([128, D], BF16, "xb", st) for _ in range(2)]
        xT = [cx.sb([128, KD, T], BF16, "xT", st) for _ in range(2)]
        hT = [cx.sb([128, KF, T], BF16, "hT", st) for _ in range(1)]
        sg = [cx.sb([128, T], F32, "sg", st) for _ in range(2)]
        r = [cx.sb([128, D], F32, "r", st) for _ in range(1)]
        xo = [cx.sb([128, D], F32, "xo", st) for _ in range(1)]
        stat = [cx.sb([128, 8], F32, "stat", st) for _ in range(2)]
        p_tp = cx.ps([128, 1024], BF16, "ptp", st)
        p_g = [cx.ps([128, 512], F32, "pg", st) for _ in range(2)]
        p_u = [cx.ps([128, 512], F32, "pu", st) for _ in range(2)]
        p_o = [cx.ps([128, 512], F32, "po", st) for _ in range(2)]

        S.op("pool", lambda e: e.memset(epst[:], eps_p), writes=[epst])
        S.dma("sp", lambda e: e.dma_start(out=gb[:], in_=g.partition_broadcast(128)), writes=[gb])
        S.dma("sp", lambda e: e.dma_start(out=bb[:], in_=b.partition_broadcast(128)), writes=[bb])

        w_in_v = w_in.rearrange("(k p) f -> p k f", p=128)
        w_out_v = w_out.rearrange("(k p) d -> p k d", p=128)
        cast_engs = ["dve", "act", "pool"]
        ci = 0
        pieces = []
        for k in range(KD):
            for hf in range(4):
                pieces.append((w_in_v[:, k, hf * 1408:(hf + 1) * 1408], win[:, k, hf * 1408:(hf + 1) * 1408],
                               (win, k), 1408))
        for k0 in range(KF):
            pieces.append((w_out_v[:, k0, :], wout[:, k0, :], (wout, k0), D))
        for i, (src, dst, key, n) in enumerate(pieces):
            sb_ = stg[i % 2]
            if len(src.shape) == 3:
                sview = sb_[:, 0:n].rearrange("p (a b) -> p a b", a=src.shape[1])
            else:
                sview = sb_[:, 0:n]
            S.dma("sp", lambda e, sview=sview, src=src: e.dma_start(out=sview, in_=src), writes=[sb_])
            ce = cast_engs[ci % 3]
            ci += 1
            if ce == "act":
                S.op("act", lambda e, dst=dst, sview=sview: e.copy(out=dst, in_=sview), reads=[sb_], writes=[key])
            else:
                S.op(ce, lambda e, dst=dst, sview=sview: e.tensor_copy(out=dst, in_=sview), reads=[sb_], writes=[key])
        win_keys = [(win, k) for k in range(KD)]
        wout_keys = [(wout, k) for k in range(KF // 2)]

        ntiles = ntok // T
        x_in_v = x_in.rearrange("(n p) d -> n p d", p=128)
        x_out_v = x_out.rearrange("(n p) d -> n p d", p=128)
        last_out = []

        def load_tile(ti):
            xTt = xT[ti % 2]
            for s in range(NS):
                xt = xs[(ti * NS + s) % 4]
                xbt = xb[s]
                S.dma("sp", lambda e, xt=xt, i=ti * NS + s: e.dma_start(out=xt[:], in_=x_in_v[i]), writes=[xt])
                S.op("pool", lambda e, xbt=xbt, xt=xt: e.tensor_copy(out=xbt[:], in_=xt[:]), reads=[xt], writes=[xbt])
                for k in range(KD):
                    S.op("pe", lambda e, k=k, xbt=xbt: e.transpose(out=p_tp[:, k * 128:(k + 1) * 128],
                                                                  in_=xbt[:, k * 128:(k + 1) * 128],
                                                                  identity=cx.ident_b[:]),
                         reads=[xbt, cx.ident_b], writes=[p_tp])
                S.op("dve", lambda e, xTt=xTt, s=s: e.tensor_copy(
                    out=xTt[:, :, s * 128:(s + 1) * 128],
                    in_=p_tp[:].rearrange("p (k t) -> p k t", k=KD)), reads=[p_tp], writes=[xTt])

        load_tile(0)
        for ti in range(ntiles):
            xTt = xT[ti % 2]
            hTt = hT[0]
            for j in range(KF):
                pg, pu, sgt = p_g[j % 2], p_u[j % 2], sg[j % 2]
                for k in range(KD):
                    S.op("pe", lambda e, k=k, j=j, pg=pg, xTt=xTt: e.matmul(pg[:, 0:T], lhsT=win[:, k, j * 128:(j + 1) * 128],
                                                                   rhs=xTt[:, k, :], start=(k == 0), stop=(k == KD - 1)),
                         reads=[(win, k), xTt], writes=[pg])
                for k in range(KD):
                    S.op("pe", lambda e, k=k, j=j, pu=pu, xTt=xTt: e.matmul(pu[:, 0:T],
                                                                   lhsT=win[:, k, DFF + j * 128:DFF + (j + 1) * 128],
                                                                   rhs=xTt[:, k, :], start=(k == 0), stop=(k == KD - 1)),
                         reads=[(win, k), xTt], writes=[pu])
                S.op("act", lambda e, pg=pg, sgt=sgt: e.activation(out=sgt[:], in_=pg[:, 0:T], func=AF.Silu),
                     reads=[pg], writes=[sgt])
                S.op("dve", lambda e, pu=pu, sgt=sgt, j=j, hTt=hTt: e.tensor_tensor(out=hTt[:, j, :], in0=pu[:, 0:T], in1=sgt[:],
                                                                            op=ALU.mult),
                     reads=[pu, sgt], writes=[(hTt, j)])
            if ti + 1 < ntiles:
                load_tile(ti + 1)
            for s in range(NS):
                xt = xs[(ti * NS + s) % 4]
                rt, xot, stt = r[0], xo[0], stat[s]
                for hd in range(2):
                    po = p_o[hd]
                    for j in range(KF):
                        S.op("pe", lambda e, j=j, hd=hd, s=s, po=po, hTt=hTt: e.matmul(
                            po[:], lhsT=hTt[:, j, s * 128:(s + 1) * 128], rhs=wout[:, j, hd * 512:(hd + 1) * 512],
                            start=(j == 0), stop=(j == KF - 1)),
                             reads=[(hTt, j), (wout, j)], writes=[po])
                    S.op("dve", lambda e, po=po, hd=hd, rt=rt, xt=xt: e.scalar_tensor_tensor(
                        out=rt[:, hd * 512:(hd + 1) * 512], in0=po[:], scalar=c_y, in1=xt[:, hd * 512:(hd + 1) * 512],
                        op0=ALU.mult, op1=ALU.add), reads=[po, xt], writes=[(rt, hd)])
                emit_ln(cx, rt, [(rt, 0), (rt, 1)], xot, stt, epst, gb, bb, xot)
                o = S.dma("pool", lambda e, xot=xot, i=ti * NS + s: e.dma_start(out=x_out_v[i], in_=xot[:]),
                          reads=[xot], writes=[("dram_xout", ti * NS + s)])
                last_out.append(o)
        return last_out


def emit_ln(cx, rt, rkeys, junk, stt, epst, gb, bb, xot):
    S = cx.S
    S.op("act", lambda e: e.activation(out=junk[:], in_=rt[:], func=AF.Square, accum_out=stt[:, 0:1]),
         reads=rkeys, writes=[junk, (stt, 0)])
    S.op("dve", lambda e: e.tensor_reduce(out=stt[:, 1:2], in_=rt[:], axis=AX.X, op=ALU.add),
         reads=rkeys, writes=[(stt, 1)])
    S.op("dve", lambda e: e.tensor_scalar(out=stt[:, 2:3], in0=stt[:, 1:2], scalar1=1.0 / D, scalar2=None, op0=ALU.mult),
         reads=[(stt, 1)], writes=[(stt, 2)])
    S.op("dve", lambda e: e.tensor_tensor(out=stt[:, 3:4], in0=stt[:, 2:3], in1=stt[:, 2:3], op=ALU.mult),
         reads=[(stt, 2)], writes=[(stt, 3)])
    S.op("dve", lambda e: e.scalar_tensor_tensor(out=stt[:, 4:5], in0=stt[:, 0:1], scalar=1.0 / D, in1=stt[:, 3:4],
                                                 op0=ALU.mult, op1=ALU.subtract),
         reads=[(stt, 0), (stt, 3)], writes=[(stt, 4)])
    S.op("act", lambda e: e.activation(out=stt[:, 5:6], in_=stt[:, 4:5], func=AF.Sqrt, bias=epst[:]),
         reads=[(stt, 4), epst], writes=[(stt, 5)])
    S.op("dve", lambda e: e.reciprocal(out=stt[:, 6:7], in_=stt[:, 5:6]), reads=[(stt, 5)], writes=[(stt, 6)])
    S.op("dve", lambda e: e.tensor_scalar(out=xot[:], in0=rt[:], scalar1=stt[:, 2:3], scalar2=stt[:, 6:7],
                                          op0=ALU.subtract, op1=ALU.mult),
         reads=rkeys + [(stt, 2), (stt, 6)], writes=[xot])
    S.op("pool", lambda e: e.tensor_tensor(out=xot[:], in0=xot[:], in1=gb[:], op=ALU.mult), reads=[xot, gb], writes=[xot])
    S.op("pool", lambda e: e.tensor_tensor(out=xot[:], in0=xot[:], in1=bb[:], op=ALU.add), reads=[xot, bb], writes=[xot])


def build_ffn_program(ntok):
    nc = bass.Bass("TRN2", target_bir_lowering=False)
    x_in = nc.dram_tensor("x_in", [ntok, D], F32, kind="ExternalInput").ap()
    w_in = nc.dram_tensor("w_in", [D, 2 * DFF], F32, kind="ExternalInput").ap()
    w_out = nc.dram_tensor("w_out", [DFF, D], F32, kind="ExternalInput").ap()
    g = nc.dram_tensor("g", [D], F32, kind="ExternalInput").ap()
    b = nc.dram_tensor("b", [D], F32, kind="ExternalInput").ap()
    x_out = nc.dram_tensor("x_out", [ntok, D], F32, kind="ExternalOutput").ap()
    with contextlib.ExitStack() as st:
        cx = Ctx(nc, st)
        setup_consts(cx)
        outs = emit_ffn(cx, x_in, x_out, w_in, w_out, g, b, ntok)
        cx.S.emit(final_wait_ops=outs[-N_DMA_SEM:])
    return nc


def mm(cx, out, lhsT, rhs, start, stop, reads, writes):
    return cx.S.op("pe", lambda e: e.matmul(out, lhsT=lhsT, rhs=rhs, start=start, stop=stop), reads, writes)


def tr(cx, out, in_, ident, reads, writes):
    return cx.S.op("pe", lambda e: e.transpose(out=out, in_=in_, identity=ident), reads, writes)


def act(cx, out, in_, func, reads, writes, **kw):
    return cx.S.op("act", lambda e: e.activation(out=out, in_=in_, func=func, **kw), reads, writes)


def tt(cx, eng, out, in0, in1, op, reads, writes):
    return cx.S.op(eng, lambda e: e.tensor_tensor(out=out, in0=in0, in1=in1, op=op), reads, writes)


def ts(cx, eng, out, in0, s1, s2, op0, op1, reads, writes, accum_out=None):
    if op1 is None:
        return cx.S.op(eng, lambda e: e.tensor_scalar(out=out, in0=in0, scalar1=s1, scalar2=None, op0=op0,
                                                      accum_out=accum_out), reads, writes)
    return cx.S.op(eng, lambda e: e.tensor_scalar(out=out, in0=in0, scalar1=s1, scalar2=s2, op0=op0, op1=op1,
                                                  accum_out=accum_out), reads, writes)


def stt(cx, eng, out, in0, scalar, in1, op0, op1, reads, writes):
    return cx.S.op(eng, lambda e: e.scalar_tensor_tensor(out=out, in0=in0, scalar=scalar, in1=in1, op0=op0, op1=op1),
                   reads, writes)


def cp(cx, eng, out, in_, reads, writes):
    if eng == "act":
        return cx.S.op("act", lambda e: e.copy(out=out, in_=in_), reads, writes)
    return cx.S.op(eng, lambda e: e.tensor_copy(out=out, in_=in_), reads, writes)


def dma(cx, q, out, in_, reads, writes):
    return cx.S.dma(q, lambda e: e.dma_start(out=out, in_=in_), reads, writes)


def memset(cx, eng, ap, val, writes):
    return cx.S.op(eng, lambda e: e.memset(ap, val), (), writes)


def load_weight_bf16(cx, dst, src, stg, n, key, idx):
    sb_ = stg[idx % len(stg)]
    if len(src.shape) == 3:
        sview = sb_[:, 0:n].rearrange("p (a b) -> p a b", a=src.shape[1])
    else:
        sview = sb_[:, 0:n]
    dma(cx, "sp", sview, src, [], [sb_])
    ce = ("dve", "act", "pool")[idx % 3]
    cp(cx, ce, dst, sview, [sb_], [key])


def emit_xT(cx, x_rows, xs, xb, p_tp, xT, col0):
    dma(cx, "sp", xs[:], x_rows, [], [xs])
    cp(cx, "pool", xb[:], xs[:], [xs], [xb])
    for k in range(D // 128):
        tr(cx, p_tp[:, k * 128:(k + 1) * 128], xb[:, k * 128:(k + 1) * 128], cx.ident_b[:], [xb, cx.ident_b], [p_tp])
    cp(cx, "dve", xT[:, :, col0:col0 + 128], p_tp[:].rearrange("p (k t) -> p k t", k=D // 128), [p_tp], [xT])


def emit_proj(cx, x_rows_fn, ntok, wcols, fm_specs, tm_specs):
    TT = 512
    KD = D // 128
    NW = sum(w.shape[1] for w in wcols)
    with contextlib.ExitStack() as st:
        wsb = cx.sb([128, KD, NW], BF16, "pw", st)
        stg = [cx.sb([128, 1024], F32, "pstg", st) for _ in range(2)]
        xs = [cx.sb([128, D], F32, "pxs", st) for _ in range(2)]
        xb = [cx.sb([128, D], BF16, "pxb", st) for _ in range(2)]
        xT = [cx.sb([128, KD, TT], BF16, "pxT", st) for _ in range(2)]
        ost = [cx.sb([128, TT], F32, "post", st) for _ in range(3)]
        p_tp = cx.ps([128, 1024], BF16, "pptp", st)
        p_fm = [cx.ps([128, 512], F32, "ppfm", st) for _ in range(2)]
        p_tm = [cx.ps([128, 512], F32, "pptm", st) for _ in range(2)]
        tm_state = {}
        idx = 0
        off = 0
        for w in wcols:
            n = w.shape[1]
            wv = w.rearrange("(k p) f -> p k f", p=128)
            for k in range(KD):
                for c0 in range(0, n, 1024):
                    c1 = min(n, c0 + 1024)
                    load_weight_bf16(cx, wsb[:, k, off + c0:off + c1], wv[:, k, c0:c1], stg, c1 - c0, wsb, idx)
                    idx += 1
            off += n
        ntiles = ntok // TT
        oi = 0
        fi = 0
        ti_ = 0
        for t in range(ntiles):
            xTt = xT[t % 2]
            for s in range(TT // 128):
                emit_xT(cx, x_rows_fn(t * 4 + s), xs[s % 2], xb[s % 2], p_tp, xTt, s * 128)
            for (coff, M, dt_, dest_fn) in fm_specs:
                pf = p_fm[fi % 2]
                fi += 1
                for k in range(KD):
                    mm(cx, pf[0:M, :], wsb[:, k, coff:coff + M], xTt[:, k, :], k == 0, k == KD - 1, [wsb, xTt], [pf])
                o = ost[oi % 3]
                oi += 1
                ov = o[0:M, :] if dt_ == F32 else o[:].bitcast(BF16)[0:M, 0:TT]
                cp(cx, "act" if oi % 2 else "dve", ov, pf[0:M, :], [pf], [o])
                dma(cx, "pool", dest_fn(t), ov, [o], [])
            for (coff, N, post_fn) in tm_specs:
                for s in range(TT // 128):
                    pt = p_tm[ti_ % 2]
                    ti_ += 1
                    for k in range(KD):
                        mm(cx, pt[:, 0:N], xTt[:, k, s * 128:(s + 1) * 128], wsb[:, k, coff:coff + N], k == 0, k == KD - 1,
                           [wsb, xTt], [pt])
                    post_fn(cx, st, tm_state, pt, t * 4 + s)


def emit_outproj_ln(cx, srcT, nch, w, x_in, x_out, g, b, ntok, scale_y):
    c_y = scale_y / ALPHA
    eps_p = LN_EPS / (ALPHA * ALPHA)
    with contextlib.ExitStack() as st:
        wsb = cx.sb([128, nch, D], BF16, "ow", st)
        stg = [cx.sb([128, 1024], F32, "ostg", st) for _ in range(2)]
        gb = cx.sb([128, D], F32, "ogb", st)
        bb = cx.sb([128, D], F32, "obb", st)
        epst = cx.sb([128, 1], F32, "oeps", st)
        oT = [cx.sb([128, nch, 128], BF16, "ooT", st) for _ in range(2)]
        xs = [cx.sb([128, D], F32, "oxs", st) for _ in range(2)]
        r = cx.sb([128, D], F32, "or", st)
        xo = [cx.sb([128, D], F32, "oxo", st) for _ in range(2)]
        stat = [cx.sb([128, 8], F32, "ostat", st) for _ in range(2)]
        p_o = [cx.ps([128, 512], F32, "opo", st) for _ in range(4)]
        memset(cx, "pool", epst[:], eps_p, [epst])
        dma(cx, "sp", gb[:], g.partition_broadcast(128), [], [gb])
        dma(cx, "sp", bb[:], b.partition_broadcast(128), [], [bb])
        wv = w.rearrange("(k p) d -> p k d", p=128)
        for k in range(nch):
            load_weight_bf16(cx, wsb[:, k, :], wv[:, k, :], stg, D, wsb, k)
        sv = srcT.rearrange("c p t -> p c t")
        xiv = x_in.rearrange("(n p) d -> n p d", p=128)
        xov = x_out.rearrange("(n p) d -> n p d", p=128)
        outs = []
        for i in range(ntok // 128):
            oTt, xt, xot, stt_ = oT[i % 2], xs[i % 2], xo[i % 2], stat[i % 2]
            dma(cx, "sp", oTt[:], sv[:, :, i * 128:(i + 1) * 128], [], [oTt])
            dma(cx, "sp", xt[:], xiv[i], [], [xt])
            for hd in range(2):
                po = p_o[(2 * i + hd) % 4]
                for c in range(nch):
                    mm(cx, po[:], oTt[:, c, :], wsb[:, c, hd * 512:(hd + 1) * 512], c == 0, c == nch - 1, [oTt, wsb], [po])
                stt(cx, "dve", r[:, hd * 512:(hd + 1) * 512], po[:], c_y, xt[:, hd * 512:(hd + 1) * 512], ALU.mult, ALU.add,
                    [po, xt], [(r, hd)])
            emit_ln(cx, r, [(r, 0), (r, 1)], xot, stt_, epst, gb, bb, xot)
            outs.append(dma(cx, "pool", xov[i], xot[:], [xot], []))
        return outs


TOPK = 256
NBIS = 18


def emit_attn_core(cx, ckvT_d, kidxT_d, qT_d, qiT_d, widx_d, qpos_d, w_uk, w_uv, oT_d, NQ, SK, natural=False):
    S = cx.S
    NT = NQ // 256
    nkb_max = SK // 128
    with contextlib.ExitStack() as st:
        ckvT = cx.sb([128, 2, SK], BF16, "ackv", st)
        kidxT = cx.sb([64, SK], BF16, "akidx", st)
        Vp = cx.sb([128, nkb_max, 128], BF16, "aV", st)
        score = cx.sb([128, SK], F32, "ascore", st)
        msk = cx.sb([128, SK], BF16, "amsk", st)
        inv = [[cx.sb([128, SK], mybir.dt.float8e4, "ainv", st) for _ in range(2)] for _ in range(2)]
        sel = [cx.sb([128, 512], mybir.dt.float8e4, "asel", st) for _ in range(2)]
        qabs = cx.sb([128, 8, 2, 512], BF16, "aqabs", st)
        qT = cx.sb([128, 8, 256], BF16, "aqT", st)
        qiT = cx.sb([64, 8, 256], BF16, "aqiT", st)
        oT = cx.sb([128, 8, 256], BF16, "aoT", st)
        pT = [cx.sb([128, 512], BF16, "apT", st) for _ in range(2)]
        rl = [cx.sb([128, 512], F32, "arl", st) for _ in range(2)]
        wuk = cx.sb([128, 8, 256], BF16, "awuk", st)
        wuv = cx.sb([128, 2, 16, 64], BF16, "awuv", st)
        stg = [cx.sb([128, 1024], F32, "astg", st) for _ in range(1)]
        iota_f = cx.sb([128, 512], F32, "aiof", st)
        pen = cx.sb([128, 512], F32, "apen", st)
        qpos = cx.sb([128, NQ // 128], F32, "aqpos", st)
        widx = [cx.sb([128, 8], F32, "awidx", st) for _ in range(2)]
        sm = cx.sb([128, 8], F32, "asm", st)
        rec = cx.sb([128, 256], F32, "arec", st)
        ones_b = cx.sb([128, 128], BF16, "aones", st)
        P = [cx.ps([128, 512], F32, "aP", st) for _ in range(8)]
        gen = [P[0], P[1]]
        gi = [0]

        def nextp():
            gi[0] += 1
            return gen[gi[0] % 2]

        memset(cx, "pool", ones_b[:], 1.0, [ones_b])
        for b_ in range(2):
            memset(cx, "pool", pen[:], 0.0, [pen])
            for h2 in range(2):
                c0_ = h2 * 256 + b_ * 128
                ts(cx, "dve", pen[:, c0_:c0_ + 128], cx.ident_f[:], -240.0, None, ALU.mult, None, [cx.ident_f, pen], [pen])
            S.op("dve", lambda e, b_=b_: e.tensor_copy(out=sel[b_][:], in_=pen[:], saturate=False), [pen], [sel[b_]])
        S.op("pool", lambda e: e.iota(pen[:].bitcast(mybir.dt.int32), pattern=[[1, 512]], base=0, channel_multiplier=0), (), [pen])
        cp(cx, "pool", iota_f[:], pen[:].bitcast(mybir.dt.int32), [pen], [iota_f])
        dma(cx, "sp", qpos[:], qpos_d, [], [qpos])
        for rc in range(2):
            dma(cx, "sp", ckvT[:, rc, :], ckvT_d[rc], [], [ckvT])
        dma(cx, "sp", kidxT[:], kidxT_d, [], [kidxT])
        wukv = w_uk.rearrange("(p h2) dh r -> (h2 dh) p r", h2=2)
        for hf in range(2):
            load_weight_bf16(cx, wuk[:, hf * 4:(hf + 1) * 4, :], wukv[:, hf * 4:(hf + 1) * 4, :], stg, 1024, wuk, hf)
        for rc in range(2):
            load_weight_bf16(cx, wuv[:, rc, :, :], w_uv[:, rc * 128:(rc + 1) * 128, :].rearrange("h r dv -> r h dv"),
                             stg, 1024, (wuv, rc), 1 + rc)
        qT_v = qT_d.rearrange("c p t -> p c t")
        qiT_v = qiT_d.rearrange("c p t -> p c t")
        oT_v = oT_d.rearrange("c p t -> p c t")
        widx_v = widx_d.rearrange("(n p) h -> n p h", p=128)

        FP8 = mybir.dt.float8e4
        side_p = [P[5], P[7]]
        si = [0]

        def nexts():
            si[0] += 1
            return side_p[si[0] % 2]

        def tile_dims(kq):
            nkc = (kq // 2 + 1) if natural else (kq + 1)
            return nkc, 4 * nkc, 512 * nkc

        def side_units(kq):
            nkc, nkb, L = tile_dims(kq)
            q0 = kq * 256
            U = []
            U.append(lambda: dma(cx, "sp", qiT[:], qiT_v[:, :, q0:q0 + 256], [], [qiT]))
            for b in range(2):
                wt = widx[b]
                U.append(lambda wt=wt, b=b: dma(cx, "sp", wt[:], widx_v[kq * 2 + b], [], [wt]))
                for kc in range(nkc):
                    for h in range(8):
                        def u(b=b, kc=kc, h=h, wt=wt):
                            sc = score[:, kc * 512:(kc + 1) * 512]
                            ps = nexts()
                            mm(cx, ps[:], qiT[:, h, b * 128:(b + 1) * 128], kidxT[:, kc * 512:(kc + 1) * 512], True, True,
                               [qiT, kidxT], [ps])
                            rt_ = rl[h % 2]
                            act(cx, rt_[:], ps[:], AF.Relu, [ps], [rt_])
                            if h == 0:
                                ts(cx, "dve", sc, rt_[:], wt[:, 0:1], None, ALU.mult, None, [rt_, wt], [(score, kc)])
                            else:
                                stt(cx, "dve", sc, rt_[:], wt[:, h:h + 1], sc, ALU.mult, ALU.add, [rt_, wt, (score, kc)], [(score, kc)])
                        U.append(u)
                skeys = [(score, kc) for kc in range(nkc)]

                def pen_u(b=b):
                    ts(cx, "dve", sm[:, 4:5], qpos[:, kq * 2 + b:kq * 2 + b + 1], float(-512 * (nkc - 1)), None, ALU.add, None,
                       [qpos], [(sm, 4)])
                    ts(cx, "dve", pen[:], iota_f[:], sm[:, 4:5], -30000.0, ALU.is_gt, ALU.mult, [iota_f, (sm, 4)], [pen])
                    lc = score[:, (nkc - 1) * 512:nkc * 512]
                    tt(cx, "dve", lc, lc, pen[:], ALU.add, [pen, (score, nkc - 1)], [(score, nkc - 1)])
                    memset(cx, "dve", sm[:, 0:1], 0.0, [(sm, 0)])
                U.append(pen_u)
                W = 128.0
                for it in range(NBIS):
                    def bis(W=W):
                        S.op("dve", lambda e: e.tensor_scalar(out=msk[:, 0:L], in0=score[:, 0:L], scalar1=sm[:, 0:1], scalar2=None,
                                                              op0=ALU.is_ge, op1=ALU.add, accum_out=sm[:, 1:2]),
                             skeys + [(sm, 0)], [msk, (sm, 1)])
                        ts(cx, "dve", sm[:, 2:3], sm[:, 1:2], TOPK - 0.5, W / 2, ALU.is_ge, ALU.mult, [(sm, 1)], [(sm, 2)])
                        stt(cx, "dve", sm[:, 0:1], sm[:, 0:1], -W / 4, sm[:, 2:3], ALU.add, ALU.add, [(sm, 0), (sm, 2)], [(sm, 0)])
                    U.append(bis)
                    W = W / 2

                def fin(W=W, b=b):
                    ts(cx, "dve", sm[:, 3:4], sm[:, 0:1], -W / 2, None, ALU.add, None, [(sm, 0)], [(sm, 3)])
                    iv = inv[kq % 2][b]
                    S.op("dve", lambda e: e.tensor_scalar(out=iv[:, 0:L], in0=score[:, 0:L], scalar1=sm[:, 3:4], scalar2=128.0,
                                                          op0=ALU.is_lt, op1=ALU.mult, saturate=False),
                         skeys + [(sm, 3)], [iv])
                U.append(fin)
            return U

        def attention(kq, side):
            nkc, nkb, L = tile_dims(kq)
            q0 = kq * 256
            n_iter = 8 * nkb
            rate = (len(side) + n_iter - 1) // n_iter if side else 0
            dma(cx, "sp", qT[:], qT_v[:, :, q0:q0 + 256], [], [qT])
            for h in range(16):
                p_, h2 = h // 2, h % 2
                pq = nextp()
                for rc in range(2):
                    mm(cx, pq[:, rc * 256:(rc + 1) * 256], wuk[h2 * 64:(h2 + 1) * 64, p_, rc * 128:(rc + 1) * 128],
                       qT[h2 * 64:(h2 + 1) * 64, p_, :], True, True, [wuk, qT], [pq])
                cp(cx, "act", qabs[:, p_, :, h2 * 256:(h2 + 1) * 256],
                   pq[:].rearrange("p (a b) -> p a b", a=2), [pq], [(qabs, h)])
            for p_ in range(8):
                for kb0 in range(0, nkb, 4):
                    pv = nextp()
                    for j in range(4):
                        for rc in range(2):
                            mm(cx, pv[:, j * 128:(j + 1) * 128], ckvT[:, rc, (kb0 + j) * 128:(kb0 + j + 1) * 128],
                               wuv[:, rc, 2 * p_:2 * p_ + 2, :].rearrange("p a b -> p (a b)"), rc == 0, rc == 1,
                               [ckvT, (wuv, rc)], [pv])
                    cp(cx, "act", Vp[:, kb0:kb0 + 4, :], pv[:].rearrange("p (a b) -> p a b", a=4), [pv], [Vp])

                def qk(kb):
                    pl = P[2 + kb % 2]
                    for rc in range(2):
                        mm(cx, pl[:], ckvT[:, rc, kb * 128:(kb + 1) * 128], qabs[:, p_, rc, :], rc == 0, False,
                           [ckvT, (qabs, 2 * p_), (qabs, 2 * p_ + 1)], [pl])
                    for b_ in range(2):
                        iv = inv[kq % 2][b_]
                        mm(cx, pl[:].rearrange("p (h b t) -> p h b t", h=2, b=2)[:, :, b_, :], iv[:, kb * 128:(kb + 1) * 128],
                           sel[b_][:].rearrange("p (h b t) -> p h b t", h=2, b=2)[:, :, b_, :], False, b_ == 1,
                           [iv, sel[b_]], [pl])

                qk(0)
                for kb in range(nkb):
                    pl = P[2 + kb % 2]
                    if kb + 1 < nkb:
                        qk(kb + 1)
                    pTt = pT[kb % 2]
                    act(cx, pTt[:], pl[:], AF.Exp, [pl], [pTt], scale=0.125)
                    first, last = kb == 0, kb == nkb - 1
                    mm(cx, P[4][:], Vp[:, kb, :], pTt[:], first, last, [Vp, pTt], [P[4]])
                    mm(cx, P[6][:], ones_b[:], pTt[:], first, last, [ones_b, pTt], [P[6]])
                    for _ in range(rate):
                        if side:
                            side.pop(0)()
                S.op("dve", lambda e: e.reciprocal(out=rec[0:64, :], in_=P[6][0:64, 0:256]), [P[6]], [(rec, 0)])
                S.op("dve", lambda e: e.reciprocal(out=rec[64:128, :], in_=P[6][64:128, 256:512]), [P[6]], [(rec, 1)])
                tt(cx, "dve", oT[0:64, p_, :], P[4][0:64, 0:256], rec[0:64, :], ALU.mult, [P[4], (rec, 0)], [(oT, p_, 0)])
                tt(cx, "dve", oT[64:128, p_, :], P[4][64:128, 256:512], rec[64:128, :], ALU.mult, [P[4], (rec, 1)], [(oT, p_, 1)])
            dma(cx, "pool", oT_v[:, :, q0:q0 + 256], oT[:], [(oT, p_, i) for p_ in range(8) for i in range(2)], [])
            while side:
                side.pop(0)()

        for u in side_units(0):
            u()
        for kq in range(NT):
            attention(kq, side_units(kq + 1) if kq + 1 < NT else [])


ATT_O1, ATT_O2, ATT_O3, ATT_O4 = 1024, 1280, 1792, 1856
RMS_EPS = 1e-6


def emit_attn_stage(cx, xq, xk, qpos_d, w_in, kv_g, w_uk, w_uv, w_o, g, b, x_out, NQ, SK, scr, natural=False):
    nc, S = cx.nc, cx.S
    ckvT_d, kidxT_d, qT_d, qiT_d, widx_d, oT_d = (scr[k] for k in ("ckvT", "kidxT", "qT", "qiT", "widx", "oT"))
    xk_v = xk.rearrange("(n p) d -> n p d", p=128)
    xq_v = xq.rearrange("(n p) d -> n p d", p=128)

    def post_ckv(cx, st, state, pt, blk):
        if "init" not in state:
            state["init"] = True
            state["gb"] = cx.sb([128, 256], F32, "kg", st)
            state["sq"] = cx.sb([128, 256], F32, "ksq", st)
            state["cn"] = [cx.sb([128, 256], BF16, "kcn", st) for _ in range(2)]
            state["cT"] = [cx.sb([128, 2, 128], BF16, "kcT", st) for _ in range(2)]
            state["sm"] = [cx.sb([128, 4], F32, "ksm", st) for _ in range(2)]
            state["eps"] = cx.sb([128, 1], F32, "keps", st)
            state["ptp"] = cx.ps([128, 512], BF16, "kptp", st)
            memset(cx, "pool", state["eps"][:], RMS_EPS, [state["eps"]])
            dma(cx, "sp", state["gb"][:], kv_g.partition_broadcast(128), [], [state["gb"]])
        gb_, sq, cn, cT, sm_, eps, ptp = (state["gb"], state["sq"], state["cn"][blk % 2], state["cT"][blk % 2],
                                        state["sm"][blk % 2], state["eps"], state["ptp"])
        act(cx, sq[:], pt[:, 0:256], AF.Square, [pt], [sq, (sm_, 0)], accum_out=sm_[:, 0:1])
        act(cx, sm_[:, 1:2], sm_[:, 0:1], AF.Sqrt, [(sm_, 0), eps], [(sm_, 1)], bias=eps[:], scale=1.0 / 256)
        cx.S.op("dve", lambda e: e.reciprocal(out=sm_[:, 2:3], in_=sm_[:, 1:2]), [(sm_, 1)], [(sm_, 2)])
        stt(cx, "dve", cn[:], pt[:, 0:256], sm_[:, 2:3], gb_[:], ALU.mult, ALU.mult, [pt, (sm_, 2), gb_], [cn])
        for rc in range(2):
            tr(cx, ptp[:, rc * 128:(rc + 1) * 128], cn[:, rc * 128:(rc + 1) * 128], cx.ident_b[:], [cn, cx.ident_b], [ptp])
        cp(cx, "act", cT[:], ptp[:, 0:256].rearrange("p (a b) -> p a b", a=2), [ptp], [cT])
        dma(cx, "pool", ckvT_d.rearrange("c p t -> p c t")[:, :, blk * 128:(blk + 1) * 128], cT[:], [cT], [])

    emit_proj(cx, lambda i: xk_v[i], SK, [w_in[:, ATT_O1:ATT_O2], w_in[:, ATT_O3:ATT_O4]],
              fm_specs=[(256, 64, BF16, lambda t: kidxT_d[:, t * 512:(t + 1) * 512])],
              tm_specs=[(0, 256, post_ckv)])
    S.barrier()

    def post_widx(cx, st, state, pt, blk):
        if "w" not in state:
            state["w"] = [cx.sb([128, 8], F32, "qw", st) for _ in range(2)]
        wt = state["w"][blk % 2]
        ts(cx, "dve", wt[:], pt[:, 0:8], (8 ** -0.5) * (64 ** -0.5), None, ALU.mult, None, [pt], [wt])
        dma(cx, "pool", widx_d[blk * 128:(blk + 1) * 128, :], wt[:], [wt], [])

    def mk_q(c):
        return lambda t: qT_d[c][:, t * 512:(t + 1) * 512]

    def mk_qi(c):
        return lambda t: qiT_d[c][:, t * 512:(t + 1) * 512]

    fm = [(c * 128, 128, BF16, mk_q(c)) for c in range(8)] + [(1024 + c * 64, 64, BF16, mk_qi(c)) for c in range(8)]
    emit_proj(cx, lambda i: xq_v[i], NQ, [w_in[:, 0:ATT_O1], w_in[:, ATT_O2:ATT_O3], w_in[:, ATT_O4:ATT_O4 + 8]],
              fm_specs=fm, tm_specs=[(1536, 8, post_widx)])
    S.barrier()
    emit_attn_core(cx, ckvT_d, kidxT_d, qT_d, qiT_d, widx_d, qpos_d, w_uk, w_uv, oT_d, NQ, SK, natural)
    S.barrier()
    outs = emit_outproj_ln(cx, oT_d, 8, w_o, xq, x_out, g, b, NQ, 1.0)
    return outs


def attn_scratch(nc, NQ, SK, tag=""):
    mk = lambda n, shp, dt_: nc.dram_tensor(n + tag, shp, dt_, kind="Internal").ap()
    return {"ckvT": mk("s_ckvT", [2, 128, SK], BF16), "kidxT": mk("s_kidxT", [64, SK], BF16),
            "qT": mk("s_qT", [8, 128, NQ], BF16), "qiT": mk("s_qiT", [8, 64, NQ], BF16),
            "widx": mk("s_widx", [NQ, 8], F32), "oT": mk("s_oT", [8, 128, NQ], BF16)}


def build_attn_program(NQ, SK):
    nc = bass.Bass("TRN2", target_bir_lowering=False)
    inp = lambda n, shp: nc.dram_tensor(n, shp, F32, kind="ExternalInput").ap()
    xq, xk = inp("xq", [NQ, D]), inp("xk", [SK, D])
    qpos = inp("qpos", [128, NQ // 128])
    w_in, kv_g = inp("w_in", [D, 1864]), inp("kv_g", [256])
    w_uk, w_uv, w_o = inp("w_uk", [16, 64, 256]), inp("w_uv", [16, 256, 64]), inp("w_o", [D, D])
    g, b = inp("g", [D]), inp("b", [D])
    x_out = nc.dram_tensor("x_out", [NQ, D], F32, kind="ExternalOutput").ap()
    scr = attn_scratch(nc, NQ, SK)
    with contextlib.ExitStack() as st:
        cx = Ctx(nc, st)
        setup_consts(cx)
        outs = emit_attn_stage(cx, xq, xk, qpos, w_in, kv_g, w_uk, w_uv, w_o, g, b, x_out, NQ, SK, scr)
        cx.S.emit(final_wait_ops=outs[-N_DMA_SEM:])
    return nc


def emit_ssd_core(cx, z_d, xbc_d, dt_d, cw_d, cb_d, dtb_d, alog_d, dsk_d, ng_d, yzT_d, SK):
    S = cx.S
    NCH = SK // 128
    with contextlib.ExitStack() as st:
        H = cx.sb([128, 1024], F32, "sH", st)
        Hb = cx.sb([128, 1024], BF16, "sHb", st)
        pre = [cx.sb([128, 515], F32, "spre", st) for _ in range(2)]
        acc = [cx.sb([128, 512], F32, "sacc", st) for _ in range(2)]
        xsT = cx.sb([128, 8, 512], F32, "sxsT", st)
        BT = cx.sb([128, 2, 512], BF16, "sBT", st)
        CT = cx.sb([128, 2, 512], BF16, "sCT", st)
        cw = cx.sb([128, 12, 4], F32, "scw", st)
        cb = cx.sb([128, 12], F32, "scb", st)
        dtb = cx.sb([128, 16], F32, "sdtb", st)
        a_bc = cx.sb([128, 16], F32, "sabc", st)
        d_bc = cx.sb([128, 16], F32, "sdbc", st)
        ng = cx.sb([128, 1024], F32, "sng", st)
        triu = cx.sb([128, 128], F32, "striu", st)
        mgt = cx.sb([128, 128], F32, "smgt", st)
        xs_tok = cx.sb([128, 1024], F32, "sxs", st)
        Btok = cx.sb([128, 2, 128], BF16, "sBtok", st)
        sm = [cx.sb([128, 8, 16], F32, "ssm", st) for _ in range(2)]
        G = [cx.sb([128, 128], F32, "sG", st) for _ in range(4)]
        dec = [cx.sb([128, 4, 128], F32, "sdec", st) for _ in range(2)]
        cbt = cx.sb([128, 2, 128], F32, "scbt", st)
        MT = cx.sb([128, 16, 128], BF16, "sMT", st)
        xdt = cx.sb([128, 1024], BF16, "sxdt", st)
        xdtd = cx.sb([128, 1024], BF16, "sxdtd", st)
        t1 = cx.sb([128, 1024], F32, "st1", st)
        xsD = cx.sb([128, 1024], F32, "sxsD", st)
        y = cx.sb([128, 1024], F32, "sy", st)
        zt = [cx.sb([128, 1024], F32, "szt", st) for _ in range(2)]
        zs = cx.sb([128, 1024], F32, "szs", st)
        yzn = cx.sb([128, 1024], BF16, "syzn", st)
        yzT = [cx.sb([128, 8, 128], BF16, "syzT", st) for _ in range(2)]
        st2 = [cx.sb([128, 8], F32, "sst2", st) for _ in range(2)]
        epst = cx.sb([128, 1], F32, "sepst", st)
        P = [cx.ps([128, 512], F32, "sP", st) for _ in range(8)]

        memset(cx, "pool", epst[:], RMS_EPS, [epst])
        memset(cx, "pool", H[:], 0.0, [H])
        memset(cx, "pool", Hb[:], 0.0, [Hb])
        S.op("pool", lambda e: e.affine_select(out=triu[:], in_=cx.ones_f[:], pattern=[[1, 128]], compare_op=ALU.is_ge,
                                               fill=0.0, base=0, channel_multiplier=-1), [cx.ones_f], [triu])
        S.op("pool", lambda e: e.affine_select(out=mgt[:], in_=cx.ones_f[:], pattern=[[-1, 128]], compare_op=ALU.is_gt,
                                               fill=0.0, base=0, channel_multiplier=1), [cx.ones_f], [mgt])
        dma(cx, "sp", cw[:], cw_d, [], [cw])
        dma(cx, "sp", cb[:], cb_d, [], [cb])
        dma(cx, "sp", dtb[:], dtb_d.partition_broadcast(128), [], [dtb])
        dma(cx, "sp", a_bc[:], alog_d.partition_broadcast(128), [], [a_bc])
        dma(cx, "sp", d_bc[:], dsk_d.partition_broadcast(128), [], [d_bc])
        dma(cx, "sp", ng[:], ng_d.partition_broadcast(128), [], [ng])
        act(cx, a_bc[:], a_bc[:], AF.Exp, [a_bc], [a_bc])
        ts(cx, "dve", a_bc[:], a_bc[:], -1.0, None, ALU.mult, None, [a_bc], [a_bc])
        z_v = z_d.rearrange("(n p) c -> n p c", p=128)
        dt_v = dt_d.rearrange("(n p) c -> n p c", p=128)
        yz_v = yzT_d.rearrange("c p t -> p c t")

        def bc3(ap2):
            return ap2.unsqueeze(2).to_broadcast([128, 16, 64])

        def v3(ap):
            return ap.rearrange("p (h d) -> p h d", h=16)

        for sc in range(SK // 512):
            t0 = sc * 512
            for cc in range(12):
                pt, ac = pre[cc % 2], acc[cc % 2]
                if sc == 0:
                    memset(cx, "pool", pt[:, 0:3], 0.0, [pt])
                    dma(cx, "sp", pt[:, 3:515], xbc_d[cc][:, 0:512], [], [pt])
                else:
                    dma(cx, "sp", pt[:, 0:515], xbc_d[cc][:, t0 - 3:t0 + 512], [], [pt])
                ts(cx, "dve", ac[:], pt[:, 0:512], cw[:, cc, 0:1], None, ALU.mult, None, [pt, cw], [ac])
                for k in range(1, 4):
                    stt(cx, "dve", ac[:], pt[:, k:k + 512], cw[:, cc, k:k + 1], ac[:], ALU.mult, ALU.add, [pt, cw, ac], [ac])
                if cc < 8:
                    dst, key = xsT[:, cc, :], (xsT, cc)
                elif cc < 10:
                    dst, key = BT[:, cc - 8, :], (BT, cc - 8)
                else:
                    dst, key = CT[:, cc - 10, :], (CT, cc - 10)
                act(cx, dst, ac[:], AF.Silu, [ac, cb], [key], bias=cb[:, cc:cc + 1])
            for ch in range(4):
                c = sc * 4 + ch
                c0 = ch * 128
                s_ = sm[c % 2]
                for k in range(8):
                    tr(cx, P[k // 4][:, (k % 4) * 128:(k % 4 + 1) * 128], xsT[:, k, c0:c0 + 128], cx.ident_f[:],
                       [(xsT, k), cx.ident_f], [P[k // 4]])
                for hf in range(2):
                    cp(cx, "act", xs_tok[:, hf * 512:(hf + 1) * 512], P[hf][:], [P[hf]], [(xs_tok, hf)])
                xk_ = [(xs_tok, 0), (xs_tok, 1)]
                p2b = P[2][:].bitcast(BF16)
                for g_ in range(2):
                    tr(cx, p2b[:, g_ * 128:(g_ + 1) * 128], BT[:, g_, c0:c0 + 128], cx.ident_b[:], [(BT, g_), cx.ident_b], [P[2]])
                cp(cx, "dve", Btok[:], p2b[:, 0:256].rearrange("p (a b) -> p a b", a=2), [P[2]], [Btok])
                dma(cx, "sp", s_[:, 0, :], dt_v[c], [], [(s_, 0)])
                tt(cx, "dve", s_[:, 0, :], s_[:, 0, :], dtb[:], ALU.add, [(s_, 0), dtb], [(s_, 0)])
                act(cx, s_[:, 1, :], s_[:, 0, :], AF.Exp, [(s_, 0)], [(s_, 1)])
                act(cx, s_[:, 1, :], s_[:, 1, :], AF.Ln, [(s_, 1)], [(s_, 1)], bias=1.0)
                tt(cx, "dve", s_[:, 2, :], s_[:, 1, :], a_bc[:], ALU.mult, [(s_, 1), a_bc], [(s_, 2)])
                mm(cx, P[2][:, 256:272], triu[:], s_[:, 2, :], True, True, [triu, (s_, 2)], [P[2]])
                mm(cx, P[2][:, 272:288], cx.ones_f[:], s_[:, 2, :], True, True, [cx.ones_f, (s_, 2)], [P[2]])
                cp(cx, "dve", s_[:, 3, :], P[2][:, 256:272], [P[2]], [(s_, 3)])
                act(cx, s_[:, 4, :], s_[:, 3, :], AF.Exp, [(s_, 3)], [(s_, 4)])
                tt(cx, "dve", s_[:, 5, :], P[2][:, 272:288], s_[:, 3, :], ALU.subtract, [P[2], (s_, 3)], [(s_, 5)])
                act(cx, s_[:, 5, :], s_[:, 5, :], AF.Exp, [(s_, 5)], [(s_, 5)])
                act(cx, s_[:, 6, :], P[2][:, 272:288], AF.Exp, [P[2]], [(s_, 6)])
                tt(cx, "dve", s_[:, 7, :], s_[:, 1, :], s_[:, 5, :], ALU.mult, [(s_, 1), (s_, 5)], [(s_, 7)])
                tt(cx, "dve", v3(xdt[:]), v3(xs_tok[:]), bc3(s_[:, 1, :]), ALU.mult, xk_ + [(s_, 1)], [xdt])
                tt(cx, "pool", v3(xdtd[:]), v3(xs_tok[:]), bc3(s_[:, 7, :]), ALU.mult, xk_ + [(s_, 7)], [xdtd])
                for g_ in range(2):
                    mm(cx, P[2][:, 288 + g_ * 128:288 + (g_ + 1) * 128][:, 0:128] if False else P[7][:, g_ * 128:(g_ + 1) * 128],
                       BT[:, g_, c0:c0 + 128], CT[:, g_, c0:c0 + 128], True, True, [(BT, g_), (CT, g_)], [P[7]])
                tt(cx, "dve", cbt[:], P[7][:, 0:256].rearrange("p (a b) -> p a b", a=2),
                   triu[:].unsqueeze(1).to_broadcast([128, 2, 128]), ALU.mult, [P[7], triu], [cbt])
                for h0 in range(0, 16, 4):
                    pseg = P[3 + (h0 // 4) % 2]
                    dc = dec[(h0 // 4) % 2]
                    for j in range(4):
                        h = h0 + j
                        ts(cx, "pool" if j % 2 else "dve", G[j][:], mgt[:], s_[:, 2, h:h + 1], None, ALU.mult, None,
                           [mgt, (s_, 2)], [G[j]])
                        mm(cx, pseg[:, j * 128:(j + 1) * 128], G[j][:], triu[:], True, True, [G[j], triu], [pseg])
                    act(cx, dc[:], pseg[:].rearrange("p (a b) -> p a b", a=4), AF.Exp, [pseg], [dc])
                    g_ = h0 // 8
                    tt(cx, "dve", MT[:, h0:h0 + 4, :], dc[:], cbt[:, g_:g_ + 1, :].to_broadcast([128, 4, 128]), ALU.mult,
                       [dc, cbt], [(MT, h0 // 4)])
                for h in range(16):
                    py = P[h // 8]
                    mm(cx, py[:, (h % 8) * 64:(h % 8 + 1) * 64], MT[:, h, :], xdt[:, h * 64:(h + 1) * 64], True, True,
                       [(MT, h // 4), xdt], [py])
                for g_ in range(2):
                    mm(cx, P[5 + g_][:], CT[:, g_, c0:c0 + 128], Hb[:, g_ * 512:(g_ + 1) * 512], True, True, [(CT, g_), Hb], [P[5 + g_]])
                for g_ in range(2):
                    tt(cx, "dve", t1[:, g_ * 512:(g_ + 1) * 512].rearrange("p (h d) -> p h d", h=8),
                       P[5 + g_][:].rearrange("p (h d) -> p h d", h=8),
                       s_[:, 4, g_ * 8:(g_ + 1) * 8].unsqueeze(2).to_broadcast([128, 8, 64]), ALU.mult,
                       [P[5 + g_], (s_, 4)], [(t1, g_)])
                tt(cx, "pool", v3(xsD[:]), v3(xs_tok[:]), bc3(d_bc[:]), ALU.mult, xk_ + [d_bc], [xsD])
                tt(cx, "pool", xsD[:], xsD[:], t1[:], ALU.add, [xsD, (t1, 0), (t1, 1)], [xsD])
                for g_ in range(2):
                    tt(cx, "dve", y[:, g_ * 512:(g_ + 1) * 512], P[g_][:], xsD[:, g_ * 512:(g_ + 1) * 512], ALU.add,
                       [P[g_], xsD], [(y, g_)])
                for g_ in range(2):
                    mm(cx, P[5 + g_][:], Btok[:, g_, :], xdtd[:, g_ * 512:(g_ + 1) * 512], True, True, [Btok, xdtd], [P[5 + g_]])
                tt(cx, "dve", v3(H[:]), v3(H[:]), bc3(s_[:, 6, :]), ALU.mult, [H, (s_, 6)], [H])
                for g_ in range(2):
                    tt(cx, "dve", H[:, g_ * 512:(g_ + 1) * 512], H[:, g_ * 512:(g_ + 1) * 512], P[5 + g_][:], ALU.add,
                       [H, P[5 + g_]], [H])
                cp(cx, "act", Hb[:], H[:], [H], [Hb])
                ztt = zt[c % 2]
                s2 = st2[c % 2]
                dma(cx, "sp", ztt[:], z_v[c], [], [ztt])
                act(cx, zs[:], ztt[:], AF.Silu, [ztt], [zs])
                tt(cx, "dve", y[:], y[:], zs[:], ALU.mult, [(y, 0), (y, 1), zs], [(y, 0), (y, 1)])
                for g_ in range(2):
                    act(cx, zs[:, g_ * 512:(g_ + 1) * 512], y[:, g_ * 512:(g_ + 1) * 512], AF.Square, [(y, g_)], [zs, (s2, g_)],
                        accum_out=s2[:, g_:g_ + 1])
                    act(cx, s2[:, 2 + g_:3 + g_], s2[:, g_:g_ + 1], AF.Sqrt, [(s2, g_), epst], [(s2, 2 + g_)], bias=epst[:],
                        scale=1.0 / 512)
                    S.op("dve", lambda e, g_=g_, s2=s2: e.reciprocal(out=s2[:, 4 + g_:5 + g_], in_=s2[:, 2 + g_:3 + g_]),
                         [(s2, 2 + g_)], [(s2, 4 + g_)])
                    stt(cx, "dve", yzn[:, g_ * 512:(g_ + 1) * 512], y[:, g_ * 512:(g_ + 1) * 512], s2[:, 4 + g_:5 + g_],
                        ng[:, g_ * 512:(g_ + 1) * 512], ALU.mult, ALU.mult, [(y, g_), (s2, 4 + g_), ng], [(yzn, g_)])
                p7b = P[7][:].bitcast(BF16)
                yT = yzT[c % 2]
                for k in range(8):
                    tr(cx, p7b[:, k * 128:(k + 1) * 128], yzn[:, k * 128:(k + 1) * 128], cx.ident_b[:],
                       [(yzn, k // 4), cx.ident_b], [P[7]])
                cp(cx, "act", yT[:], p7b[:].rearrange("p (a b) -> p a b", a=8), [P[7]], [yT])
                dma(cx, "pool", yz_v[:, :, c * 128:(c + 1) * 128], yT[:], [yT], [])


SSD_NW = 2576


def emit_ssd_stage(cx, xk, w_in_c, cw_d, cb_d, dtb_d, alog_d, dsk_d, ng_d, yzT_d, SK, scr):
    S = cx.S
    z_d, xbc_d, dt_d = scr["z"], scr["xbc"], scr["dt"]
    xk_v = xk.rearrange("(n p) d -> n p d", p=128)

    def post_z(half):
        def f(cx, st, state, pt, blk):
            if "z" not in state:
                state["z"] = [cx.sb([128, 512], F32, "zz", st) for _ in range(2)]
                state["i"] = 0
            state["i"] += 1
            zt_ = state["z"][state["i"] % 2]
            cp(cx, "act", zt_[:], pt[:, 0:512], [pt], [zt_])
            dma(cx, "pool", z_d[blk * 128:(blk + 1) * 128, half * 512:(half + 1) * 512], zt_[:], [zt_], [])
        return f

    def post_dt(cx, st, state, pt, blk):
        if "d" not in state:
            state["d"] = [cx.sb([128, 16], F32, "zd", st) for _ in range(2)]
        d_ = state["d"][blk % 2]
        cp(cx, "dve", d_[:], pt[:, 0:16], [pt], [d_])
        dma(cx, "pool", dt_d[blk * 128:(blk + 1) * 128, :], d_[:], [d_], [])

    def mk(c):
        return lambda t: xbc_d[c][:, t * 512:(t + 1) * 512]

    fm = [(1024 + c * 128, 128, F32, mk(c)) for c in range(12)]
    emit_proj(cx, lambda i: xk_v[i], SK, w_in_c if isinstance(w_in_c, list) else [w_in_c], fm_specs=fm,
              tm_specs=[(0, 512, post_z(0)), (512, 512, post_z(1)), (2560, 16, post_dt)])
    S.barrier()
    emit_ssd_core(cx, z_d, xbc_d, dt_d, cw_d, cb_d, dtb_d, alog_d, dsk_d, ng_d, yzT_d, SK)


def ssd_scratch(nc, SK, tag=""):
    mk = lambda n, shp, dt_: nc.dram_tensor(n + tag, shp, dt_, kind="Internal").ap()
    return {"z": mk("s_z", [SK, 1024], F32), "xbc": mk("s_xbc", [12, 128, SK], F32), "dt": mk("s_dt", [SK, 16], F32)}


def build_ssd_program(SK):
    nc = bass.Bass("TRN2", target_bir_lowering=False)
    inp = lambda n, shp: nc.dram_tensor(n, shp, F32, kind="ExternalInput").ap()
    xk = inp("xk", [SK, D])
    w_in_c = inp("w_in_c", [D, SSD_NW])
    cw, cb = inp("cw", [128, 12, 4]), inp("cb", [128, 12])
    dtb, alog, dsk, ng = inp("dtb", [16]), inp("alog", [16]), inp("dsk", [16]), inp("ng", [1024])
    yzT = nc.dram_tensor("yzT", [8, 128, SK], BF16, kind="ExternalOutput").ap()
    scr = ssd_scratch(nc, SK)
    with contextlib.ExitStack() as st:
        cx = Ctx(nc, st)
        setup_consts(cx)
        emit_ssd_stage(cx, xk, w_in_c, cw, cb, dtb, alog, dsk, ng, yzT, SK, scr)
        cx.S.emit(final_wait_ops=cx.S.dma_hist["pool"][-N_DMA_SEM:])
    return nc


def build_outproj_program(NQ, nch):
    nc = bass.Bass("TRN2", target_bir_lowering=False)
    inp = lambda n, shp: nc.dram_tensor(n, shp, F32, kind="ExternalInput").ap()
    srcT = nc.dram_tensor("srcT", [nch, 128, NQ], BF16, kind="ExternalInput").ap()
    w, xq, g, b = inp("w", [nch * 128, D]), inp("xq", [NQ, D]), inp("g", [D]), inp("b", [D])
    x_out = nc.dram_tensor("x_out", [NQ, D], F32, kind="ExternalOutput").ap()
    with contextlib.ExitStack() as st:
        cx = Ctx(nc, st)
        setup_consts(cx)
        outs = emit_outproj_ln(cx, srcT, nch, w, xq, x_out, g, b, NQ, 1.0)
        cx.S.emit(final_wait_ops=outs[-N_DMA_SEM:])
    return nc


SEQ = 8192
BATCH = 4
NQ_CORE = SEQ // 2


def _own_tokens(j):
    bl = []
    for k in range(SEQ // 512):
        bl += [4 * k, 4 * k + 3] if j == 0 else [4 * k + 1, 4 * k + 2]
    return np.concatenate([np.arange(b * 128, (b + 1) * 128) for b in bl])


def _ssd_core_inputs(j, w_in, conv_w, conv_b, dt_bias, a_log, d_skip, norm_g):
    cols = np.concatenate([np.arange(j * 1024, (j + 1) * 1024), 2048 + np.arange(j * 1024, (j + 1) * 1024),
                           4096 + np.arange(j * 256, (j + 1) * 256), 4608 + np.arange(j * 256, (j + 1) * 256),
                           5120 + np.arange(j * 16, (j + 1) * 16)])
    ch = np.concatenate([np.arange(j * 1024, (j + 1) * 1024), 2048 + np.arange(j * 256, (j + 1) * 256),
                         2560 + np.arange(j * 256, (j + 1) * 256)])
    cw = np.ascontiguousarray(conv_w[:, ch].T.reshape(12, 128, 4).transpose(1, 0, 2))
    cb = np.ascontiguousarray(conv_b[ch].reshape(12, 128).T)
    return {"w_in_c": np.ascontiguousarray(w_in[:, cols]), "cw": cw, "cb": cb,
            "dtb": np.ascontiguousarray(dt_bias[j * 16:(j + 1) * 16]), "alog": np.ascontiguousarray(a_log[j * 16:(j + 1) * 16]),
            "dsk": np.ascontiguousarray(d_skip[j * 16:(j + 1) * 16]), "ng": np.ascontiguousarray(norm_g[j * 1024:(j + 1) * 1024])}


_PROGS = {}


def _prog(name, fn):
    if name not in _PROGS:
        _PROGS[name] = fn()
    return _PROGS[name]


def kernel_unfused(x, ln_g, ln_b, ffn1_w_in, ffn1_w_out, ffn2_w_in, ffn2_w_out, attn_w_in, attn_kv_norm, attn_w_uk, attn_w_uv,
           attn_w_out, ssm_w_in, ssm_conv_w, ssm_conv_b, ssm_dt_bias, ssm_a_log, ssm_d, ssm_norm_g, ssm_w_out):
    f32 = lambda a: np.ascontiguousarray(np.asarray(a, dtype=np.float32))
    x = f32(x)
    cores = list(range(NCORES))
    tok = [_own_tokens(c % 2) for c in cores]
    qpos = [np.ascontiguousarray(tok[c].reshape(-1, 128).T.astype(np.float32)) for c in cores]
    x_own = [np.ascontiguousarray(x[c // 2][tok[c]]) for c in cores]

    def run(nc, in_maps, key):
        res = run_bass_kernel_spmd(nc, in_maps, core_ids=cores)
        return [r[key] for r in res.results]

    def full_seq(x_own):
        out = []
        for s in range(BATCH):
            xs = np.empty((SEQ, D), np.float32)
            for j in range(2):
                xs[tok[2 * s + j]] = x_own[2 * s + j]
            out.append(xs)
        return out

    def ffn(x_own, w_in, w_out, g, b):
        nc = _prog("ffn", lambda: build_ffn_program(NQ_CORE))
        return run(nc, [{"x_in": x_own[c], "w_in": f32(w_in), "w_out": f32(w_out), "g": f32(g), "b": f32(b)} for c in cores],
                   "x_out")

    for i in range(DEPTH):
        x_own = ffn(x_own, ffn1_w_in[i], ffn1_w_out[i], ln_g[i, 0], ln_b[i, 0])
        j_ = i // 2
        xk = full_seq(x_own)
        if i % 2 == 0:
            nc = _prog("attn", lambda: build_attn_program(NQ_CORE, SEQ))
            x_own = run(nc, [{"xq": x_own[c], "xk": xk[c // 2], "qpos": qpos[c], "w_in": f32(attn_w_in[j_]),
                              "kv_g": f32(attn_kv_norm[j_]), "w_uk": f32(attn_w_uk[j_]), "w_uv": f32(attn_w_uv[j_]),
                              "w_o": f32(attn_w_out[j_]), "g": f32(ln_g[i, 1]), "b": f32(ln_b[i, 1])} for c in cores], "x_out")
        else:
            nc = _prog("ssd", lambda: build_ssd_program(SEQ))
            ci = [_ssd_core_inputs(jj, f32(ssm_w_in[j_]), f32(ssm_conv_w[j_]), f32(ssm_conv_b[j_]), f32(ssm_dt_bias[j_]),
                                   f32(ssm_a_log[j_]), f32(ssm_d[j_]), f32(ssm_norm_g[j_])) for jj in range(2)]
            yz = run(nc, [dict(ci[c % 2], xk=xk[c // 2]) for c in cores], "yzT")
            nc2 = _prog("oproj", lambda: build_outproj_program(NQ_CORE, 16))
            maps = []
            for c in cores:
                s = c // 2
                full = np.concatenate([np.asarray(yz[2 * s]), np.asarray(yz[2 * s + 1])], axis=0)
                maps.append({"srcT": np.ascontiguousarray(full[:, :, tok[c]]), "w": f32(ssm_w_out[j_]), "xq": x_own[c],
                             "g": f32(ln_g[i, 1]), "b": f32(ln_b[i, 1])})
            x_own = run(nc2, maps, "x_out")
        x_own = ffn(x_own, ffn2_w_in[i], ffn2_w_out[i], ln_g[i, 2], ln_b[i, 2])
    out = np.stack(full_seq(x_own)).astype(np.float32)
    return out


def build_full_program():
    nc = bass.Bass("TRN2", target_bir_lowering=False)
    inp = lambda n, shp: nc.dram_tensor(n, shp, F32, kind="ExternalInput").ap()
    x = inp("x", [SEQ, D])
    qpos = inp("qpos", [128, SEQ // 128])
    ln_g, ln_b = inp("ln_g", [DEPTH, 3, D]), inp("ln_b", [DEPTH, 3, D])
    f1i, f1o = inp("ffn1_w_in", [DEPTH, D, 2 * DFF]), inp("ffn1_w_out", [DEPTH, DFF, D])
    f2i, f2o = inp("ffn2_w_in", [DEPTH, D, 2 * DFF]), inp("ffn2_w_out", [DEPTH, DFF, D])
    a_in, a_kv = inp("attn_w_in", [2, D, 1864]), inp("attn_kv_norm", [2, 256])
    a_uk, a_uv, a_o = inp("attn_w_uk", [2, 16, 64, 256]), inp("attn_w_uv", [2, 16, 256, 64]), inp("attn_w_out", [2, D, D])
    s_in = inp("ssm_w_in", [2, D, 5152])
    s_cw, s_cb = inp("ssm_cw", [2, 2, 128, 12, 4]), inp("ssm_cb", [2, 2, 128, 12])
    s_dtb, s_alog, s_d = inp("ssm_dt_bias", [2, 32]), inp("ssm_a_log", [2, 32]), inp("ssm_d", [2, 32])
    s_ng, s_o = inp("ssm_norm_g", [2, 2048]), inp("ssm_w_out", [2, 2048, D])
    out = nc.dram_tensor("out", [SEQ, D], F32, kind="ExternalOutput").ap()
    bufs = [nc.dram_tensor("xbuf%d" % i, [SEQ, D], F32, kind="Internal").ap() for i in range(3)]
    ascr = attn_scratch(nc, SEQ, SEQ)
    sscr = ssd_scratch(nc, SEQ)
    yzT = nc.dram_tensor("s_yzT", [16, 128, SEQ], BF16, kind="Internal").ap()
    with contextlib.ExitStack() as st:
        cx = Ctx(nc, st)
        S = cx.S
        setup_consts(cx)
        cur = x
        outs = None
        for i in range(DEPTH):
            j_ = i // 2
            b0, b1, b2 = bufs[0], bufs[1], bufs[2]
            emit_ffn(cx, cur, b1, f1i[i], f1o[i], ln_g[i, 0], ln_b[i, 0], SEQ)
            S.barrier()
            if i % 2 == 0:
                emit_attn_stage(cx, b1, b1, qpos, a_in[j_], a_kv[j_], a_uk[j_], a_uv[j_], a_o[j_], ln_g[i, 1], ln_b[i, 1], b2,
                                SEQ, SEQ, ascr, natural=True)
            else:
                w = s_in[j_]
                for jj in range(2):
                    wcols = [w[:, jj * 1024:(jj + 1) * 1024], w[:, 2048 + jj * 1024:2048 + (jj + 1) * 1024],
                             w[:, 4096 + jj * 256:4096 + (jj + 1) * 256], w[:, 4608 + jj * 256:4608 + (jj + 1) * 256],
                             w[:, 5120 + jj * 16:5120 + (jj + 1) * 16]]
                    emit_ssd_stage(cx, b1, wcols, s_cw[j_, jj], s_cb[j_, jj], s_dtb[j_, jj * 16:(jj + 1) * 16],
                                   s_alog[j_, jj * 16:(jj + 1) * 16], s_d[j_, jj * 16:(jj + 1) * 16],
                                   s_ng[j_, jj * 1024:(jj + 1) * 1024], yzT[jj * 8:(jj + 1) * 8], SEQ, sscr)
                    S.barrier()
                emit_outproj_ln(cx, yzT, 16, s_o[j_], b1, b2, ln_g[i, 1], ln_b[i, 1], SEQ, 1.0)
            S.barrier()
            last = i == DEPTH - 1
            dst = out if last else b0
            outs = emit_ffn(cx, b2, dst, f2i[i], f2o[i], ln_g[i, 2], ln_b[i, 2], SEQ)
            if not last:
                S.barrier()
            cur = b0
        print("ops per engine:", {e: len(S.ops[e]) for e in ENGS}, flush=True)
        S.emit(final_wait_ops=outs[-N_DMA_SEM:])
        print("max semval per segment:", S.max_semval, "-> per-sem max", {e: v // N_CSEM + CSEM_CH for e, v in S.max_semval.items()},
              "max dma sem value:", S.max_dval, flush=True)
    return nc


def _ssd_conv_layout(conv_w, conv_b):
    cw = np.zeros((2, 2, 128, 12, 4), np.float32)
    cb = np.zeros((2, 2, 128, 12), np.float32)
    for l in range(2):
        for j in range(2):
            ch = np.concatenate([np.arange(j * 1024, (j + 1) * 1024), 2048 + np.arange(j * 256, (j + 1) * 256),
                                 2560 + np.arange(j * 256, (j + 1) * 256)])
            cw[l, j] = conv_w[l][:, ch].T.reshape(12, 128, 4).transpose(1, 0, 2)
            cb[l, j] = conv_b[l][ch].reshape(12, 128).T
    return cw, cb


def kernel_fused(x, ln_g, ln_b, ffn1_w_in, ffn1_w_out, ffn2_w_in, ffn2_w_out, attn_w_in, attn_kv_norm, attn_w_uk, attn_w_uv,
                 attn_w_out, ssm_w_in, ssm_conv_w, ssm_conv_b, ssm_dt_bias, ssm_a_log, ssm_d, ssm_norm_g, ssm_w_out):
    f32 = lambda a: np.ascontiguousarray(np.asarray(a, dtype=np.float32))
    x = f32(x)
    cw, cb = _ssd_conv_layout(f32(ssm_conv_w), f32(ssm_conv_b))
    qpos = np.ascontiguousarray(np.arange(SEQ, dtype=np.float32).reshape(-1, 128).T)
    shared = {"qpos": qpos, "ln_g": f32(ln_g), "ln_b": f32(ln_b), "ffn1_w_in": f32(ffn1_w_in), "ffn1_w_out": f32(ffn1_w_out),
              "ffn2_w_in": f32(ffn2_w_in), "ffn2_w_out": f32(ffn2_w_out), "attn_w_in": f32(attn_w_in),
              "attn_kv_norm": f32(attn_kv_norm), "attn_w_uk": f32(attn_w_uk), "attn_w_uv": f32(attn_w_uv),
              "attn_w_out": f32(attn_w_out), "ssm_w_in": f32(ssm_w_in), "ssm_cw": cw, "ssm_cb": cb,
              "ssm_dt_bias": f32(ssm_dt_bias), "ssm_a_log": f32(ssm_a_log), "ssm_d": f32(ssm_d), "ssm_norm_g": f32(ssm_norm_g),
              "ssm_w_out": f32(ssm_w_out)}
    nc = _prog("full", build_full_program)
    in_maps = [dict(shared, x=np.ascontiguousarray(x[c % BATCH])) for c in range(NCORES)]
    res = run_bass_kernel_spmd(nc, in_maps, core_ids=list(range(NCORES)))
    return np.stack([np.asarray(res.results[c]["out"], dtype=np.float32) for c in range(BATCH)])


kernel = kernel_fused
```
